# Optimizing a Trainium2 kernel written in Bass

```python
import jax
import jax.numpy as jnp
from jax import lax
import numpy as np

D_MODEL = 1024
BATCH = 4
SEQ = 8192
DEPTH = 2

GRID_W = 64
CTX_LEN = 256
N_BRANCH = 4
BRANCH_W = 512
MLA_HEADS = 8
MLA_NOPE = 64
MLA_ROPE = 32
MLA_V = 64
MLA_QK = MLA_NOPE + MLA_ROPE
Q_LORA = 256
KV_LORA = 128
Q_BLOCK = 128
ROPE_BASE = 10000.0
NAT_HEADS = 8
NAT_DH = 64
NAT_KH = 8
NAT_KW = 16
HG_HEADS = 4
HG_DK = 128
HG_DV = 128
GLA_HEADS = 4
GLA_DK = 64
GLA_DV = 128
GLA_LOWRANK = 16
GLA_NORMALIZER = 16.0
CHUNK = 64
EPS = 1e-6

W_MLA = Q_LORA + KV_LORA + MLA_ROPE
W_NAT = 3 * NAT_HEADS * NAT_DH
W_HG = 3 * HG_HEADS * HG_DK + HG_HEADS * HG_DV
W_GLA = 2 * GLA_HEADS * GLA_DK + GLA_HEADS * GLA_DV + 2 * GLA_LOWRANK
SEGMENTS = (W_MLA, W_NAT, W_HG, W_GLA, BRANCH_W, BRANCH_W, BRANCH_W, BRANCH_W)
IN_W = W_MLA + W_NAT + W_HG + W_GLA + N_BRANCH * BRANCH_W

kernel_name = 'hybrid_diffusion_mla_nat_hgrn2_gla'


def rms_norm(x, g):
    xf = x.astype(jnp.float32)
    y = xf * lax.rsqrt(jnp.mean(xf * xf, axis=-1, keepdims=True) + EPS)
    return (y * g.astype(jnp.float32)).astype(x.dtype)


def split_cols(t, widths):
    out, start = [], 0
    for w in widths:
        out.append(t[..., start:start + w])
        start += w
    return out


def axial_rope(n_tok, dtype):
    quarter = MLA_ROPE // 4
    inv_freq = ROPE_BASE ** (-jnp.arange(quarter, dtype=jnp.float32) / quarter)
    t = jnp.arange(n_tok, dtype=jnp.int32)
    row = (t // GRID_W).astype(jnp.float32)
    col = (t % GRID_W).astype(jnp.float32)
    ang = jnp.concatenate([row[:, None] * inv_freq, col[:, None] * inv_freq], axis=-1)
    return jnp.cos(ang).astype(dtype), jnp.sin(ang).astype(dtype)


def apply_rope(x, cos, sin):
    x1, x2 = jnp.split(x, 2, axis=-1)
    cs, sn = cos[None, :, None, :], sin[None, :, None, :]
    return jnp.concatenate([x1 * cs - x2 * sn, x1 * sn + x2 * cs], axis=-1)


def softmax_attend(q, k, v, scale):
    s = jnp.einsum('bqhd,bkhd->bhqk', q, k).astype(jnp.float32) * scale
    p = jax.nn.softmax(s, axis=-1).astype(v.dtype)
    return jnp.einsum('bhqk,bkhd->bqhd', p, v)


def joint_attend(q_a, k_a, v_a, q_b, k_b, v_b, scale):
    s_a = jnp.einsum('bqhd,bkhd->bhqk', q_a, k_a).astype(jnp.float32)
    s_b = jnp.einsum('bqhd,bkhd->bhqk', q_b, k_b).astype(jnp.float32)
    p = jax.nn.softmax(jnp.concatenate([s_a, s_b], axis=-1) * scale, axis=-1).astype(v_a.dtype)
    n_a = k_a.shape[1]
    return (jnp.einsum('bhqk,bkhd->bqhd', p[..., :n_a], v_a)
            + jnp.einsum('bhqk,bkhd->bqhd', p[..., n_a:], v_b))


def blocked_joint_attend(q_a, q_b, k_a, v_a, k_b, v_b, scale):
    B, N, H, _ = q_a.shape

    def blocks(t):
        return t.reshape(B, N // Q_BLOCK, Q_BLOCK, H, t.shape[-1]).transpose(1, 0, 2, 3, 4)

    o = lax.map(lambda qs: joint_attend(qs[0], k_a, v_a, qs[1], k_b, v_b, scale), (blocks(q_a), blocks(q_b)))
    return o.transpose(1, 0, 2, 3, 4).reshape(B, N, H, v_a.shape[-1])


def mla_mixer(pl, pc, w_uq, w_ukv, g_cq, g_ckv, g_q, g_k, need_ctx):
    B, n = pl.shape[:2]

    def queries(c_q):
        q = (rms_norm(c_q, g_cq) @ w_uq).reshape(c_q.shape[0], c_q.shape[1], MLA_HEADS, MLA_QK)
        return rms_norm(q, g_q)

    def keys_values(c_kv, k_r):
        b, m = c_kv.shape[:2]
        kv = (rms_norm(c_kv, g_ckv) @ w_ukv).reshape(b, m, MLA_HEADS, MLA_NOPE + MLA_V)
        k_rope = jnp.broadcast_to(k_r[:, :, None, :], (b, m, MLA_HEADS, MLA_ROPE))
        k = jnp.concatenate([kv[..., :MLA_NOPE], k_rope], axis=-1)
        return rms_norm(k, g_k), kv[..., MLA_NOPE:]

    cq_l, ckv_l, kr_l = split_cols(pl, (Q_LORA, KV_LORA, MLA_ROPE))
    cq_c, ckv_c, kr_c = split_cols(pc, (Q_LORA, KV_LORA, MLA_ROPE))
    cos, sin = axial_rope(n, pl.dtype)

    def rotate(t):
        return jnp.concatenate([t[..., :MLA_NOPE], apply_rope(t[..., MLA_NOPE:], cos, sin)], axis=-1)

    q_plain = queries(cq_l)
    k_l, v_l = keys_values(ckv_l, kr_l)
    k_c, v_c = keys_values(ckv_c, kr_c)
    scale = MLA_QK ** -0.5
    o_l = blocked_joint_attend(rotate(q_plain), q_plain, rotate(k_l), v_l, k_c, v_c, scale)
    o_c = softmax_attend(queries(cq_c), k_c, v_c, scale).reshape(B, -1, BRANCH_W) if need_ctx else None
    return o_l.reshape(B, n, BRANCH_W), o_c


def nat_mixer(pl, pc, rpb, g_q, g_k, need_ctx):
    B, n = pl.shape[:2]
    rows = n // GRID_W
    kh, kw = min(NAT_KH, rows), min(NAT_KW, GRID_W)

    def heads(t, g=None):
        t = t.reshape(t.shape[0], t.shape[1], NAT_HEADS, NAT_DH)
        return t if g is None else rms_norm(t, g)

    q_l, k_l, v_l = split_cols(pl, (BRANCH_W,) * 3)
    q_c, k_c, v_c = split_cols(pc, (BRANCH_W,) * 3)
    k_c, v_c = heads(k_c, g_k), heads(v_c)
    grid = (B, rows, GRID_W, NAT_HEADS, NAT_DH)
    kg = heads(k_l, g_k).reshape(grid)
    vg = heads(v_l).reshape(grid)
    qg = heads(q_l, g_q).reshape(grid).transpose(1, 0, 2, 3, 4)
    scale = NAT_DH ** -0.5
    col = jnp.arange(GRID_W)
    c0 = jnp.clip(col - kw // 2, 0, GRID_W - kw)
    col_ok = (col[None, :] >= c0[:, None]) & (col[None, :] < c0[:, None] + kw)
    dc = jnp.clip(col[None, :] - col[:, None], -(NAT_KW - 1), NAT_KW - 1) + NAT_KW - 1

    def row_block(args):
        r, q_r = args
        r0 = jnp.clip(r - kh // 2, 0, rows - kh)
        k_r = lax.dynamic_slice_in_dim(kg, r0, kh, axis=1)
        v_r = lax.dynamic_slice_in_dim(vg, r0, kh, axis=1)
        s_win = jnp.einsum('bqhd,brkhd->bhqrk', q_r, k_r).astype(jnp.float32) * scale
        dr = r0 + jnp.arange(kh) - r + NAT_KH - 1
        bias = rpb[:, dr][:, :, dc].transpose(0, 2, 1, 3)
        s_win = jnp.where(col_ok[None, None, :, None, :], s_win + bias[None].astype(jnp.float32), -jnp.inf)
        s_ctx = jnp.einsum('bqhd,bkhd->bhqk', q_r, k_c).astype(jnp.float32) * scale
        s = jnp.concatenate([s_win.reshape(B, NAT_HEADS, GRID_W, kh * GRID_W), s_ctx], axis=-1)
        p = jax.nn.softmax(s, axis=-1).astype(v_r.dtype)
        p_win = p[..., :kh * GRID_W].reshape(B, NAT_HEADS, GRID_W, kh, GRID_W)
        return (jnp.einsum('bhqrk,brkhd->bqhd', p_win, v_r)
                + jnp.einsum('bhqk,bkhd->bqhd', p[..., kh * GRID_W:], v_c))

    og = lax.map(row_block, (jnp.arange(rows), qg))
    o_l = og.transpose(1, 0, 2, 3, 4).reshape(B, n, BRANCH_W)
    o_c = softmax_attend(heads(q_c, g_q), k_c, v_c, scale).reshape(B, -1, BRANCH_W) if need_ctx else None
    return o_l, o_c


def chunk_scan(q, k, v, log_a, s0):
    B, n, H, _ = k.shape
    dv = v.shape[-1]
    nc = n // CHUNK

    def chunks(t):
        return t.astype(jnp.float32).reshape(B, nc, CHUNK, H, t.shape[-1]).transpose(1, 0, 3, 2, 4)

    lower = jnp.tril(jnp.ones((CHUNK, CHUNK), dtype=bool))[:, :, None]

    def step(S, xs):
        kc, vc, ac = xs[:3]
        b = jnp.cumsum(ac, axis=2)
        b_end = b[:, :, -1:, :]
        S_new = (jnp.exp(b_end[:, :, 0, :])[..., None] * S
                 + jnp.einsum('bhjd,bhjv->bhdv', kc * jnp.exp(b_end - b), vc))
        if len(xs) == 3:
            return S_new, None
        qc = xs[3]
        rel = jnp.exp(jnp.where(lower, b[:, :, :, None, :] - b[:, :, None, :, :], -jnp.inf))
        scores = jnp.einsum('bhid,bhjd,bhijd->bhij', qc, kc, rel)
        o = jnp.einsum('bhij,bhjv->bhiv', scores, vc) + jnp.einsum('bhid,bhdv->bhiv', qc * jnp.exp(b), S)
        return S_new, o

    xs = (chunks(k), chunks(v), chunks(log_a))
    if q is None:
        S, _ = lax.scan(step, s0, xs)
        return None, S
    S, o = lax.scan(step, s0, xs + (chunks(q),))
    return o.transpose(1, 0, 3, 2, 4).reshape(B, n, H, dv).astype(v.dtype), S


def _dirn(t, d):
    if t is None or d == 0:
        return t
    return jnp.flip(t, axis=1)


def bidir_scan(lat, ctx, need_ctx):
    q_l, v_l, k_l, a_l = lat
    q_c, v_c, k_c, a_c = ctx
    B, _, H, dk = k_l[0].shape
    s0 = jnp.zeros((B, H, dk, v_l.shape[-1]), jnp.float32)
    o_lat, o_ctx = None, None
    for d in range(2):
        oc, s_ctx = chunk_scan(_dirn(q_c, d), _dirn(k_c[d], d), _dirn(v_c, d), _dirn(a_c[d], d), s0)
        ol, _ = chunk_scan(_dirn(q_l, d), _dirn(k_l[d], d), _dirn(v_l, d), _dirn(a_l[d], d), s_ctx)
        ol = _dirn(ol, d)
        o_lat = ol if o_lat is None else o_lat + ol
        if need_ctx:
            oc = _dirn(oc, d)
            o_ctx = oc if o_ctx is None else o_ctx + oc
    return o_lat, o_ctx


def hgrn2_mixer(pl, pc, lb, g_o, need_ctx):
    def feats(p, with_q):
        b, m = p.shape[:2]
        shp = (b, m, HG_HEADS, HG_DK)
        q, f_f, f_b, i = split_cols(p, (HG_HEADS * HG_DK,) * 3 + (HG_HEADS * HG_DV,))
        ks, las = [], []
        for d, f in enumerate((f_f, f_b)):
            f = f.astype(jnp.float32)
            lbd = lb[d]
            log_f = jnp.logaddexp(jnp.log(lbd), jnp.log1p(-lbd) + jax.nn.log_sigmoid(f))
            ks.append(((1.0 - lbd) * jax.nn.sigmoid(-f)).reshape(shp))
            las.append(log_f.reshape(shp))
        qq = (jax.nn.silu(q) * HG_DK ** -0.5).reshape(shp) if with_q else None
        return qq, i.reshape(b, m, HG_HEADS, HG_DV), ks, las

    o_l, o_c = bidir_scan(feats(pl, True), feats(pc, need_ctx), need_ctx)
    B, n = pl.shape[:2]
    o_l = rms_norm(o_l, g_o).reshape(B, n, BRANCH_W)
    if need_ctx:
        o_c = rms_norm(o_c, g_o).reshape(B, -1, BRANCH_W)
    return o_l, o_c


def gla_mixer(pl, pc, w2, b2, g_o, need_ctx):
    def feats(p, with_q):
        b, m = p.shape[:2]
        shp = (b, m, GLA_HEADS, GLA_DK)
        q, k, v, r_f, r_b = split_cols(p, (GLA_HEADS * GLA_DK,) * 2 + (GLA_HEADS * GLA_DV,) + (GLA_LOWRANK,) * 2)
        las = [(jax.nn.log_sigmoid((r @ w2[d] + b2[d]).astype(jnp.float32)) / GLA_NORMALIZER).reshape(shp)
               for d, r in enumerate((r_f, r_b))]
        k = k.reshape(shp)
        qq = (q * GLA_DK ** -0.5).reshape(shp) if with_q else None
        return qq, v.reshape(b, m, GLA_HEADS, GLA_DV), (k, k), las

    o_l, o_c = bidir_scan(feats(pl, True), feats(pc, need_ctx), need_ctx)
    B, n = pl.shape[:2]
    o_l = rms_norm(o_l, g_o).reshape(B, n, BRANCH_W)
    if need_ctx:
        o_c = rms_norm(o_c, g_o).reshape(B, -1, BRANCH_W)
    return o_l, o_c


def branch_merge(h, ys, zs, w_br, w_merge, b_merge, w_out):
    acc = None
    for br in range(N_BRANCH):
        gate = jax.nn.sigmoid(h @ w_merge[br] + b_merge[br])
        part = gate * ((ys[br] * jax.nn.silu(zs[br])) @ w_br[br])
        acc = part if acc is None else acc + part
    return acc @ w_out


def setup_inputs(seed: int = 0) -> dict:
    key = jax.random.key(seed)
    ks = iter(jax.random.split(key, 32))

    def nrm(shape, std):
        return jax.random.normal(next(ks), shape, jnp.float32) * std

    def gain(shape):
        return 1.0 + nrm(shape, 0.02)

    L, D = DEPTH, D_MODEL
    return {
        'x': nrm((BATCH, SEQ, D), 1.0),
        'c': nrm((BATCH, D), 1.0),
        'ctx': nrm((BATCH, CTX_LEN, D), 1.0),
        'c_ctx': nrm((D,), 1.0),
        'ada_w': nrm((L, D, 3 * D), 0.5 * D ** -0.5),
        'ada_b': nrm((L, 3 * D), 0.02),
        'norm_g': gain((L, D)),
        'w_in': nrm((L, D, IN_W), D ** -0.5),
        'mla_w_uq': nrm((L, Q_LORA, MLA_HEADS * MLA_QK), Q_LORA ** -0.5),
        'mla_w_ukv': nrm((L, KV_LORA, MLA_HEADS * (MLA_NOPE + MLA_V)), KV_LORA ** -0.5),
        'mla_g_cq': gain((L, Q_LORA)),
        'mla_g_ckv': gain((L, KV_LORA)),
        'mla_g_q': gain((L, MLA_QK)),
        'mla_g_k': gain((L, MLA_QK)),
        'nat_rpb': nrm((L, NAT_HEADS, 2 * NAT_KH - 1, 2 * NAT_KW - 1), 0.2),
        'nat_g_q': gain((L, NAT_DH)),
        'nat_g_k': gain((L, NAT_DH)),
        'hg_lb_logits': nrm((L, 2, HG_HEADS * HG_DK), 1.0),
        'hg_g_o': gain((L, HG_DV)),
        'gla_w2': nrm((L, 2, GLA_LOWRANK, GLA_HEADS * GLA_DK), GLA_LOWRANK ** -0.5),
        'gla_b2': nrm((L, 2, GLA_HEADS * GLA_DK), 0.1),
        'gla_g_o': gain((L, GLA_DV)),
        'w_br': nrm((L, N_BRANCH, BRANCH_W, D), BRANCH_W ** -0.5),
        'w_merge': nrm((L, N_BRANCH, D, D), D ** -0.5),
        'b_merge': nrm((L, N_BRANCH, D), 0.02),
        'w_out': nrm((L, D, D), D ** -0.5),
    }


def reference(x, c, ctx, c_ctx, ada_w, ada_b, norm_g, w_in, mla_w_uq, mla_w_ukv, mla_g_cq, mla_g_ckv, mla_g_q,
              mla_g_k, nat_rpb, nat_g_q, nat_g_k, hg_lb_logits, hg_g_o, gla_w2, gla_b2, gla_g_o, w_br, w_merge,
              b_merge, w_out):
    sm = jax.nn.softmax(hg_lb_logits.astype(jnp.float32), axis=0)
    lower_bounds = jnp.maximum(jnp.cumsum(sm, axis=0) - sm[0], 0.0)
    cx = ctx
    for l in range(DEPTH):
        need_ctx = l < DEPTH - 1
        sh, sc, gt = jnp.split(jax.nn.silu(c) @ ada_w[l] + ada_b[l], 3, axis=-1)
        sh_c, sc_c, gt_c = jnp.split(jax.nn.silu(c_ctx) @ ada_w[l] + ada_b[l], 3, axis=-1)
        h = rms_norm(x, norm_g[l]) * (1.0 + sc[:, None]) + sh[:, None]
        hc = rms_norm(cx, norm_g[l]) * (1.0 + sc_c) + sh_c
        seg_l = split_cols(h @ w_in[l], SEGMENTS)
        seg_c = split_cols(hc @ w_in[l], SEGMENTS)
        out_a = mla_mixer(seg_l[0], seg_c[0], mla_w_uq[l], mla_w_ukv[l], mla_g_cq[l], mla_g_ckv[l],
                          mla_g_q[l], mla_g_k[l], need_ctx)
        out_b = nat_mixer(seg_l[1], seg_c[1], nat_rpb[l], nat_g_q[l], nat_g_k[l], need_ctx)
        out_c = hgrn2_mixer(seg_l[2], seg_c[2], lower_bounds[l], hg_g_o[l], need_ctx)
        out_d = gla_mixer(seg_l[3], seg_c[3], gla_w2[l], gla_b2[l], gla_g_o[l], need_ctx)
        outs = (out_a, out_b, out_c, out_d)
        x = x + gt[:, None] * branch_merge(h, [o[0] for o in outs], seg_l[4:], w_br[l], w_merge[l],
                                           b_merge[l], w_out[l])
        if need_ctx:
            cx = cx + gt_c * branch_merge(hc, [o[1] for o in outs], seg_c[4:], w_br[l], w_merge[l],
                                          b_merge[l], w_out[l])
    return x
```

```python
import numpy as np
from contextlib import ExitStack
import concourse.bass as bass
import concourse.mybir as mybir
from concourse.bass_utils import run_bass_kernel_spmd

F32 = mybir.dt.float32
BF16 = mybir.dt.bfloat16
AF = mybir.ActivationFunctionType
ALU = mybir.AluOpType
AX = mybir.AxisListType

ENGS = ("pe", "act", "dve", "pool", "sp")
NDMASEM = 56


class Buf:
    __slots__ = ("name", "w", "r", "sem", "excl")

    def __init__(self, name):
        self.name = name
        self.excl = False
        self.w = None
        self.r = {}
        self.sem = None


class Op:
    __slots__ = ("eng", "fn", "waits", "signal", "token", "ndma")

    def __init__(self, eng, fn):
        self.eng = eng
        self.fn = fn
        self.waits = {}
        self.signal = False
        self.token = None
        self.ndma = 0


class Prog:
    def __init__(self, nc):
        self.nc = nc
        self.ges = ExitStack()
        self.es = None
        self.cnt = {}
        self.sigbase = {e: 0 for e in ENGS}
        self.ntile = 0
        self.sems = {}
        for e in ENGS:
            self.sems[e] = self.ges.enter_context(nc.semaphore("s_" + e))
        for i in range(NDMASEM):
            self.sems[f"d{i}"] = self.ges.enter_context(nc.semaphore(f"s_d{i}"))
        self.gbufs = []
        self.nphase = 0
        self._reset_phase()

    def _reset_phase(self):
        self.ops = {e: [] for e in ENGS}
        self.tok_op = {}
        self.pbufs = []
        self.ndsem = 0
        self.ndsem_sw = 0

    def _stack(self, glob):
        return self.ges if glob else self.es

    def sb(self, shape, dt, name="t", glob=False):
        self.ntile += 1
        t = self._stack(glob).enter_context(self.nc.sbuf_tensor(f"{name}_{self.ntile}", list(shape), dt))
        b = Buf(name)
        (self.gbufs if glob else self.pbufs).append(b)
        return t, b

    def ps(self, shape, dt, name="p"):
        self.ntile += 1
        t = self.es.enter_context(self.nc.psum_tensor(f"{name}_{self.ntile}", list(shape), dt))
        b = Buf(name)
        b.excl = True
        self.pbufs.append(b)
        return t, b

    def begin(self):
        self.es = ExitStack()
        self._reset_phase()

    def _dep(self, op, tok):
        if tok is None:
            return
        key, c = tok
        if key == "pe" and op.eng == "pe":
            return
        if op.waits.get(key, 0) < c:
            op.waits[key] = c
        if key in ENGS:
            self.tok_op[tok].signal = True

    def _track(self, o, reads, writes):
        ex = [b for b in reads if b.excl and o.eng != "pe"]
        if ex:
            reads = [b for b in reads if not (b.excl and o.eng != "pe")]
            writes = list(writes) + ex
        for b in reads:
            self._dep(o, b.w)
        for b in writes:
            self._dep(o, b.w)
            for t in b.r.items():
                self._dep(o, t)
        for b in reads:
            if b.r.get(o.token[0], 0) < o.token[1]:
                b.r[o.token[0]] = o.token[1]
        for b in writes:
            b.w = o.token
            b.r = {}

    def op(self, eng, fn, reads=(), writes=()):
        o = Op(eng, fn)
        c = self.cnt.get(eng, 0) + 1
        self.cnt[eng] = c
        o.token = (eng, c)
        self.tok_op[o.token] = o
        self._track(o, reads, writes)
        self.ops[eng].append(o)
        return o

    def dma(self, q, pairs, sbuf, load):
        sw = (q == "pool")
        if sbuf.sem is None or sbuf.sem[1] != self.nphase:
            if sw:
                assert self.ndsem_sw < NDMASEM // 2, "out of sw dma semaphores"
                sbuf.sem = (f"d{NDMASEM // 2 + self.ndsem_sw}", self.nphase, sw)
                self.ndsem_sw += 1
            else:
                assert self.ndsem < NDMASEM // 2, "out of hw dma semaphores"
                sbuf.sem = (f"d{self.ndsem}", self.nphase, sw)
                self.ndsem += 1
        assert sbuf.sem[2] == sw, "buffer used from both DMA queue kinds: " + sbuf.name
        key = sbuf.sem[0]

        def fn(e, pairs=pairs):
            return [e.dma_start(out=o_, in_=i_) for (o_, i_) in pairs]
        o = Op(q, fn)
        o.ndma = len(pairs)
        c = self.cnt.get(key, 0) + 16 * len(pairs)
        self.cnt[key] = c
        o.token = (key, c)
        o.signal = True
        if load:
            self._track(o, [], [sbuf])
        else:
            self._track(o, [sbuf], [])
        self.ops[q].append(o)
        return o

    def end(self):
        nc = self.nc
        for e in ENGS:
            for o in reversed(self.ops[e]):
                if o.ndma == 0 and o.fn is not None:
                    o.signal = True
                    break
        snap = dict(self.cnt)
        for e in ENGS:
            o = Op(e, None)
            for key, c in snap.items():
                if key == e and e == "pe":
                    continue
                if c > 0:
                    o.waits[key] = c
            self.ops[e].append(o)
        sig_index = {}
        last_sig = dict(self.sigbase)
        for e in ENGS:
            k = self.sigbase[e]
            for o in self.ops[e]:
                if o.fn is None or o.ndma:
                    continue
                if o.signal:
                    k += 1
                sig_index[o.token] = k
            last_sig[e] = k
        prog = self
        sems = self.sems
        base = dict(self.sigbase)

        def section(e):
            def body(eng):
                waited = {}
                for o in prog.ops[e]:
                    for key, c in o.waits.items():
                        if key in ENGS:
                            v = sig_index.get((key, c))
                            if v is None:
                                continue
                            if v <= base[key]:
                                continue
                        else:
                            v = c
                        if waited.get(key, 0) < v:
                            eng.wait_ge(sems[key], v)
                            waited[key] = v
                    if o.fn is None:
                        continue
                    r = o.fn(eng)
                    if o.ndma:
                        for ins in r:
                            ins.then_inc(sems[o.token[0]], 16)
                    elif o.signal:
                        r.then_inc(sems[e], 1)
            return body

        with nc.Block() as block:
            bl = {"pe": block.tensor, "act": block.scalar, "dve": block.vector,
                  "pool": block.gpsimd, "sp": block.sync}
            for e in ENGS:
                bl[e](section(e))
        self.sigbase = last_sig
        for b in self.gbufs + self.pbufs:
            b.w = None
            b.r = {}
        self.es.close()
        self.es = None
        self.nphase += 1

    def finish(self):
        self.ges.close()


D = 1024
NCTX = 256
NLAT = 8192
T = NCTX + NLAT
NT = T // 128
IN_W = 7104
EPS = 1e-6
SEG = {"mla": (0, 416), "nat": (416, 1536), "hg": (1952, 2048), "gla": (4000, 1056), "z": (5056, 2048)}


def dram_in(nc, name, shape, dt=F32):
    return nc.dram_tensor(name, list(shape), dt, kind="ExternalInput").ap()


def dram_scr(nc, name, shape, dt):
    return nc.dram_tensor(name, list(shape), dt, kind="Internal").ap()


class Ctx:
    pass


def phase0_adaln(P, C):
    P.begin()
    cv, cvb = P.sb([128, 8, 2], F32, "cv")
    sl, slb = P.sb([128, 8, 2], F32, "sl")
    ab, abb = P.sb([128, 2, 24], F32, "ab")
    ng, ngb = P.sb([128, 2, 8], F32, "ng")
    P.dma("sp", [(cv[:], C.cvec)], cvb, True)
    P.dma("sp", [(ab[:], C.ada_b_fm)], abb, True)
    P.dma("sp", [(ng[:], C.norm_g_fm)], ngb, True)
    P.op("act", lambda e: e.activation(out=sl[:], in_=cv[:], func=AF.Silu), [cvb], [slb])
    wbuf = [P.sb([128, 8, 1024], F32, f"adaw{i}") for i in range(2)]
    pp = [P.ps([128, 512], F32, f"pm{i}") for i in range(2)]
    it = 0
    for l in range(2):
        for part in range(3):
            w, wb = wbuf[it % 2]
            src = C.ada_w[l, :, part * 1024:(part + 1) * 1024].rearrange("(k p) n -> p k n", p=128)
            P.dma("sp" if it % 2 == 0 else "pool", [(w[:, 0:4, :], src[:, 0:4, :]), (w[:, 4:8, :], src[:, 4:8, :])], wb, True)
            for j in range(8):
                ps_, psb = pp[j % 2]
                for k in range(8):
                    P.op("pe", lambda e, ps_=ps_, w=w, k=k, j=j: e.matmul(ps_[:, 0:2], lhsT=w[:, k, j * 128:(j + 1) * 128], rhs=sl[:, k, :],
                                                                       start=(k == 0), stop=(k == 7)), [wb, slb], [psb])
                P.op("act", lambda e, ps_=ps_, l=l, part=part, j=j: e.activation(out=C.modt[:, l, part, j, :], in_=ps_[:, 0:2], func=AF.Identity,
                                                                             bias=ab[:, l, part * 8 + j:part * 8 + j + 1], scale=1.0),
                     [psb, abb], [C.modtb])
            it += 1
    for l in range(2):
        for v in range(2):
            P.op("dve", lambda e, l=l, v=v: e.scalar_tensor_tensor(out=C.A[:, l, :, v], in0=C.modt[:, l, 1, :, v], scalar=1.0, in1=ng[:, l, :],
                                                                  op0=ALU.add, op1=ALU.mult), [C.modtb, ngb], [C.Ab])
    P.end()


def phase1_inproj(P, C, l):
    src_rows = C.xsrc[l]
    halves = [(0, 3552), (3552, 3552)]
    for hi, (c0, cw) in enumerate(halves):
        P.begin()
        ident, identb = P.sb([128, 128], BF16, "ident")
        P.dma("sp", [(ident[:], C.ident)], identb, True)
        wsb, wsbb = P.sb([128, 8, cw], BF16, "win")
        stg = [P.sb([128, 1776], F32, f"wstg{i}") for i in range(2)]
        n = 0
        for k in range(8):
            for q in range(cw // 1776):
                s_, sb_ = stg[n % 2]
                P.dma("sp" if n % 2 == 0 else "pool", [(s_[:], C.w_in[l, k * 128:(k + 1) * 128, c0 + q * 1776:c0 + (q + 1) * 1776])], sb_, True)
                eng = "pool" if n % 2 == 0 else "dve"
                P.op(eng, lambda e, s_=s_, k=k, q=q: e.tensor_copy(out=wsb[:, k, q * 1776:(q + 1) * 1776], in_=s_[:]), [sb_], [wsbb])
                n += 1
        xt = [P.sb([128, 1024], F32, f"xt{i}") for i in range(2)]
        sq, sqb = P.sb([128, 1024], F32, "sq")
        st = [P.sb([128, 4], F32, f"st{i}") for i in range(2)]
        xn = [P.sb([128, 1024], BF16, f"xn{i}") for i in range(2)]
        hT = [P.sb([128, 8, 128], BF16, f"hT{i}") for i in range(3)]
        og = [P.sb([128, cw], F32, f"og{i}") for i in range(2)]
        pT = [P.ps([128, 8, 128], BF16, f"pT{i}") for i in range(1)]
        pY = [P.ps([128, 512], F32, f"pY{i}") for i in range(6)]
        ngrp = (cw + 511) // 512
        ev = 0
        for i in range(NT):
            v = 1 if i < 2 else 0
            h_, hb = hT[i % 3]
            if hi == 0:
                x_, xb = xt[i % 2]
                s_, sb_ = st[i % 2]
                n_, nb = xn[i % 2]
                p_, pb = pT[0]
                P.dma("sp", [(x_[:], src_rows(i))], xb, True)
                P.op("act", lambda e, x_=x_, s_=s_: e.activation(out=sq[:], in_=x_[:], func=AF.Square, accum_out=s_[:, 0:1]), [xb], [sqb, sb_])
                P.op("dve", lambda e, s_=s_: e.tensor_scalar(out=s_[:, 1:2], in0=s_[:, 0:1], scalar1=1.0 / D, scalar2=EPS, op0=ALU.mult, op1=ALU.add), [sb_], [sb_])
                P.op("act", lambda e, s_=s_: e.activation(out=s_[:, 2:3], in_=s_[:, 1:2], func=AF.Ln), [sb_], [sb_])
                P.op("act", lambda e, s_=s_: e.activation(out=s_[:, 3:4], in_=s_[:, 2:3], func=AF.Exp, scale=-0.5), [sb_], [sb_])
                P.op("pool", lambda e, x_=x_, s_=s_, n_=n_: e.tensor_scalar(out=n_[:], in0=x_[:], scalar1=s_[:, 3:4], scalar2=None, op0=ALU.mult), [xb, sb_], [nb])
                for k in range(8):
                    P.op("pe", lambda e, k=k, p_=p_, n_=n_: e.transpose(out=p_[:, k, :], in_=n_[:, k * 128:(k + 1) * 128], identity=ident[:]), [nb, identb], [pb])
                for k in range(8):
                    P.op("act", lambda e, k=k, p_=p_, h_=h_, v=v: e.activation(out=h_[:, k, :], in_=p_[:, k, :], func=AF.Identity,
                                                                            scale=C.A[:, l, k, v:v + 1], bias=C.modt[:, l, 0, k, v:v + 1]), [pb, C.Ab, C.modtb], [hb])
                P.dma("pool", [(C.hT[i], h_[:])], hb, False)
            else:
                P.dma("sp", [(h_[:], C.hT[i])], hb, True)
            o_, ob = og[i % 2]
            for g in range(ngrp):
                gw = min(512, cw - g * 512)
                y_, yb = pY[ev % 6]
                for k in range(8):
                    P.op("pe", lambda e, y_=y_, h_=h_, k=k, g=g, gw=gw: e.matmul(y_[:, 0:gw], lhsT=h_[:, k, :], rhs=wsb[:, k, g * 512:g * 512 + gw],
                                                                              start=(k == 0), stop=(k == 7)), [hb, wsbb], [yb])
                if ev % 2 == 0:
                    P.op("act", lambda e, y_=y_, o_=o_, g=g, gw=gw: e.activation(func=AF.Identity, out=o_[:, g * 512:g * 512 + gw], in_=y_[:, 0:gw]), [yb], [ob])
                else:
                    P.op("dve", lambda e, y_=y_, o_=o_, g=g, gw=gw: e.tensor_copy(out=o_[:, g * 512:g * 512 + gw], in_=y_[:, 0:gw]), [yb], [ob])
                ev += 1
            P.dma("sp" if i % 2 == 0 else "pool", [(C.proj[i * 128:(i + 1) * 128, c0:c0 + cw], o_[:])], ob, False)
        P.end()


def groups():
    g = [(0, 2, 0)]
    for j in range(16):
        g.append((2 + 4 * j, 4, 1 + j))
    return g


def rstd_from_ms(P, st, stb, n, eps_done=False):
    P.op("dve", lambda e: e.tensor_scalar(out=st[:, n:2 * n], in0=st[:, 0:n], scalar1=1.0, scalar2=EPS, op0=ALU.mult, op1=ALU.add), [stb], [stb])
    P.op("act", lambda e: e.activation(out=st[:, n:2 * n], in_=st[:, n:2 * n], func=AF.Ln), [stb], [stb])
    P.op("act", lambda e: e.activation(out=st[:, 2 * n:3 * n], in_=st[:, n:2 * n], func=AF.Exp, scale=-0.5), [stb], [stb])


def head_norm(P, src, srcb, H, dh, gain, gainb, out, outb, sq, sqb, st, stb, tmp, tmpb, eng2="dve"):
    W = H * dh
    P.op("act", lambda e: e.activation(out=sq[:, 0:W], in_=src, func=AF.Square, scale=float(dh) ** -0.5), [srcb], [sqb])
    P.op("dve", lambda e: e.tensor_reduce(out=st[:, 0:H], in_=sq[:, 0:W].rearrange("p (h d) -> p h d", h=H), op=ALU.add, axis=AX.X), [sqb], [stb])
    rstd_from_ms(P, st, stb, H)
    P.op("dve", lambda e: e.tensor_tensor(out=tmp[:, 0:W].rearrange("p (h d) -> p h d", h=H), in0=src.rearrange("p (h d) -> p h d", h=H),
                                          in1=st[:, 2 * H:3 * H].unsqueeze(2).to_broadcast([128, H, dh]), op=ALU.mult), [srcb, stb], [tmpb])
    P.op(eng2, lambda e: e.tensor_tensor(out=out, in0=tmp[:, 0:W].rearrange("p (h d) -> p h d", h=H),
                                         in1=gain[:, None, :].to_broadcast([128, H, dh]), op=ALU.mult), [tmpb, gainb], [outb])


def rope(P, x, xb, cs, csb, out, outb, tt, ttb, H):
    x1, x2 = x[:, :, 0:16], x[:, :, 16:32]
    cos = cs[:, None, 0:16].to_broadcast([128, H, 16])
    sin = cs[:, None, 16:32].to_broadcast([128, H, 16])
    P.op("dve", lambda e: e.tensor_tensor(out=tt[:, 0], in0=x1, in1=cos, op=ALU.mult), [xb, csb], [ttb])
    P.op("dve", lambda e: e.tensor_tensor(out=tt[:, 1], in0=x2, in1=sin, op=ALU.mult), [xb, csb], [ttb])
    P.op("dve", lambda e: e.tensor_tensor(out=tt[:, 2], in0=x1, in1=sin, op=ALU.mult), [xb, csb], [ttb])
    P.op("dve", lambda e: e.tensor_tensor(out=tt[:, 3], in0=x2, in1=cos, op=ALU.mult), [xb, csb], [ttb])
    P.op("dve", lambda e: e.tensor_tensor(out=out[:, :, 0:16], in0=tt[:, 0], in1=tt[:, 1], op=ALU.subtract), [ttb], [outb])
    P.op("dve", lambda e: e.tensor_tensor(out=out[:, :, 16:32], in0=tt[:, 2], in1=tt[:, 3], op=ALU.add), [ttb], [outb])


def load_cast(P, dst, dstb, src, shape, q="sp", eng="pool", name="wst"):
    s_, sb_ = P.sb(shape, F32, name)
    P.dma(q, [(s_[:], src)], sb_, True)
    P.op(eng, lambda e: e.tensor_copy(out=dst, in_=s_[:]), [sb_], [dstb])


def mla_prep(P, C, l):
    P.begin()
    ident, identb = P.sb([128, 128], BF16, "ident")
    P.dma("sp", [(ident[:], C.ident)], identb, True)
    wuq, wuqb = P.sb([128, 2, 768], BF16, "wuq")
    wukv, wukvb = P.sb([128, 1024], BF16, "wukv")
    load_cast(P, wuq[:], wuqb, C.mla_w_uq[l].rearrange("(k p) n -> p k n", p=128), [128, 2, 768], "sp", "pool", "wst1")
    load_cast(P, wukv[:], wukvb, C.mla_w_ukv[l], [128, 1024], "pool", "dve", "wst2")
    gc, gcb = P.sb([128, 384], F32, "gc")
    gq, gqb = P.sb([128, 96], F32, "gq")
    gk, gkb = P.sb([128, 96], F32, "gk")
    P.dma("sp", [(gc[:], C.mla_gc_rep[l])], gcb, True)
    P.dma("sp", [(gq[:], C.mla_gq_rep[l])], gqb, True)
    P.dma("sp", [(gk[:], C.mla_gk_rep[l])], gkb, True)
    seg = [P.sb([128, 416], F32, f"seg{i}") for i in range(2)]
    cs = [P.sb([128, 32], F32, f"cs{i}") for i in range(2)]
    sq, sqb = P.sb([128, 768], F32, "sq")
    st0, st0b = P.sb([128, 8], F32, "st0")
    stq, stqb = P.sb([128, 24], F32, "stq")
    stk, stkb = P.sb([128, 24], F32, "stk")
    cn, cnb = P.sb([128, 384], BF16, "cn")
    cT, cTb = P.sb([128, 3, 128], BF16, "cT")
    qf, qfb = P.sb([128, 768], F32, "qf")
    qn, qnb = P.sb([128, 8, 96], F32, "qn")
    kf, kfb = P.sb([128, 768], F32, "kf")
    kn, knb = P.sb([128, 8, 96], F32, "kn")
    tmp, tmpb = P.sb([128, 768], F32, "tmp")
    tmp2, tmp2b = P.sb([128, 768], F32, "tmp2")
    qb_, qbb = P.sb([128, 2, 8, 96], BF16, "qb")
    kb_, kbb = P.sb([128, 8, 96], BF16, "kb")
    tt, ttb = P.sb([128, 4, 8, 16], F32, "tt")
    tt2, tt2b = P.sb([128, 4, 8, 16], F32, "tt2")
    qst = [P.sb([96, 2, 8, 512], BF16, f"qst{i}") for i in range(2)]
    kst = [P.sb([96, 8, 512], BF16, f"kst{i}") for i in range(2)]
    vst = [P.sb([128, 8, 4, 66], BF16, f"vst{i}") for i in range(2)]
    for i in range(2):
        P.op("dve", lambda e, i=i: e.memset(vst[i][0][:], 1.0), [], [vst[i][1]])
    pTc, pTcb = P.ps([128, 8, 128], BF16, "pTc")
    pq = [P.ps([128, 512], F32, f"pq{i}") for i in range(2)]
    pk = [P.ps([128, 512], F32, f"pk{i}") for i in range(2)]
    pTq, pTqb = P.ps([128, 16, 128], BF16, "pTq")
    pTk, pTkb = P.ps([128, 8, 128], BF16, "pTk")
    for gi, (t0, nt, qt) in enumerate(groups()[:C.max_groups]):
        qs, qsb = qst[gi % 2]
        ks, ksb = kst[gi % 2]
        vs, vsb = vst[gi % 2]
        for s in range(nt):
            i = t0 + s
            sg, sgb = seg[i % 2]
            c_, c_b = cs[i % 2]
            P.dma("sp", [(sg[:], C.proj[i * 128:(i + 1) * 128, 0:416])], sgb, True)
            P.dma("sp", [(c_[:], C.rope_cs[i * 128:(i + 1) * 128, :])], c_b, True)
            P.op("act", lambda e, sg=sg: e.activation(out=sq[:, 0:256], in_=sg[:, 0:256], func=AF.Square, scale=1.0 / 16, accum_out=st0[:, 0:1]), [sgb], [sqb, st0b])
            P.op("act", lambda e, sg=sg: e.activation(out=sq[:, 256:384], in_=sg[:, 256:384], func=AF.Square, scale=128 ** -0.5, accum_out=st0[:, 1:2]), [sgb], [sqb, st0b])
            rstd_from_ms(P, st0, st0b, 2)
            P.op("dve", lambda e, sg=sg: e.scalar_tensor_tensor(out=cn[:, 0:256], in0=sg[:, 0:256], scalar=st0[:, 4:5], in1=gc[:, 0:256], op0=ALU.mult, op1=ALU.mult), [sgb, st0b, gcb], [cnb])
            P.op("dve", lambda e, sg=sg: e.scalar_tensor_tensor(out=cn[:, 256:384], in0=sg[:, 256:384], scalar=st0[:, 5:6], in1=gc[:, 256:384], op0=ALU.mult, op1=ALU.mult), [sgb, st0b, gcb], [cnb])
            if C.stage <= 1:
                continue
            for k in range(3):
                P.op("pe", lambda e, k=k: e.transpose(out=pTc[:, k, :], in_=cn[:, k * 128:(k + 1) * 128], identity=ident[:]), [cnb, identb], [pTcb])
            P.op("act", lambda e: e.activation(func=AF.Identity, out=cT[:], in_=pTc[:, 0:3, :]), [pTcb], [cTb])
            for k in range(2):
                P.op("pe", lambda e, k=k: e.matmul(pq[0][0][:], lhsT=cT[:, k, :], rhs=wuq[:, k, 0:512], start=(k == 0), stop=(k == 1)), [cTb, wuqb], [pq[0][1]])
            for k in range(2):
                P.op("pe", lambda e, k=k: e.matmul(pq[1][0][:, 0:256], lhsT=cT[:, k, :], rhs=wuq[:, k, 512:768], start=(k == 0), stop=(k == 1)), [cTb, wuqb], [pq[1][1]])
            for j in range(2):
                P.op("pe", lambda e, j=j: e.matmul(pk[j][0][:], lhsT=cT[:, 2, :], rhs=wukv[:, j * 512:(j + 1) * 512], start=True, stop=True), [cTb, wukvb], [pk[j][1]])
            P.op("act", lambda e: e.activation(func=AF.Identity, out=qf[:, 0:512], in_=pq[0][0][:]), [pq[0][1]], [qfb])
            P.op("act", lambda e: e.activation(func=AF.Identity, out=qf[:, 512:768], in_=pq[1][0][:, 0:256]), [pq[1][1]], [qfb])
            if C.stage <= 2.1:
                continue
            for j in range(2):
                if C.stage > 2.2:
                    P.op("dve", lambda e, j=j: e.tensor_copy(out=kf[:].rearrange("p (h d) -> p h d", h=8)[:, j * 4:(j + 1) * 4, 0:64],
                                                         in_=pk[j][0][:].rearrange("p (h d) -> p h d", h=4)[:, :, 0:64]), [pk[j][1]], [kfb])
                if C.stage > 2.4:
                    P.op("act", lambda e, j=j, vs=vs, s=s: e.activation(func=AF.Identity, out=vs[:, j * 4:(j + 1) * 4, s, 0:64],
                                                             in_=pk[j][0][:].rearrange("p (h d) -> p h d", h=4)[:, :, 64:128]), [pk[j][1]], [vsb])
            if C.stage > 2.6:
                P.op("dve", lambda e, sg=sg: e.tensor_copy(out=kf[:].rearrange("p (h d) -> p h d", h=8)[:, :, 64:96],
                                                        in_=sg[:, None, 384:416].to_broadcast([128, 8, 32])), [sgb], [kfb])
            if C.stage <= 3:
                continue
            head_norm(P, qf[:], qfb, 8, 96, gq, gqb, qn[:], qnb, sq, sqb, stq, stqb, tmp, tmpb)
            head_norm(P, kf[:], kfb, 8, 96, gk, gkb, kn[:], knb, sq, sqb, stk, stkb, tmp2, tmp2b)
            if C.stage <= 4:
                continue
            P.op("dve", lambda e: e.tensor_copy(out=qb_[:, 0], in_=qn[:]), [qnb], [qbb])
            P.op("dve", lambda e: e.tensor_copy(out=qb_[:, 1, :, 0:64], in_=qn[:, :, 0:64]), [qnb], [qbb])
            rope(P, qn[:, :, 64:96], qnb, c_, c_b, qb_[:, 1, :, 64:96], qbb, tt, ttb, 8)
            P.op("dve", lambda e: e.tensor_copy(out=kb_[:, :, 0:64], in_=kn[:, :, 0:64]), [knb], [kbb])
            rope(P, kn[:, :, 64:96], knb, c_, c_b, kb_[:, :, 64:96], kbb, tt2, tt2b, 8)
            if C.stage <= 5:
                continue
            for ver in range(2):
                for h in range(8):
                    P.op("pe", lambda e, ver=ver, h=h: e.transpose(out=pTq[0:96, ver * 8 + h, :], in_=qb_[:, ver, h, :], identity=ident[:]), [qbb, identb], [pTqb])
            for ver in range(2):
                P.op("act" if ver == 0 else "dve",
                     (lambda e, ver=ver, qs=qs, s=s: e.activation(func=AF.Identity, out=qs[:, ver, :, s * 128:(s + 1) * 128], in_=pTq[0:96, ver * 8:(ver + 1) * 8, :])) if ver == 0 else
                     (lambda e, ver=ver, qs=qs, s=s: e.tensor_copy(out=qs[:, ver, :, s * 128:(s + 1) * 128], in_=pTq[0:96, ver * 8:(ver + 1) * 8, :])),
                     [pTqb], [qsb])
            for h in range(8):
                P.op("pe", lambda e, h=h: e.transpose(out=pTk[0:96, h, :], in_=kb_[:, h, :], identity=ident[:]), [kbb, identb], [pTkb])
            P.op("dve", lambda e, ks=ks, s=s: e.tensor_copy(out=ks[:, :, s * 128:(s + 1) * 128], in_=pTk[0:96, :, :]), [pTkb], [ksb])
        n = nt * 128
        if C.stage <= 6:
            continue
        P.dma("pool", [(C.mla_qT[:, :, qt, :, 0:n].rearrange("v h p n -> p v h n"), qs[:, :, :, 0:n])], qsb, False)
        P.dma("pool", [(C.mla_kT[:, :, t0 * 128:t0 * 128 + n].rearrange("h p n -> p h n"), ks[:, :, 0:n])], ksb, False)
        P.dma("sp", [(C.mla_va[:, :, t0:t0 + nt, :].rearrange("h p k e -> p h k e"), vs[:, :, 0:nt, :])], vsb, False)
    P.end()


def attend_main(P, C, l, name, H, dk, scale, kT_ap, q_ap, va, ybr, chunks_of, bias_src=None, with_ctx_q=True):
    P.begin()
    identf, identfb = P.sb([128, 128], F32, "identf")
    P.dma("sp", [(identf[:], C.identf)], identfb, True)
    kT = [P.sb([dk, T], BF16, f"kT{i}") for i in range(2)]
    vv = [P.sb([128, NT, 66], BF16, f"vv{i}") for i in range(2)]
    nver = 2 if name == "mla" else 1
    qq = [[P.sb([dk, 512], BF16, f"q{v}_{i}") for v in range(nver)] for i in range(2)]
    pt = [P.sb([128, 512], BF16, f"pt{i}") for i in range(3)]
    sbias = [P.sb([128, 512], F32, f"sb{i}") for i in range(2)]
    nb_tiles = 8
    bt = [[P.sb([128, 512], F32, f"bt{j}_{i}") for i in range(nb_tiles)] for j in range(2)] if bias_src else None
    oT, oTb = P.sb([65, 512], F32, "oT")
    rec, recb = P.sb([128, 4], F32, "rec")
    ost = [P.sb([128, 4, 64], F32, f"ost{i}") for i in range(2)]
    pS = [P.ps([128, 512], F32, f"pS{i}") for i in range(3)]
    pO = [P.ps([128, 512], F32, f"pO{i}") for i in range(2)]
    pXf, pXb = P.ps([128, 512], F32, "pX")
    pX = pXf[:, 0:260].rearrange("p (t e) -> p t e", e=65)
    qts = ([0] if with_ctx_q else []) + list(range(1, 17))
    it = 0
    ci = 0
    cur_bias = [None, None]
    for h in range(H):
        k_, kb = kT[h % 2]
        v_, vb = vv[h % 2]
        P.dma("sp", [(k_[:, 0:T // 2], kT_ap(h)[:, 0:T // 2]), (k_[:, T // 2:T], kT_ap(h)[:, T // 2:T])], kb, True)
        P.dma("pool", [(v_[:], va[h])], vb, True)
        for qt in qts:
            nq = 256 if qt == 0 else 512
            tok0 = 0 if qt == 0 else 256 + (qt - 1) * 512
            qs = qq[it % 2]
            for v in range(nver):
                P.dma("sp", [(qs[v][0][:, 0:nq], q_ap(h, qt, v)[:, 0:nq])], qs[v][1], True)
            o_, ob = pO[it % 2]
            chunks = chunks_of(qt)
            bmap = {}
            if bias_src:
                ids = [b for (_, _, b) in chunks if b is not None]
                key = (h, tuple(ids))
                if ids:
                    slot = None
                    for j in range(2):
                        if cur_bias[j] == key:
                            slot = j
                    if slot is None:
                        slot = 0 if len(ids) == 8 else 1
                        cur_bias[slot] = key
                        for n_, b in enumerate(ids):
                            P.dma("pool" if n_ % 2 else "sp", [(bt[slot][n_][0][:], bias_src(h, b))], bt[slot][n_][1], True)
                    for n_, b in enumerate(ids):
                        bmap[b] = bt[slot][n_]
            for n_, (kc, ver, bid) in enumerate(chunks):
                s_, sb_ = pS[ci % 3]
                p_, pb = pt[ci % 3]
                q_, qb = qs[ver]
                P.op("pe", lambda e, s_=s_, k_=k_, kc=kc, q_=q_, nq=nq: e.matmul(s_[:, 0:nq], lhsT=k_[:, kc * 128:(kc + 1) * 128], rhs=q_[:, 0:nq], start=True, stop=True), [kb, qb], [sb_])
                if bid is None:
                    P.op("act", lambda e, s_=s_, p_=p_, nq=nq: e.activation(out=p_[:, 0:nq], in_=s_[:, 0:nq], func=AF.Exp, scale=scale), [sb_], [pb])
                else:
                    b_, bb = bmap[bid]
                    x_, xb = sbias[ci % 2]
                    P.op("dve", lambda e, s_=s_, x_=x_, b_=b_: e.scalar_tensor_tensor(out=x_[:], in0=s_[:], scalar=scale, in1=b_[:], op0=ALU.mult, op1=ALU.add), [sb_, bb], [xb])
                    P.op("act", lambda e, x_=x_, p_=p_: e.activation(out=p_[:], in_=x_[:], func=AF.Exp), [xb], [pb])
                P.op("pe", lambda e, o_=o_, v_=v_, kc=kc, p_=p_, nq=nq, n_=n_, nch=len(chunks): e.matmul(o_[0:65, 0:nq], lhsT=v_[:, kc, 0:65], rhs=p_[:, 0:nq], start=(n_ == 0), stop=(n_ == nch - 1)), [vb, pb], [ob])
                ci += 1
            nt4 = nq // 128
            P.op("act", lambda e, o_=o_, nq=nq: e.activation(func=AF.Identity, out=oT[:, 0:nq], in_=o_[0:65, 0:nq]), [ob], [oTb])
            for t in range(nt4):
                P.op("pe", lambda e, t=t: e.transpose(out=pX[:, t, :], in_=oT[:, t * 128:(t + 1) * 128], identity=identf[0:65, 0:65]), [oTb, identfb], [pXb])
            P.op("dve", lambda e, nt4=nt4: e.reciprocal(out=rec[:, 0:nt4], in_=pX[:, 0:nt4, 64]), [pXb], [recb])
            os_, osb = ost[it % 2]
            P.op("dve", lambda e, nt4=nt4, os_=os_: e.tensor_tensor(out=os_[:, 0:nt4, :], in0=pX[:, 0:nt4, 0:64], in1=rec[:, 0:nt4].unsqueeze(2).to_broadcast([128, nt4, 64]), op=ALU.mult), [pXb, recb], [osb])
            P.dma("pool", [(ybr[tok0:tok0 + nq, h * 64:(h + 1) * 64].rearrange("(t p) d -> p t d", p=128), os_[:, 0:nt4, :])], osb, False)
            it += 1
    P.end()


def mla_main(P, C, l):
    def chunks_of(qt):
        if qt == 0:
            return [(0, 0, None), (1, 0, None)]
        return [(0, 0, None), (1, 0, None)] + [(kc, 1, None) for kc in range(2, NT)]
    attend_main(P, C, l, "mla", 8, 96, 96 ** -0.5, lambda h: C.mla_kT[h], lambda h, qt, v: C.mla_qT[v, h, qt], C.mla_va, C.ybr[0], chunks_of,
                with_ctx_q=(l == 0))


def nat_prep(P, C, l):
    P.begin()
    ident, identb = P.sb([128, 128], BF16, "ident")
    P.dma("sp", [(ident[:], C.ident)], identb, True)
    gq, gqb = P.sb([128, 64], F32, "gq")
    gk, gkb = P.sb([128, 64], F32, "gk")
    P.dma("sp", [(gq[:], C.nat_gq_rep[l])], gqb, True)
    P.dma("sp", [(gk[:], C.nat_gk_rep[l])], gkb, True)
    seg = [P.sb([128, 1536], F32, f"seg{i}") for i in range(2)]
    sq, sqb = P.sb([128, 512], F32, "sq")
    stq, stqb = P.sb([128, 24], F32, "stq")
    stk, stkb = P.sb([128, 24], F32, "stk")
    tmp, tmpb = P.sb([128, 512], F32, "tmp")
    tmp2, tmp2b = P.sb([128, 512], F32, "tmp2")
    qk, qkb = P.sb([128, 2, 8, 64], BF16, "qk")
    qst = [P.sb([128, 4, 512], BF16, f"qst{i}") for i in range(2)]
    kst = [P.sb([128, 4, 512], BF16, f"kst{i}") for i in range(2)]
    vst = [P.sb([128, 8, 4, 66], BF16, f"vst{i}") for i in range(2)]
    for i in range(2):
        P.op("dve", lambda e, i=i: e.memset(vst[i][0][:], 1.0), [], [vst[i][1]])
    pT, pTb = P.ps([128, 8, 128], BF16, "pT")
    for gi, (t0, nt, qt) in enumerate(groups()[:C.max_groups]):
        qs, qsb = qst[gi % 2]
        ks, ksb = kst[gi % 2]
        vs, vsb = vst[gi % 2]
        for s in range(nt):
            i = t0 + s
            sg, sgb = seg[i % 2]
            P.dma("sp" if i % 2 == 0 else "pool", [(sg[:], C.proj[i * 128:(i + 1) * 128, 416:1952])], sgb, True)
            head_norm(P, sg[:, 0:512], sgb, 8, 64, gq, gqb, qk[:, 0], qkb, sq, sqb, stq, stqb, tmp, tmpb)
            head_norm(P, sg[:, 512:1024], sgb, 8, 64, gk, gkb, qk[:, 1], qkb, sq, sqb, stk, stkb, tmp2, tmp2b)
            P.op("dve", lambda e, sg=sg, vs=vs, s=s: e.tensor_copy(out=vs[:, :, s, 0:64], in_=sg[:, 1024:1536].rearrange("p (h d) -> p h d", h=8)), [sgb], [vsb])
            for w in range(2):
                for pr in range(4):
                    P.op("pe", lambda e, w=w, pr=pr: e.transpose(out=pT[:, w * 4 + pr, :], in_=qk[:, w, 2 * pr:2 * pr + 2, :].rearrange("p h d -> p (h d)"), identity=ident[:]), [qkb, identb], [pTb])
            P.op("act", lambda e, qs=qs, s=s: e.activation(func=AF.Identity, out=qs[:, :, s * 128:(s + 1) * 128], in_=pT[:, 0:4, :]), [pTb], [qsb])
            P.op("act", lambda e, ks=ks, s=s: e.activation(func=AF.Identity, out=ks[:, :, s * 128:(s + 1) * 128], in_=pT[:, 4:8, :]), [pTb], [ksb])
        n = nt * 128
        P.dma("pool", [(C.nat_qT[:, qt, :, 0:n].rearrange("r p n -> p r n"), qs[:, :, 0:n])], qsb, False)
        P.dma("pool", [(C.nat_kT[:, :, t0 * 128:t0 * 128 + n].rearrange("r p n -> p r n"), ks[:, :, 0:n])], ksb, False)
        P.dma("sp", [(C.nat_va[:, :, t0:t0 + nt, :].rearrange("h p k e -> p h k e"), vs[:, :, 0:nt, :])], vsb, False)
    P.end()


def nat_block(j):
    kb = min(max(8 * j - 4, 0), 112)
    pat = 0 if j == 0 else (2 if j == 15 else 1)
    cs_ = range(0, 6) if j == 0 else (range(2, 8) if j == 15 else range(0, 8))
    return kb, pat, list(cs_)


def nat_main(P, C, l):
    def chunks_of(qt):
        if qt == 0:
            return [(0, 0, None), (1, 0, None)]
        kb, pat, cs_ = nat_block(qt - 1)
        return [(0, 0, None), (1, 0, None)] + [(2 + kb // 2 + c, 0, (pat, c)) for c in cs_]
    attend_main(P, C, l, "nat", 8, 64, 64 ** -0.5,
                lambda h: C.nat_kT[h // 2, (h % 2) * 64:(h % 2 + 1) * 64, :],
                lambda h, qt, v: C.nat_qT[h // 2, qt, (h % 2) * 64:(h % 2 + 1) * 64, :],
                C.nat_va, C.ybr[1], chunks_of, bias_src=lambda h, b: C.nat_bias[l, b[0], h, b[1]], with_ctx_q=(l == 0))


def nat_bias_host(rpb):
    L = rpb.shape[0]
    out = np.full((L, 3, 8, 8, 128, 512), -30000.0, np.float32)
    ck = np.arange(64)[:, None]
    cq = np.arange(64)[None, :]
    c0 = np.clip(cq - 8, 0, 48)
    col_ok = (ck >= c0) & (ck < c0 + 16)
    dc = np.clip(ck - cq, -15, 15) + 15
    for pat, j in enumerate((0, 1, 15)):
        kb, _, cs_ = nat_block(j)
        for c in cs_:
            for a in range(2):
                kr = kb + 2 * c + a
                for r in range(8):
                    qr = 8 * j + r
                    r0 = min(max(qr - 4, 0), 120)
                    if not (r0 <= kr < r0 + 8):
                        continue
                    dr = kr - qr + 7
                    blk = np.where(col_ok[None, None], rpb[:, :, dr][:, :, dc], np.float32(-30000.0))
                    out[:, pat, :, c, a * 64:(a + 1) * 64, r * 64:(r + 1) * 64] = blk
    return out


def scan_consts_host():
    j = np.arange(128)[:, None]
    i = np.arange(128)[None, :]
    same = (j // 32) == (i // 32)
    mf = (same & (j <= i)).astype(np.float32)
    mb = (same & (j >= i)).astype(np.float32)
    blk = same.astype(np.float32)
    ind = (j // 32 == np.arange(4)[None, :]).astype(np.float32)
    return np.ascontiguousarray(np.concatenate([mf, mb, blk, ind], 1))


def scan_mixer(P, C, l, kind):
    hg = kind == "hg"
    H, dk, dv = (4, 128, 128) if hg else (4, 64, 128)
    W = H * dk
    c0, cw = SEG["hg"] if hg else SEG["gla"]
    qscale = float(dk) ** -0.5
    br = 2 if hg else 3
    for d in range(2):
        P.begin()
        ident, identb = P.sb([128, 128], BF16, "ident")
        P.dma("sp", [(ident[:], C.ident)], identb, True)
        identf, identfb = P.sb([128, 128], F32, "identf")
        P.dma("sp", [(identf[:], C.identf)], identfb, True)
        sc, scb = P.sb([128, 388], F32, "sconst")
        P.dma("sp", [(sc[:], C.scan_consts)], scb, True)
        Md = sc[:, 0:128] if d == 0 else sc[:, 128:256]
        Blk = sc[:, 256:384]
        Ind = sc[:, 384:388]
        go, gob = P.sb([128, 128], F32, "go")
        P.dma("sp", [(go[:], (C.hg_go_rep if hg else C.gla_go_rep)[l])], gob, True)
        if hg:
            lbt, lbb = P.sb([128, 512], F32, "lbt")
            oml, omlb = P.sb([128, 512], F32, "oml")
            if l == 0:
                P.op("dve", lambda e: e.memset(lbt[:], 0.0), [], [lbb])
                P.op("dve", lambda e: e.memset(oml[:], 1.0), [], [omlb])
            else:
                zz, zzb = P.sb([128, 2, 512], F32, "zz")
                P.dma("sp", [(zz[:], C.hg_lb_rep[:, :, d, :].rearrange("l p n -> p l n"))], zzb, True)
                P.op("dve", lambda e: e.tensor_tensor(out=lbt[:], in0=zz[:, 1, :], in1=zz[:, 0, :], op=ALU.subtract), [zzb], [lbb])
                P.op("act", lambda e: e.activation(out=lbt[:], in_=lbt[:], func=AF.Sigmoid), [lbb], [lbb])
                P.op("dve", lambda e: e.tensor_scalar(out=oml[:], in0=lbt[:], scalar1=-1.0, scalar2=1.0, op0=ALU.mult, op1=ALU.add), [lbb], [omlb])
        else:
            w2, w2b = P.sb([16, 256], F32, "w2")
            b2, b2b = P.sb([128, 256], F32, "b2")
            P.dma("sp", [(w2[:], C.gla_w2[l, d])], w2b, True)
            P.dma("sp", [(b2[:], C.gla_b2_rep[l, :, d, :])], b2b, True)
            rT, rTb = P.sb([16, 128], F32, "rT")
            gx, gxb = P.sb([128, 256], F32, "gx")
        seg = [P.sb([128, cw], F32, f"seg{i}") for i in range(2)]
        qf, qfb = P.sb([128, W], F32, "qf")
        kf, kfb = P.sb([128, W], F32, "kf")
        la, lab = P.sb([128, W], F32, "la")
        vb, vbb = P.sb([128, H * dv], BF16, "vb")
        sg1, sg1b = P.sb([128, W], F32, "sg1")
        bs, bsb = P.sb([128, W], F32, "bs")
        eb, ebb = P.sb([128, W], F32, "eb")
        enb, enbb = P.sb([128, W], F32, "enb")
        ebe, ebeb = P.sb([128, W], F32, "ebe")
        dcy, dcyb = P.sb([128, H * 4], F32, "dcy")
        qt_, qtb = P.sb([128, W], BF16, "qt")
        kt_, ktb = P.sb([128, W], BF16, "kt")
        kh, khb = P.sb([128, W], BF16, "kh")
        kpad, kpadb = P.sb([128, H, 4, dk], BF16, "kpad")
        qT, qTb = P.sb([128, H, 128], BF16, "qT")
        kT, kTb = P.sb([128, H, 128], BF16, "kT")
        qpad, qpadb = P.sb([128, H, 608], BF16, "qpad")
        P.op("dve", lambda e: e.memset(qpad[:], 0.0), [], [qpadb])
        AT, ATb = P.sb([128, H, 128], BF16, "AT")
        S, Sb_ = P.sb([128, H, dv], F32, "S")
        P.op("dve", lambda e: e.memset(S[:], 0.0), [], [Sb_])
        Sbf = [P.sb([128, H, 4, dv], BF16, f"Sbf{i}") for i in range(2)]
        ofw = [P.sb([128, 512], F32, f"ofw{i}") for i in range(2)]
        ost = [P.sb([128, 512], F32, f"ost{i}") for i in range(2)]
        osum, osumb = P.sb([128, 512], F32, "osum")
        sq, sqb = P.sb([128, 512], F32, "sq")
        stn, stnb = P.sb([128, 12], F32, "stn")
        tmpn, tmpnb = P.sb([128, 512], F32, "tmpn")
        pb, pbb = P.ps([128, 512], F32, "pb")
        pbe, pbeb = P.ps([128, 512], F32, "pbe")
        pbT, pbTb = P.ps([128, 512], F32, "pbT")
        pT, pTb = P.ps([128, 8, 128], BF16, "pT")
        pS, pSb = P.ps([128, 512], F32, "pS")
        pO, pOb = P.ps([128, 512], F32, "pO")
        pU = [P.ps([128, 512], F32, f"pU{i}") for i in range(2)]
        tiles = list(range(NT)) if d == 0 else [1, 0] + list(range(NT - 1, 1, -1))
        if C.max_groups:
            tiles = tiles[:C.max_groups] if d == 0 else ([1, 0] + list(range(C.max_groups - 1, 1, -1)))
        cseq = [0, 1, 2, 3] if d == 0 else [3, 2, 1, 0]
        P.op("dve", lambda e: e.memset(Sbf[0][0][:, :, cseq[0], :], 0.0), [], [Sbf[0][1]])
        for n, i in enumerate(tiles):
            sg, sgb = seg[n % 2]
            P.dma("sp" if n % 2 == 0 else "pool", [(sg[:], C.proj[i * 128:(i + 1) * 128, c0:c0 + cw])], sgb, True)
            if d == 1:
                of_, ofb = ofw[n % 2]
                P.dma("pool" if n % 2 == 0 else "sp", [(of_[:], C.osc[i * 128:(i + 1) * 128, :])], ofb, True)
            if hg:
                fcol = 512 if d == 0 else 1024
                P.op("act", lambda e, sg=sg: e.activation(out=qf[:], in_=sg[:, 0:512], func=AF.Silu), [sgb], [qfb])
                P.op("act", lambda e, sg=sg: e.activation(out=sg1[:], in_=sg[:, fcol:fcol + 512], func=AF.Sigmoid), [sgb], [sg1b])
                P.op("act", lambda e, sg=sg: e.activation(out=kf[:], in_=sg[:, fcol:fcol + 512], func=AF.Sigmoid, scale=-1.0), [sgb], [kfb])
                P.op("dve", lambda e: e.tensor_tensor(out=kf[:], in0=kf[:], in1=oml[:], op=ALU.mult), [kfb, omlb], [kfb])
                P.op("dve", lambda e: e.tensor_tensor(out=sg1[:], in0=sg1[:], in1=oml[:], op=ALU.mult), [sg1b, omlb], [sg1b])
                P.op("dve", lambda e: e.tensor_tensor(out=sg1[:], in0=sg1[:], in1=lbt[:], op=ALU.add), [sg1b, lbb], [sg1b])
                P.op("act", lambda e: e.activation(out=la[:], in_=sg1[:], func=AF.Ln), [sg1b], [lab])
                P.op("dve", lambda e, sg=sg: e.tensor_copy(out=vb[:], in_=sg[:, 1536:2048]), [sgb], [vbb])
                qsrc, qsrcb, ksrc, ksrcb = qf[:], qfb, kf[:], kfb
            else:
                rc = 1024 + 16 * d
                P.op("pe", lambda e, sg=sg: e.transpose(out=pbT[0:16, 0:128], in_=sg[:, rc:rc + 16], identity=identf[:]), [sgb, identfb], [pbTb])
                P.op("act", lambda e: e.activation(func=AF.Identity, out=rT[:], in_=pbT[0:16, 0:128]), [pbTb], [rTb])
                P.op("pe", lambda e: e.matmul(pS[:, 0:256], lhsT=rT[:], rhs=w2[:], start=True, stop=True), [rTb, w2b], [pSb])
                P.op("dve", lambda e: e.tensor_tensor(out=gx[:], in0=pS[:, 0:256], in1=b2[:], op=ALU.add), [pSb, b2b], [gxb])
                P.op("act", lambda e: e.activation(out=gx[:], in_=gx[:], func=AF.Sigmoid), [gxb], [gxb])
                P.op("act", lambda e: e.activation(out=gx[:], in_=gx[:], func=AF.Ln), [gxb], [gxb])
                P.op("dve", lambda e: e.tensor_scalar(out=la[:], in0=gx[:], scalar1=1.0 / 16, scalar2=None, op0=ALU.mult), [gxb], [lab])
                P.op("dve", lambda e, sg=sg: e.tensor_copy(out=vb[:], in_=sg[:, 512:1024]), [sgb], [vbb])
                qsrc, qsrcb, ksrc, ksrcb = sg[:, 0:256], sgb, sg[:, 256:512], sgb
            if C.stage <= 1:
                continue
            P.op("pe", lambda e: e.matmul(pb[:, 0:W], lhsT=Md, rhs=la[:], start=True, stop=True), [scb, lab], [pbb])
            P.op("pe", lambda e: e.matmul(pbe[:, 0:W], lhsT=Blk, rhs=la[:], start=True, stop=True), [scb, lab], [pbeb])
            for h in range(H):
                P.op("pe", lambda e, h=h: e.matmul(pbT[0:dk, h * 4:(h + 1) * 4], lhsT=la[:, h * dk:(h + 1) * dk], rhs=Ind, start=True, stop=True), [lab, scb], [pbTb])
            P.op("act", lambda e: e.activation(func=AF.Identity, out=bs[:], in_=pb[:, 0:W]), [pbb], [bsb])
            P.op("act", lambda e: e.activation(out=eb[:], in_=pb[:, 0:W], func=AF.Exp), [pbb], [ebb])
            P.op("act", lambda e: e.activation(out=enb[:], in_=pb[:, 0:W], func=AF.Exp, scale=-1.0), [pbb], [enbb])
            P.op("dve", lambda e: e.tensor_tensor(out=ebe[:], in0=pbe[:, 0:W], in1=bs[:], op=ALU.subtract), [pbeb, bsb], [ebeb])
            P.op("act", lambda e: e.activation(out=ebe[:], in_=ebe[:], func=AF.Exp), [ebeb], [ebeb])
            P.op("act", lambda e: e.activation(out=dcy[0:dk, :], in_=pbT[0:dk, 0:H * 4], func=AF.Exp), [pbTb], [dcyb])
            if C.stage <= 2:
                continue
            P.op("dve", lambda e, qsrc=qsrc: e.scalar_tensor_tensor(out=qt_[:], in0=qsrc, scalar=qscale, in1=eb[:], op0=ALU.mult, op1=ALU.mult), [qsrcb, ebb], [qtb])
            P.op("dve", lambda e, ksrc=ksrc: e.tensor_tensor(out=kt_[:], in0=ksrc, in1=enb[:], op=ALU.mult), [ksrcb, enbb], [ktb])
            P.op("dve", lambda e, ksrc=ksrc: e.tensor_tensor(out=kh[:], in0=ksrc, in1=ebe[:], op=ALU.mult), [ksrcb, ebeb], [khb])
            for h in range(H):
                P.op("dve", lambda e, h=h: e.tensor_tensor(out=kpad[:, h], in0=kh[:, None, h * dk:(h + 1) * dk].to_broadcast([128, 4, dk]),
                                                           in1=Ind.unsqueeze(2).to_broadcast([128, 4, dk]), op=ALU.mult), [khb, scb], [kpadb])
            if C.stage <= 3:
                continue
            for h in range(H):
                P.op("pe", lambda e, h=h: e.transpose(out=pT[0:dk, h, :], in_=qt_[:, h * dk:(h + 1) * dk], identity=ident[:]), [qtb, identb], [pTb])
                P.op("pe", lambda e, h=h: e.transpose(out=pT[0:dk, H + h, :], in_=kt_[:, h * dk:(h + 1) * dk], identity=ident[:]), [ktb, identb], [pTb])
            P.op("act", lambda e: e.activation(func=AF.Identity, out=qT[0:dk], in_=pT[0:dk, 0:H, :]), [pTb], [qTb])
            P.op("act", lambda e: e.activation(func=AF.Identity, out=kT[0:dk], in_=pT[0:dk, H:2 * H, :]), [pTb], [kTb])
            for h in range(H):
                P.op("dve", lambda e, h=h: e.tensor_copy(out=qpad[0:dk, h, 96:608].rearrange("p (c n) -> p c n", n=128)[:, :, 0:32],
                                                         in_=pT[0:dk, h, :].rearrange("p (c t) -> p c t", t=32)), [pTb], [qpadb])
            if C.stage <= 4:
                continue
            for h in range(H):
                P.op("pe", lambda e, h=h: e.matmul(pS[:, h * 128:(h + 1) * 128], lhsT=kT[0:dk, h, :], rhs=qT[0:dk, h, :], start=True, stop=True), [kTb, qTb], [pSb])
            P.op("dve", lambda e: e.tensor_tensor(out=AT[:], in0=pS[:].rearrange("p (h n) -> p h n", h=H), in1=Md[:, None, :].to_broadcast([128, H, 128]), op=ALU.mult), [pSb, scb], [ATb])
            if C.stage <= 5:
                continue
            sb_cur, sb_curb = Sbf[n % 2]
            sb_nxt, sb_nxtb = Sbf[(n + 1) % 2]
            for ci, c in enumerate(cseq):
                u_, ub = pU[ci % 2]
                for h in range(H):
                    P.op("pe", lambda e, h=h, c=c, u_=u_: e.matmul(u_[0:dk, h * dv:(h + 1) * dv], lhsT=kpad[:, h, c, :], rhs=vb[:, h * dv:(h + 1) * dv], start=True, stop=True), [kpadb, vbb], [ub])
                for h in range(H):
                    P.op("dve", lambda e, h=h, c=c, u_=u_: e.scalar_tensor_tensor(out=S[0:dk, h, :], in0=S[0:dk, h, :], scalar=dcy[0:dk, h * 4 + c:h * 4 + c + 1],
                                                                                 in1=u_[0:dk, h * dv:(h + 1) * dv], op0=ALU.mult, op1=ALU.add), [Sb_, dcyb, ub], [Sb_])
                if ci < 3:
                    P.op("act", lambda e, ci=ci, sb_cur=sb_cur: e.activation(func=AF.Identity, out=sb_cur[0:dk, :, cseq[ci + 1], :], in_=S[0:dk]), [Sb_], [sb_curb])
                else:
                    P.op("act", lambda e, sb_nxt=sb_nxt: e.activation(func=AF.Identity, out=sb_nxt[0:dk, :, cseq[0], :], in_=S[0:dk]), [Sb_], [sb_nxtb])
            if C.stage <= 6:
                continue
            for h in range(H):
                P.op("pe", lambda e, h=h: e.matmul(pO[:, h * dv:(h + 1) * dv], lhsT=AT[:, h, :], rhs=vb[:, h * dv:(h + 1) * dv], start=True, stop=False), [ATb, vbb], [pOb])
                for ci, c in enumerate(cseq):
                    P.op("pe", lambda e, h=h, c=c, ci=ci, sb_cur=sb_cur: e.matmul(pO[:, h * dv:(h + 1) * dv], lhsT=qpad[0:dk, h, 96 + 96 * c:224 + 96 * c], rhs=sb_cur[0:dk, h, c, :],
                                                                  start=False, stop=(ci == 3)), [qpadb, sb_curb], [pOb])
            if d == 0:
                o_, ob = ost[n % 2]
                P.op("act", lambda e, o_=o_: e.activation(func=AF.Identity, out=o_[:], in_=pO[:]), [pOb], [ob])
                P.dma("pool", [(C.osc[i * 128:(i + 1) * 128, :], o_[:])], ob, False)
            else:
                o_, ob = ost[n % 2]
                P.op("dve", lambda e, of_=of_: e.tensor_tensor(out=osum[:], in0=pO[:], in1=of_[:], op=ALU.add), [pOb, ofb], [osumb])
                head_norm(P, osum[:], osumb, 4, 128, go, gob, o_[:].rearrange("p (h d) -> p h d", h=4), ob, sq, sqb, stn, stnb, tmpn, tmpnb)
                P.dma("pool", [(C.ybr[br][i * 128:(i + 1) * 128, :], o_[:])], ob, False)
        P.end()


def merge_phase(P, C, l):
    tiles = list(range(NT)) if l == 0 else list(range(2, NT))
    if C.max_groups:
        tiles = tiles[:C.max_groups]
    P.begin()
    ident, identb = P.sb([128, 128], BF16, "ident")
    P.dma("sp", [(ident[:], C.ident)], identb, True)
    wbr, wbrb = P.sb([128, 4, 4, 1024], BF16, "wbr")
    wmg, wmgb = P.sb([128, 4, 8, 1024], BF16, "wmg")
    bmg, bmgb = P.sb([1, 4, 1024], BF16, "bmg")
    ones, onesb = P.sb([1, 128], BF16, "ones")
    P.op("dve", lambda e: e.memset(ones[:], 1.0), [], [onesb])
    stg = [P.sb([128, 1024], F32, f"wstg{i}") for i in range(2)]
    n = 0
    for br in range(4):
        for k in range(4):
            s_, sb_ = stg[n % 2]
            P.dma("sp" if n % 2 == 0 else "pool", [(s_[:], C.w_br[l, br, k * 128:(k + 1) * 128, :])], sb_, True)
            P.op("pool" if n % 2 == 0 else "dve", lambda e, s_=s_, br=br, k=k: e.tensor_copy(out=wbr[:, br, k, :], in_=s_[:]), [sb_], [wbrb])
            n += 1
        for k in range(8):
            s_, sb_ = stg[n % 2]
            P.dma("sp" if n % 2 == 0 else "pool", [(s_[:], C.w_merge[l, br, k * 128:(k + 1) * 128, :])], sb_, True)
            P.op("pool" if n % 2 == 0 else "dve", lambda e, s_=s_, br=br, k=k: e.tensor_copy(out=wmg[:, br, k, :], in_=s_[:]), [sb_], [wmgb])
            n += 1
    bst, bstb = P.sb([1, 4, 1024], F32, "bst")
    P.dma("sp", [(bst[:], C.b_merge[l:l + 1])], bstb, True)
    P.op("dve", lambda e: e.tensor_copy(out=bmg[:], in_=bst[:]), [bstb], [bmgb])
    yb = [[P.sb([128, 512], F32, f"y{br}_{i}") for br in range(4)] for i in range(2)]
    zt = [P.sb([128, 2048], F32, f"z{i}") for i in range(2)]
    hT = [P.sb([128, 8, 128], BF16, f"hT{i}") for i in range(2)]
    sz, szb = P.sb([128, 2048], F32, "sz")
    u, ub = P.sb([128, 2048], BF16, "u")
    uT, uTb = P.sb([128, 16, 128], BF16, "uT")
    g, gb = P.sb([128, 1024], F32, "g")
    tmp, tmpb = P.sb([128, 1024], F32, "tmp")
    acc, accb = P.sb([128, 1024], F32, "acc")
    accbf, accbfb = P.sb([128, 1024], BF16, "accbf")
    aT = [P.sb([128, 8, 128], BF16, f"aT{i}") for i in range(2)]
    pT = [P.ps([128, 8, 128], BF16, f"pT{i}") for i in range(2)]
    pP = [P.ps([128, 512], F32, f"pP{i}") for i in range(2)]
    pG = [P.ps([128, 512], F32, f"pG{i}") for i in range(2)]
    for n, i in enumerate(tiles):
        ys = yb[n % 2]
        z_, zb = zt[n % 2]
        h_, hb = hT[n % 2]
        for br in range(4):
            P.dma("sp" if br % 2 == 0 else "pool", [(ys[br][0][:], C.ybr[br][i * 128:(i + 1) * 128, :])], ys[br][1], True)
        P.dma("sp", [(z_[:], C.proj[i * 128:(i + 1) * 128, 5056:7104])], zb, True)
        P.dma("pool", [(h_[:], C.hT[i])], hb, True)
        P.op("act", lambda e, z_=z_: e.activation(out=sz[:], in_=z_[:], func=AF.Silu), [zb], [szb])
        for br in range(4):
            P.op("dve" if br % 2 == 0 else "pool", lambda e, br=br, ys=ys: e.tensor_tensor(out=u[:, br * 512:(br + 1) * 512], in0=ys[br][0][:], in1=sz[:, br * 512:(br + 1) * 512], op=ALU.mult),
                 [ys[br][1], szb], [ub])
        for half in range(2):
            p_, pb_ = pT[half]
            for k in range(8):
                P.op("pe", lambda e, p_=p_, k=k, half=half: e.transpose(out=p_[:, k, :], in_=u[:, (half * 8 + k) * 128:(half * 8 + k + 1) * 128], identity=ident[:]), [ub, identb], [pb_])
            P.op("act", lambda e, p_=p_, half=half: e.activation(func=AF.Identity, out=uT[:, half * 8:(half + 1) * 8, :], in_=p_[:]), [pb_], [uTb])
        for br in range(4):
            for hf in range(2):
                pp, ppb = pP[hf]
                pg, pgb = pG[hf]
                for k in range(4):
                    P.op("pe", lambda e, pp=pp, br=br, k=k, hf=hf: e.matmul(pp[:], lhsT=uT[:, br * 4 + k, :], rhs=wbr[:, br, k, hf * 512:(hf + 1) * 512], start=(k == 0), stop=(k == 3)), [uTb, wbrb], [ppb])
                for k in range(8):
                    P.op("pe", lambda e, pg=pg, br=br, k=k, hf=hf, h_=h_: e.matmul(pg[:], lhsT=h_[:, k, :], rhs=wmg[:, br, k, hf * 512:(hf + 1) * 512], start=(k == 0), stop=False), [hb, wmgb], [pgb])
                P.op("pe", lambda e, pg=pg, br=br, hf=hf: e.matmul(pg[:], lhsT=ones[:], rhs=bmg[:, br, hf * 512:(hf + 1) * 512], start=False, stop=True), [onesb, bmgb], [pgb])
                P.op("act", lambda e, pg=pg, hf=hf: e.activation(out=g[:, hf * 512:(hf + 1) * 512], in_=pg[:], func=AF.Sigmoid), [pgb], [gb])
                if br == 0:
                    P.op("dve", lambda e, pp=pp, hf=hf: e.tensor_tensor(out=acc[:, hf * 512:(hf + 1) * 512], in0=pp[:], in1=g[:, hf * 512:(hf + 1) * 512], op=ALU.mult), [ppb, gb], [accb])
                else:
                    P.op("dve", lambda e, pp=pp, hf=hf: e.tensor_tensor(out=tmp[:, hf * 512:(hf + 1) * 512], in0=pp[:], in1=g[:, hf * 512:(hf + 1) * 512], op=ALU.mult), [ppb, gb], [tmpb])
                    P.op("pool", lambda e, hf=hf: e.tensor_tensor(out=acc[:, hf * 512:(hf + 1) * 512], in0=acc[:, hf * 512:(hf + 1) * 512], in1=tmp[:, hf * 512:(hf + 1) * 512], op=ALU.add), [accb, tmpb], [accb])
        P.op("dve", lambda e: e.tensor_copy(out=accbf[:], in_=acc[:]), [accb], [accbfb])
        a_, ab = aT[n % 2]
        p_, pb_ = pT[0]
        for k in range(8):
            P.op("pe", lambda e, p_=p_, k=k: e.transpose(out=p_[:, k, :], in_=accbf[:, k * 128:(k + 1) * 128], identity=ident[:]), [accbfb, identb], [pb_])
        P.op("act", lambda e, p_=p_, a_=a_: e.activation(func=AF.Identity, out=a_[:], in_=p_[:]), [pb_], [ab])
        P.dma("pool", [(C.accT[i], a_[:])], ab, False)
    P.end()
    P.begin()
    identf, identfb = P.sb([128, 128], F32, "identf")
    P.dma("sp", [(identf[:], C.identf)], identfb, True)
    onef, onefb = P.sb([128, 128], F32, "onef")
    P.op("dve", lambda e: e.memset(onef[:], 1.0), [], [onefb])
    wo, wob = P.sb([128, 8, 1024], BF16, "wo")
    stg = [P.sb([128, 1024], F32, f"wstg{i}") for i in range(2)]
    for k in range(8):
        s_, sb_ = stg[k % 2]
        P.dma("sp" if k % 2 == 0 else "pool", [(s_[:], C.w_out[l, k * 128:(k + 1) * 128, :])], sb_, True)
        P.op("pool" if k % 2 == 0 else "dve", lambda e, s_=s_, k=k: e.tensor_copy(out=wo[:, k, :], in_=s_[:]), [sb_], [wob])
    gtr = [P.sb([128, 1024], F32, f"gtr{v}") for v in range(2)]
    dg, dgb = P.sb([128, 128], F32, "dg")
    pR = [P.ps([128, 512], F32, f"pR{i}") for i in range(2)]
    for v in range(2):
        for j in range(8):
            P.op("dve", lambda e, v=v, j=j: e.tensor_scalar(out=dg[:], in0=identf[:], scalar1=C.modt[:, l, 2, j, v:v + 1], scalar2=None, op0=ALU.mult), [identfb, C.modtb], [dgb])
            P.op("pe", lambda e, j=j: e.matmul(pR[j // 4][0][:, (j % 4) * 128:(j % 4 + 1) * 128], lhsT=onef[:], rhs=dg[:], start=True, stop=True), [onefb, dgb], [pR[j // 4][1]])
        for hf in range(2):
            P.op("act", lambda e, v=v, hf=hf: e.activation(func=AF.Identity, out=gtr[v][0][:, hf * 512:(hf + 1) * 512], in_=pR[hf][0][:]), [pR[hf][1]], [gtr[v][1]])
    aT = [P.sb([128, 8, 128], BF16, f"aT{i}") for i in range(2)]
    xt = [P.sb([128, 1024], F32, f"xt{i}") for i in range(2)]
    xo = [P.sb([128, 1024], F32, f"xo{i}") for i in range(2)]
    t2, t2b = P.sb([128, 1024], F32, "t2")
    pO = [P.ps([128, 512], F32, f"pO{i}") for i in range(4)]
    for n, i in enumerate(tiles):
        v = 1 if i < 2 else 0
        a_, ab = aT[n % 2]
        x_, xb = xt[n % 2]
        o_, ob = xo[n % 2]
        P.dma("sp", [(a_[:], C.accT[i])], ab, True)
        P.dma("pool", [(x_[:], C.xsrc[l](i))], xb, True)
        for hf in range(2):
            po, pob = pO[(n % 2) * 2 + hf]
            for k in range(8):
                P.op("pe", lambda e, po=po, a_=a_, k=k, hf=hf: e.matmul(po[:], lhsT=a_[:, k, :], rhs=wo[:, k, hf * 512:(hf + 1) * 512], start=(k == 0), stop=(k == 7)), [ab, wob], [pob])
            P.op("dve", lambda e, po=po, hf=hf, v=v: e.tensor_tensor(out=t2[:, hf * 512:(hf + 1) * 512], in0=po[:], in1=gtr[v][0][:, hf * 512:(hf + 1) * 512], op=ALU.mult), [pob, gtr[v][1]], [t2b])
            P.op("pool", lambda e, hf=hf, x_=x_, o_=o_: e.tensor_tensor(out=o_[:, hf * 512:(hf + 1) * 512], in0=t2[:, hf * 512:(hf + 1) * 512], in1=x_[:, hf * 512:(hf + 1) * 512], op=ALU.add), [t2b, xb], [ob])
        dst = C.xres[i * 128:(i + 1) * 128, :] if l == 0 else C.y[(i - 2) * 128:(i - 1) * 128, :]
        P.dma("sp", [(dst, o_[:])], ob, False)
    P.end()

def build(layers=(0, 1), phases=("p0", "p1", "mla", "nat", "hg", "gla", "merge"), dbg=(), dbg_in=(), max_groups=None, stage=99):
    nc = bass.Bass("TRN2", target_bir_lowering=False)
    P = Prog(nc)
    C = Ctx()
    C.dbg = dbg
    C.max_groups = max_groups
    C.stage = stage

    def scr(name, shape, dt):
        kind = "ExternalOutput" if name in dbg else ("ExternalInput" if name in dbg_in else "Internal")
        return nc.dram_tensor(name, list(shape), dt, kind=kind).ap()
    C.x = dram_in(nc, "x", [NLAT, D])
    C.ctx = dram_in(nc, "ctx", [NCTX, D])
    C.cvec = dram_in(nc, "cvec", [128, 8, 2])
    C.ada_w = dram_in(nc, "ada_w", [2, D, 3 * D])
    C.ada_b_fm = dram_in(nc, "ada_b_fm", [128, 2, 24])
    C.norm_g_fm = dram_in(nc, "norm_g_fm", [128, 2, 8])
    C.w_in = dram_in(nc, "w_in", [2, D, IN_W])
    C.ident = dram_in(nc, "ident", [128, 128], BF16)
    C.identf = dram_in(nc, "identf", [128, 128])
    C.mla_w_uq = dram_in(nc, "mla_w_uq", [2, 256, 768])
    C.mla_w_ukv = dram_in(nc, "mla_w_ukv", [2, 128, 1024])
    C.mla_gc_rep = dram_in(nc, "mla_gc_rep", [2, 128, 384])
    C.mla_gq_rep = dram_in(nc, "mla_gq_rep", [2, 128, 96])
    C.mla_gk_rep = dram_in(nc, "mla_gk_rep", [2, 128, 96])
    C.rope_cs = dram_in(nc, "rope_cs", [T, 32])
    C.mla_qT = scr("mla_qT", [2, 8, 17, 96, 512], BF16)
    C.mla_kT = scr("mla_kT", [8, 96, T], BF16)
    C.mla_va = scr("mla_va", [8, 128, NT, 66], BF16)
    C.nat_gq_rep = dram_in(nc, "nat_gq_rep", [2, 128, 64])
    C.nat_gk_rep = dram_in(nc, "nat_gk_rep", [2, 128, 64])
    C.nat_bias = dram_in(nc, "nat_bias", [2, 3, 8, 8, 128, 512])
    C.nat_qT = scr("nat_qT", [4, 17, 128, 512], BF16)
    C.nat_kT = scr("nat_kT", [4, 128, T], BF16)
    C.nat_va = scr("nat_va", [8, 128, NT, 66], BF16)
    C.scan_consts = dram_in(nc, "scan_consts", [128, 388])
    C.hg_go_rep = dram_in(nc, "hg_go_rep", [2, 128, 128])
    C.gla_go_rep = dram_in(nc, "gla_go_rep", [2, 128, 128])
    C.hg_lb_rep = dram_in(nc, "hg_lb_rep", [2, 128, 2, 512])
    C.gla_w2 = dram_in(nc, "gla_w2", [2, 2, 16, 256])
    C.gla_b2_rep = dram_in(nc, "gla_b2_rep", [2, 128, 2, 256])
    C.w_br = dram_in(nc, "w_br", [2, 4, 512, D])
    C.w_merge = dram_in(nc, "w_merge", [2, 4, D, D])
    C.b_merge = dram_in(nc, "b_merge", [2, 4, D])
    C.w_out = dram_in(nc, "w_out", [2, D, D])
    C.accT = scr("accT", [NT, 128, 8, 128], BF16)
    C.osc = scr("osc", [T, 512], F32)
    C.ybr = [scr(f"ybr{i}", [T, 512], F32) for i in range(4)]
    C.y = nc.dram_tensor("y", [NLAT, D], F32, kind="ExternalOutput").ap()
    C.xres = scr("xres", [T, D], F32)
    C.hT = scr("hT", [NT, 128, 8, 128], BF16)
    C.proj = scr("proj", [T, IN_W], F32)
    C.modt, C.modtb = P.sb([128, 2, 3, 8, 2], F32, "modt", glob=True)
    C.A, C.Ab = P.sb([128, 2, 8, 2], F32, "Amod", glob=True)

    def src0(i):
        return C.ctx[i * 128:(i + 1) * 128, :] if i < 2 else C.x[(i - 2) * 128:(i - 1) * 128, :]

    def src1(i):
        return C.xres[i * 128:(i + 1) * 128, :]
    C.xsrc = [src0, src1]
    if "p0" in phases:
        phase0_adaln(P, C)
    for l in layers:
        if "p1" in phases:
            phase1_inproj(P, C, l)
        if "mla" in phases or "mlaprep" in phases:
            mla_prep(P, C, l)
        if "mla" in phases or "mlamain" in phases:
            mla_main(P, C, l)
        if "nat" in phases:
            nat_prep(P, C, l)
            nat_main(P, C, l)
        if "hg" in phases:
            scan_mixer(P, C, l, "hg")
        if "gla" in phases:
            scan_mixer(P, C, l, "gla")
        if "merge" in phases:
            merge_phase(P, C, l)
    P.finish()
    return nc


def rope_table():
    quarter = 8
    inv_freq = (10000.0 ** (-np.arange(quarter, dtype=np.float32) / quarter)).astype(np.float32)
    t = np.arange(NLAT)
    row = (t // 64).astype(np.float32)
    col = (t % 64).astype(np.float32)
    ang = np.concatenate([row[:, None] * inv_freq, col[:, None] * inv_freq], axis=-1).astype(np.float32)
    cs = np.zeros((T, 32), np.float32)
    cs[:NCTX, 0:16] = 1.0
    cs[NCTX:, 0:16] = np.cos(ang)
    cs[NCTX:, 16:32] = np.sin(ang)
    return cs


def rep(v):
    return np.ascontiguousarray(np.broadcast_to(v[:, None, :], (v.shape[0], 128, v.shape[1]))).astype(np.float32)


def host_inputs(inp, b):
    import ml_dtypes
    d = {}
    d["x"] = np.ascontiguousarray(inp["x"][b])
    d["ctx"] = np.ascontiguousarray(inp["ctx"][b])
    cv = np.stack([inp["c"][b], inp["c_ctx"]], -1)
    d["cvec"] = np.ascontiguousarray(cv.reshape(8, 128, 2).transpose(1, 0, 2))
    d["ada_w"] = inp["ada_w"]
    d["ada_b_fm"] = np.ascontiguousarray(inp["ada_b"].reshape(2, 24, 128).transpose(2, 0, 1))
    d["norm_g_fm"] = np.ascontiguousarray(inp["norm_g"].reshape(2, 8, 128).transpose(2, 0, 1))
    d["w_in"] = inp["w_in"]
    d["ident"] = np.eye(128).astype(ml_dtypes.bfloat16)
    d["identf"] = np.eye(128).astype(np.float32)
    d["mla_w_uq"] = inp["mla_w_uq"]
    d["mla_w_ukv"] = inp["mla_w_ukv"]
    d["mla_gc_rep"] = rep(np.concatenate([inp["mla_g_cq"], inp["mla_g_ckv"]], -1))
    d["mla_gq_rep"] = rep(inp["mla_g_q"])
    d["mla_gk_rep"] = rep(inp["mla_g_k"])
    d["rope_cs"] = rope_table()
    d["nat_gq_rep"] = rep(inp["nat_g_q"])
    d["nat_gk_rep"] = rep(inp["nat_g_k"])
    d["nat_bias"] = nat_bias_host(inp["nat_rpb"])
    d["scan_consts"] = scan_consts_host()
    d["hg_go_rep"] = rep(inp["hg_g_o"])
    d["gla_go_rep"] = rep(inp["gla_g_o"])
    d["hg_lb_rep"] = np.ascontiguousarray(np.broadcast_to(inp["hg_lb_logits"][:, None], (2, 128, 2, 512))).astype(np.float32)
    d["gla_w2"] = inp["gla_w2"]
    d["w_br"] = inp["w_br"]
    d["w_merge"] = inp["w_merge"]
    d["b_merge"] = inp["b_merge"]
    d["w_out"] = inp["w_out"]
    d["gla_b2_rep"] = np.ascontiguousarray(np.broadcast_to(inp["gla_b2"][:, None], (2, 128, 2, 256))).astype(np.float32)
    return d


def kernel(**inputs):
    inp = {k: np.asarray(v) for k, v in inputs.items()}
    nc = build(layers=(0, 1))
    in_maps = [host_inputs(inp, b) for b in range(4)]
    res = run_bass_kernel_spmd(nc, in_maps, core_ids=list(range(4)))
    return np.stack([r["y"] for r in res.results], 0).astype(np.float32)
```

```python
import numpy as np
from contextlib import ExitStack
import concourse.bass as bass
import concourse.mybir as mybir
from concourse.bass_utils import run_bass_kernel_spmd

F32 = mybir.dt.float32
BF16 = mybir.dt.bfloat16
AF = mybir.ActivationFunctionType
ALU = mybir.AluOpType
AX = mybir.AxisListType

ENGS = ("pe", "act", "dve", "pool", "sp")
NDMASEM = 56


class Buf:
    __slots__ = ("name", "w", "r", "sem", "excl")

    def __init__(self, name):
        self.name = name
        self.excl = False
        self.w = None
        self.r = {}
        self.sem = None


class Op:
    __slots__ = ("eng", "fn", "waits", "signal", "token", "ndma")

    def __init__(self, eng, fn):
        self.eng = eng
        self.fn = fn
        self.waits = {}
        self.signal = False
        self.token = None
        self.ndma = 0


class Prog:
    def __init__(self, nc):
        self.nc = nc
        self.ges = ExitStack()
        self.es = None
        self.cnt = {}
        self.sigbase = {e: 0 for e in ENGS}
        self.ntile = 0
        self.sems = {}
        for e in ENGS:
            self.sems[e] = self.ges.enter_context(nc.semaphore("s_" + e))
        for i in range(NDMASEM):
            self.sems[f"d{i}"] = self.ges.enter_context(nc.semaphore(f"s_d{i}"))
        self.gbufs = []
        self.nphase = 0
        self._reset_phase()

    def _reset_phase(self):
        self.ops = {e: [] for e in ENGS}
        self.tok_op = {}
        self.pbufs = []
        self.ndsem = 0
        self.ndsem_sw = 0

    def _stack(self, glob):
        return self.ges if glob else self.es

    def sb(self, shape, dt, name="t", glob=False):
        self.ntile += 1
        t = self._stack(glob).enter_context(self.nc.sbuf_tensor(f"{name}_{self.ntile}", list(shape), dt))
        b = Buf(name)
        (self.gbufs if glob else self.pbufs).append(b)
        return t, b

    def ps(self, shape, dt, name="p"):
        self.ntile += 1
        t = self.es.enter_context(self.nc.psum_tensor(f"{name}_{self.ntile}", list(shape), dt))
        b = Buf(name)
        b.excl = True
        self.pbufs.append(b)
        return t, b

    def begin(self):
        self.es = ExitStack()
        self._reset_phase()

    def _dep(self, op, tok):
        if tok is None:
            return
        key, c = tok
        if key == "pe" and op.eng == "pe":
            return
        if op.waits.get(key, 0) < c:
            op.waits[key] = c
        if key in ENGS:
            self.tok_op[tok].signal = True

    def _track(self, o, reads, writes):
        ex = [b for b in reads if b.excl and o.eng != "pe"]
        if ex:
            reads = [b for b in reads if not (b.excl and o.eng != "pe")]
            writes = list(writes) + ex
        for b in reads:
            self._dep(o, b.w)
        for b in writes:
            self._dep(o, b.w)
            for t in b.r.items():
                self._dep(o, t)
        for b in reads:
            if b.r.get(o.token[0], 0) < o.token[1]:
                b.r[o.token[0]] = o.token[1]
        for b in writes:
            b.w = o.token
            b.r = {}

    def op(self, eng, fn, reads=(), writes=()):
        o = Op(eng, fn)
        c = self.cnt.get(eng, 0) + 1
        self.cnt[eng] = c
        o.token = (eng, c)
        self.tok_op[o.token] = o
        self._track(o, reads, writes)
        self.ops[eng].append(o)
        return o

    def dma(self, q, pairs, sbuf, load):
        sw = (q == "pool")
        if sbuf.sem is None or sbuf.sem[1] != self.nphase:
            if sw:
                assert self.ndsem_sw < NDMASEM // 2, "out of sw dma semaphores"
                sbuf.sem = (f"d{NDMASEM // 2 + self.ndsem_sw}", self.nphase, sw)
                self.ndsem_sw += 1
            else:
                assert self.ndsem < NDMASEM // 2, "out of hw dma semaphores"
                sbuf.sem = (f"d{self.ndsem}", self.nphase, sw)
                self.ndsem += 1
        assert sbuf.sem[2] == sw, "buffer used from both DMA queue kinds: " + sbuf.name
        key = sbuf.sem[0]

        def fn(e, pairs=pairs):
            return [e.dma_start(out=o_, in_=i_) for (o_, i_) in pairs]
        o = Op(q, fn)
        o.ndma = len(pairs)
        c = self.cnt.get(key, 0) + 16 * len(pairs)
        self.cnt[key] = c
        o.token = (key, c)
        o.signal = True
        if load:
            self._track(o, [], [sbuf])
        else:
            self._track(o, [sbuf], [])
        self.ops[q].append(o)
        return o

    def end(self):
        nc = self.nc
        for e in ENGS:
            for o in reversed(self.ops[e]):
                if o.ndma == 0 and o.fn is not None:
                    o.signal = True
                    break
        snap = dict(self.cnt)
        for e in ENGS:
            o = Op(e, None)
            for key, c in snap.items():
                if key == e and e == "pe":
                    continue
                if c > 0:
                    o.waits[key] = c
            self.ops[e].append(o)
        sig_index = {}
        last_sig = dict(self.sigbase)
        for e in ENGS:
            k = self.sigbase[e]
            for o in self.ops[e]:
                if o.fn is None or o.ndma:
                    continue
                if o.signal:
                    k += 1
                sig_index[o.token] = k
            last_sig[e] = k
        prog = self
        sems = self.sems
        base = dict(self.sigbase)

        def section(e):
            def body(eng):
                waited = {}
                for o in prog.ops[e]:
                    for key, c in o.waits.items():
                        if key in ENGS:
                            v = sig_index.get((key, c))
                            if v is None:
                                continue
                            if v <= base[key]:
                                continue
                        else:
                            v = c
                        if waited.get(key, 0) < v:
                            eng.wait_ge(sems[key], v)
                            waited[key] = v
                    if o.fn is None:
                        continue
                    r = o.fn(eng)
                    if o.ndma:
                        for ins in r:
                            ins.then_inc(sems[o.token[0]], 16)
                    elif o.signal:
                        r.then_inc(sems[e], 1)
            return body

        with nc.Block() as block:
            bl = {"pe": block.tensor, "act": block.scalar, "dve": block.vector,
                  "pool": block.gpsimd, "sp": block.sync}
            for e in ENGS:
                bl[e](section(e))
        self.sigbase = last_sig
        for b in self.gbufs + self.pbufs:
            b.w = None
            b.r = {}
        self.es.close()
        self.es = None
        self.nphase += 1

    def finish(self):
        self.ges.close()


D = 1024
NCTX = 256
NLAT = 8192
T = NCTX + NLAT
NT = T // 128
IN_W = 7104
EPS = 1e-6
SEG = {"mla": (0, 416), "nat": (416, 1536), "hg": (1952, 2048), "gla": (4000, 1056), "z": (5056, 2048)}


def dram_in(nc, name, shape, dt=F32):
    return nc.dram_tensor(name, list(shape), dt, kind="ExternalInput").ap()


def dram_scr(nc, name, shape, dt):
    return nc.dram_tensor(name, list(shape), dt, kind="Internal").ap()


class Ctx:
    pass


def phase0_adaln(P, C):
    P.begin()
    cv, cvb = P.sb([128, 8, 2], F32, "cv")
    sl, slb = P.sb([128, 8, 2], F32, "sl")
    ab, abb = P.sb([128, 2, 24], F32, "ab")
    ng, ngb = P.sb([128, 2, 8], F32, "ng")
    P.dma("sp", [(cv[:], C.cvec)], cvb, True)
    P.dma("sp", [(ab[:], C.ada_b_fm)], abb, True)
    P.dma("sp", [(ng[:], C.norm_g_fm)], ngb, True)
    P.op("act", lambda e: e.activation(out=sl[:], in_=cv[:], func=AF.Silu), [cvb], [slb])
    wbuf = [P.sb([128, 8, 1024], F32, f"adaw{i}") for i in range(2)]
    pp = [P.ps([128, 512], F32, f"pm{i}") for i in range(2)]
    it = 0
    for l in range(2):
        for part in range(3):
            w, wb = wbuf[it % 2]
            src = C.ada_w[l, :, part * 1024:(part + 1) * 1024].rearrange("(k p) n -> p k n", p=128)
            P.dma("sp" if it % 2 == 0 else "pool", [(w[:, 0:4, :], src[:, 0:4, :]), (w[:, 4:8, :], src[:, 4:8, :])], wb, True)
            for j in range(8):
                ps_, psb = pp[j % 2]
                for k in range(8):
                    P.op("pe", lambda e, ps_=ps_, w=w, k=k, j=j: e.matmul(ps_[:, 0:2], lhsT=w[:, k, j * 128:(j + 1) * 128], rhs=sl[:, k, :],
                                                                       start=(k == 0), stop=(k == 7)), [wb, slb], [psb])
                P.op("act", lambda e, ps_=ps_, l=l, part=part, j=j: e.activation(out=C.modt[:, l, part, j, :], in_=ps_[:, 0:2], func=AF.Identity,
                                                                             bias=ab[:, l, part * 8 + j:part * 8 + j + 1], scale=1.0),
                     [psb, abb], [C.modtb])
            it += 1
    for l in range(2):
        for v in range(2):
            P.op("dve", lambda e, l=l, v=v: e.scalar_tensor_tensor(out=C.A[:, l, :, v], in0=C.modt[:, l, 1, :, v], scalar=1.0, in1=ng[:, l, :],
                                                                  op0=ALU.add, op1=ALU.mult), [C.modtb, ngb], [C.Ab])
    P.end()


def phase1_inproj(P, C, l):
    src_rows = C.xsrc[l]
    halves = [(0, 3552), (3552, 3552)]
    for hi, (c0, cw) in enumerate(halves):
        P.begin()
        ident, identb = P.sb([128, 128], BF16, "ident")
        P.dma("sp", [(ident[:], C.ident)], identb, True)
        wsb, wsbb = P.sb([128, 8, cw], BF16, "win")
        stg = [P.sb([128, 1776], F32, f"wstg{i}") for i in range(2)]
        n = 0
        for k in range(8):
            for q in range(cw // 1776):
                s_, sb_ = stg[n % 2]
                P.dma("sp" if n % 2 == 0 else "pool", [(s_[:], C.w_in[l, k * 128:(k + 1) * 128, c0 + q * 1776:c0 + (q + 1) * 1776])], sb_, True)
                eng = "pool" if n % 2 == 0 else "dve"
                P.op(eng, lambda e, s_=s_, k=k, q=q: e.tensor_copy(out=wsb[:, k, q * 1776:(q + 1) * 1776], in_=s_[:]), [sb_], [wsbb])
                n += 1
        xt = [P.sb([128, 1024], F32, f"xt{i}") for i in range(2)]
        sq, sqb = P.sb([128, 1024], F32, "sq")
        st = [P.sb([128, 4], F32, f"st{i}") for i in range(2)]
        xn = [P.sb([128, 1024], BF16, f"xn{i}") for i in range(2)]
        hT = [P.sb([128, 8, 128], BF16, f"hT{i}") for i in range(3)]
        og = [P.sb([128, cw], F32, f"og{i}") for i in range(2)]
        pT = [P.ps([128, 8, 128], BF16, f"pT{i}") for i in range(1)]
        pY = [P.ps([128, 512], F32, f"pY{i}") for i in range(6)]
        ngrp = (cw + 511) // 512
        ev = 0
        for i in range(NT):
            v = 1 if i < 2 else 0
            h_, hb = hT[i % 3]
            if hi == 0:
                x_, xb = xt[i % 2]
                s_, sb_ = st[i % 2]
                n_, nb = xn[i % 2]
                p_, pb = pT[0]
                P.dma("sp", [(x_[:], src_rows(i))], xb, True)
                P.op("act", lambda e, x_=x_, s_=s_: e.activation(out=sq[:], in_=x_[:], func=AF.Square, accum_out=s_[:, 0:1]), [xb], [sqb, sb_])
                P.op("dve", lambda e, s_=s_: e.tensor_scalar(out=s_[:, 1:2], in0=s_[:, 0:1], scalar1=1.0 / D, scalar2=EPS, op0=ALU.mult, op1=ALU.add), [sb_], [sb_])
                P.op("act", lambda e, s_=s_: e.activation(out=s_[:, 2:3], in_=s_[:, 1:2], func=AF.Ln), [sb_], [sb_])
                P.op("act", lambda e, s_=s_: e.activation(out=s_[:, 3:4], in_=s_[:, 2:3], func=AF.Exp, scale=-0.5), [sb_], [sb_])
                P.op("pool", lambda e, x_=x_, s_=s_, n_=n_: e.tensor_scalar(out=n_[:], in0=x_[:], scalar1=s_[:, 3:4], scalar2=None, op0=ALU.mult), [xb, sb_], [nb])
                for k in range(8):
                    P.op("pe", lambda e, k=k, p_=p_, n_=n_: e.transpose(out=p_[:, k, :], in_=n_[:, k * 128:(k + 1) * 128], identity=ident[:]), [nb, identb], [pb])
                for k in range(8):
                    P.op("act", lambda e, k=k, p_=p_, h_=h_, v=v: e.activation(out=h_[:, k, :], in_=p_[:, k, :], func=AF.Identity,
                                                                            scale=C.A[:, l, k, v:v + 1], bias=C.modt[:, l, 0, k, v:v + 1]), [pb, C.Ab, C.modtb], [hb])
                P.dma("pool", [(C.hT[i], h_[:])], hb, False)
            else:
                P.dma("sp", [(h_[:], C.hT[i])], hb, True)
            o_, ob = og[i % 2]
            for g in range(ngrp):
                gw = min(512, cw - g * 512)
                y_, yb = pY[ev % 6]
                for k in range(8):
                    P.op("pe", lambda e, y_=y_, h_=h_, k=k, g=g, gw=gw: e.matmul(y_[:, 0:gw], lhsT=h_[:, k, :], rhs=wsb[:, k, g * 512:g * 512 + gw],
                                                                              start=(k == 0), stop=(k == 7)), [hb, wsbb], [yb])
                if ev % 2 == 0:
                    P.op("act", lambda e, y_=y_, o_=o_, g=g, gw=gw: e.activation(func=AF.Identity, out=o_[:, g * 512:g * 512 + gw], in_=y_[:, 0:gw]), [yb], [ob])
                else:
                    P.op("dve", lambda e, y_=y_, o_=o_, g=g, gw=gw: e.tensor_copy(out=o_[:, g * 512:g * 512 + gw], in_=y_[:, 0:gw]), [yb], [ob])
                ev += 1
            P.dma("sp" if i % 2 == 0 else "pool", [(C.proj[i * 128:(i + 1) * 128, c0:c0 + cw], o_[:])], ob, False)
        P.end()


def groups():
    g = [(0, 2, 0)]
    for j in range(16):
        g.append((2 + 4 * j, 4, 1 + j))
    return g


def rstd_from_ms(P, st, stb, n, eps_done=False):
    P.op("dve", lambda e: e.tensor_scalar(out=st[:, n:2 * n], in0=st[:, 0:n], scalar1=1.0, scalar2=EPS, op0=ALU.mult, op1=ALU.add), [stb], [stb])
    P.op("act", lambda e: e.activation(out=st[:, n:2 * n], in_=st[:, n:2 * n], func=AF.Ln), [stb], [stb])
    P.op("act", lambda e: e.activation(out=st[:, 2 * n:3 * n], in_=st[:, n:2 * n], func=AF.Exp, scale=-0.5), [stb], [stb])


def head_norm(P, src, srcb, H, dh, gain, gainb, out, outb, sq, sqb, st, stb, tmp, tmpb, eng2="dve"):
    W = H * dh
    P.op("act", lambda e: e.activation(out=sq[:, 0:W], in_=src, func=AF.Square, scale=float(dh) ** -0.5), [srcb], [sqb])
    P.op("dve", lambda e: e.tensor_reduce(out=st[:, 0:H], in_=sq[:, 0:W].rearrange("p (h d) -> p h d", h=H), op=ALU.add, axis=AX.X), [sqb], [stb])
    rstd_from_ms(P, st, stb, H)
    P.op("dve", lambda e: e.tensor_tensor(out=tmp[:, 0:W].rearrange("p (h d) -> p h d", h=H), in0=src.rearrange("p (h d) -> p h d", h=H),
                                          in1=st[:, 2 * H:3 * H].unsqueeze(2).to_broadcast([128, H, dh]), op=ALU.mult), [srcb, stb], [tmpb])
    P.op(eng2, lambda e: e.tensor_tensor(out=out, in0=tmp[:, 0:W].rearrange("p (h d) -> p h d", h=H),
                                         in1=gain[:, None, :].to_broadcast([128, H, dh]), op=ALU.mult), [tmpb, gainb], [outb])


def rope(P, x, xb, cs, csb, out, outb, tt, ttb, H):
    x1, x2 = x[:, :, 0:16], x[:, :, 16:32]
    cos = cs[:, None, 0:16].to_broadcast([128, H, 16])
    sin = cs[:, None, 16:32].to_broadcast([128, H, 16])
    P.op("dve", lambda e: e.tensor_tensor(out=tt[:, 0], in0=x1, in1=cos, op=ALU.mult), [xb, csb], [ttb])
    P.op("dve", lambda e: e.tensor_tensor(out=tt[:, 1], in0=x2, in1=sin, op=ALU.mult), [xb, csb], [ttb])
    P.op("dve", lambda e: e.tensor_tensor(out=tt[:, 2], in0=x1, in1=sin, op=ALU.mult), [xb, csb], [ttb])
    P.op("dve", lambda e: e.tensor_tensor(out=tt[:, 3], in0=x2, in1=cos, op=ALU.mult), [xb, csb], [ttb])
    P.op("dve", lambda e: e.tensor_tensor(out=out[:, :, 0:16], in0=tt[:, 0], in1=tt[:, 1], op=ALU.subtract), [ttb], [outb])
    P.op("dve", lambda e: e.tensor_tensor(out=out[:, :, 16:32], in0=tt[:, 2], in1=tt[:, 3], op=ALU.add), [ttb], [outb])


def load_cast(P, dst, dstb, src, shape, q="sp", eng="pool", name="wst"):
    s_, sb_ = P.sb(shape, F32, name)
    P.dma(q, [(s_[:], src)], sb_, True)
    P.op(eng, lambda e: e.tensor_copy(out=dst, in_=s_[:]), [sb_], [dstb])


def mla_prep(P, C, l):
    P.begin()
    ident, identb = P.sb([128, 128], BF16, "ident")
    P.dma("sp", [(ident[:], C.ident)], identb, True)
    wuq, wuqb = P.sb([128, 2, 768], BF16, "wuq")
    wukv, wukvb = P.sb([128, 1024], BF16, "wukv")
    load_cast(P, wuq[:], wuqb, C.mla_w_uq[l].rearrange("(k p) n -> p k n", p=128), [128, 2, 768], "sp", "pool", "wst1")
    load_cast(P, wukv[:], wukvb, C.mla_w_ukv[l], [128, 1024], "pool", "dve", "wst2")
    gc, gcb = P.sb([128, 384], F32, "gc")
    gq, gqb = P.sb([128, 96], F32, "gq")
    gk, gkb = P.sb([128, 96], F32, "gk")
    P.dma("sp", [(gc[:], C.mla_gc_rep[l])], gcb, True)
    P.dma("sp", [(gq[:], C.mla_gq_rep[l])], gqb, True)
    P.dma("sp", [(gk[:], C.mla_gk_rep[l])], gkb, True)
    seg = [P.sb([128, 416], F32, f"seg{i}") for i in range(2)]
    cs = [P.sb([128, 32], F32, f"cs{i}") for i in range(2)]
    sq, sqb = P.sb([128, 768], F32, "sq")
    st0, st0b = P.sb([128, 8], F32, "st0")
    stq, stqb = P.sb([128, 24], F32, "stq")
    stk, stkb = P.sb([128, 24], F32, "stk")
    cn, cnb = P.sb([128, 384], BF16, "cn")
    cT, cTb = P.sb([128, 3, 128], BF16, "cT")
    qf, qfb = P.sb([128, 768], F32, "qf")
    qn, qnb = P.sb([128, 8, 96], F32, "qn")
    kf, kfb = P.sb([128, 768], F32, "kf")
    kn, knb = P.sb([128, 8, 96], F32, "kn")
    tmp, tmpb = P.sb([128, 768], F32, "tmp")
    tmp2, tmp2b = P.sb([128, 768], F32, "tmp2")
    qb_, qbb = P.sb([128, 2, 8, 96], BF16, "qb")
    kb_, kbb = P.sb([128, 8, 96], BF16, "kb")
    tt, ttb = P.sb([128, 4, 8, 16], F32, "tt")
    tt2, tt2b = P.sb([128, 4, 8, 16], F32, "tt2")
    qst = [P.sb([96, 2, 8, 512], BF16, f"qst{i}") for i in range(2)]
    kst = [P.sb([96, 8, 512], BF16, f"kst{i}") for i in range(2)]
    vst = [P.sb([128, 8, 4, 66], BF16, f"vst{i}") for i in range(2)]
    for i in range(2):
        P.op("dve", lambda e, i=i: e.memset(vst[i][0][:], 1.0), [], [vst[i][1]])
    pTc, pTcb = P.ps([128, 8, 128], BF16, "pTc")
    pq = [P.ps([128, 512], F32, f"pq{i}") for i in range(2)]
    pk = [P.ps([128, 512], F32, f"pk{i}") for i in range(2)]
    pTq, pTqb = P.ps([128, 16, 128], BF16, "pTq")
    pTk, pTkb = P.ps([128, 8, 128], BF16, "pTk")
    for gi, (t0, nt, qt) in enumerate(groups()[:C.max_groups]):
        qs, qsb = qst[gi % 2]
        ks, ksb = kst[gi % 2]
        vs, vsb = vst[gi % 2]
        for s in range(nt):
            i = t0 + s
            sg, sgb = seg[i % 2]
            c_, c_b = cs[i % 2]
            P.dma("sp", [(sg[:], C.proj[i * 128:(i + 1) * 128, 0:416])], sgb, True)
            P.dma("sp", [(c_[:], C.rope_cs[i * 128:(i + 1) * 128, :])], c_b, True)
            P.op("act", lambda e, sg=sg: e.activation(out=sq[:, 0:256], in_=sg[:, 0:256], func=AF.Square, scale=1.0 / 16, accum_out=st0[:, 0:1]), [sgb], [sqb, st0b])
            P.op("act", lambda e, sg=sg: e.activation(out=sq[:, 256:384], in_=sg[:, 256:384], func=AF.Square, scale=128 ** -0.5, accum_out=st0[:, 1:2]), [sgb], [sqb, st0b])
            rstd_from_ms(P, st0, st0b, 2)
            P.op("dve", lambda e, sg=sg: e.scalar_tensor_tensor(out=cn[:, 0:256], in0=sg[:, 0:256], scalar=st0[:, 4:5], in1=gc[:, 0:256], op0=ALU.mult, op1=ALU.mult), [sgb, st0b, gcb], [cnb])
            P.op("dve", lambda e, sg=sg: e.scalar_tensor_tensor(out=cn[:, 256:384], in0=sg[:, 256:384], scalar=st0[:, 5:6], in1=gc[:, 256:384], op0=ALU.mult, op1=ALU.mult), [sgb, st0b, gcb], [cnb])
            if C.stage <= 1:
                continue
            for k in range(3):
                P.op("pe", lambda e, k=k: e.transpose(out=pTc[:, k, :], in_=cn[:, k * 128:(k + 1) * 128], identity=ident[:]), [cnb, identb], [pTcb])
            P.op("act", lambda e: e.activation(func=AF.Identity, out=cT[:], in_=pTc[:, 0:3, :]), [pTcb], [cTb])
            for k in range(2):
                P.op("pe", lambda e, k=k: e.matmul(pq[0][0][:], lhsT=cT[:, k, :], rhs=wuq[:, k, 0:512], start=(k == 0), stop=(k == 1)), [cTb, wuqb], [pq[0][1]])
            for k in range(2):
                P.op("pe", lambda e, k=k: e.matmul(pq[1][0][:, 0:256], lhsT=cT[:, k, :], rhs=wuq[:, k, 512:768], start=(k == 0), stop=(k == 1)), [cTb, wuqb], [pq[1][1]])
            for j in range(2):
                P.op("pe", lambda e, j=j: e.matmul(pk[j][0][:], lhsT=cT[:, 2, :], rhs=wukv[:, j * 512:(j + 1) * 512], start=True, stop=True), [cTb, wukvb], [pk[j][1]])
            P.op("act", lambda e: e.activation(func=AF.Identity, out=qf[:, 0:512], in_=pq[0][0][:]), [pq[0][1]], [qfb])
            P.op("act", lambda e: e.activation(func=AF.Identity, out=qf[:, 512:768], in_=pq[1][0][:, 0:256]), [pq[1][1]], [qfb])
            if C.stage <= 2.1:
                continue
            for j in range(2):
                if C.stage > 2.2:
                    P.op("dve", lambda e, j=j: e.tensor_copy(out=kf[:].rearrange("p (h d) -> p h d", h=8)[:, j * 4:(j + 1) * 4, 0:64],
                                                         in_=pk[j][0][:].rearrange("p (h d) -> p h d", h=4)[:, :, 0:64]), [pk[j][1]], [kfb])
                if C.stage > 2.4:
                    P.op("act", lambda e, j=j, vs=vs, s=s: e.activation(func=AF.Identity, out=vs[:, j * 4:(j + 1) * 4, s, 0:64],
                                                             in_=pk[j][0][:].rearrange("p (h d) -> p h d", h=4)[:, :, 64:128]), [pk[j][1]], [vsb])
            if C.stage > 2.6:
                P.op("dve", lambda e, sg=sg: e.tensor_copy(out=kf[:].rearrange("p (h d) -> p h d", h=8)[:, :, 64:96],
                                                        in_=sg[:, None, 384:416].to_broadcast([128, 8, 32])), [sgb], [kfb])
            if C.stage <= 3:
                continue
            head_norm(P, qf[:], qfb, 8, 96, gq, gqb, qn[:], qnb, sq, sqb, stq, stqb, tmp, tmpb)
            head_norm(P, kf[:], kfb, 8, 96, gk, gkb, kn[:], knb, sq, sqb, stk, stkb, tmp2, tmp2b)
            if C.stage <= 4:
                continue
            P.op("dve", lambda e: e.tensor_copy(out=qb_[:, 0], in_=qn[:]), [qnb], [qbb])
            P.op("dve", lambda e: e.tensor_copy(out=qb_[:, 1, :, 0:64], in_=qn[:, :, 0:64]), [qnb], [qbb])
            rope(P, qn[:, :, 64:96], qnb, c_, c_b, qb_[:, 1, :, 64:96], qbb, tt, ttb, 8)
            P.op("dve", lambda e: e.tensor_copy(out=kb_[:, :, 0:64], in_=kn[:, :, 0:64]), [knb], [kbb])
            rope(P, kn[:, :, 64:96], knb, c_, c_b, kb_[:, :, 64:96], kbb, tt2, tt2b, 8)
            if C.stage <= 5:
                continue
            for ver in range(2):
                for h in range(8):
                    P.op("pe", lambda e, ver=ver, h=h: e.transpose(out=pTq[0:96, ver * 8 + h, :], in_=qb_[:, ver, h, :], identity=ident[:]), [qbb, identb], [pTqb])
            for ver in range(2):
                P.op("act" if ver == 0 else "dve",
                     (lambda e, ver=ver, qs=qs, s=s: e.activation(func=AF.Identity, out=qs[:, ver, :, s * 128:(s + 1) * 128], in_=pTq[0:96, ver * 8:(ver + 1) * 8, :])) if ver == 0 else
                     (lambda e, ver=ver, qs=qs, s=s: e.tensor_copy(out=qs[:, ver, :, s * 128:(s + 1) * 128], in_=pTq[0:96, ver * 8:(ver + 1) * 8, :])),
                     [pTqb], [qsb])
            for h in range(8):
                P.op("pe", lambda e, h=h: e.transpose(out=pTk[0:96, h, :], in_=kb_[:, h, :], identity=ident[:]), [kbb, identb], [pTkb])
            P.op("dve", lambda e, ks=ks, s=s: e.tensor_copy(out=ks[:, :, s * 128:(s + 1) * 128], in_=pTk[0:96, :, :]), [pTkb], [ksb])
        n = nt * 128
        if C.stage <= 6:
            continue
        P.dma("pool", [(C.mla_qT[:, :, qt, :, 0:n].rearrange("v h p n -> p v h n"), qs[:, :, :, 0:n])], qsb, False)
        P.dma("pool", [(C.mla_kT[:, :, t0 * 128:t0 * 128 + n].rearrange("h p n -> p h n"), ks[:, :, 0:n])], ksb, False)
        P.dma("sp", [(C.mla_va[:, :, t0:t0 + nt, :].rearrange("h p k e -> p h k e"), vs[:, :, 0:nt, :])], vsb, False)
    P.end()


def attend_main(P, C, l, name, H, dk, scale, kT_ap, q_ap, va, ybr, chunks_of, bias_src=None, with_ctx_q=True):
    P.begin()
    identf, identfb = P.sb([128, 128], F32, "identf")
    P.dma("sp", [(identf[:], C.identf)], identfb, True)
    kT = [P.sb([dk, T], BF16, f"kT{i}") for i in range(2)]
    vv = [P.sb([128, NT, 66], BF16, f"vv{i}") for i in range(2)]
    nver = 2 if name == "mla" else 1
    qq = [[P.sb([dk, 512], BF16, f"q{v}_{i}") for v in range(nver)] for i in range(2)]
    pt = [P.sb([128, 512], BF16, f"pt{i}") for i in range(3)]
    sbias = [P.sb([128, 512], F32, f"sb{i}") for i in range(2)]
    nb_tiles = 8
    bt = [[P.sb([128, 512], F32, f"bt{j}_{i}") for i in range(nb_tiles)] for j in range(2)] if bias_src else None
    oT, oTb = P.sb([65, 512], F32, "oT")
    rec, recb = P.sb([128, 4], F32, "rec")
    ost = [P.sb([128, 4, 64], F32, f"ost{i}") for i in range(2)]
    pS = [P.ps([128, 512], F32, f"pS{i}") for i in range(3)]
    pO = [P.ps([128, 512], F32, f"pO{i}") for i in range(2)]
    pXf, pXb = P.ps([128, 512], F32, "pX")
    pX = pXf[:, 0:260].rearrange("p (t e) -> p t e", e=65)
    qts = ([0] if with_ctx_q else []) + list(range(1, 17))
    LA = 2
    groups_ = [(h, qt) for h in range(H) for qt in qts]
    items = []
    for gi, (h, qt) in enumerate(groups_):
        ch = chunks_of(qt)
        for n_, (kc, ver, bid) in enumerate(ch):
            items.append((gi, n_, len(ch), kc, ver, bid))
    cur_bias = [None, None]
    st = {"bmap": {}, "pend": []}
    loaded_heads = set()
    loaded_groups = set()

    def load_head(h):
        if h >= H or h in loaded_heads:
            return
        loaded_heads.add(h)
        k_, kb = kT[h % 2]
        v_, vb = vv[h % 2]
        P.dma("sp", [(k_[:, 0:T // 2], kT_ap(h)[:, 0:T // 2]), (k_[:, T // 2:T], kT_ap(h)[:, T // 2:T])], kb, True)
        P.dma("pool", [(v_[:], va[h])], vb, True)

    def load_group(gi):
        if gi >= len(groups_) or gi in loaded_groups:
            return
        loaded_groups.add(gi)
        h, qt = groups_[gi]
        nq = 256 if qt == 0 else 512
        for v in range(nver):
            P.dma("sp", [(qq[gi % 2][v][0][:, 0:nq], q_ap(h, qt, v)[:, 0:nq])], qq[gi % 2][v][1], True)
        if bias_src:
            ids = [b for (_, _, b) in chunks_of(qt) if b is not None]
            bm = {}
            if ids:
                key = (h, tuple(ids))
                slot = None
                for j in range(2):
                    if cur_bias[j] == key:
                        slot = j
                if slot is None:
                    slot = 0 if len(ids) == 8 else 1
                    cur_bias[slot] = key
                    for n_, b in enumerate(ids):
                        P.dma("pool" if n_ % 2 else "sp", [(bt[slot][n_][0][:], bias_src(h, b))], bt[slot][n_][1], True)
                for n_, b in enumerate(ids):
                    bm[b] = bt[slot][n_]
            st["bmap"][gi] = bm

    def stage_a(n):
        gi, n_, nch, kc, ver, bid = items[n]
        h, qt = groups_[gi]
        if n_ == 0:
            load_head(h)
            load_group(gi)
            load_group(gi + 1)
            if gi + 1 < len(groups_) and groups_[gi + 1][0] != h:
                load_head(h + 1)
        nq = 256 if qt == 0 else 512
        k_, kb = kT[h % 2]
        s_, sb_ = pS[n % 3]
        p_, pb = pt[n % 3]
        q_, qb = qq[gi % 2][ver]
        P.op("pe", lambda e: e.matmul(s_[:, 0:nq], lhsT=k_[:, kc * 128:(kc + 1) * 128], rhs=q_[:, 0:nq], start=True, stop=True), [kb, qb], [sb_])
        if bid is None:
            P.op("act", lambda e: e.activation(out=p_[:, 0:nq], in_=s_[:, 0:nq], func=AF.Exp, scale=scale), [sb_], [pb])
        else:
            b_, bb = st["bmap"][gi][bid]
            x_, xb = sbias[n % 2]
            P.op("dve", lambda e: e.scalar_tensor_tensor(out=x_[:], in0=s_[:], scalar=scale, in1=b_[:], op0=ALU.mult, op1=ALU.add), [sb_, bb], [xb])
            P.op("act", lambda e: e.activation(out=p_[:], in_=x_[:], func=AF.Exp), [xb], [pb])

    def epilogue(gi):
        h, qt = groups_[gi]
        nq = 256 if qt == 0 else 512
        tok0 = 0 if qt == 0 else 256 + (qt - 1) * 512
        o_, ob = pO[gi % 2]
        nt4 = nq // 128
        P.op("act", lambda e: e.activation(func=AF.Identity, out=oT[:, 0:nq], in_=o_[0:65, 0:nq]), [ob], [oTb])
        for t in range(nt4):
            P.op("pe", lambda e, t=t: e.transpose(out=pX[:, t, :], in_=oT[:, t * 128:(t + 1) * 128], identity=identf[0:65, 0:65]), [oTb, identfb], [pXb])
        P.op("dve", lambda e: e.reciprocal(out=rec[:, 0:nt4], in_=pX[:, 0:nt4, 64]), [pXb], [recb])
        os_, osb = ost[gi % 2]
        P.op("dve", lambda e: e.tensor_tensor(out=os_[:, 0:nt4, :], in0=pX[:, 0:nt4, 0:64], in1=rec[:, 0:nt4].unsqueeze(2).to_broadcast([128, nt4, 64]), op=ALU.mult), [pXb, recb], [osb])
        P.dma("pool", [(ybr[tok0:tok0 + nq, h * 64:(h + 1) * 64].rearrange("(t p) d -> p t d", p=128), os_[:, 0:nt4, :])], osb, False)

    def stage_b(n):
        gi, n_, nch, kc, ver, bid = items[n]
        h, qt = groups_[gi]
        nq = 256 if qt == 0 else 512
        v_, vb = vv[h % 2]
        p_, pb = pt[n % 3]
        o_, ob = pO[gi % 2]
        P.op("pe", lambda e: e.matmul(o_[0:65, 0:nq], lhsT=v_[:, kc, 0:65], rhs=p_[:, 0:nq], start=(n_ == 0), stop=(n_ == nch - 1)), [vb, pb], [ob])
        if n_ == nch - 1:
            st["pend"].append((n + 3, gi))

    N = len(items)
    for n in range(N + LA):
        if n < N:
            stage_a(n)
        if n - LA >= 0:
            stage_b(n - LA)
        while st["pend"] and st["pend"][0][0] <= n:
            epilogue(st["pend"].pop(0)[1])
    while st["pend"]:
        epilogue(st["pend"].pop(0)[1])
    P.end()


def mla_main(P, C, l):
    def chunks_of(qt):
        if qt == 0:
            return [(0, 0, None), (1, 0, None)]
        return [(0, 0, None), (1, 0, None)] + [(kc, 1, None) for kc in range(2, NT)]
    attend_main(P, C, l, "mla", 8, 96, 96 ** -0.5, lambda h: C.mla_kT[h], lambda h, qt, v: C.mla_qT[v, h, qt], C.mla_va, C.ybr[0], chunks_of,
                with_ctx_q=(l == 0))


def nat_prep(P, C, l):
    P.begin()
    ident, identb = P.sb([128, 128], BF16, "ident")
    P.dma("sp", [(ident[:], C.ident)], identb, True)
    gq, gqb = P.sb([128, 64], F32, "gq")
    gk, gkb = P.sb([128, 64], F32, "gk")
    P.dma("sp", [(gq[:], C.nat_gq_rep[l])], gqb, True)
    P.dma("sp", [(gk[:], C.nat_gk_rep[l])], gkb, True)
    seg = [P.sb([128, 1536], F32, f"seg{i}") for i in range(2)]
    sq, sqb = P.sb([128, 512], F32, "sq")
    stq, stqb = P.sb([128, 24], F32, "stq")
    stk, stkb = P.sb([128, 24], F32, "stk")
    tmp, tmpb = P.sb([128, 512], F32, "tmp")
    tmp2, tmp2b = P.sb([128, 512], F32, "tmp2")
    qk, qkb = P.sb([128, 2, 8, 64], BF16, "qk")
    qst = [P.sb([128, 4, 512], BF16, f"qst{i}") for i in range(2)]
    kst = [P.sb([128, 4, 512], BF16, f"kst{i}") for i in range(2)]
    vst = [P.sb([128, 8, 4, 66], BF16, f"vst{i}") for i in range(2)]
    for i in range(2):
        P.op("dve", lambda e, i=i: e.memset(vst[i][0][:], 1.0), [], [vst[i][1]])
    pT, pTb = P.ps([128, 8, 128], BF16, "pT")
    for gi, (t0, nt, qt) in enumerate(groups()[:C.max_groups]):
        qs, qsb = qst[gi % 2]
        ks, ksb = kst[gi % 2]
        vs, vsb = vst[gi % 2]
        for s in range(nt):
            i = t0 + s
            sg, sgb = seg[i % 2]
            P.dma("sp" if i % 2 == 0 else "pool", [(sg[:], C.proj[i * 128:(i + 1) * 128, 416:1952])], sgb, True)
            head_norm(P, sg[:, 0:512], sgb, 8, 64, gq, gqb, qk[:, 0], qkb, sq, sqb, stq, stqb, tmp, tmpb)
            head_norm(P, sg[:, 512:1024], sgb, 8, 64, gk, gkb, qk[:, 1], qkb, sq, sqb, stk, stkb, tmp2, tmp2b)
            P.op("dve", lambda e, sg=sg, vs=vs, s=s: e.tensor_copy(out=vs[:, :, s, 0:64], in_=sg[:, 1024:1536].rearrange("p (h d) -> p h d", h=8)), [sgb], [vsb])
            for w in range(2):
                for pr in range(4):
                    P.op("pe", lambda e, w=w, pr=pr: e.transpose(out=pT[:, w * 4 + pr, :], in_=qk[:, w, 2 * pr:2 * pr + 2, :].rearrange("p h d -> p (h d)"), identity=ident[:]), [qkb, identb], [pTb])
            P.op("act", lambda e, qs=qs, s=s: e.activation(func=AF.Identity, out=qs[:, :, s * 128:(s + 1) * 128], in_=pT[:, 0:4, :]), [pTb], [qsb])
            P.op("act", lambda e, ks=ks, s=s: e.activation(func=AF.Identity, out=ks[:, :, s * 128:(s + 1) * 128], in_=pT[:, 4:8, :]), [pTb], [ksb])
        n = nt * 128
        P.dma("pool", [(C.nat_qT[:, qt, :, 0:n].rearrange("r p n -> p r n"), qs[:, :, 0:n])], qsb, False)
        P.dma("pool", [(C.nat_kT[:, :, t0 * 128:t0 * 128 + n].rearrange("r p n -> p r n"), ks[:, :, 0:n])], ksb, False)
        P.dma("sp", [(C.nat_va[:, :, t0:t0 + nt, :].rearrange("h p k e -> p h k e"), vs[:, :, 0:nt, :])], vsb, False)
    P.end()


def nat_block(j):
    kb = min(max(8 * j - 4, 0), 112)
    pat = 0 if j == 0 else (2 if j == 15 else 1)
    cs_ = range(0, 6) if j == 0 else (range(2, 8) if j == 15 else range(0, 8))
    return kb, pat, list(cs_)


def nat_main(P, C, l):
    def chunks_of(qt):
        if qt == 0:
            return [(0, 0, None), (1, 0, None)]
        kb, pat, cs_ = nat_block(qt - 1)
        return [(0, 0, None), (1, 0, None)] + [(2 + kb // 2 + c, 0, (pat, c)) for c in cs_]
    attend_main(P, C, l, "nat", 8, 64, 64 ** -0.5,
                lambda h: C.nat_kT[h // 2, (h % 2) * 64:(h % 2 + 1) * 64, :],
                lambda h, qt, v: C.nat_qT[h // 2, qt, (h % 2) * 64:(h % 2 + 1) * 64, :],
                C.nat_va, C.ybr[1], chunks_of, bias_src=lambda h, b: C.nat_bias[l, b[0], h, b[1]], with_ctx_q=(l == 0))


def nat_bias_host(rpb):
    L = rpb.shape[0]
    out = np.full((L, 3, 8, 8, 128, 512), -30000.0, np.float32)
    ck = np.arange(64)[:, None]
    cq = np.arange(64)[None, :]
    c0 = np.clip(cq - 8, 0, 48)
    col_ok = (ck >= c0) & (ck < c0 + 16)
    dc = np.clip(ck - cq, -15, 15) + 15
    for pat, j in enumerate((0, 1, 15)):
        kb, _, cs_ = nat_block(j)
        for c in cs_:
            for a in range(2):
                kr = kb + 2 * c + a
                for r in range(8):
                    qr = 8 * j + r
                    r0 = min(max(qr - 4, 0), 120)
                    if not (r0 <= kr < r0 + 8):
                        continue
                    dr = kr - qr + 7
                    blk = np.where(col_ok[None, None], rpb[:, :, dr][:, :, dc], np.float32(-30000.0))
                    out[:, pat, :, c, a * 64:(a + 1) * 64, r * 64:(r + 1) * 64] = blk
    return out


def scan_consts_host():
    j = np.arange(128)[:, None]
    i = np.arange(128)[None, :]
    same = (j // 32) == (i // 32)
    mf = (same & (j <= i)).astype(np.float32)
    mb = (same & (j >= i)).astype(np.float32)
    blk = same.astype(np.float32)
    ind = (j // 32 == np.arange(4)[None, :]).astype(np.float32)
    return np.ascontiguousarray(np.concatenate([mf, mb, blk, ind], 1))


def scan_mixer(P, C, l, kind):
    hg = kind == "hg"
    H, dk, dv = (4, 128, 128) if hg else (4, 64, 128)
    W = H * dk
    c0, cw = SEG["hg"] if hg else SEG["gla"]
    qscale = float(dk) ** -0.5
    br = 2 if hg else 3
    for d in range(2):
        P.begin()
        ident, identb = P.sb([128, 128], BF16, "ident")
        P.dma("sp", [(ident[:], C.ident)], identb, True)
        identf, identfb = P.sb([128, 128], F32, "identf")
        P.dma("sp", [(identf[:], C.identf)], identfb, True)
        sc, scb = P.sb([128, 388], F32, "sconst")
        P.dma("sp", [(sc[:], C.scan_consts)], scb, True)
        Md = sc[:, 0:128] if d == 0 else sc[:, 128:256]
        Blk = sc[:, 256:384]
        Ind = sc[:, 384:388]
        go, gob = P.sb([128, 128], F32, "go")
        P.dma("sp", [(go[:], (C.hg_go_rep if hg else C.gla_go_rep)[l])], gob, True)
        if hg:
            lbt, lbb = P.sb([128, 512], F32, "lbt")
            oml, omlb = P.sb([128, 512], F32, "oml")
            if l == 0:
                P.op("dve", lambda e: e.memset(lbt[:], 0.0), [], [lbb])
                P.op("dve", lambda e: e.memset(oml[:], 1.0), [], [omlb])
            else:
                zz, zzb = P.sb([128, 2, 512], F32, "zz")
                P.dma("sp", [(zz[:], C.hg_lb_rep[:, :, d, :].rearrange("l p n -> p l n"))], zzb, True)
                P.op("dve", lambda e: e.tensor_tensor(out=lbt[:], in0=zz[:, 1, :], in1=zz[:, 0, :], op=ALU.subtract), [zzb], [lbb])
                P.op("act", lambda e: e.activation(out=lbt[:], in_=lbt[:], func=AF.Sigmoid), [lbb], [lbb])
                P.op("dve", lambda e: e.tensor_scalar(out=oml[:], in0=lbt[:], scalar1=-1.0, scalar2=1.0, op0=ALU.mult, op1=ALU.add), [lbb], [omlb])
        else:
            w2, w2b = P.sb([16, 256], F32, "w2")
            b2, b2b = P.sb([128, 256], F32, "b2")
            P.dma("sp", [(w2[:], C.gla_w2[l, d])], w2b, True)
            P.dma("sp", [(b2[:], C.gla_b2_rep[l, :, d, :])], b2b, True)
            rT, rTb = P.sb([16, 128], F32, "rT")
            gx, gxb = P.sb([128, 256], F32, "gx")
        seg = [P.sb([128, cw], F32, f"seg{i}") for i in range(2)]
        qf, qfb = P.sb([128, W], F32, "qf")
        kf, kfb = P.sb([128, W], F32, "kf")
        la, lab = P.sb([128, W], F32, "la")
        vb, vbb = P.sb([128, H * dv], BF16, "vb")
        sg1, sg1b = P.sb([128, W], F32, "sg1")
        bs, bsb = P.sb([128, W], F32, "bs")
        eb, ebb = P.sb([128, W], F32, "eb")
        enb, enbb = P.sb([128, W], F32, "enb")
        ebe, ebeb = P.sb([128, W], F32, "ebe")
        dcy, dcyb = P.sb([128, H * 4], F32, "dcy")
        qt_, qtb = P.sb([128, W], BF16, "qt")
        kt_, ktb = P.sb([128, W], BF16, "kt")
        kh, khb = P.sb([128, W], BF16, "kh")
        kpad, kpadb = P.sb([128, H, 4, dk], BF16, "kpad")
        qT, qTb = P.sb([128, H, 128], BF16, "qT")
        kT, kTb = P.sb([128, H, 128], BF16, "kT")
        qpad, qpadb = P.sb([128, H, 608], BF16, "qpad")
        P.op("dve", lambda e: e.memset(qpad[:], 0.0), [], [qpadb])
        AT, ATb = P.sb([128, H, 128], BF16, "AT")
        S, Sb_ = P.sb([128, H, dv], F32, "S")
        P.op("dve", lambda e: e.memset(S[:], 0.0), [], [Sb_])
        Sbf = [P.sb([128, H, 4, dv], BF16, f"Sbf{i}") for i in range(2)]
        ofw = [P.sb([128, 512], F32, f"ofw{i}") for i in range(2)]
        ost = [P.sb([128, 512], F32, f"ost{i}") for i in range(2)]
        osum, osumb = P.sb([128, 512], F32, "osum")
        sq, sqb = P.sb([128, 512], F32, "sq")
        stn, stnb = P.sb([128, 12], F32, "stn")
        tmpn, tmpnb = P.sb([128, 512], F32, "tmpn")
        pb, pbb = P.ps([128, 512], F32, "pb")
        pbe, pbeb = P.ps([128, 512], F32, "pbe")
        pbT, pbTb = P.ps([128, 512], F32, "pbT")
        pT, pTb = P.ps([128, 8, 128], BF16, "pT")
        pS, pSb = P.ps([128, 512], F32, "pS")
        pO, pOb = P.ps([128, 512], F32, "pO")
        pU = [P.ps([128, 512], F32, f"pU{i}") for i in range(2)]
        tiles = list(range(NT)) if d == 0 else [1, 0] + list(range(NT - 1, 1, -1))
        if C.max_groups:
            tiles = tiles[:C.max_groups] if d == 0 else ([1, 0] + list(range(C.max_groups - 1, 1, -1)))
        cseq = [0, 1, 2, 3] if d == 0 else [3, 2, 1, 0]
        P.op("dve", lambda e: e.memset(Sbf[0][0][:, :, cseq[0], :], 0.0), [], [Sbf[0][1]])
        for n, i in enumerate(tiles):
            sg, sgb = seg[n % 2]
            P.dma("sp" if n % 2 == 0 else "pool", [(sg[:], C.proj[i * 128:(i + 1) * 128, c0:c0 + cw])], sgb, True)
            if d == 1:
                of_, ofb = ofw[n % 2]
                P.dma("pool" if n % 2 == 0 else "sp", [(of_[:], C.osc[i * 128:(i + 1) * 128, :])], ofb, True)
            if hg:
                fcol = 512 if d == 0 else 1024
                P.op("act", lambda e, sg=sg: e.activation(out=qf[:], in_=sg[:, 0:512], func=AF.Silu), [sgb], [qfb])
                P.op("act", lambda e, sg=sg: e.activation(out=sg1[:], in_=sg[:, fcol:fcol + 512], func=AF.Sigmoid), [sgb], [sg1b])
                P.op("act", lambda e, sg=sg: e.activation(out=kf[:], in_=sg[:, fcol:fcol + 512], func=AF.Sigmoid, scale=-1.0), [sgb], [kfb])
                P.op("dve", lambda e: e.tensor_tensor(out=kf[:], in0=kf[:], in1=oml[:], op=ALU.mult), [kfb, omlb], [kfb])
                P.op("dve", lambda e: e.tensor_tensor(out=sg1[:], in0=sg1[:], in1=oml[:], op=ALU.mult), [sg1b, omlb], [sg1b])
                P.op("dve", lambda e: e.tensor_tensor(out=sg1[:], in0=sg1[:], in1=lbt[:], op=ALU.add), [sg1b, lbb], [sg1b])
                P.op("act", lambda e: e.activation(out=la[:], in_=sg1[:], func=AF.Ln), [sg1b], [lab])
                P.op("dve", lambda e, sg=sg: e.tensor_copy(out=vb[:], in_=sg[:, 1536:2048]), [sgb], [vbb])
                qsrc, qsrcb, ksrc, ksrcb = qf[:], qfb, kf[:], kfb
            else:
                rc = 1024 + 16 * d
                P.op("pe", lambda e, sg=sg: e.transpose(out=pbT[0:16, 0:128], in_=sg[:, rc:rc + 16], identity=identf[:]), [sgb, identfb], [pbTb])
                P.op("act", lambda e: e.activation(func=AF.Identity, out=rT[:], in_=pbT[0:16, 0:128]), [pbTb], [rTb])
                P.op("pe", lambda e: e.matmul(pS[:, 0:256], lhsT=rT[:], rhs=w2[:], start=True, stop=True), [rTb, w2b], [pSb])
                P.op("dve", lambda e: e.tensor_tensor(out=gx[:], in0=pS[:, 0:256], in1=b2[:], op=ALU.add), [pSb, b2b], [gxb])
                P.op("act", lambda e: e.activation(out=gx[:], in_=gx[:], func=AF.Sigmoid), [gxb], [gxb])
                P.op("act", lambda e: e.activation(out=gx[:], in_=gx[:], func=AF.Ln), [gxb], [gxb])
                P.op("dve", lambda e: e.tensor_scalar(out=la[:], in0=gx[:], scalar1=1.0 / 16, scalar2=None, op0=ALU.mult), [gxb], [lab])
                P.op("dve", lambda e, sg=sg: e.tensor_copy(out=vb[:], in_=sg[:, 512:1024]), [sgb], [vbb])
                qsrc, qsrcb, ksrc, ksrcb = sg[:, 0:256], sgb, sg[:, 256:512], sgb
            if C.stage <= 1:
                continue
            P.op("pe", lambda e: e.matmul(pb[:, 0:W], lhsT=Md, rhs=la[:], start=True, stop=True), [scb, lab], [pbb])
            P.op("pe", lambda e: e.matmul(pbe[:, 0:W], lhsT=Blk, rhs=la[:], start=True, stop=True), [scb, lab], [pbeb])
            for h in range(H):
                P.op("pe", lambda e, h=h: e.matmul(pbT[0:dk, h * 4:(h + 1) * 4], lhsT=la[:, h * dk:(h + 1) * dk], rhs=Ind, start=True, stop=True), [lab, scb], [pbTb])
            P.op("act", lambda e: e.activation(func=AF.Identity, out=bs[:], in_=pb[:, 0:W]), [pbb], [bsb])
            P.op("act", lambda e: e.activation(out=eb[:], in_=pb[:, 0:W], func=AF.Exp), [pbb], [ebb])
            P.op("act", lambda e: e.activation(out=enb[:], in_=pb[:, 0:W], func=AF.Exp, scale=-1.0), [pbb], [enbb])
            P.op("dve", lambda e: e.tensor_tensor(out=ebe[:], in0=pbe[:, 0:W], in1=bs[:], op=ALU.subtract), [pbeb, bsb], [ebeb])
            P.op("act", lambda e: e.activation(out=ebe[:], in_=ebe[:], func=AF.Exp), [ebeb], [ebeb])
            P.op("act", lambda e: e.activation(out=dcy[0:dk, :], in_=pbT[0:dk, 0:H * 4], func=AF.Exp), [pbTb], [dcyb])
            if C.stage <= 2:
                continue
            P.op("dve", lambda e, qsrc=qsrc: e.scalar_tensor_tensor(out=qt_[:], in0=qsrc, scalar=qscale, in1=eb[:], op0=ALU.mult, op1=ALU.mult), [qsrcb, ebb], [qtb])
            P.op("dve", lambda e, ksrc=ksrc: e.tensor_tensor(out=kt_[:], in0=ksrc, in1=enb[:], op=ALU.mult), [ksrcb, enbb], [ktb])
            P.op("dve", lambda e, ksrc=ksrc: e.tensor_tensor(out=kh[:], in0=ksrc, in1=ebe[:], op=ALU.mult), [ksrcb, ebeb], [khb])
            for h in range(H):
                P.op("dve", lambda e, h=h: e.tensor_tensor(out=kpad[:, h], in0=kh[:, None, h * dk:(h + 1) * dk].to_broadcast([128, 4, dk]),
                                                           in1=Ind.unsqueeze(2).to_broadcast([128, 4, dk]), op=ALU.mult), [khb, scb], [kpadb])
            if C.stage <= 3:
                continue
            for h in range(H):
                P.op("pe", lambda e, h=h: e.transpose(out=pT[0:dk, h, :], in_=qt_[:, h * dk:(h + 1) * dk], identity=ident[:]), [qtb, identb], [pTb])
                P.op("pe", lambda e, h=h: e.transpose(out=pT[0:dk, H + h, :], in_=kt_[:, h * dk:(h + 1) * dk], identity=ident[:]), [ktb, identb], [pTb])
            P.op("act", lambda e: e.activation(func=AF.Identity, out=qT[0:dk], in_=pT[0:dk, 0:H, :]), [pTb], [qTb])
            P.op("act", lambda e: e.activation(func=AF.Identity, out=kT[0:dk], in_=pT[0:dk, H:2 * H, :]), [pTb], [kTb])
            for h in range(H):
                P.op("dve", lambda e, h=h: e.tensor_copy(out=qpad[0:dk, h, 96:608].rearrange("p (c n) -> p c n", n=128)[:, :, 0:32],
                                                         in_=pT[0:dk, h, :].rearrange("p (c t) -> p c t", t=32)), [pTb], [qpadb])
            if C.stage <= 4:
                continue
            for h in range(H):
                P.op("pe", lambda e, h=h: e.matmul(pS[:, h * 128:(h + 1) * 128], lhsT=kT[0:dk, h, :], rhs=qT[0:dk, h, :], start=True, stop=True), [kTb, qTb], [pSb])
            P.op("dve", lambda e: e.tensor_tensor(out=AT[:], in0=pS[:].rearrange("p (h n) -> p h n", h=H), in1=Md[:, None, :].to_broadcast([128, H, 128]), op=ALU.mult), [pSb, scb], [ATb])
            if C.stage <= 5:
                continue
            sb_cur, sb_curb = Sbf[n % 2]
            sb_nxt, sb_nxtb = Sbf[(n + 1) % 2]
            for ci, c in enumerate(cseq):
                u_, ub = pU[ci % 2]
                for h in range(H):
                    P.op("pe", lambda e, h=h, c=c, u_=u_: e.matmul(u_[0:dk, h * dv:(h + 1) * dv], lhsT=kpad[:, h, c, :], rhs=vb[:, h * dv:(h + 1) * dv], start=True, stop=True), [kpadb, vbb], [ub])
                for h in range(H):
                    P.op("dve", lambda e, h=h, c=c, u_=u_: e.scalar_tensor_tensor(out=S[0:dk, h, :], in0=S[0:dk, h, :], scalar=dcy[0:dk, h * 4 + c:h * 4 + c + 1],
                                                                                 in1=u_[0:dk, h * dv:(h + 1) * dv], op0=ALU.mult, op1=ALU.add), [Sb_, dcyb, ub], [Sb_])
                if ci < 3:
                    P.op("act", lambda e, ci=ci, sb_cur=sb_cur: e.activation(func=AF.Identity, out=sb_cur[0:dk, :, cseq[ci + 1], :], in_=S[0:dk]), [Sb_], [sb_curb])
                else:
                    P.op("act", lambda e, sb_nxt=sb_nxt: e.activation(func=AF.Identity, out=sb_nxt[0:dk, :, cseq[0], :], in_=S[0:dk]), [Sb_], [sb_nxtb])
            if C.stage <= 6:
                continue
            for h in range(H):
                P.op("pe", lambda e, h=h: e.matmul(pO[:, h * dv:(h + 1) * dv], lhsT=AT[:, h, :], rhs=vb[:, h * dv:(h + 1) * dv], start=True, stop=False), [ATb, vbb], [pOb])
                for ci, c in enumerate(cseq):
                    P.op("pe", lambda e, h=h, c=c, ci=ci, sb_cur=sb_cur: e.matmul(pO[:, h * dv:(h + 1) * dv], lhsT=qpad[0:dk, h, 96 + 96 * c:224 + 96 * c], rhs=sb_cur[0:dk, h, c, :],
                                                                  start=False, stop=(ci == 3)), [qpadb, sb_curb], [pOb])
            if d == 0:
                o_, ob = ost[n % 2]
                P.op("act", lambda e, o_=o_: e.activation(func=AF.Identity, out=o_[:], in_=pO[:]), [pOb], [ob])
                P.dma("pool", [(C.osc[i * 128:(i + 1) * 128, :], o_[:])], ob, False)
            else:
                o_, ob = ost[n % 2]
                P.op("dve", lambda e, of_=of_: e.tensor_tensor(out=osum[:], in0=pO[:], in1=of_[:], op=ALU.add), [pOb, ofb], [osumb])
                head_norm(P, osum[:], osumb, 4, 128, go, gob, o_[:].rearrange("p (h d) -> p h d", h=4), ob, sq, sqb, stn, stnb, tmpn, tmpnb)
                P.dma("pool", [(C.ybr[br][i * 128:(i + 1) * 128, :], o_[:])], ob, False)
        P.end()


def merge_phase(P, C, l):
    tiles = list(range(NT)) if l == 0 else list(range(2, NT))
    if C.max_groups:
        tiles = tiles[:C.max_groups]
    P.begin()
    ident, identb = P.sb([128, 128], BF16, "ident")
    P.dma("sp", [(ident[:], C.ident)], identb, True)
    wbr, wbrb = P.sb([128, 4, 4, 1024], BF16, "wbr")
    wmg, wmgb = P.sb([128, 4, 8, 1024], BF16, "wmg")
    bmg, bmgb = P.sb([1, 4, 1024], BF16, "bmg")
    ones, onesb = P.sb([1, 128], BF16, "ones")
    P.op("dve", lambda e: e.memset(ones[:], 1.0), [], [onesb])
    stg = [P.sb([128, 1024], F32, f"wstg{i}") for i in range(2)]
    n = 0
    for br in range(4):
        for k in range(4):
            s_, sb_ = stg[n % 2]
            P.dma("sp" if n % 2 == 0 else "pool", [(s_[:], C.w_br[l, br, k * 128:(k + 1) * 128, :])], sb_, True)
            P.op("pool" if n % 2 == 0 else "dve", lambda e, s_=s_, br=br, k=k: e.tensor_copy(out=wbr[:, br, k, :], in_=s_[:]), [sb_], [wbrb])
            n += 1
        for k in range(8):
            s_, sb_ = stg[n % 2]
            P.dma("sp" if n % 2 == 0 else "pool", [(s_[:], C.w_merge[l, br, k * 128:(k + 1) * 128, :])], sb_, True)
            P.op("pool" if n % 2 == 0 else "dve", lambda e, s_=s_, br=br, k=k: e.tensor_copy(out=wmg[:, br, k, :], in_=s_[:]), [sb_], [wmgb])
            n += 1
    bst, bstb = P.sb([1, 4, 1024], F32, "bst")
    P.dma("sp", [(bst[:], C.b_merge[l:l + 1])], bstb, True)
    P.op("dve", lambda e: e.tensor_copy(out=bmg[:], in_=bst[:]), [bstb], [bmgb])
    yb = [[P.sb([128, 512], F32, f"y{br}_{i}") for br in range(4)] for i in range(2)]
    zt = [P.sb([128, 2048], F32, f"z{i}") for i in range(2)]
    hT = [P.sb([128, 8, 128], BF16, f"hT{i}") for i in range(2)]
    sz, szb = P.sb([128, 2048], F32, "sz")
    u, ub = P.sb([128, 2048], BF16, "u")
    uT, uTb = P.sb([128, 16, 128], BF16, "uT")
    g, gb = P.sb([128, 1024], F32, "g")
    tmp, tmpb = P.sb([128, 1024], F32, "tmp")
    acc, accb = P.sb([128, 1024], F32, "acc")
    accbf, accbfb = P.sb([128, 1024], BF16, "accbf")
    aT = [P.sb([128, 8, 128], BF16, f"aT{i}") for i in range(2)]
    pT = [P.ps([128, 8, 128], BF16, f"pT{i}") for i in range(2)]
    pP = [P.ps([128, 512], F32, f"pP{i}") for i in range(2)]
    pG = [P.ps([128, 512], F32, f"pG{i}") for i in range(2)]
    for n, i in enumerate(tiles):
        ys = yb[n % 2]
        z_, zb = zt[n % 2]
        h_, hb = hT[n % 2]
        for br in range(4):
            P.dma("sp" if br % 2 == 0 else "pool", [(ys[br][0][:], C.ybr[br][i * 128:(i + 1) * 128, :])], ys[br][1], True)
        P.dma("sp", [(z_[:], C.proj[i * 128:(i + 1) * 128, 5056:7104])], zb, True)
        P.dma("pool", [(h_[:], C.hT[i])], hb, True)
        P.op("act", lambda e, z_=z_: e.activation(out=sz[:], in_=z_[:], func=AF.Silu), [zb], [szb])
        for br in range(4):
            P.op("dve" if br % 2 == 0 else "pool", lambda e, br=br, ys=ys: e.tensor_tensor(out=u[:, br * 512:(br + 1) * 512], in0=ys[br][0][:], in1=sz[:, br * 512:(br + 1) * 512], op=ALU.mult),
                 [ys[br][1], szb], [ub])
        for half in range(2):
            p_, pb_ = pT[half]
            for k in range(8):
                P.op("pe", lambda e, p_=p_, k=k, half=half: e.transpose(out=p_[:, k, :], in_=u[:, (half * 8 + k) * 128:(half * 8 + k + 1) * 128], identity=ident[:]), [ub, identb], [pb_])
            P.op("act", lambda e, p_=p_, half=half: e.activation(func=AF.Identity, out=uT[:, half * 8:(half + 1) * 8, :], in_=p_[:]), [pb_], [uTb])
        for br in range(4):
            for hf in range(2):
                pp, ppb = pP[hf]
                pg, pgb = pG[hf]
                for k in range(4):
                    P.op("pe", lambda e, pp=pp, br=br, k=k, hf=hf: e.matmul(pp[:], lhsT=uT[:, br * 4 + k, :], rhs=wbr[:, br, k, hf * 512:(hf + 1) * 512], start=(k == 0), stop=(k == 3)), [uTb, wbrb], [ppb])
                for k in range(8):
                    P.op("pe", lambda e, pg=pg, br=br, k=k, hf=hf, h_=h_: e.matmul(pg[:], lhsT=h_[:, k, :], rhs=wmg[:, br, k, hf * 512:(hf + 1) * 512], start=(k == 0), stop=False), [hb, wmgb], [pgb])
                P.op("pe", lambda e, pg=pg, br=br, hf=hf: e.matmul(pg[:], lhsT=ones[:], rhs=bmg[:, br, hf * 512:(hf + 1) * 512], start=False, stop=True), [onesb, bmgb], [pgb])
                P.op("act", lambda e, pg=pg, hf=hf: e.activation(out=g[:, hf * 512:(hf + 1) * 512], in_=pg[:], func=AF.Sigmoid), [pgb], [gb])
                if br == 0:
                    P.op("dve", lambda e, pp=pp, hf=hf: e.tensor_tensor(out=acc[:, hf * 512:(hf + 1) * 512], in0=pp[:], in1=g[:, hf * 512:(hf + 1) * 512], op=ALU.mult), [ppb, gb], [accb])
                else:
                    P.op("dve", lambda e, pp=pp, hf=hf: e.tensor_tensor(out=tmp[:, hf * 512:(hf + 1) * 512], in0=pp[:], in1=g[:, hf * 512:(hf + 1) * 512], op=ALU.mult), [ppb, gb], [tmpb])
                    P.op("pool", lambda e, hf=hf: e.tensor_tensor(out=acc[:, hf * 512:(hf + 1) * 512], in0=acc[:, hf * 512:(hf + 1) * 512], in1=tmp[:, hf * 512:(hf + 1) * 512], op=ALU.add), [accb, tmpb], [accb])
        P.op("dve", lambda e: e.tensor_copy(out=accbf[:], in_=acc[:]), [accb], [accbfb])
        a_, ab = aT[n % 2]
        p_, pb_ = pT[0]
        for k in range(8):
            P.op("pe", lambda e, p_=p_, k=k: e.transpose(out=p_[:, k, :], in_=accbf[:, k * 128:(k + 1) * 128], identity=ident[:]), [accbfb, identb], [pb_])
        P.op("act", lambda e, p_=p_, a_=a_: e.activation(func=AF.Identity, out=a_[:], in_=p_[:]), [pb_], [ab])
        P.dma("pool", [(C.accT[i], a_[:])], ab, False)
    P.end()
    P.begin()
    identf, identfb = P.sb([128, 128], F32, "identf")
    P.dma("sp", [(identf[:], C.identf)], identfb, True)
    onef, onefb = P.sb([128, 128], F32, "onef")
    P.op("dve", lambda e: e.memset(onef[:], 1.0), [], [onefb])
    wo, wob = P.sb([128, 8, 1024], BF16, "wo")
    stg = [P.sb([128, 1024], F32, f"wstg{i}") for i in range(2)]
    for k in range(8):
        s_, sb_ = stg[k % 2]
        P.dma("sp" if k % 2 == 0 else "pool", [(s_[:], C.w_out[l, k * 128:(k + 1) * 128, :])], sb_, True)
        P.op("pool" if k % 2 == 0 else "dve", lambda e, s_=s_, k=k: e.tensor_copy(out=wo[:, k, :], in_=s_[:]), [sb_], [wob])
    gtr = [P.sb([128, 1024], F32, f"gtr{v}") for v in range(2)]
    dg, dgb = P.sb([128, 128], F32, "dg")
    pR = [P.ps([128, 512], F32, f"pR{i}") for i in range(2)]
    for v in range(2):
        for j in range(8):
            P.op("dve", lambda e, v=v, j=j: e.tensor_scalar(out=dg[:], in0=identf[:], scalar1=C.modt[:, l, 2, j, v:v + 1], scalar2=None, op0=ALU.mult), [identfb, C.modtb], [dgb])
            P.op("pe", lambda e, j=j: e.matmul(pR[j // 4][0][:, (j % 4) * 128:(j % 4 + 1) * 128], lhsT=onef[:], rhs=dg[:], start=True, stop=True), [onefb, dgb], [pR[j // 4][1]])
        for hf in range(2):
            P.op("act", lambda e, v=v, hf=hf: e.activation(func=AF.Identity, out=gtr[v][0][:, hf * 512:(hf + 1) * 512], in_=pR[hf][0][:]), [pR[hf][1]], [gtr[v][1]])
    aT = [P.sb([128, 8, 128], BF16, f"aT{i}") for i in range(2)]
    xt = [P.sb([128, 1024], F32, f"xt{i}") for i in range(2)]
    xo = [P.sb([128, 1024], F32, f"xo{i}") for i in range(2)]
    t2, t2b = P.sb([128, 1024], F32, "t2")
    pO = [P.ps([128, 512], F32, f"pO{i}") for i in range(4)]
    for n, i in enumerate(tiles):
        v = 1 if i < 2 else 0
        a_, ab = aT[n % 2]
        x_, xb = xt[n % 2]
        o_, ob = xo[n % 2]
        P.dma("sp", [(a_[:], C.accT[i])], ab, True)
        P.dma("pool", [(x_[:], C.xsrc[l](i))], xb, True)
        for hf in range(2):
            po, pob = pO[(n % 2) * 2 + hf]
            for k in range(8):
                P.op("pe", lambda e, po=po, a_=a_, k=k, hf=hf: e.matmul(po[:], lhsT=a_[:, k, :], rhs=wo[:, k, hf * 512:(hf + 1) * 512], start=(k == 0), stop=(k == 7)), [ab, wob], [pob])
            P.op("dve", lambda e, po=po, hf=hf, v=v: e.tensor_tensor(out=t2[:, hf * 512:(hf + 1) * 512], in0=po[:], in1=gtr[v][0][:, hf * 512:(hf + 1) * 512], op=ALU.mult), [pob, gtr[v][1]], [t2b])
            P.op("pool", lambda e, hf=hf, x_=x_, o_=o_: e.tensor_tensor(out=o_[:, hf * 512:(hf + 1) * 512], in0=t2[:, hf * 512:(hf + 1) * 512], in1=x_[:, hf * 512:(hf + 1) * 512], op=ALU.add), [t2b, xb], [ob])
        dst = C.xres[i * 128:(i + 1) * 128, :] if l == 0 else C.y[(i - 2) * 128:(i - 1) * 128, :]
        P.dma("sp", [(dst, o_[:])], ob, False)
    P.end()

def build(layers=(0, 1), phases=("p0", "p1", "mla", "nat", "hg", "gla", "merge"), dbg=(), dbg_in=(), max_groups=None, stage=99):
    nc = bass.Bass("TRN2", target_bir_lowering=False)
    P = Prog(nc)
    C = Ctx()
    C.dbg = dbg
    C.max_groups = max_groups
    C.stage = stage

    def scr(name, shape, dt):
        kind = "ExternalOutput" if name in dbg else ("ExternalInput" if name in dbg_in else "Internal")
        return nc.dram_tensor(name, list(shape), dt, kind=kind).ap()
    C.x = dram_in(nc, "x", [NLAT, D])
    C.ctx = dram_in(nc, "ctx", [NCTX, D])
    C.cvec = dram_in(nc, "cvec", [128, 8, 2])
    C.ada_w = dram_in(nc, "ada_w", [2, D, 3 * D])
    C.ada_b_fm = dram_in(nc, "ada_b_fm", [128, 2, 24])
    C.norm_g_fm = dram_in(nc, "norm_g_fm", [128, 2, 8])
    C.w_in = dram_in(nc, "w_in", [2, D, IN_W])
    C.ident = dram_in(nc, "ident", [128, 128], BF16)
    C.identf = dram_in(nc, "identf", [128, 128])
    C.mla_w_uq = dram_in(nc, "mla_w_uq", [2, 256, 768])
    C.mla_w_ukv = dram_in(nc, "mla_w_ukv", [2, 128, 1024])
    C.mla_gc_rep = dram_in(nc, "mla_gc_rep", [2, 128, 384])
    C.mla_gq_rep = dram_in(nc, "mla_gq_rep", [2, 128, 96])
    C.mla_gk_rep = dram_in(nc, "mla_gk_rep", [2, 128, 96])
    C.rope_cs = dram_in(nc, "rope_cs", [T, 32])
    C.mla_qT = scr("mla_qT", [2, 8, 17, 96, 512], BF16)
    C.mla_kT = scr("mla_kT", [8, 96, T], BF16)
    C.mla_va = scr("mla_va", [8, 128, NT, 66], BF16)
    C.nat_gq_rep = dram_in(nc, "nat_gq_rep", [2, 128, 64])
    C.nat_gk_rep = dram_in(nc, "nat_gk_rep", [2, 128, 64])
    C.nat_bias = dram_in(nc, "nat_bias", [2, 3, 8, 8, 128, 512])
    C.nat_qT = scr("nat_qT", [4, 17, 128, 512], BF16)
    C.nat_kT = scr("nat_kT", [4, 128, T], BF16)
    C.nat_va = scr("nat_va", [8, 128, NT, 66], BF16)
    C.scan_consts = dram_in(nc, "scan_consts", [128, 388])
    C.hg_go_rep = dram_in(nc, "hg_go_rep", [2, 128, 128])
    C.gla_go_rep = dram_in(nc, "gla_go_rep", [2, 128, 128])
    C.hg_lb_rep = dram_in(nc, "hg_lb_rep", [2, 128, 2, 512])
    C.gla_w2 = dram_in(nc, "gla_w2", [2, 2, 16, 256])
    C.gla_b2_rep = dram_in(nc, "gla_b2_rep", [2, 128, 2, 256])
    C.w_br = dram_in(nc, "w_br", [2, 4, 512, D])
    C.w_merge = dram_in(nc, "w_merge", [2, 4, D, D])
    C.b_merge = dram_in(nc, "b_merge", [2, 4, D])
    C.w_out = dram_in(nc, "w_out", [2, D, D])
    C.accT = scr("accT", [NT, 128, 8, 128], BF16)
    C.osc = scr("osc", [T, 512], F32)
    C.ybr = [scr(f"ybr{i}", [T, 512], F32) for i in range(4)]
    C.y = nc.dram_tensor("y", [NLAT, D], F32, kind="ExternalOutput").ap()
    C.xres = scr("xres", [T, D], F32)
    C.hT = scr("hT", [NT, 128, 8, 128], BF16)
    C.proj = scr("proj", [T, IN_W], F32)
    C.modt, C.modtb = P.sb([128, 2, 3, 8, 2], F32, "modt", glob=True)
    C.A, C.Ab = P.sb([128, 2, 8, 2], F32, "Amod", glob=True)

    def src0(i):
        return C.ctx[i * 128:(i + 1) * 128, :] if i < 2 else C.x[(i - 2) * 128:(i - 1) * 128, :]

    def src1(i):
        return C.xres[i * 128:(i + 1) * 128, :]
    C.xsrc = [src0, src1]
    if "p0" in phases:
        phase0_adaln(P, C)
    for l in layers:
        if "p1" in phases:
            phase1_inproj(P, C, l)
        if "mla" in phases or "mlaprep" in phases:
            mla_prep(P, C, l)
        if "mla" in phases or "mlamain" in phases:
            mla_main(P, C, l)
        if "nat" in phases:
            nat_prep(P, C, l)
            nat_main(P, C, l)
        if "hg" in phases:
            scan_mixer(P, C, l, "hg")
        if "gla" in phases:
            scan_mixer(P, C, l, "gla")
        if "merge" in phases:
            merge_phase(P, C, l)
    P.finish()
    return nc


def rope_table():
    quarter = 8
    inv_freq = (10000.0 ** (-np.arange(quarter, dtype=np.float32) / quarter)).astype(np.float32)
    t = np.arange(NLAT)
    row = (t // 64).astype(np.float32)
    col = (t % 64).astype(np.float32)
    ang = np.concatenate([row[:, None] * inv_freq, col[:, None] * inv_freq], axis=-1).astype(np.float32)
    cs = np.zeros((T, 32), np.float32)
    cs[:NCTX, 0:16] = 1.0
    cs[NCTX:, 0:16] = np.cos(ang)
    cs[NCTX:, 16:32] = np.sin(ang)
    return cs


def rep(v):
    return np.ascontiguousarray(np.broadcast_to(v[:, None, :], (v.shape[0], 128, v.shape[1]))).astype(np.float32)


def host_inputs(inp, b):
    import ml_dtypes
    d = {}
    d["x"] = np.ascontiguousarray(inp["x"][b])
    d["ctx"] = np.ascontiguousarray(inp["ctx"][b])
    cv = np.stack([inp["c"][b], inp["c_ctx"]], -1)
    d["cvec"] = np.ascontiguousarray(cv.reshape(8, 128, 2).transpose(1, 0, 2))
    d["ada_w"] = inp["ada_w"]
    d["ada_b_fm"] = np.ascontiguousarray(inp["ada_b"].reshape(2, 24, 128).transpose(2, 0, 1))
    d["norm_g_fm"] = np.ascontiguousarray(inp["norm_g"].reshape(2, 8, 128).transpose(2, 0, 1))
    d["w_in"] = inp["w_in"]
    d["ident"] = np.eye(128).astype(ml_dtypes.bfloat16)
    d["identf"] = np.eye(128).astype(np.float32)
    d["mla_w_uq"] = inp["mla_w_uq"]
    d["mla_w_ukv"] = inp["mla_w_ukv"]
    d["mla_gc_rep"] = rep(np.concatenate([inp["mla_g_cq"], inp["mla_g_ckv"]], -1))
    d["mla_gq_rep"] = rep(inp["mla_g_q"])
    d["mla_gk_rep"] = rep(inp["mla_g_k"])
    d["rope_cs"] = rope_table()
    d["nat_gq_rep"] = rep(inp["nat_g_q"])
    d["nat_gk_rep"] = rep(inp["nat_g_k"])
    d["nat_bias"] = nat_bias_host(inp["nat_rpb"])
    d["scan_consts"] = scan_consts_host()
    d["hg_go_rep"] = rep(inp["hg_g_o"])
    d["gla_go_rep"] = rep(inp["gla_g_o"])
    d["hg_lb_rep"] = np.ascontiguousarray(np.broadcast_to(inp["hg_lb_logits"][:, None], (2, 128, 2, 512))).astype(np.float32)
    d["gla_w2"] = inp["gla_w2"]
    d["w_br"] = inp["w_br"]
    d["w_merge"] = inp["w_merge"]
    d["b_merge"] = inp["b_merge"]
    d["w_out"] = inp["w_out"]
    d["gla_b2_rep"] = np.ascontiguousarray(np.broadcast_to(inp["gla_b2"][:, None], (2, 128, 2, 256))).astype(np.float32)
    return d


def kernel(**inputs):
    inp = {k: np.asarray(v) for k, v in inputs.items()}
    nc = build(layers=(0, 1))
    in_maps = [host_inputs(inp, b) for b in range(4)]
    res = run_bass_kernel_spmd(nc, in_maps, core_ids=list(range(4)))
    return np.stack([r["y"] for r in res.results], 0).astype(np.float32)
```

```python
import numpy as np
from contextlib import ExitStack
import concourse.bass as bass
import concourse.mybir as mybir
from concourse.bass_utils import run_bass_kernel_spmd

F32 = mybir.dt.float32
BF16 = mybir.dt.bfloat16
AF = mybir.ActivationFunctionType
ALU = mybir.AluOpType
AX = mybir.AxisListType

ENGS = ("pe", "act", "dve", "pool", "sp")
NDMASEM = 56


class Buf:
    __slots__ = ("name", "w", "r", "sem", "excl")

    def __init__(self, name):
        self.name = name
        self.excl = False
        self.w = None
        self.r = {}
        self.sem = None


class Op:
    __slots__ = ("eng", "fn", "waits", "signal", "token", "ndma")

    def __init__(self, eng, fn):
        self.eng = eng
        self.fn = fn
        self.waits = {}
        self.signal = False
        self.token = None
        self.ndma = 0


class Prog:
    def __init__(self, nc):
        self.nc = nc
        self.ges = ExitStack()
        self.es = None
        self.cnt = {}
        self.sigbase = {e: 0 for e in ENGS}
        self.ntile = 0
        self.sems = {}
        for e in ENGS:
            self.sems[e] = self.ges.enter_context(nc.semaphore("s_" + e))
        for i in range(NDMASEM):
            self.sems[f"d{i}"] = self.ges.enter_context(nc.semaphore(f"s_d{i}"))
        self.gbufs = []
        self.nphase = 0
        self._reset_phase()

    def _reset_phase(self):
        self.ops = {e: [] for e in ENGS}
        self.tok_op = {}
        self.pbufs = []
        self.ndsem = 0
        self.ndsem_sw = 0

    def _stack(self, glob):
        return self.ges if glob else self.es

    def sb(self, shape, dt, name="t", glob=False):
        self.ntile += 1
        t = self._stack(glob).enter_context(self.nc.sbuf_tensor(f"{name}_{self.ntile}", list(shape), dt))
        b = Buf(name)
        (self.gbufs if glob else self.pbufs).append(b)
        return t, b

    def ps(self, shape, dt, name="p"):
        self.ntile += 1
        t = self.es.enter_context(self.nc.psum_tensor(f"{name}_{self.ntile}", list(shape), dt))
        b = Buf(name)
        b.excl = True
        self.pbufs.append(b)
        return t, b

    def begin(self):
        self.es = ExitStack()
        self._reset_phase()

    def _dep(self, op, tok):
        if tok is None:
            return
        key, c = tok
        if key == "pe" and op.eng == "pe":
            return
        if op.waits.get(key, 0) < c:
            op.waits[key] = c
        if key in ENGS:
            self.tok_op[tok].signal = True

    def _track(self, o, reads, writes):
        ex = [b for b in reads if b.excl and o.eng != "pe"]
        if ex:
            reads = [b for b in reads if not (b.excl and o.eng != "pe")]
            writes = list(writes) + ex
        for b in reads:
            self._dep(o, b.w)
        for b in writes:
            self._dep(o, b.w)
            for t in b.r.items():
                self._dep(o, t)
        for b in reads:
            if b.r.get(o.token[0], 0) < o.token[1]:
                b.r[o.token[0]] = o.token[1]
        for b in writes:
            b.w = o.token
            b.r = {}

    def op(self, eng, fn, reads=(), writes=()):
        o = Op(eng, fn)
        c = self.cnt.get(eng, 0) + 1
        self.cnt[eng] = c
        o.token = (eng, c)
        self.tok_op[o.token] = o
        self._track(o, reads, writes)
        self.ops[eng].append(o)
        return o

    def dma(self, q, pairs, sbuf, load):
        sw = (q == "pool")
        if sbuf.sem is None or sbuf.sem[1] != self.nphase:
            if sw:
                assert self.ndsem_sw < NDMASEM // 2, "out of sw dma semaphores"
                sbuf.sem = (f"d{NDMASEM // 2 + self.ndsem_sw}", self.nphase, sw)
                self.ndsem_sw += 1
            else:
                assert self.ndsem < NDMASEM // 2, "out of hw dma semaphores"
                sbuf.sem = (f"d{self.ndsem}", self.nphase, sw)
                self.ndsem += 1
        assert sbuf.sem[2] == sw, "buffer used from both DMA queue kinds: " + sbuf.name
        key = sbuf.sem[0]

        def fn(e, pairs=pairs):
            return [e.dma_start(out=o_, in_=i_) for (o_, i_) in pairs]
        o = Op(q, fn)
        o.ndma = len(pairs)
        c = self.cnt.get(key, 0) + 16 * len(pairs)
        self.cnt[key] = c
        o.token = (key, c)
        o.signal = True
        if load:
            self._track(o, [], [sbuf])
        else:
            self._track(o, [sbuf], [])
        self.ops[q].append(o)
        return o

    def end(self):
        nc = self.nc
        for e in ENGS:
            for o in reversed(self.ops[e]):
                if o.ndma == 0 and o.fn is not None:
                    o.signal = True
                    break
        snap = dict(self.cnt)
        for e in ENGS:
            o = Op(e, None)
            for key, c in snap.items():
                if key == e and e == "pe":
                    continue
                if c > 0:
                    o.waits[key] = c
            self.ops[e].append(o)
        sig_index = {}
        last_sig = dict(self.sigbase)
        for e in ENGS:
            k = self.sigbase[e]
            for o in self.ops[e]:
                if o.fn is None or o.ndma:
                    continue
                if o.signal:
                    k += 1
                sig_index[o.token] = k
            last_sig[e] = k
        prog = self
        sems = self.sems
        base = dict(self.sigbase)

        def section(e):
            def body(eng):
                waited = {}
                for o in prog.ops[e]:
                    for key, c in o.waits.items():
                        if key in ENGS:
                            v = sig_index.get((key, c))
                            if v is None:
                                continue
                            if v <= base[key]:
                                continue
                        else:
                            v = c
                        if waited.get(key, 0) < v:
                            eng.wait_ge(sems[key], v)
                            waited[key] = v
                    if o.fn is None:
                        continue
                    r = o.fn(eng)
                    if o.ndma:
                        for ins in r:
                            ins.then_inc(sems[o.token[0]], 16)
                    elif o.signal:
                        r.then_inc(sems[e], 1)
            return body

        with nc.Block() as block:
            bl = {"pe": block.tensor, "act": block.scalar, "dve": block.vector,
                  "pool": block.gpsimd, "sp": block.sync}
            for e in ENGS:
                bl[e](section(e))
        self.sigbase = last_sig
        for b in self.gbufs + self.pbufs:
            b.w = None
            b.r = {}
        self.es.close()
        self.es = None
        self.nphase += 1

    def finish(self):
        self.ges.close()


D = 1024
NCTX = 256
NLAT = 8192
T = NCTX + NLAT
NT = T // 128
IN_W = 7104
EPS = 1e-6
SEG = {"mla": (0, 416), "nat": (416, 1536), "hg": (1952, 2048), "gla": (4000, 1056), "z": (5056, 2048)}


def dram_in(nc, name, shape, dt=F32):
    return nc.dram_tensor(name, list(shape), dt, kind="ExternalInput").ap()


def dram_scr(nc, name, shape, dt):
    return nc.dram_tensor(name, list(shape), dt, kind="Internal").ap()


class Ctx:
    pass


def phase0_adaln(P, C):
    P.begin()
    cv, cvb = P.sb([128, 8, 2], F32, "cv")
    sl, slb = P.sb([128, 8, 2], F32, "sl")
    ab, abb = P.sb([128, 2, 24], F32, "ab")
    ng, ngb = P.sb([128, 2, 8], F32, "ng")
    P.dma("sp", [(cv[:], C.cvec)], cvb, True)
    P.dma("sp", [(ab[:], C.ada_b_fm)], abb, True)
    P.dma("sp", [(ng[:], C.norm_g_fm)], ngb, True)
    P.op("act", lambda e: e.activation(out=sl[:], in_=cv[:], func=AF.Silu), [cvb], [slb])
    wbuf = [P.sb([128, 8, 1024], F32, f"adaw{i}") for i in range(2)]
    pp = [P.ps([128, 512], F32, f"pm{i}") for i in range(2)]
    it = 0
    for l in range(2):
        for part in range(3):
            w, wb = wbuf[it % 2]
            src = C.ada_w[l, :, part * 1024:(part + 1) * 1024].rearrange("(k p) n -> p k n", p=128)
            P.dma("sp" if it % 2 == 0 else "pool", [(w[:, 0:4, :], src[:, 0:4, :]), (w[:, 4:8, :], src[:, 4:8, :])], wb, True)
            for j in range(8):
                ps_, psb = pp[j % 2]
                for k in range(8):
                    P.op("pe", lambda e, ps_=ps_, w=w, k=k, j=j: e.matmul(ps_[:, 0:2], lhsT=w[:, k, j * 128:(j + 1) * 128], rhs=sl[:, k, :],
                                                                       start=(k == 0), stop=(k == 7)), [wb, slb], [psb])
                P.op("act", lambda e, ps_=ps_, l=l, part=part, j=j: e.activation(out=C.modt[:, l, part, j, :], in_=ps_[:, 0:2], func=AF.Identity,
                                                                             bias=ab[:, l, part * 8 + j:part * 8 + j + 1], scale=1.0),
                     [psb, abb], [C.modtb])
            it += 1
    for l in range(2):
        for v in range(2):
            P.op("dve", lambda e, l=l, v=v: e.scalar_tensor_tensor(out=C.A[:, l, :, v], in0=C.modt[:, l, 1, :, v], scalar=1.0, in1=ng[:, l, :],
                                                                  op0=ALU.add, op1=ALU.mult), [C.modtb, ngb], [C.Ab])
    P.end()


def phase1_inproj(P, C, l):
    src_rows = C.xsrc[l]
    halves = [(0, 3552), (3552, 3552)]
    for hi, (c0, cw) in enumerate(halves):
        P.begin()
        ident, identb = P.sb([128, 128], BF16, "ident")
        P.dma("sp", [(ident[:], C.ident)], identb, True)
        wsb, wsbb = P.sb([128, 8, cw], BF16, "win")
        stg = [P.sb([128, 1776], F32, f"wstg{i}") for i in range(2)]
        n = 0
        for k in range(8):
            for q in range(cw // 1776):
                s_, sb_ = stg[n % 2]
                P.dma("sp" if n % 2 == 0 else "act", [(s_[:], C.w_in[l, k * 128:(k + 1) * 128, c0 + q * 1776:c0 + (q + 1) * 1776])], sb_, True)
                if n % 2 == 0:
                    P.op("act", lambda e, s_=s_, k=k, q=q: e.activation(func=AF.Identity, out=wsb[:, k, q * 1776:(q + 1) * 1776], in_=s_[:]), [sb_], [wsbb])
                else:
                    P.op("dve", lambda e, s_=s_, k=k, q=q: e.tensor_copy(out=wsb[:, k, q * 1776:(q + 1) * 1776], in_=s_[:]), [sb_], [wsbb])
                n += 1
        xt = [P.sb([128, 1024], F32, f"xt{i}") for i in range(2)]
        sq, sqb = P.sb([128, 1024], F32, "sq")
        st = [P.sb([128, 4], F32, f"st{i}") for i in range(2)]
        xn = [P.sb([128, 1024], BF16, f"xn{i}") for i in range(2)]
        hT = [P.sb([128, 8, 128], BF16, f"hT{i}") for i in range(3)]
        og = [P.sb([128, cw], F32, f"og{i}") for i in range(2)]
        pT = [P.ps([128, 8, 128], BF16, f"pT{i}") for i in range(1)]
        pY = [P.ps([128, 512], F32, f"pY{i}") for i in range(7)]
        ngrp = (cw + 511) // 512
        evc = [0]

        def front(i):
            v = 1 if i < 2 else 0
            h_, hb = hT[i % 3]
            if hi == 0:
                x_, xb = xt[i % 2]
                s_, sb_ = st[i % 2]
                n_, nb = xn[i % 2]
                p_, pb = pT[0]
                P.dma("sp", [(x_[:], src_rows(i))], xb, True)
                P.op("act", lambda e: e.activation(out=sq[:], in_=x_[:], func=AF.Square, accum_out=s_[:, 0:1]), [xb], [sqb, sb_])
                P.op("dve", lambda e: e.tensor_scalar(out=s_[:, 1:2], in0=s_[:, 0:1], scalar1=1.0 / D, scalar2=EPS, op0=ALU.mult, op1=ALU.add), [sb_], [sb_])
                P.op("act", lambda e: e.activation(out=s_[:, 2:3], in_=s_[:, 1:2], func=AF.Ln), [sb_], [sb_])
                P.op("act", lambda e: e.activation(out=s_[:, 3:4], in_=s_[:, 2:3], func=AF.Exp, scale=-0.5), [sb_], [sb_])
                P.op("dve", lambda e: e.tensor_scalar(out=n_[:], in0=x_[:], scalar1=s_[:, 3:4], scalar2=None, op0=ALU.mult), [xb, sb_], [nb])
                for k in range(8):
                    P.op("pe", lambda e, k=k: e.transpose(out=p_[:, k, :], in_=n_[:, k * 128:(k + 1) * 128], identity=ident[:]), [nb, identb], [pb])
                for k in range(8):
                    P.op("act", lambda e, k=k: e.activation(out=h_[:, k, :], in_=p_[:, k, :], func=AF.Identity,
                                                            scale=C.A[:, l, k, v:v + 1], bias=C.modt[:, l, 0, k, v:v + 1]), [pb, C.Ab, C.modtb], [hb])
                P.dma("act", [(C.hT[i], h_[:])], hb, False)
            else:
                P.dma("sp", [(h_[:], C.hT[i])], hb, True)

        def back(i):
            h_, hb = hT[i % 3]
            o_, ob = og[i % 2]
            for (g0, g1) in ((0, 4), (4, ngrp)):
                for k in range(8):
                    for g in range(g0, g1):
                        gw = min(512, cw - g * 512)
                        y_, yb = pY[g]
                        P.op("pe", lambda e, k=k, g=g, gw=gw, y_=y_: e.matmul(y_[:, 0:gw], lhsT=h_[:, k, :], rhs=wsb[:, k, g * 512:g * 512 + gw],
                                                                          start=(k == 0), stop=(k == 7)), [hb, wsbb], [yb])
                for g in range(g0, g1):
                    gw = min(512, cw - g * 512)
                    y_, yb = pY[g]
                    if evc[0] % 2 == 0:
                        P.op("act", lambda e, g=g, gw=gw, y_=y_: e.activation(func=AF.Identity, out=o_[:, g * 512:g * 512 + gw], in_=y_[:, 0:gw]), [yb], [ob])
                    else:
                        P.op("dve", lambda e, g=g, gw=gw, y_=y_: e.tensor_copy(out=o_[:, g * 512:g * 512 + gw], in_=y_[:, 0:gw]), [yb], [ob])
                    evc[0] += 1
            P.dma("sp" if i % 2 == 0 else "act", [(C.proj[i * 128:(i + 1) * 128, c0:c0 + cw], o_[:])], ob, False)

        front(0)
        for i in range(NT):
            if i + 1 < NT:
                front(i + 1)
            back(i)
        P.end()


def groups():
    g = [(0, 2, 0)]
    for j in range(16):
        g.append((2 + 4 * j, 4, 1 + j))
    return g


def rstd_from_ms(P, st, stb, n, eps_done=False):
    P.op("dve", lambda e: e.tensor_scalar(out=st[:, n:2 * n], in0=st[:, 0:n], scalar1=1.0, scalar2=EPS, op0=ALU.mult, op1=ALU.add), [stb], [stb])
    P.op("act", lambda e: e.activation(out=st[:, n:2 * n], in_=st[:, n:2 * n], func=AF.Ln), [stb], [stb])
    P.op("act", lambda e: e.activation(out=st[:, 2 * n:3 * n], in_=st[:, n:2 * n], func=AF.Exp, scale=-0.5), [stb], [stb])


def head_norm(P, src, srcb, H, dh, gain, gainb, out, outb, sq, sqb, st, stb, tmp, tmpb, eng2="dve"):
    W = H * dh
    P.op("act", lambda e: e.activation(out=sq[:, 0:W], in_=src, func=AF.Square, scale=float(dh) ** -0.5), [srcb], [sqb])
    P.op("dve", lambda e: e.tensor_reduce(out=st[:, 0:H], in_=sq[:, 0:W].rearrange("p (h d) -> p h d", h=H), op=ALU.add, axis=AX.X), [sqb], [stb])
    rstd_from_ms(P, st, stb, H)
    P.op("dve", lambda e: e.tensor_tensor(out=tmp[:, 0:W].rearrange("p (h d) -> p h d", h=H), in0=src.rearrange("p (h d) -> p h d", h=H),
                                          in1=st[:, 2 * H:3 * H].unsqueeze(2).to_broadcast([128, H, dh]), op=ALU.mult), [srcb, stb], [tmpb])
    P.op(eng2, lambda e: e.tensor_tensor(out=out, in0=tmp[:, 0:W].rearrange("p (h d) -> p h d", h=H),
                                         in1=gain[:, None, :].to_broadcast([128, H, dh]), op=ALU.mult), [tmpb, gainb], [outb])


def rope(P, x, xb, cs, csb, out, outb, tt, ttb, H):
    x1, x2 = x[:, :, 0:16], x[:, :, 16:32]
    cos = cs[:, None, 0:16].to_broadcast([128, H, 16])
    sin = cs[:, None, 16:32].to_broadcast([128, H, 16])
    P.op("dve", lambda e: e.tensor_tensor(out=tt[:, 0], in0=x1, in1=cos, op=ALU.mult), [xb, csb], [ttb])
    P.op("dve", lambda e: e.tensor_tensor(out=tt[:, 1], in0=x2, in1=sin, op=ALU.mult), [xb, csb], [ttb])
    P.op("dve", lambda e: e.tensor_tensor(out=tt[:, 2], in0=x1, in1=sin, op=ALU.mult), [xb, csb], [ttb])
    P.op("dve", lambda e: e.tensor_tensor(out=tt[:, 3], in0=x2, in1=cos, op=ALU.mult), [xb, csb], [ttb])
    P.op("dve", lambda e: e.tensor_tensor(out=out[:, :, 0:16], in0=tt[:, 0], in1=tt[:, 1], op=ALU.subtract), [ttb], [outb])
    P.op("dve", lambda e: e.tensor_tensor(out=out[:, :, 16:32], in0=tt[:, 2], in1=tt[:, 3], op=ALU.add), [ttb], [outb])


def load_cast(P, dst, dstb, src, shape, q="sp", eng="pool", name="wst"):
    s_, sb_ = P.sb(shape, F32, name)
    P.dma(q, [(s_[:], src)], sb_, True)
    P.op(eng, lambda e: e.tensor_copy(out=dst, in_=s_[:]), [sb_], [dstb])


def mla_prep(P, C, l):
    P.begin()
    ident, identb = P.sb([128, 128], BF16, "ident")
    P.dma("sp", [(ident[:], C.ident)], identb, True)
    wuq, wuqb = P.sb([128, 2, 768], BF16, "wuq")
    wukv, wukvb = P.sb([128, 1024], BF16, "wukv")
    load_cast(P, wuq[:], wuqb, C.mla_w_uq[l].rearrange("(k p) n -> p k n", p=128), [128, 2, 768], "sp", "dve", "wst1")
    load_cast(P, wukv[:], wukvb, C.mla_w_ukv[l], [128, 1024], "pool", "dve", "wst2")
    gc, gcb = P.sb([128, 384], F32, "gc")
    gq, gqb = P.sb([128, 96], F32, "gq")
    gk, gkb = P.sb([128, 96], F32, "gk")
    P.dma("sp", [(gc[:], C.mla_gc_rep[l])], gcb, True)
    P.dma("sp", [(gq[:], C.mla_gq_rep[l])], gqb, True)
    P.dma("sp", [(gk[:], C.mla_gk_rep[l])], gkb, True)
    seg = [P.sb([128, 416], F32, f"seg{i}") for i in range(2)]
    cs = [P.sb([128, 32], F32, f"cs{i}") for i in range(2)]
    sq, sqb = P.sb([128, 768], F32, "sq")
    st0, st0b = P.sb([128, 8], F32, "st0")
    stq, stqb = P.sb([128, 24], F32, "stq")
    stk, stkb = P.sb([128, 24], F32, "stk")
    cn, cnb = P.sb([128, 384], BF16, "cn")
    cT, cTb = P.sb([128, 3, 128], BF16, "cT")
    qf, qfb = P.sb([128, 768], F32, "qf")
    qn, qnb = P.sb([128, 8, 96], F32, "qn")
    kf, kfb = P.sb([128, 768], F32, "kf")
    kn, knb = P.sb([128, 8, 96], F32, "kn")
    tmp, tmpb = P.sb([128, 768], F32, "tmp")
    tmp2, tmp2b = P.sb([128, 768], F32, "tmp2")
    qb_, qbb = P.sb([128, 2, 8, 96], BF16, "qb")
    kb_, kbb = P.sb([128, 8, 96], BF16, "kb")
    tt, ttb = P.sb([128, 4, 8, 16], F32, "tt")
    tt2, tt2b = P.sb([128, 4, 8, 16], F32, "tt2")
    qst = [P.sb([96, 2, 8, 512], BF16, f"qst{i}") for i in range(2)]
    kst = [P.sb([96, 8, 512], BF16, f"kst{i}") for i in range(2)]
    vst = [P.sb([128, 8, 4, 66], BF16, f"vst{i}") for i in range(2)]
    for i in range(2):
        P.op("dve", lambda e, i=i: e.memset(vst[i][0][:], 1.0), [], [vst[i][1]])
    pTc, pTcb = P.ps([128, 8, 128], BF16, "pTc")
    pq = [P.ps([128, 512], F32, f"pq{i}") for i in range(2)]
    pk = [P.ps([128, 512], F32, f"pk{i}") for i in range(2)]
    pTq, pTqb = P.ps([128, 16, 128], BF16, "pTq")
    pTk, pTkb = P.ps([128, 8, 128], BF16, "pTk")
    for gi, (t0, nt, qt) in enumerate(groups()[:C.max_groups]):
        qs, qsb = qst[gi % 2]
        ks, ksb = kst[gi % 2]
        vs, vsb = vst[gi % 2]
        for s in range(nt):
            i = t0 + s
            sg, sgb = seg[i % 2]
            c_, c_b = cs[i % 2]
            P.dma("sp", [(sg[:], C.proj[i * 128:(i + 1) * 128, 0:416])], sgb, True)
            P.dma("sp", [(c_[:], C.rope_cs[i * 128:(i + 1) * 128, :])], c_b, True)
            P.op("act", lambda e, sg=sg: e.activation(out=sq[:, 0:256], in_=sg[:, 0:256], func=AF.Square, scale=1.0 / 16, accum_out=st0[:, 0:1]), [sgb], [sqb, st0b])
            P.op("act", lambda e, sg=sg: e.activation(out=sq[:, 256:384], in_=sg[:, 256:384], func=AF.Square, scale=128 ** -0.5, accum_out=st0[:, 1:2]), [sgb], [sqb, st0b])
            rstd_from_ms(P, st0, st0b, 2)
            P.op("dve", lambda e, sg=sg: e.scalar_tensor_tensor(out=cn[:, 0:256], in0=sg[:, 0:256], scalar=st0[:, 4:5], in1=gc[:, 0:256], op0=ALU.mult, op1=ALU.mult), [sgb, st0b, gcb], [cnb])
            P.op("dve", lambda e, sg=sg: e.scalar_tensor_tensor(out=cn[:, 256:384], in0=sg[:, 256:384], scalar=st0[:, 5:6], in1=gc[:, 256:384], op0=ALU.mult, op1=ALU.mult), [sgb, st0b, gcb], [cnb])
            if C.stage <= 1:
                continue
            for k in range(3):
                P.op("pe", lambda e, k=k: e.transpose(out=pTc[:, k, :], in_=cn[:, k * 128:(k + 1) * 128], identity=ident[:]), [cnb, identb], [pTcb])
            P.op("act", lambda e: e.activation(func=AF.Identity, out=cT[:], in_=pTc[:, 0:3, :]), [pTcb], [cTb])
            for k in range(2):
                P.op("pe", lambda e, k=k: e.matmul(pq[0][0][:], lhsT=cT[:, k, :], rhs=wuq[:, k, 0:512], start=(k == 0), stop=(k == 1)), [cTb, wuqb], [pq[0][1]])
            for k in range(2):
                P.op("pe", lambda e, k=k: e.matmul(pq[1][0][:, 0:256], lhsT=cT[:, k, :], rhs=wuq[:, k, 512:768], start=(k == 0), stop=(k == 1)), [cTb, wuqb], [pq[1][1]])
            for j in range(2):
                P.op("pe", lambda e, j=j: e.matmul(pk[j][0][:], lhsT=cT[:, 2, :], rhs=wukv[:, j * 512:(j + 1) * 512], start=True, stop=True), [cTb, wukvb], [pk[j][1]])
            P.op("act", lambda e: e.activation(func=AF.Identity, out=qf[:, 0:512], in_=pq[0][0][:]), [pq[0][1]], [qfb])
            P.op("act", lambda e: e.activation(func=AF.Identity, out=qf[:, 512:768], in_=pq[1][0][:, 0:256]), [pq[1][1]], [qfb])
            if C.stage <= 2.1:
                continue
            for j in range(2):
                if C.stage > 2.2:
                    P.op("dve", lambda e, j=j: e.tensor_copy(out=kf[:].rearrange("p (h d) -> p h d", h=8)[:, j * 4:(j + 1) * 4, 0:64],
                                                         in_=pk[j][0][:].rearrange("p (h d) -> p h d", h=4)[:, :, 0:64]), [pk[j][1]], [kfb])
                if C.stage > 2.4:
                    P.op("act", lambda e, j=j, vs=vs, s=s: e.activation(func=AF.Identity, out=vs[:, j * 4:(j + 1) * 4, s, 0:64],
                                                             in_=pk[j][0][:].rearrange("p (h d) -> p h d", h=4)[:, :, 64:128]), [pk[j][1]], [vsb])
            if C.stage > 2.6:
                P.op("dve", lambda e, sg=sg: e.tensor_copy(out=kf[:].rearrange("p (h d) -> p h d", h=8)[:, :, 64:96],
                                                        in_=sg[:, None, 384:416].to_broadcast([128, 8, 32])), [sgb], [kfb])
            if C.stage <= 3:
                continue
            head_norm(P, qf[:], qfb, 8, 96, gq, gqb, qn[:], qnb, sq, sqb, stq, stqb, tmp, tmpb)
            head_norm(P, kf[:], kfb, 8, 96, gk, gkb, kn[:], knb, sq, sqb, stk, stkb, tmp2, tmp2b)
            if C.stage <= 4:
                continue
            P.op("dve", lambda e: e.tensor_copy(out=qb_[:, 0], in_=qn[:]), [qnb], [qbb])
            P.op("dve", lambda e: e.tensor_copy(out=qb_[:, 1, :, 0:64], in_=qn[:, :, 0:64]), [qnb], [qbb])
            rope(P, qn[:, :, 64:96], qnb, c_, c_b, qb_[:, 1, :, 64:96], qbb, tt, ttb, 8)
            P.op("dve", lambda e: e.tensor_copy(out=kb_[:, :, 0:64], in_=kn[:, :, 0:64]), [knb], [kbb])
            rope(P, kn[:, :, 64:96], knb, c_, c_b, kb_[:, :, 64:96], kbb, tt2, tt2b, 8)
            if C.stage <= 5:
                continue
            for ver in range(2):
                for h in range(8):
                    P.op("pe", lambda e, ver=ver, h=h: e.transpose(out=pTq[0:96, ver * 8 + h, :], in_=qb_[:, ver, h, :], identity=ident[:]), [qbb, identb], [pTqb])
            for ver in range(2):
                P.op("act" if ver == 0 else "dve",
                     (lambda e, ver=ver, qs=qs, s=s: e.activation(func=AF.Identity, out=qs[:, ver, :, s * 128:(s + 1) * 128], in_=pTq[0:96, ver * 8:(ver + 1) * 8, :])) if ver == 0 else
                     (lambda e, ver=ver, qs=qs, s=s: e.tensor_copy(out=qs[:, ver, :, s * 128:(s + 1) * 128], in_=pTq[0:96, ver * 8:(ver + 1) * 8, :])),
                     [pTqb], [qsb])
            for h in range(8):
                P.op("pe", lambda e, h=h: e.transpose(out=pTk[0:96, h, :], in_=kb_[:, h, :], identity=ident[:]), [kbb, identb], [pTkb])
            P.op("dve", lambda e, ks=ks, s=s: e.tensor_copy(out=ks[:, :, s * 128:(s + 1) * 128], in_=pTk[0:96, :, :]), [pTkb], [ksb])
        n = nt * 128
        if C.stage <= 6:
            continue
        P.dma("pool", [(C.mla_qT[:, :, qt, :, 0:n].rearrange("v h p n -> p v h n"), qs[:, :, :, 0:n])], qsb, False)
        P.dma("pool", [(C.mla_kT[:, :, t0 * 128:t0 * 128 + n].rearrange("h p n -> p h n"), ks[:, :, 0:n])], ksb, False)
        P.dma("sp", [(C.mla_va[:, :, t0:t0 + nt, :].rearrange("h p k e -> p h k e"), vs[:, :, 0:nt, :])], vsb, False)
    P.end()


def attend_main(P, C, l, name, H, dk, scale, kT_ap, q_ap, va, ybr, chunks_of, bias_src=None, with_ctx_q=True):
    P.begin()
    identf, identfb = P.sb([128, 128], F32, "identf")
    P.dma("sp", [(identf[:], C.identf)], identfb, True)
    kT = [P.sb([dk, T], BF16, f"kT{i}") for i in range(2)]
    vv = [P.sb([128, NT, 66], BF16, f"vv{i}") for i in range(2)]
    nver = 2 if name == "mla" else 1
    qq = [[P.sb([dk, 512], BF16, f"q{v}_{i}") for v in range(nver)] for i in range(2)]
    pt = [P.sb([128, 512], BF16, f"pt{i}") for i in range(3)]
    sbias = [P.sb([128, 512], F32, f"sb{i}") for i in range(2)]
    nb_tiles = 8
    bt = [[P.sb([128, 512], F32, f"bt{j}_{i}") for i in range(nb_tiles)] for j in range(3)] if bias_src else None
    oT, oTb = P.sb([65, 512], F32, "oT")
    rec, recb = P.sb([128, 4], F32, "rec")
    ost = [P.sb([128, 4, 64], F32, f"ost{i}") for i in range(2)]
    pS = [P.ps([128, 512], F32, f"pS{i}") for i in range(3)]
    pO = [P.ps([128, 512], F32, f"pO{i}") for i in range(2)]
    pXf, pXb = P.ps([128, 512], F32, "pX")
    pX = pXf[:, 0:260].rearrange("p (t e) -> p t e", e=65)
    qts = ([0] if with_ctx_q else []) + list(range(1, 17))
    LA = 2
    groups_ = [(h, qt) for h in range(H) for qt in qts]
    items = []
    for gi, (h, qt) in enumerate(groups_):
        ch = chunks_of(qt)
        for n_, (kc, ver, bid) in enumerate(ch):
            items.append((gi, n_, len(ch), kc, ver, bid))
    cur_bias = [None, None, None]
    st = {"bmap": {}, "pend": []}
    loaded_heads = set()
    loaded_groups = set()

    def load_head(h):
        if h >= H or h in loaded_heads:
            return
        loaded_heads.add(h)
        k_, kb = kT[h % 2]
        v_, vb = vv[h % 2]
        P.dma("sp", [(k_[:, 0:T // 2], kT_ap(h)[:, 0:T // 2]), (k_[:, T // 2:T], kT_ap(h)[:, T // 2:T])], kb, True)
        P.dma("pool", [(v_[:], va[h])], vb, True)

    def load_group(gi):
        if gi >= len(groups_) or gi in loaded_groups:
            return
        loaded_groups.add(gi)
        h, qt = groups_[gi]
        nq = 256 if qt == 0 else 512
        for v in range(nver):
            P.dma("sp", [(qq[gi % 2][v][0][:, 0:nq], q_ap(h, qt, v)[:, 0:nq])], qq[gi % 2][v][1], True)
        if bias_src:
            ids = [b for (_, _, b) in chunks_of(qt) if b is not None]
            bm = {}
            if ids:
                key = (h, tuple(ids))
                slot = None
                for j in range(3):
                    if cur_bias[j] == key:
                        slot = j
                if slot is None:
                    slot = ids[0][0]
                    cur_bias[slot] = key
                    for n_, b in enumerate(ids):
                        P.dma("pool" if n_ % 2 else "sp", [(bt[slot][n_][0][:], bias_src(h, b))], bt[slot][n_][1], True)
                for n_, b in enumerate(ids):
                    bm[b] = bt[slot][n_]
            st["bmap"][gi] = bm

    def stage_a(n):
        gi, n_, nch, kc, ver, bid = items[n]
        h, qt = groups_[gi]
        if n_ == 0:
            load_head(h)
            load_group(gi)
            load_group(gi + 1)
            if gi + 1 < len(groups_) and groups_[gi + 1][0] != h:
                load_head(h + 1)
        nq = 256 if qt == 0 else 512
        k_, kb = kT[h % 2]
        s_, sb_ = pS[n % 3]
        p_, pb = pt[n % 3]
        q_, qb = qq[gi % 2][ver]
        P.op("pe", lambda e: e.matmul(s_[:, 0:nq], lhsT=k_[:, kc * 128:(kc + 1) * 128], rhs=q_[:, 0:nq], start=True, stop=True), [kb, qb], [sb_])
        if bid is None:
            P.op("act", lambda e: e.activation(out=p_[:, 0:nq], in_=s_[:, 0:nq], func=AF.Exp, scale=scale), [sb_], [pb])
        else:
            b_, bb = st["bmap"][gi][bid]
            x_, xb = sbias[n % 2]
            P.op("dve", lambda e: e.scalar_tensor_tensor(out=x_[:], in0=s_[:], scalar=scale, in1=b_[:], op0=ALU.mult, op1=ALU.add), [sb_, bb], [xb])
            P.op("act", lambda e: e.activation(out=p_[:], in_=x_[:], func=AF.Exp), [xb], [pb])

    def epilogue(gi):
        h, qt = groups_[gi]
        nq = 256 if qt == 0 else 512
        tok0 = 0 if qt == 0 else 256 + (qt - 1) * 512
        o_, ob = pO[gi % 2]
        nt4 = nq // 128
        P.op("act", lambda e: e.activation(func=AF.Identity, out=oT[:, 0:nq], in_=o_[0:65, 0:nq]), [ob], [oTb])
        for t in range(nt4):
            P.op("pe", lambda e, t=t: e.transpose(out=pX[:, t, :], in_=oT[:, t * 128:(t + 1) * 128], identity=identf[0:65, 0:65]), [oTb, identfb], [pXb])
        P.op("dve", lambda e: e.reciprocal(out=rec[:, 0:nt4], in_=pX[:, 0:nt4, 64]), [pXb], [recb])
        os_, osb = ost[gi % 2]
        P.op("dve", lambda e: e.tensor_tensor(out=os_[:, 0:nt4, :], in0=pX[:, 0:nt4, 0:64], in1=rec[:, 0:nt4].unsqueeze(2).to_broadcast([128, nt4, 64]), op=ALU.mult), [pXb, recb], [osb])
        P.dma("pool", [(ybr[tok0:tok0 + nq, h * 64:(h + 1) * 64].rearrange("(t p) d -> p t d", p=128), os_[:, 0:nt4, :])], osb, False)

    def stage_b(n):
        gi, n_, nch, kc, ver, bid = items[n]
        h, qt = groups_[gi]
        nq = 256 if qt == 0 else 512
        v_, vb = vv[h % 2]
        p_, pb = pt[n % 3]
        o_, ob = pO[gi % 2]
        P.op("pe", lambda e: e.matmul(o_[0:65, 0:nq], lhsT=v_[:, kc, 0:65], rhs=p_[:, 0:nq], start=(n_ == 0), stop=(n_ == nch - 1)), [vb, pb], [ob])
        if n_ == nch - 1:
            st["pend"].append((n + 3, gi))

    N = len(items)
    for n in range(N + LA):
        if n < N:
            stage_a(n)
        if n - LA >= 0:
            stage_b(n - LA)
        while st["pend"] and st["pend"][0][0] <= n:
            epilogue(st["pend"].pop(0)[1])
    while st["pend"]:
        epilogue(st["pend"].pop(0)[1])
    P.end()


def mla_main(P, C, l):
    def chunks_of(qt):
        if qt == 0:
            return [(0, 0, None), (1, 0, None)]
        return [(0, 0, None), (1, 0, None)] + [(kc, 1, None) for kc in range(2, NT)]
    attend_main(P, C, l, "mla", 8, 96, 96 ** -0.5, lambda h: C.mla_kT[h], lambda h, qt, v: C.mla_qT[v, h, qt], C.mla_va, C.ybr[0], chunks_of,
                with_ctx_q=(l == 0))


def nat_prep(P, C, l):
    P.begin()
    ident, identb = P.sb([128, 128], BF16, "ident")
    P.dma("sp", [(ident[:], C.ident)], identb, True)
    gq, gqb = P.sb([128, 64], F32, "gq")
    gk, gkb = P.sb([128, 64], F32, "gk")
    P.dma("sp", [(gq[:], C.nat_gq_rep[l])], gqb, True)
    P.dma("sp", [(gk[:], C.nat_gk_rep[l])], gkb, True)
    seg = [P.sb([128, 1536], F32, f"seg{i}") for i in range(2)]
    sq, sqb = P.sb([128, 512], F32, "sq")
    stq, stqb = P.sb([128, 24], F32, "stq")
    stk, stkb = P.sb([128, 24], F32, "stk")
    tmp, tmpb = P.sb([128, 512], F32, "tmp")
    tmp2, tmp2b = P.sb([128, 512], F32, "tmp2")
    qk, qkb = P.sb([128, 2, 8, 64], BF16, "qk")
    qst = [P.sb([128, 4, 512], BF16, f"qst{i}") for i in range(2)]
    kst = [P.sb([128, 4, 512], BF16, f"kst{i}") for i in range(2)]
    vst = [P.sb([128, 8, 4, 66], BF16, f"vst{i}") for i in range(2)]
    for i in range(2):
        P.op("dve", lambda e, i=i: e.memset(vst[i][0][:], 1.0), [], [vst[i][1]])
    pT, pTb = P.ps([128, 8, 128], BF16, "pT")
    for gi, (t0, nt, qt) in enumerate(groups()[:C.max_groups]):
        qs, qsb = qst[gi % 2]
        ks, ksb = kst[gi % 2]
        vs, vsb = vst[gi % 2]
        for s in range(nt):
            i = t0 + s
            sg, sgb = seg[i % 2]
            P.dma("sp" if i % 2 == 0 else "pool", [(sg[:], C.proj[i * 128:(i + 1) * 128, 416:1952])], sgb, True)
            head_norm(P, sg[:, 0:512], sgb, 8, 64, gq, gqb, qk[:, 0], qkb, sq, sqb, stq, stqb, tmp, tmpb)
            head_norm(P, sg[:, 512:1024], sgb, 8, 64, gk, gkb, qk[:, 1], qkb, sq, sqb, stk, stkb, tmp2, tmp2b)
            P.op("dve", lambda e, sg=sg, vs=vs, s=s: e.tensor_copy(out=vs[:, :, s, 0:64], in_=sg[:, 1024:1536].rearrange("p (h d) -> p h d", h=8)), [sgb], [vsb])
            for w in range(2):
                for pr in range(4):
                    P.op("pe", lambda e, w=w, pr=pr: e.transpose(out=pT[:, w * 4 + pr, :], in_=qk[:, w, 2 * pr:2 * pr + 2, :].rearrange("p h d -> p (h d)"), identity=ident[:]), [qkb, identb], [pTb])
            P.op("act", lambda e, qs=qs, s=s: e.activation(func=AF.Identity, out=qs[:, :, s * 128:(s + 1) * 128], in_=pT[:, 0:4, :]), [pTb], [qsb])
            P.op("act", lambda e, ks=ks, s=s: e.activation(func=AF.Identity, out=ks[:, :, s * 128:(s + 1) * 128], in_=pT[:, 4:8, :]), [pTb], [ksb])
        n = nt * 128
        P.dma("pool", [(C.nat_qT[:, qt, :, 0:n].rearrange("r p n -> p r n"), qs[:, :, 0:n])], qsb, False)
        P.dma("pool", [(C.nat_kT[:, :, t0 * 128:t0 * 128 + n].rearrange("r p n -> p r n"), ks[:, :, 0:n])], ksb, False)
        P.dma("sp", [(C.nat_va[:, :, t0:t0 + nt, :].rearrange("h p k e -> p h k e"), vs[:, :, 0:nt, :])], vsb, False)
    P.end()


def nat_block(j):
    kb = min(max(8 * j - 4, 0), 112)
    pat = 0 if j == 0 else (2 if j == 15 else 1)
    cs_ = range(0, 6) if j == 0 else (range(2, 8) if j == 15 else range(0, 8))
    return kb, pat, list(cs_)


def nat_main(P, C, l):
    def chunks_of(qt):
        if qt == 0:
            return [(0, 0, None), (1, 0, None)]
        kb, pat, cs_ = nat_block(qt - 1)
        return [(0, 0, None), (1, 0, None)] + [(2 + kb // 2 + c, 0, (pat, c)) for c in cs_]
    attend_main(P, C, l, "nat", 8, 64, 64 ** -0.5,
                lambda h: C.nat_kT[h // 2, (h % 2) * 64:(h % 2 + 1) * 64, :],
                lambda h, qt, v: C.nat_qT[h // 2, qt, (h % 2) * 64:(h % 2 + 1) * 64, :],
                C.nat_va, C.ybr[1], chunks_of, bias_src=lambda h, b: C.nat_bias[l, b[0], h, b[1]], with_ctx_q=(l == 0))


def nat_bias_host(rpb):
    L = rpb.shape[0]
    out = np.full((L, 3, 8, 8, 128, 512), -30000.0, np.float32)
    ck = np.arange(64)[:, None]
    cq = np.arange(64)[None, :]
    c0 = np.clip(cq - 8, 0, 48)
    col_ok = (ck >= c0) & (ck < c0 + 16)
    dc = np.clip(ck - cq, -15, 15) + 15
    for pat, j in enumerate((0, 1, 15)):
        kb, _, cs_ = nat_block(j)
        for c in cs_:
            for a in range(2):
                kr = kb + 2 * c + a
                for r in range(8):
                    qr = 8 * j + r
                    r0 = min(max(qr - 4, 0), 120)
                    if not (r0 <= kr < r0 + 8):
                        continue
                    dr = kr - qr + 7
                    blk = np.where(col_ok[None, None], rpb[:, :, dr][:, :, dc], np.float32(-30000.0))
                    out[:, pat, :, c, a * 64:(a + 1) * 64, r * 64:(r + 1) * 64] = blk
    return out


def scan_consts_host():
    j = np.arange(128)[:, None]
    i = np.arange(128)[None, :]
    same = (j // 32) == (i // 32)
    mf = (same & (j <= i)).astype(np.float32)
    mb = (same & (j >= i)).astype(np.float32)
    blk = same.astype(np.float32)
    ind = (j // 32 == np.arange(4)[None, :]).astype(np.float32)
    return np.ascontiguousarray(np.concatenate([mf, mb, blk, ind], 1))


def scan_mixer(P, C, l, kind):
    hg = kind == "hg"
    H, dk, dv = (4, 128, 128) if hg else (4, 64, 128)
    W = H * dk
    c0, cw = SEG["hg"] if hg else SEG["gla"]
    qscale = float(dk) ** -0.5
    br = 2 if hg else 3
    for d in range(2):
        P.begin()
        ident, identb = P.sb([128, 128], BF16, "ident")
        P.dma("sp", [(ident[:], C.ident)], identb, True)
        identf, identfb = P.sb([128, 128], F32, "identf")
        P.dma("sp", [(identf[:], C.identf)], identfb, True)
        sc, scb = P.sb([128, 388], F32, "sconst")
        P.dma("sp", [(sc[:], C.scan_consts)], scb, True)
        Md = sc[:, 0:128] if d == 0 else sc[:, 128:256]
        Blk = sc[:, 256:384]
        Ind = sc[:, 384:388]
        go, gob = P.sb([128, 128], F32, "go")
        P.dma("sp", [(go[:], (C.hg_go_rep if hg else C.gla_go_rep)[l])], gob, True)
        if hg:
            lbt, lbb = P.sb([128, 512], F32, "lbt")
            oml, omlb = P.sb([128, 512], F32, "oml")
            if l == 0:
                P.op("dve", lambda e: e.memset(lbt[:], 0.0), [], [lbb])
                P.op("dve", lambda e: e.memset(oml[:], 1.0), [], [omlb])
            else:
                zz, zzb = P.sb([128, 2, 512], F32, "zz")
                P.dma("sp", [(zz[:], C.hg_lb_rep[:, :, d, :].rearrange("l p n -> p l n"))], zzb, True)
                P.op("dve", lambda e: e.tensor_tensor(out=lbt[:], in0=zz[:, 1, :], in1=zz[:, 0, :], op=ALU.subtract), [zzb], [lbb])
                P.op("act", lambda e: e.activation(out=lbt[:], in_=lbt[:], func=AF.Sigmoid), [lbb], [lbb])
                P.op("dve", lambda e: e.tensor_scalar(out=oml[:], in0=lbt[:], scalar1=-1.0, scalar2=1.0, op0=ALU.mult, op1=ALU.add), [lbb], [omlb])
        else:
            w2, w2b = P.sb([16, 256], F32, "w2")
            b2, b2b = P.sb([128, 256], F32, "b2")
            P.dma("sp", [(w2[:], C.gla_w2[l, d])], w2b, True)
            P.dma("sp", [(b2[:], C.gla_b2_rep[l, :, d, :])], b2b, True)
            rT, rTb = P.sb([16, 128], F32, "rT")
            gx, gxb = P.sb([128, 256], F32, "gx")
        seg = [P.sb([128, cw], F32, f"seg{i}") for i in range(2)]
        qf, qfb = P.sb([128, W], F32, "qf")
        kf, kfb = P.sb([128, W], F32, "kf")
        la, lab = P.sb([128, W], F32, "la")
        vb2 = [P.sb([128, H * dv], BF16, f"vb{i}") for i in range(2)]
        sg1, sg1b = P.sb([128, W], F32, "sg1")
        bs, bsb = P.sb([128, W], F32, "bs")
        eb, ebb = P.sb([128, W], F32, "eb")
        enb, enbb = P.sb([128, W], F32, "enb")
        ebe, ebeb = P.sb([128, W], F32, "ebe")
        dcy2 = [P.sb([128, H * 4], F32, f"dcy{i}") for i in range(2)]
        qt_, qtb = P.sb([128, W], BF16, "qt")
        kt_, ktb = P.sb([128, W], BF16, "kt")
        kh, khb = P.sb([128, W], BF16, "kh")
        kpad2 = [P.sb([128, H, 4, dk], BF16, f"kpad{i}") for i in range(2)]
        qT, qTb = P.sb([128, H, 128], BF16, "qT")
        kT, kTb = P.sb([128, H, 128], BF16, "kT")
        qpad2 = [P.sb([128, H, 608], BF16, f"qpad{i}") for i in range(2)]
        for i_ in range(2):
            P.op("dve", lambda e, i_=i_: e.memset(qpad2[i_][0][:], 0.0), [], [qpad2[i_][1]])
        AT2 = [P.sb([128, H, 128], BF16, f"AT{i}") for i in range(2)]
        S, Sb_ = P.sb([128, H, dv], F32, "S")
        P.op("dve", lambda e: e.memset(S[:], 0.0), [], [Sb_])
        Sbf = [P.sb([128, H, 4, dv], BF16, f"Sbf{i}") for i in range(2)]
        ofw = [P.sb([128, 512], F32, f"ofw{i}") for i in range(2)]
        ost = [P.sb([128, 512], F32, f"ost{i}") for i in range(2)]
        osum, osumb = P.sb([128, 512], F32, "osum")
        sq, sqb = P.sb([128, 512], F32, "sq")
        stn, stnb = P.sb([128, 12], F32, "stn")
        tmpn, tmpnb = P.sb([128, 512], F32, "tmpn")
        pb, pbb = P.ps([128, 512], F32, "pb")
        pbe, pbeb = P.ps([128, 512], F32, "pbe")
        pbT, pbTb = P.ps([128, 512], F32, "pbT")
        pT, pTb = P.ps([128, 8, 128], BF16, "pT")
        pS, pSb = P.ps([128, 512], F32, "pS")
        pO, pOb = P.ps([128, 512], F32, "pO")
        pU = [P.ps([128, 512], F32, f"pU{i}") for i in range(2)]
        tiles = list(range(NT)) if d == 0 else [1, 0] + list(range(NT - 1, 1, -1))
        if C.max_groups:
            tiles = tiles[:C.max_groups] if d == 0 else ([1, 0] + list(range(C.max_groups - 1, 1, -1)))
        cseq = [0, 1, 2, 3] if d == 0 else [3, 2, 1, 0]
        P.op("dve", lambda e: e.memset(Sbf[0][0][:, :, cseq[0], :], 0.0), [], [Sbf[0][1]])
        def front(n, i):
            sg, sgb = seg[n % 2]
            vb, vbb = vb2[n % 2]
            dcy, dcyb = dcy2[n % 2]
            kpad, kpadb = kpad2[n % 2]
            qpad, qpadb = qpad2[n % 2]
            AT, ATb = AT2[n % 2]
            yield
            P.dma("sp" if n % 2 == 0 else "pool", [(sg[:], C.proj[i * 128:(i + 1) * 128, c0:c0 + cw])], sgb, True)
            if d == 1:
                of_, ofb = ofw[n % 2]
                yield
                P.dma("pool" if n % 2 == 0 else "sp", [(of_[:], C.osc[i * 128:(i + 1) * 128, :])], ofb, True)
            if hg:
                fcol = 512 if d == 0 else 1024
                yield
                P.op("act", lambda e, sg=sg: e.activation(out=qf[:], in_=sg[:, 0:512], func=AF.Silu), [sgb], [qfb])
                yield
                P.op("act", lambda e, sg=sg: e.activation(out=sg1[:], in_=sg[:, fcol:fcol + 512], func=AF.Sigmoid), [sgb], [sg1b])
                yield
                P.op("act", lambda e, sg=sg: e.activation(out=kf[:], in_=sg[:, fcol:fcol + 512], func=AF.Sigmoid, scale=-1.0), [sgb], [kfb])
                yield
                P.op("dve", lambda e: e.tensor_tensor(out=kf[:], in0=kf[:], in1=oml[:], op=ALU.mult), [kfb, omlb], [kfb])
                yield
                P.op("dve", lambda e: e.tensor_tensor(out=sg1[:], in0=sg1[:], in1=oml[:], op=ALU.mult), [sg1b, omlb], [sg1b])
                yield
                P.op("dve", lambda e: e.tensor_tensor(out=sg1[:], in0=sg1[:], in1=lbt[:], op=ALU.add), [sg1b, lbb], [sg1b])
                yield
                P.op("act", lambda e: e.activation(out=la[:], in_=sg1[:], func=AF.Ln), [sg1b], [lab])
                yield
                P.op("dve", lambda e, sg=sg: e.tensor_copy(out=vb[:], in_=sg[:, 1536:2048]), [sgb], [vbb])
                qsrc, qsrcb, ksrc, ksrcb = qf[:], qfb, kf[:], kfb
            else:
                rc = 1024 + 16 * d
                yield
                P.op("pe", lambda e, sg=sg: e.transpose(out=pbT[0:16, 0:128], in_=sg[:, rc:rc + 16], identity=identf[:]), [sgb, identfb], [pbTb])
                yield
                P.op("act", lambda e: e.activation(func=AF.Identity, out=rT[:], in_=pbT[0:16, 0:128]), [pbTb], [rTb])
                yield
                P.op("pe", lambda e: e.matmul(pS[:, 0:256], lhsT=rT[:], rhs=w2[:], start=True, stop=True), [rTb, w2b], [pSb])
                yield
                P.op("dve", lambda e: e.tensor_tensor(out=gx[:], in0=pS[:, 0:256], in1=b2[:], op=ALU.add), [pSb, b2b], [gxb])
                yield
                P.op("act", lambda e: e.activation(out=gx[:], in_=gx[:], func=AF.Sigmoid), [gxb], [gxb])
                yield
                P.op("act", lambda e: e.activation(out=gx[:], in_=gx[:], func=AF.Ln), [gxb], [gxb])
                yield
                P.op("dve", lambda e: e.tensor_scalar(out=la[:], in0=gx[:], scalar1=1.0 / 16, scalar2=None, op0=ALU.mult), [gxb], [lab])
                yield
                P.op("dve", lambda e, sg=sg: e.tensor_copy(out=vb[:], in_=sg[:, 512:1024]), [sgb], [vbb])
                qsrc, qsrcb, ksrc, ksrcb = sg[:, 0:256], sgb, sg[:, 256:512], sgb
            yield
            P.op("pe", lambda e: e.matmul(pb[:, 0:W], lhsT=Md, rhs=la[:], start=True, stop=True), [scb, lab], [pbb])
            yield
            P.op("pe", lambda e: e.matmul(pbe[:, 0:W], lhsT=Blk, rhs=la[:], start=True, stop=True), [scb, lab], [pbeb])
            for h in range(H):
                yield
                P.op("pe", lambda e, h=h: e.matmul(pbT[0:dk, h * 4:(h + 1) * 4], lhsT=la[:, h * dk:(h + 1) * dk], rhs=Ind, start=True, stop=True), [lab, scb], [pbTb])
            yield
            P.op("act", lambda e: e.activation(func=AF.Identity, out=bs[:], in_=pb[:, 0:W]), [pbb], [bsb])
            yield
            P.op("act", lambda e: e.activation(out=eb[:], in_=pb[:, 0:W], func=AF.Exp), [pbb], [ebb])
            yield
            P.op("act", lambda e: e.activation(out=enb[:], in_=pb[:, 0:W], func=AF.Exp, scale=-1.0), [pbb], [enbb])
            yield
            P.op("dve", lambda e: e.tensor_tensor(out=ebe[:], in0=pbe[:, 0:W], in1=bs[:], op=ALU.subtract), [pbeb, bsb], [ebeb])
            yield
            P.op("act", lambda e: e.activation(out=ebe[:], in_=ebe[:], func=AF.Exp), [ebeb], [ebeb])
            yield
            P.op("act", lambda e: e.activation(out=dcy[0:dk, :], in_=pbT[0:dk, 0:H * 4], func=AF.Exp), [pbTb], [dcyb])
            yield
            P.op("dve", lambda e, qsrc=qsrc: e.scalar_tensor_tensor(out=qt_[:], in0=qsrc, scalar=qscale, in1=eb[:], op0=ALU.mult, op1=ALU.mult), [qsrcb, ebb], [qtb])
            yield
            P.op("dve", lambda e, ksrc=ksrc: e.tensor_tensor(out=kt_[:], in0=ksrc, in1=enb[:], op=ALU.mult), [ksrcb, enbb], [ktb])
            yield
            P.op("dve", lambda e, ksrc=ksrc: e.tensor_tensor(out=kh[:], in0=ksrc, in1=ebe[:], op=ALU.mult), [ksrcb, ebeb], [khb])
            for h in range(H):
                yield
                P.op("dve", lambda e, h=h: e.tensor_tensor(out=kpad[:, h], in0=kh[:, None, h * dk:(h + 1) * dk].to_broadcast([128, 4, dk]),
                                                           in1=Ind.unsqueeze(2).to_broadcast([128, 4, dk]), op=ALU.mult), [khb, scb], [kpadb])
            for h in range(H):
                yield
                P.op("pe", lambda e, h=h: e.transpose(out=pT[0:dk, h, :], in_=qt_[:, h * dk:(h + 1) * dk], identity=ident[:]), [qtb, identb], [pTb])
                yield
                P.op("pe", lambda e, h=h: e.transpose(out=pT[0:dk, H + h, :], in_=kt_[:, h * dk:(h + 1) * dk], identity=ident[:]), [ktb, identb], [pTb])
            yield
            P.op("act", lambda e: e.activation(func=AF.Identity, out=qT[0:dk], in_=pT[0:dk, 0:H, :]), [pTb], [qTb])
            yield
            P.op("act", lambda e: e.activation(func=AF.Identity, out=kT[0:dk], in_=pT[0:dk, H:2 * H, :]), [pTb], [kTb])
            for h in range(H):
                yield
                P.op("dve", lambda e, h=h: e.tensor_copy(out=qpad[0:dk, h, 96:608].rearrange("p (c n) -> p c n", n=128)[:, :, 0:32],
                                                         in_=pT[0:dk, h, :].rearrange("p (c t) -> p c t", t=32)), [pTb], [qpadb])
            for h in range(H):
                yield
                P.op("pe", lambda e, h=h: e.matmul(pS[:, h * 128:(h + 1) * 128], lhsT=kT[0:dk, h, :], rhs=qT[0:dk, h, :], start=True, stop=True), [kTb, qTb], [pSb])
            yield
            P.op("dve", lambda e: e.tensor_tensor(out=AT[:], in0=pS[:].rearrange("p (h n) -> p h n", h=H), in1=Md[:, None, :].to_broadcast([128, H, 128]), op=ALU.mult), [pSb, scb], [ATb])
        def back(n, i):
            vb, vbb = vb2[n % 2]
            dcy, dcyb = dcy2[n % 2]
            kpad, kpadb = kpad2[n % 2]
            qpad, qpadb = qpad2[n % 2]
            AT, ATb = AT2[n % 2]
            of_, ofb = ofw[n % 2]
            sb_cur, sb_curb = Sbf[n % 2]
            sb_nxt, sb_nxtb = Sbf[(n + 1) % 2]
            for ci, c in enumerate(cseq):
                u_, ub = pU[ci % 2]
                for h in range(H):
                    yield
                    P.op("pe", lambda e, h=h, c=c, u_=u_: e.matmul(u_[0:dk, h * dv:(h + 1) * dv], lhsT=kpad[:, h, c, :], rhs=vb[:, h * dv:(h + 1) * dv], start=True, stop=True), [kpadb, vbb], [ub])
                for h in range(H):
                    yield
                    P.op("dve", lambda e, h=h, c=c, u_=u_: e.scalar_tensor_tensor(out=S[0:dk, h, :], in0=S[0:dk, h, :], scalar=dcy[0:dk, h * 4 + c:h * 4 + c + 1],
                                                                                 in1=u_[0:dk, h * dv:(h + 1) * dv], op0=ALU.mult, op1=ALU.add), [Sb_, dcyb, ub], [Sb_])
                if ci < 3:
                    yield
                    P.op("act", lambda e, ci=ci, sb_cur=sb_cur: e.activation(func=AF.Identity, out=sb_cur[0:dk, :, cseq[ci + 1], :], in_=S[0:dk]), [Sb_], [sb_curb])
                else:
                    yield
                    P.op("act", lambda e, sb_nxt=sb_nxt: e.activation(func=AF.Identity, out=sb_nxt[0:dk, :, cseq[0], :], in_=S[0:dk]), [Sb_], [sb_nxtb])
            for h in range(H):
                yield
                P.op("pe", lambda e, h=h: e.matmul(pO[:, h * dv:(h + 1) * dv], lhsT=AT[:, h, :], rhs=vb[:, h * dv:(h + 1) * dv], start=True, stop=False), [ATb, vbb], [pOb])
                for ci, c in enumerate(cseq):
                    yield
                    P.op("pe", lambda e, h=h, c=c, ci=ci, sb_cur=sb_cur: e.matmul(pO[:, h * dv:(h + 1) * dv], lhsT=qpad[0:dk, h, 96 + 96 * c:224 + 96 * c], rhs=sb_cur[0:dk, h, c, :],
                                                                  start=False, stop=(ci == 3)), [qpadb, sb_curb], [pOb])
            if d == 0:
                o_, ob = ost[n % 2]
                yield
                P.op("act", lambda e, o_=o_: e.activation(func=AF.Identity, out=o_[:], in_=pO[:]), [pOb], [ob])
                yield
                P.dma("pool", [(C.osc[i * 128:(i + 1) * 128, :], o_[:])], ob, False)
            else:
                o_, ob = ost[n % 2]
                yield
                P.op("dve", lambda e, of_=of_: e.tensor_tensor(out=osum[:], in0=pO[:], in1=of_[:], op=ALU.add), [pOb, ofb], [osumb])
                head_norm(P, osum[:], osumb, 4, 128, go, gob, o_[:].rearrange("p (h d) -> p h d", h=4), ob, sq, sqb, stn, stnb, tmpn, tmpnb)
                yield
                P.dma("pool", [(C.ybr[br][i * 128:(i + 1) * 128, :], o_[:])], ob, False)

        def drive(gens):
            while gens:
                for g_ in list(gens):
                    try:
                        next(g_)
                    except StopIteration:
                        gens.remove(g_)

        drive([front(0, tiles[0])])
        for n, i in enumerate(tiles):
            gs = [back(n, i)]
            if n + 1 < len(tiles):
                gs.insert(0, front(n + 1, tiles[n + 1]))
            drive(gs)
        P.end()


def merge_phase(P, C, l):
    tiles = list(range(NT)) if l == 0 else list(range(2, NT))
    if C.max_groups:
        tiles = tiles[:C.max_groups]
    P.begin()
    ident, identb = P.sb([128, 128], BF16, "ident")
    P.dma("sp", [(ident[:], C.ident)], identb, True)
    wbr, wbrb = P.sb([128, 4, 4, 1024], BF16, "wbr")
    wmg, wmgb = P.sb([128, 4, 8, 1024], BF16, "wmg")
    bmg, bmgb = P.sb([1, 4, 1024], BF16, "bmg")
    ones, onesb = P.sb([1, 128], BF16, "ones")
    P.op("dve", lambda e: e.memset(ones[:], 1.0), [], [onesb])
    stg = [P.sb([128, 1024], F32, f"wstg{i}") for i in range(2)]
    n = 0
    for br in range(4):
        for k in range(4):
            s_, sb_ = stg[n % 2]
            P.dma("sp" if n % 2 == 0 else "pool", [(s_[:], C.w_br[l, br, k * 128:(k + 1) * 128, :])], sb_, True)
            if n % 2 == 0:
                P.op("act", lambda e, s_=s_, br=br, k=k: e.activation(func=AF.Identity, out=wbr[:, br, k, :], in_=s_[:]), [sb_], [wbrb])
            else:
                P.op("dve", lambda e, s_=s_, br=br, k=k: e.tensor_copy(out=wbr[:, br, k, :], in_=s_[:]), [sb_], [wbrb])
            n += 1
        for k in range(8):
            s_, sb_ = stg[n % 2]
            P.dma("sp" if n % 2 == 0 else "pool", [(s_[:], C.w_merge[l, br, k * 128:(k + 1) * 128, :])], sb_, True)
            if n % 2 == 0:
                P.op("act", lambda e, s_=s_, br=br, k=k: e.activation(func=AF.Identity, out=wmg[:, br, k, :], in_=s_[:]), [sb_], [wmgb])
            else:
                P.op("dve", lambda e, s_=s_, br=br, k=k: e.tensor_copy(out=wmg[:, br, k, :], in_=s_[:]), [sb_], [wmgb])
            n += 1
    bst, bstb = P.sb([1, 4, 1024], F32, "bst")
    P.dma("sp", [(bst[:], C.b_merge[l:l + 1])], bstb, True)
    P.op("dve", lambda e: e.tensor_copy(out=bmg[:], in_=bst[:]), [bstb], [bmgb])
    yb = [[P.sb([128, 512], F32, f"y{br}_{i}") for br in range(4)] for i in range(2)]
    zt = [P.sb([128, 2048], F32, f"z{i}") for i in range(2)]
    hT = [P.sb([128, 8, 128], BF16, f"hT{i}") for i in range(2)]
    sz, szb = P.sb([128, 2048], F32, "sz")
    u, ub = P.sb([128, 2048], BF16, "u")
    uT, uTb = P.sb([128, 16, 128], BF16, "uT")
    g, gb = P.sb([128, 1024], F32, "g")
    tmp, tmpb = P.sb([128, 1024], F32, "tmp")
    acc, accb = P.sb([128, 1024], F32, "acc")
    accbf, accbfb = P.sb([128, 1024], BF16, "accbf")
    aT = [P.sb([128, 8, 128], BF16, f"aT{i}") for i in range(2)]
    pT = [P.ps([128, 8, 128], BF16, f"pT{i}") for i in range(2)]
    pP = [P.ps([128, 512], F32, f"pP{i}") for i in range(2)]
    pG = [P.ps([128, 512], F32, f"pG{i}") for i in range(2)]
    uTs = [(uT, uTb), P.sb([128, 16, 128], BF16, "uT2")]
    pA, pAb = P.ps([128, 8, 128], BF16, "pA")

    def front(n, i):
        ys = yb[n % 2]
        z_, zb = zt[n % 2]
        h_, hb = hT[n % 2]
        uT_, uT_b = uTs[n % 2]
        for br in range(4):
            P.dma("sp" if br % 2 == 0 else "pool", [(ys[br][0][:], C.ybr[br][i * 128:(i + 1) * 128, :])], ys[br][1], True)
        P.dma("sp", [(z_[:], C.proj[i * 128:(i + 1) * 128, 5056:7104])], zb, True)
        P.dma("pool", [(h_[:], C.hT[i])], hb, True)
        P.op("act", lambda e: e.activation(out=sz[:], in_=z_[:], func=AF.Silu), [zb], [szb])
        for br in range(4):
            P.op("dve", lambda e, br=br: e.tensor_tensor(out=u[:, br * 512:(br + 1) * 512], in0=ys[br][0][:], in1=sz[:, br * 512:(br + 1) * 512], op=ALU.mult),
                 [ys[br][1], szb], [ub])
        for half in range(2):
            p_, pb_ = pT[half]
            for k in range(8):
                P.op("pe", lambda e, p_=p_, k=k, half=half: e.transpose(out=p_[:, k, :], in_=u[:, (half * 8 + k) * 128:(half * 8 + k + 1) * 128], identity=ident[:]), [ub, identb], [pb_])
            P.op("act", lambda e, p_=p_, half=half: e.activation(func=AF.Identity, out=uT_[:, half * 8:(half + 1) * 8, :], in_=p_[:]), [pb_], [uT_b])

    def back(n, i):
        h_, hb = hT[n % 2]
        uT_, uT_b = uTs[n % 2]
        for br in range(4):
            for hf in range(2):
                pp, ppb = pP[hf]
                pg, pgb = pG[hf]
                for k in range(4):
                    P.op("pe", lambda e, pp=pp, br=br, k=k, hf=hf: e.matmul(pp[:], lhsT=uT_[:, br * 4 + k, :], rhs=wbr[:, br, k, hf * 512:(hf + 1) * 512], start=(k == 0), stop=(k == 3)), [uT_b, wbrb], [ppb])
                for k in range(8):
                    P.op("pe", lambda e, pg=pg, br=br, k=k, hf=hf: e.matmul(pg[:], lhsT=h_[:, k, :], rhs=wmg[:, br, k, hf * 512:(hf + 1) * 512], start=(k == 0), stop=False), [hb, wmgb], [pgb])
                P.op("pe", lambda e, pg=pg, br=br, hf=hf: e.matmul(pg[:], lhsT=ones[:], rhs=bmg[:, br, hf * 512:(hf + 1) * 512], start=False, stop=True), [onesb, bmgb], [pgb])
                P.op("act", lambda e, pg=pg, hf=hf: e.activation(out=g[:, hf * 512:(hf + 1) * 512], in_=pg[:], func=AF.Sigmoid), [pgb], [gb])
                if br == 0:
                    P.op("dve", lambda e, pp=pp, hf=hf: e.tensor_tensor(out=acc[:, hf * 512:(hf + 1) * 512], in0=pp[:], in1=g[:, hf * 512:(hf + 1) * 512], op=ALU.mult), [ppb, gb], [accb])
                else:
                    P.op("dve", lambda e, pp=pp, hf=hf: e.tensor_tensor(out=tmp[:, hf * 512:(hf + 1) * 512], in0=pp[:], in1=g[:, hf * 512:(hf + 1) * 512], op=ALU.mult), [ppb, gb], [tmpb])
                    P.op("dve", lambda e, hf=hf: e.tensor_tensor(out=acc[:, hf * 512:(hf + 1) * 512], in0=acc[:, hf * 512:(hf + 1) * 512], in1=tmp[:, hf * 512:(hf + 1) * 512], op=ALU.add), [accb, tmpb], [accb])
        P.op("dve", lambda e: e.tensor_copy(out=accbf[:], in_=acc[:]), [accb], [accbfb])
        a_, ab = aT[n % 2]
        for k in range(8):
            P.op("pe", lambda e, k=k: e.transpose(out=pA[:, k, :], in_=accbf[:, k * 128:(k + 1) * 128], identity=ident[:]), [accbfb, identb], [pAb])
        P.op("act", lambda e: e.activation(func=AF.Identity, out=a_[:], in_=pA[:]), [pAb], [ab])
        P.dma("pool", [(C.accT[i], a_[:])], ab, False)

    front(0, tiles[0])
    for n, i in enumerate(tiles):
        if n + 1 < len(tiles):
            front(n + 1, tiles[n + 1])
        back(n, i)
    P.end()
    P.begin()
    identf, identfb = P.sb([128, 128], F32, "identf")
    P.dma("sp", [(identf[:], C.identf)], identfb, True)
    onef, onefb = P.sb([128, 128], F32, "onef")
    P.op("dve", lambda e: e.memset(onef[:], 1.0), [], [onefb])
    wo, wob = P.sb([128, 8, 1024], BF16, "wo")
    stg = [P.sb([128, 1024], F32, f"wstg{i}") for i in range(2)]
    for k in range(8):
        s_, sb_ = stg[k % 2]
        P.dma("sp" if k % 2 == 0 else "pool", [(s_[:], C.w_out[l, k * 128:(k + 1) * 128, :])], sb_, True)
        if k % 2 == 0:
            P.op("act", lambda e, s_=s_, k=k: e.activation(func=AF.Identity, out=wo[:, k, :], in_=s_[:]), [sb_], [wob])
        else:
            P.op("dve", lambda e, s_=s_, k=k: e.tensor_copy(out=wo[:, k, :], in_=s_[:]), [sb_], [wob])
    gtr = [P.sb([128, 1024], F32, f"gtr{v}") for v in range(2)]
    dg, dgb = P.sb([128, 128], F32, "dg")
    pR = [P.ps([128, 512], F32, f"pR{i}") for i in range(2)]
    for v in range(2):
        for j in range(8):
            P.op("dve", lambda e, v=v, j=j: e.tensor_scalar(out=dg[:], in0=identf[:], scalar1=C.modt[:, l, 2, j, v:v + 1], scalar2=None, op0=ALU.mult), [identfb, C.modtb], [dgb])
            P.op("pe", lambda e, j=j: e.matmul(pR[j // 4][0][:, (j % 4) * 128:(j % 4 + 1) * 128], lhsT=onef[:], rhs=dg[:], start=True, stop=True), [onefb, dgb], [pR[j // 4][1]])
        for hf in range(2):
            P.op("act", lambda e, v=v, hf=hf: e.activation(func=AF.Identity, out=gtr[v][0][:, hf * 512:(hf + 1) * 512], in_=pR[hf][0][:]), [pR[hf][1]], [gtr[v][1]])
    aT = [P.sb([128, 8, 128], BF16, f"aT{i}") for i in range(2)]
    xt = [P.sb([128, 1024], F32, f"xt{i}") for i in range(2)]
    xo = [P.sb([128, 1024], F32, f"xo{i}") for i in range(2)]
    t2, t2b = P.sb([128, 1024], F32, "t2")
    pO = [P.ps([128, 512], F32, f"pO{i}") for i in range(4)]
    for n, i in enumerate(tiles):
        v = 1 if i < 2 else 0
        a_, ab = aT[n % 2]
        x_, xb = xt[n % 2]
        o_, ob = xo[n % 2]
        P.dma("sp", [(a_[:], C.accT[i])], ab, True)
        P.dma("pool", [(x_[:], C.xsrc[l](i))], xb, True)
        for hf in range(2):
            po, pob = pO[(n % 2) * 2 + hf]
            for k in range(8):
                P.op("pe", lambda e, po=po, a_=a_, k=k, hf=hf: e.matmul(po[:], lhsT=a_[:, k, :], rhs=wo[:, k, hf * 512:(hf + 1) * 512], start=(k == 0), stop=(k == 7)), [ab, wob], [pob])
            P.op("dve", lambda e, po=po, hf=hf, v=v: e.tensor_tensor(out=t2[:, hf * 512:(hf + 1) * 512], in0=po[:], in1=gtr[v][0][:, hf * 512:(hf + 1) * 512], op=ALU.mult), [pob, gtr[v][1]], [t2b])
            P.op("dve", lambda e, hf=hf, x_=x_, o_=o_: e.tensor_tensor(out=o_[:, hf * 512:(hf + 1) * 512], in0=t2[:, hf * 512:(hf + 1) * 512], in1=x_[:, hf * 512:(hf + 1) * 512], op=ALU.add), [t2b, xb], [ob])
        dst = C.xres[i * 128:(i + 1) * 128, :] if l == 0 else C.y[(i - 2) * 128:(i - 1) * 128, :]
        P.dma("sp", [(dst, o_[:])], ob, False)
    P.end()

def build(layers=(0, 1), phases=("p0", "p1", "mla", "nat", "hg", "gla", "merge"), dbg=(), dbg_in=(), max_groups=None, stage=99):
    nc = bass.Bass("TRN2", target_bir_lowering=False)
    P = Prog(nc)
    C = Ctx()
    C.dbg = dbg
    C.max_groups = max_groups
    C.stage = stage

    def scr(name, shape, dt):
        kind = "ExternalOutput" if name in dbg else ("ExternalInput" if name in dbg_in else "Internal")
        return nc.dram_tensor(name, list(shape), dt, kind=kind).ap()
    C.x = dram_in(nc, "x", [NLAT, D])
    C.ctx = dram_in(nc, "ctx", [NCTX, D])
    C.cvec = dram_in(nc, "cvec", [128, 8, 2])
    C.ada_w = dram_in(nc, "ada_w", [2, D, 3 * D])
    C.ada_b_fm = dram_in(nc, "ada_b_fm", [128, 2, 24])
    C.norm_g_fm = dram_in(nc, "norm_g_fm", [128, 2, 8])
    C.w_in = dram_in(nc, "w_in", [2, D, IN_W])
    C.ident = dram_in(nc, "ident", [128, 128], BF16)
    C.identf = dram_in(nc, "identf", [128, 128])
    C.mla_w_uq = dram_in(nc, "mla_w_uq", [2, 256, 768])
    C.mla_w_ukv = dram_in(nc, "mla_w_ukv", [2, 128, 1024])
    C.mla_gc_rep = dram_in(nc, "mla_gc_rep", [2, 128, 384])
    C.mla_gq_rep = dram_in(nc, "mla_gq_rep", [2, 128, 96])
    C.mla_gk_rep = dram_in(nc, "mla_gk_rep", [2, 128, 96])
    C.rope_cs = dram_in(nc, "rope_cs", [T, 32])
    C.mla_qT = scr("mla_qT", [2, 8, 17, 96, 512], BF16)
    C.mla_kT = scr("mla_kT", [8, 96, T], BF16)
    C.mla_va = scr("mla_va", [8, 128, NT, 66], BF16)
    C.nat_gq_rep = dram_in(nc, "nat_gq_rep", [2, 128, 64])
    C.nat_gk_rep = dram_in(nc, "nat_gk_rep", [2, 128, 64])
    C.nat_bias = dram_in(nc, "nat_bias", [2, 3, 8, 8, 128, 512])
    C.nat_qT = scr("nat_qT", [4, 17, 128, 512], BF16)
    C.nat_kT = scr("nat_kT", [4, 128, T], BF16)
    C.nat_va = scr("nat_va", [8, 128, NT, 66], BF16)
    C.scan_consts = dram_in(nc, "scan_consts", [128, 388])
    C.hg_go_rep = dram_in(nc, "hg_go_rep", [2, 128, 128])
    C.gla_go_rep = dram_in(nc, "gla_go_rep", [2, 128, 128])
    C.hg_lb_rep = dram_in(nc, "hg_lb_rep", [2, 128, 2, 512])
    C.gla_w2 = dram_in(nc, "gla_w2", [2, 2, 16, 256])
    C.gla_b2_rep = dram_in(nc, "gla_b2_rep", [2, 128, 2, 256])
    C.w_br = dram_in(nc, "w_br", [2, 4, 512, D])
    C.w_merge = dram_in(nc, "w_merge", [2, 4, D, D])
    C.b_merge = dram_in(nc, "b_merge", [2, 4, D])
    C.w_out = dram_in(nc, "w_out", [2, D, D])
    C.accT = scr("accT", [NT, 128, 8, 128], BF16)
    C.osc = scr("osc", [T, 512], F32)
    C.ybr = [scr(f"ybr{i}", [T, 512], F32) for i in range(4)]
    C.y = nc.dram_tensor("y", [NLAT, D], F32, kind="ExternalOutput").ap()
    C.xres = scr("xres", [T, D], F32)
    C.hT = scr("hT", [NT, 128, 8, 128], BF16)
    C.proj = scr("proj", [T, IN_W], F32)
    C.modt, C.modtb = P.sb([128, 2, 3, 8, 2], F32, "modt", glob=True)
    C.A, C.Ab = P.sb([128, 2, 8, 2], F32, "Amod", glob=True)

    def src0(i):
        return C.ctx[i * 128:(i + 1) * 128, :] if i < 2 else C.x[(i - 2) * 128:(i - 1) * 128, :]

    def src1(i):
        return C.xres[i * 128:(i + 1) * 128, :]
    C.xsrc = [src0, src1]
    if "p0" in phases:
        phase0_adaln(P, C)
    for l in layers:
        if "p1" in phases:
            phase1_inproj(P, C, l)
        if "mla" in phases or "mlaprep" in phases:
            mla_prep(P, C, l)
        if "mla" in phases or "mlamain" in phases:
            mla_main(P, C, l)
        if "nat" in phases:
            nat_prep(P, C, l)
            nat_main(P, C, l)
        if "hg" in phases:
            scan_mixer(P, C, l, "hg")
        if "gla" in phases:
            scan_mixer(P, C, l, "gla")
        if "merge" in phases:
            merge_phase(P, C, l)
    P.finish()
    return nc


def rope_table():
    quarter = 8
    inv_freq = (10000.0 ** (-np.arange(quarter, dtype=np.float32) / quarter)).astype(np.float32)
    t = np.arange(NLAT)
    row = (t // 64).astype(np.float32)
    col = (t % 64).astype(np.float32)
    ang = np.concatenate([row[:, None] * inv_freq, col[:, None] * inv_freq], axis=-1).astype(np.float32)
    cs = np.zeros((T, 32), np.float32)
    cs[:NCTX, 0:16] = 1.0
    cs[NCTX:, 0:16] = np.cos(ang)
    cs[NCTX:, 16:32] = np.sin(ang)
    return cs


def rep(v):
    return np.ascontiguousarray(np.broadcast_to(v[:, None, :], (v.shape[0], 128, v.shape[1]))).astype(np.float32)


def host_inputs(inp, b):
    import ml_dtypes
    d = {}
    d["x"] = np.ascontiguousarray(inp["x"][b])
    d["ctx"] = np.ascontiguousarray(inp["ctx"][b])
    cv = np.stack([inp["c"][b], inp["c_ctx"]], -1)
    d["cvec"] = np.ascontiguousarray(cv.reshape(8, 128, 2).transpose(1, 0, 2))
    d["ada_w"] = inp["ada_w"]
    d["ada_b_fm"] = np.ascontiguousarray(inp["ada_b"].reshape(2, 24, 128).transpose(2, 0, 1))
    d["norm_g_fm"] = np.ascontiguousarray(inp["norm_g"].reshape(2, 8, 128).transpose(2, 0, 1))
    d["w_in"] = inp["w_in"]
    d["ident"] = np.eye(128).astype(ml_dtypes.bfloat16)
    d["identf"] = np.eye(128).astype(np.float32)
    d["mla_w_uq"] = inp["mla_w_uq"]
    d["mla_w_ukv"] = inp["mla_w_ukv"]
    d["mla_gc_rep"] = rep(np.concatenate([inp["mla_g_cq"], inp["mla_g_ckv"]], -1))
    d["mla_gq_rep"] = rep(inp["mla_g_q"])
    d["mla_gk_rep"] = rep(inp["mla_g_k"])
    d["rope_cs"] = rope_table()
    d["nat_gq_rep"] = rep(inp["nat_g_q"])
    d["nat_gk_rep"] = rep(inp["nat_g_k"])
    d["nat_bias"] = nat_bias_host(inp["nat_rpb"])
    d["scan_consts"] = scan_consts_host()
    d["hg_go_rep"] = rep(inp["hg_g_o"])
    d["gla_go_rep"] = rep(inp["gla_g_o"])
    d["hg_lb_rep"] = np.ascontiguousarray(np.broadcast_to(inp["hg_lb_logits"][:, None], (2, 128, 2, 512))).astype(np.float32)
    d["gla_w2"] = inp["gla_w2"]
    d["w_br"] = inp["w_br"]
    d["w_merge"] = inp["w_merge"]
    d["b_merge"] = inp["b_merge"]
    d["w_out"] = inp["w_out"]
    d["gla_b2_rep"] = np.ascontiguousarray(np.broadcast_to(inp["gla_b2"][:, None], (2, 128, 2, 256))).astype(np.float32)
    return d


def kernel(**inputs):
    inp = {k: np.asarray(v) for k, v in inputs.items()}
    nc = build(layers=(0, 1))
    in_maps = [host_inputs(inp, b) for b in range(4)]
    res = run_bass_kernel_spmd(nc, in_maps, core_ids=list(range(4)))
    return np.stack([r["y"] for r in res.results], 0).astype(np.float32)
```

```python
import numpy as np
from contextlib import ExitStack
import concourse.bass as bass
import concourse.mybir as mybir
from concourse.bass_utils import run_bass_kernel_spmd

F32 = mybir.dt.float32
BF16 = mybir.dt.bfloat16
AF = mybir.ActivationFunctionType
ALU = mybir.AluOpType
AX = mybir.AxisListType

ENGS = ("pe", "act", "dve", "pool", "sp")
NDMASEM = 56


class Buf:
    __slots__ = ("name", "w", "r", "sem", "excl")

    def __init__(self, name):
        self.name = name
        self.excl = False
        self.w = None
        self.r = {}
        self.sem = None


class Op:
    __slots__ = ("eng", "fn", "waits", "signal", "token", "ndma")

    def __init__(self, eng, fn):
        self.eng = eng
        self.fn = fn
        self.waits = {}
        self.signal = False
        self.token = None
        self.ndma = 0


class Prog:
    def __init__(self, nc):
        self.nc = nc
        self.ges = ExitStack()
        self.es = None
        self.cnt = {}
        self.sigbase = {e: 0 for e in ENGS}
        self.ntile = 0
        self.sems = {}
        for e in ENGS:
            self.sems[e] = self.ges.enter_context(nc.semaphore("s_" + e))
        for i in range(NDMASEM):
            self.sems[f"d{i}"] = self.ges.enter_context(nc.semaphore(f"s_d{i}"))
        self.gbufs = []
        self.nphase = 0
        self._reset_phase()

    def _reset_phase(self):
        self.ops = {e: [] for e in ENGS}
        self.tok_op = {}
        self.pbufs = []
        self.ndsem = 0
        self.ndsem_sw = 0

    def _stack(self, glob):
        return self.ges if glob else self.es

    def sb(self, shape, dt, name="t", glob=False):
        self.ntile += 1
        t = self._stack(glob).enter_context(self.nc.sbuf_tensor(f"{name}_{self.ntile}", list(shape), dt))
        b = Buf(name)
        (self.gbufs if glob else self.pbufs).append(b)
        return t, b

    def ps(self, shape, dt, name="p"):
        self.ntile += 1
        t = self.es.enter_context(self.nc.psum_tensor(f"{name}_{self.ntile}", list(shape), dt))
        b = Buf(name)
        b.excl = True
        self.pbufs.append(b)
        return t, b

    def begin(self):
        self.es = ExitStack()
        self._reset_phase()

    def _dep(self, op, tok):
        if tok is None:
            return
        key, c = tok
        if key == "pe" and op.eng == "pe":
            return
        if op.waits.get(key, 0) < c:
            op.waits[key] = c
        if key in ENGS:
            self.tok_op[tok].signal = True

    def _track(self, o, reads, writes):
        ex = [b for b in reads if b.excl and o.eng != "pe"]
        if ex:
            reads = [b for b in reads if not (b.excl and o.eng != "pe")]
            writes = list(writes) + ex
        for b in reads:
            self._dep(o, b.w)
        for b in writes:
            self._dep(o, b.w)
            for t in b.r.items():
                self._dep(o, t)
        for b in reads:
            if b.r.get(o.token[0], 0) < o.token[1]:
                b.r[o.token[0]] = o.token[1]
        for b in writes:
            b.w = o.token
            b.r = {}

    def op(self, eng, fn, reads=(), writes=()):
        o = Op(eng, fn)
        c = self.cnt.get(eng, 0) + 1
        self.cnt[eng] = c
        o.token = (eng, c)
        self.tok_op[o.token] = o
        self._track(o, reads, writes)
        self.ops[eng].append(o)
        return o

    def dma(self, q, pairs, sbuf, load):
        sw = (q == "pool")
        if sbuf.sem is None or sbuf.sem[1] != self.nphase:
            if sw:
                assert self.ndsem_sw < NDMASEM // 2, "out of sw dma semaphores"
                sbuf.sem = (f"d{NDMASEM // 2 + self.ndsem_sw}", self.nphase, sw)
                self.ndsem_sw += 1
            else:
                assert self.ndsem < NDMASEM // 2, "out of hw dma semaphores"
                sbuf.sem = (f"d{self.ndsem}", self.nphase, sw)
                self.ndsem += 1
        assert sbuf.sem[2] == sw, "buffer used from both DMA queue kinds: " + sbuf.name
        key = sbuf.sem[0]

        def fn(e, pairs=pairs):
            return [e.dma_start(out=o_, in_=i_) for (o_, i_) in pairs]
        o = Op(q, fn)
        o.ndma = len(pairs)
        c = self.cnt.get(key, 0) + 16 * len(pairs)
        self.cnt[key] = c
        o.token = (key, c)
        o.signal = True
        if load:
            self._track(o, [], [sbuf])
        else:
            self._track(o, [sbuf], [])
        self.ops[q].append(o)
        return o

    def end(self):
        nc = self.nc
        for e in ENGS:
            for o in reversed(self.ops[e]):
                if o.ndma == 0 and o.fn is not None:
                    o.signal = True
                    break
        snap = dict(self.cnt)
        for e in ENGS:
            o = Op(e, None)
            for key, c in snap.items():
                if key == e and e == "pe":
                    continue
                if c > 0:
                    o.waits[key] = c
            self.ops[e].append(o)
        sig_index = {}
        last_sig = dict(self.sigbase)
        for e in ENGS:
            k = self.sigbase[e]
            for o in self.ops[e]:
                if o.fn is None or o.ndma:
                    continue
                if o.signal:
                    k += 1
                sig_index[o.token] = k
            last_sig[e] = k
        prog = self
        sems = self.sems
        base = dict(self.sigbase)

        def section(e):
            def body(eng):
                waited = {}
                for o in prog.ops[e]:
                    for key, c in o.waits.items():
                        if key in ENGS:
                            v = sig_index.get((key, c))
                            if v is None:
                                continue
                            if v <= base[key]:
                                continue
                        else:
                            v = c
                        if waited.get(key, 0) < v:
                            eng.wait_ge(sems[key], v)
                            waited[key] = v
                    if o.fn is None:
                        continue
                    r = o.fn(eng)
                    if o.ndma:
                        for ins in r:
                            ins.then_inc(sems[o.token[0]], 16)
                    elif o.signal:
                        r.then_inc(sems[e], 1)
            return body

        with nc.Block() as block:
            bl = {"pe": block.tensor, "act": block.scalar, "dve": block.vector,
                  "pool": block.gpsimd, "sp": block.sync}
            for e in ENGS:
                bl[e](section(e))
        self.sigbase = last_sig
        for b in self.gbufs + self.pbufs:
            b.w = None
            b.r = {}
        self.es.close()
        self.es = None
        self.nphase += 1

    def finish(self):
        self.ges.close()


D = 1024
NCTX = 256
NLAT = 8192
T = NCTX + NLAT
NT = T // 128
IN_W = 7104
EPS = 1e-6
SEG = {"mla": (0, 416), "nat": (416, 1536), "hg": (1952, 2048), "gla": (4000, 1056), "z": (5056, 2048)}


def dram_in(nc, name, shape, dt=F32):
    return nc.dram_tensor(name, list(shape), dt, kind="ExternalInput").ap()


def dram_scr(nc, name, shape, dt):
    return nc.dram_tensor(name, list(shape), dt, kind="Internal").ap()


class Ctx:
    pass


def drive(gens):
    gens = list(gens)
    while gens:
        for g_ in list(gens):
            try:
                next(g_)
            except StopIteration:
                gens.remove(g_)


def phase0_adaln(P, C):
    P.begin()
    cv, cvb = P.sb([128, 8, 2], F32, "cv")
    sl, slb = P.sb([128, 8, 2], F32, "sl")
    ab, abb = P.sb([128, 2, 24], F32, "ab")
    ng, ngb = P.sb([128, 2, 8], F32, "ng")
    P.dma("sp", [(cv[:], C.cvec)], cvb, True)
    P.dma("sp", [(ab[:], C.ada_b_fm)], abb, True)
    P.dma("sp", [(ng[:], C.norm_g_fm)], ngb, True)
    P.op("act", lambda e: e.activation(out=sl[:], in_=cv[:], func=AF.Silu), [cvb], [slb])
    wbuf = [P.sb([128, 8, 1024], F32, f"adaw{i}") for i in range(2)]
    pp = [P.ps([128, 512], F32, f"pm{i}") for i in range(2)]
    it = 0
    for l in range(2):
        for part in range(3):
            w, wb = wbuf[it % 2]
            src = C.ada_w[l, :, part * 1024:(part + 1) * 1024].rearrange("(k p) n -> p k n", p=128)
            P.dma("sp" if it % 2 == 0 else "pool", [(w[:, 0:4, :], src[:, 0:4, :]), (w[:, 4:8, :], src[:, 4:8, :])], wb, True)
            for j in range(8):
                ps_, psb = pp[j % 2]
                for k in range(8):
                    P.op("pe", lambda e, ps_=ps_, w=w, k=k, j=j: e.matmul(ps_[:, 0:2], lhsT=w[:, k, j * 128:(j + 1) * 128], rhs=sl[:, k, :],
                                                                       start=(k == 0), stop=(k == 7)), [wb, slb], [psb])
                P.op("act", lambda e, ps_=ps_, l=l, part=part, j=j: e.activation(out=C.modt[:, l, part, j, :], in_=ps_[:, 0:2], func=AF.Identity,
                                                                             bias=ab[:, l, part * 8 + j:part * 8 + j + 1], scale=1.0),
                     [psb, abb], [C.modtb])
            it += 1
    for l in range(2):
        for v in range(2):
            P.op("dve", lambda e, l=l, v=v: e.scalar_tensor_tensor(out=C.A[:, l, :, v], in0=C.modt[:, l, 1, :, v], scalar=1.0, in1=ng[:, l, :],
                                                                  op0=ALU.add, op1=ALU.mult), [C.modtb, ngb], [C.Ab])
    P.end()


def phase1_inproj(P, C, l):
    src_rows = C.xsrc[l]
    halves = [(0, 3552), (3552, 3552)]
    for hi, (c0, cw) in enumerate(halves):
        P.begin()
        ident, identb = P.sb([128, 128], BF16, "ident")
        P.dma("sp", [(ident[:], C.ident)], identb, True)
        wsb, wsbb = P.sb([128, 8, cw], BF16, "win")
        stg = [P.sb([128, 1776], F32, f"wstg{i}") for i in range(2)]
        n = 0
        for k in range(8):
            for q in range(cw // 1776):
                s_, sb_ = stg[n % 2]
                P.dma("sp" if n % 2 == 0 else "act", [(s_[:], C.w_in[l, k * 128:(k + 1) * 128, c0 + q * 1776:c0 + (q + 1) * 1776])], sb_, True)
                if n % 2 == 0:
                    P.op("act", lambda e, s_=s_, k=k, q=q: e.activation(func=AF.Identity, out=wsb[:, k, q * 1776:(q + 1) * 1776], in_=s_[:]), [sb_], [wsbb])
                else:
                    P.op("dve", lambda e, s_=s_, k=k, q=q: e.tensor_copy(out=wsb[:, k, q * 1776:(q + 1) * 1776], in_=s_[:]), [sb_], [wsbb])
                n += 1
        xt = [P.sb([128, 1024], F32, f"xt{i}") for i in range(2)]
        sq, sqb = P.sb([128, 1024], F32, "sq")
        st = [P.sb([128, 4], F32, f"st{i}") for i in range(2)]
        xn = [P.sb([128, 1024], BF16, f"xn{i}") for i in range(2)]
        hT = [P.sb([128, 8, 128], BF16, f"hT{i}") for i in range(3)]
        og = [P.sb([128, cw], F32, f"og{i}") for i in range(2)]
        pT = [P.ps([128, 8, 128], BF16, f"pT{i}") for i in range(1)]
        pY = [P.ps([128, 512], F32, f"pY{i}") for i in range(7)]
        ngrp = (cw + 511) // 512
        evc = [0]

        def front(i):
            v = 1 if i < 2 else 0
            h_, hb = hT[i % 3]
            if hi == 0:
                x_, xb = xt[i % 2]
                s_, sb_ = st[i % 2]
                n_, nb = xn[i % 2]
                p_, pb = pT[0]
                yield
                P.dma("sp", [(x_[:], src_rows(i))], xb, True)
                yield
                P.op("act", lambda e: e.activation(out=sq[:], in_=x_[:], func=AF.Square, accum_out=s_[:, 0:1]), [xb], [sqb, sb_])
                yield
                P.op("dve", lambda e: e.tensor_scalar(out=s_[:, 1:2], in0=s_[:, 0:1], scalar1=1.0 / D, scalar2=EPS, op0=ALU.mult, op1=ALU.add), [sb_], [sb_])
                yield
                P.op("act", lambda e: e.activation(out=s_[:, 2:3], in_=s_[:, 1:2], func=AF.Ln), [sb_], [sb_])
                yield
                P.op("act", lambda e: e.activation(out=s_[:, 3:4], in_=s_[:, 2:3], func=AF.Exp, scale=-0.5), [sb_], [sb_])
                yield
                P.op("dve", lambda e: e.tensor_scalar(out=n_[:], in0=x_[:], scalar1=s_[:, 3:4], scalar2=None, op0=ALU.mult), [xb, sb_], [nb])
                for k in range(8):
                    yield
                    P.op("pe", lambda e, k=k: e.transpose(out=p_[:, k, :], in_=n_[:, k * 128:(k + 1) * 128], identity=ident[:]), [nb, identb], [pb])
                for k in range(8):
                    yield
                    P.op("act", lambda e, k=k: e.activation(out=h_[:, k, :], in_=p_[:, k, :], func=AF.Identity,
                                                            scale=C.A[:, l, k, v:v + 1], bias=C.modt[:, l, 0, k, v:v + 1]), [pb, C.Ab, C.modtb], [hb])
                yield
                P.dma("act", [(C.hT[i], h_[:])], hb, False)
            else:
                yield
                P.dma("sp", [(h_[:], C.hT[i])], hb, True)

        def back(i):
            h_, hb = hT[i % 3]
            o_, ob = og[i % 2]
            for (g0, g1) in ((0, 4), (4, ngrp)):
                for k in range(8):
                    for g in range(g0, g1):
                        gw = min(512, cw - g * 512)
                        y_, yb = pY[g]
                        yield
                        P.op("pe", lambda e, k=k, g=g, gw=gw, y_=y_: e.matmul(y_[:, 0:gw], lhsT=h_[:, k, :], rhs=wsb[:, k, g * 512:g * 512 + gw],
                                                                          start=(k == 0), stop=(k == 7)), [hb, wsbb], [yb])
                for g in range(g0, g1):
                    gw = min(512, cw - g * 512)
                    y_, yb = pY[g]
                    if evc[0] % 2 == 0:
                        yield
                        P.op("act", lambda e, g=g, gw=gw, y_=y_: e.activation(func=AF.Identity, out=o_[:, g * 512:g * 512 + gw], in_=y_[:, 0:gw]), [yb], [ob])
                    else:
                        yield
                        P.op("dve", lambda e, g=g, gw=gw, y_=y_: e.tensor_copy(out=o_[:, g * 512:g * 512 + gw], in_=y_[:, 0:gw]), [yb], [ob])
                    evc[0] += 1
            yield
            P.dma("sp" if i % 2 == 0 else "act", [(C.proj[i * 128:(i + 1) * 128, c0:c0 + cw], o_[:])], ob, False)

        drive([front(0)])
        for i in range(NT):
            gs = [back(i)]
            if i + 1 < NT:
                gs.append(front(i + 1))
            drive(gs)
        P.end()


def groups():
    g = [(0, 2, 0)]
    for j in range(16):
        g.append((2 + 4 * j, 4, 1 + j))
    return g


def rstd_from_ms(P, st, stb, n, eps_done=False):
    P.op("dve", lambda e: e.tensor_scalar(out=st[:, n:2 * n], in0=st[:, 0:n], scalar1=1.0, scalar2=EPS, op0=ALU.mult, op1=ALU.add), [stb], [stb])
    P.op("act", lambda e: e.activation(out=st[:, n:2 * n], in_=st[:, n:2 * n], func=AF.Ln), [stb], [stb])
    P.op("act", lambda e: e.activation(out=st[:, 2 * n:3 * n], in_=st[:, n:2 * n], func=AF.Exp, scale=-0.5), [stb], [stb])


def head_norm(P, src, srcb, H, dh, gain, gainb, out, outb, sq, sqb, st, stb, tmp, tmpb, eng2="dve"):
    W = H * dh
    P.op("act", lambda e: e.activation(out=sq[:, 0:W], in_=src, func=AF.Square, scale=float(dh) ** -0.5), [srcb], [sqb])
    P.op("dve", lambda e: e.tensor_reduce(out=st[:, 0:H], in_=sq[:, 0:W].rearrange("p (h d) -> p h d", h=H), op=ALU.add, axis=AX.X), [sqb], [stb])
    rstd_from_ms(P, st, stb, H)
    P.op("dve", lambda e: e.tensor_tensor(out=tmp[:, 0:W].rearrange("p (h d) -> p h d", h=H), in0=src.rearrange("p (h d) -> p h d", h=H),
                                          in1=st[:, 2 * H:3 * H].unsqueeze(2).to_broadcast([128, H, dh]), op=ALU.mult), [srcb, stb], [tmpb])
    P.op(eng2, lambda e: e.tensor_tensor(out=out, in0=tmp[:, 0:W].rearrange("p (h d) -> p h d", h=H),
                                         in1=gain[:, None, :].to_broadcast([128, H, dh]), op=ALU.mult), [tmpb, gainb], [outb])


def rope(P, x, xb, cs, csb, out, outb, tt, ttb, H):
    x1, x2 = x[:, :, 0:16], x[:, :, 16:32]
    cos = cs[:, None, 0:16].to_broadcast([128, H, 16])
    sin = cs[:, None, 16:32].to_broadcast([128, H, 16])
    P.op("dve", lambda e: e.tensor_tensor(out=tt[:, 0], in0=x1, in1=cos, op=ALU.mult), [xb, csb], [ttb])
    P.op("dve", lambda e: e.tensor_tensor(out=tt[:, 1], in0=x2, in1=sin, op=ALU.mult), [xb, csb], [ttb])
    P.op("dve", lambda e: e.tensor_tensor(out=tt[:, 2], in0=x1, in1=sin, op=ALU.mult), [xb, csb], [ttb])
    P.op("dve", lambda e: e.tensor_tensor(out=tt[:, 3], in0=x2, in1=cos, op=ALU.mult), [xb, csb], [ttb])
    P.op("dve", lambda e: e.tensor_tensor(out=out[:, :, 0:16], in0=tt[:, 0], in1=tt[:, 1], op=ALU.subtract), [ttb], [outb])
    P.op("dve", lambda e: e.tensor_tensor(out=out[:, :, 16:32], in0=tt[:, 2], in1=tt[:, 3], op=ALU.add), [ttb], [outb])


def load_cast(P, dst, dstb, src, shape, q="sp", eng="pool", name="wst"):
    s_, sb_ = P.sb(shape, F32, name)
    P.dma(q, [(s_[:], src)], sb_, True)
    P.op(eng, lambda e: e.tensor_copy(out=dst, in_=s_[:]), [sb_], [dstb])


def mla_prep(P, C, l):
    P.begin()
    ident, identb = P.sb([128, 128], BF16, "ident")
    P.dma("sp", [(ident[:], C.ident)], identb, True)
    wuq, wuqb = P.sb([128, 2, 768], BF16, "wuq")
    wukv, wukvb = P.sb([128, 1024], BF16, "wukv")
    load_cast(P, wuq[:], wuqb, C.mla_w_uq[l].rearrange("(k p) n -> p k n", p=128), [128, 2, 768], "sp", "dve", "wst1")
    load_cast(P, wukv[:], wukvb, C.mla_w_ukv[l], [128, 1024], "pool", "dve", "wst2")
    gc, gcb = P.sb([128, 384], F32, "gc")
    gq, gqb = P.sb([128, 96], F32, "gq")
    gk, gkb = P.sb([128, 96], F32, "gk")
    P.dma("sp", [(gc[:], C.mla_gc_rep[l])], gcb, True)
    P.dma("sp", [(gq[:], C.mla_gq_rep[l])], gqb, True)
    P.dma("sp", [(gk[:], C.mla_gk_rep[l])], gkb, True)
    seg = [P.sb([128, 416], F32, f"seg{i}") for i in range(2)]
    cs = [P.sb([128, 32], F32, f"cs{i}") for i in range(2)]
    sq, sqb = P.sb([128, 768], F32, "sq")
    st0, st0b = P.sb([128, 8], F32, "st0")
    stq, stqb = P.sb([128, 24], F32, "stq")
    stk, stkb = P.sb([128, 24], F32, "stk")
    cn, cnb = P.sb([128, 384], BF16, "cn")
    cT, cTb = P.sb([128, 3, 128], BF16, "cT")
    qf, qfb = P.sb([128, 768], F32, "qf")
    qn, qnb = P.sb([128, 8, 96], F32, "qn")
    kf, kfb = P.sb([128, 768], F32, "kf")
    kn, knb = P.sb([128, 8, 96], F32, "kn")
    tmp, tmpb = P.sb([128, 768], F32, "tmp")
    tmp2, tmp2b = P.sb([128, 768], F32, "tmp2")
    qb_, qbb = P.sb([128, 2, 8, 96], BF16, "qb")
    kb_, kbb = P.sb([128, 8, 96], BF16, "kb")
    tt, ttb = P.sb([128, 4, 8, 16], F32, "tt")
    tt2, tt2b = P.sb([128, 4, 8, 16], F32, "tt2")
    qst = [P.sb([96, 2, 8, 512], BF16, f"qst{i}") for i in range(2)]
    kst = [P.sb([96, 8, 512], BF16, f"kst{i}") for i in range(2)]
    vst = [P.sb([128, 8, 4, 66], BF16, f"vst{i}") for i in range(2)]
    for i in range(2):
        P.op("dve", lambda e, i=i: e.memset(vst[i][0][:], 1.0), [], [vst[i][1]])
    pTc, pTcb = P.ps([128, 8, 128], BF16, "pTc")
    pq = [P.ps([128, 512], F32, f"pq{i}") for i in range(2)]
    pk = [P.ps([128, 512], F32, f"pk{i}") for i in range(2)]
    pTq, pTqb = P.ps([128, 16, 128], BF16, "pTq")
    pTk, pTkb = P.ps([128, 8, 128], BF16, "pTk")
    for gi, (t0, nt, qt) in enumerate(groups()[:C.max_groups]):
        qs, qsb = qst[gi % 2]
        ks, ksb = kst[gi % 2]
        vs, vsb = vst[gi % 2]
        for s in range(nt):
            i = t0 + s
            sg, sgb = seg[i % 2]
            c_, c_b = cs[i % 2]
            P.dma("sp", [(sg[:], C.proj[i * 128:(i + 1) * 128, 0:416])], sgb, True)
            P.dma("sp", [(c_[:], C.rope_cs[i * 128:(i + 1) * 128, :])], c_b, True)
            P.op("act", lambda e, sg=sg: e.activation(out=sq[:, 0:256], in_=sg[:, 0:256], func=AF.Square, scale=1.0 / 16, accum_out=st0[:, 0:1]), [sgb], [sqb, st0b])
            P.op("act", lambda e, sg=sg: e.activation(out=sq[:, 256:384], in_=sg[:, 256:384], func=AF.Square, scale=128 ** -0.5, accum_out=st0[:, 1:2]), [sgb], [sqb, st0b])
            rstd_from_ms(P, st0, st0b, 2)
            P.op("dve", lambda e, sg=sg: e.scalar_tensor_tensor(out=cn[:, 0:256], in0=sg[:, 0:256], scalar=st0[:, 4:5], in1=gc[:, 0:256], op0=ALU.mult, op1=ALU.mult), [sgb, st0b, gcb], [cnb])
            P.op("dve", lambda e, sg=sg: e.scalar_tensor_tensor(out=cn[:, 256:384], in0=sg[:, 256:384], scalar=st0[:, 5:6], in1=gc[:, 256:384], op0=ALU.mult, op1=ALU.mult), [sgb, st0b, gcb], [cnb])
            if C.stage <= 1:
                continue
            for k in range(3):
                P.op("pe", lambda e, k=k: e.transpose(out=pTc[:, k, :], in_=cn[:, k * 128:(k + 1) * 128], identity=ident[:]), [cnb, identb], [pTcb])
            P.op("act", lambda e: e.activation(func=AF.Identity, out=cT[:], in_=pTc[:, 0:3, :]), [pTcb], [cTb])
            for k in range(2):
                P.op("pe", lambda e, k=k: e.matmul(pq[0][0][:], lhsT=cT[:, k, :], rhs=wuq[:, k, 0:512], start=(k == 0), stop=(k == 1)), [cTb, wuqb], [pq[0][1]])
            for k in range(2):
                P.op("pe", lambda e, k=k: e.matmul(pq[1][0][:, 0:256], lhsT=cT[:, k, :], rhs=wuq[:, k, 512:768], start=(k == 0), stop=(k == 1)), [cTb, wuqb], [pq[1][1]])
            for j in range(2):
                P.op("pe", lambda e, j=j: e.matmul(pk[j][0][:], lhsT=cT[:, 2, :], rhs=wukv[:, j * 512:(j + 1) * 512], start=True, stop=True), [cTb, wukvb], [pk[j][1]])
            P.op("act", lambda e: e.activation(func=AF.Identity, out=qf[:, 0:512], in_=pq[0][0][:]), [pq[0][1]], [qfb])
            P.op("act", lambda e: e.activation(func=AF.Identity, out=qf[:, 512:768], in_=pq[1][0][:, 0:256]), [pq[1][1]], [qfb])
            if C.stage <= 2.1:
                continue
            for j in range(2):
                if C.stage > 2.2:
                    P.op("dve", lambda e, j=j: e.tensor_copy(out=kf[:].rearrange("p (h d) -> p h d", h=8)[:, j * 4:(j + 1) * 4, 0:64],
                                                         in_=pk[j][0][:].rearrange("p (h d) -> p h d", h=4)[:, :, 0:64]), [pk[j][1]], [kfb])
                if C.stage > 2.4:
                    P.op("act", lambda e, j=j, vs=vs, s=s: e.activation(func=AF.Identity, out=vs[:, j * 4:(j + 1) * 4, s, 0:64],
                                                             in_=pk[j][0][:].rearrange("p (h d) -> p h d", h=4)[:, :, 64:128]), [pk[j][1]], [vsb])
            if C.stage > 2.6:
                P.op("dve", lambda e, sg=sg: e.tensor_copy(out=kf[:].rearrange("p (h d) -> p h d", h=8)[:, :, 64:96],
                                                        in_=sg[:, None, 384:416].to_broadcast([128, 8, 32])), [sgb], [kfb])
            if C.stage <= 3:
                continue
            head_norm(P, qf[:], qfb, 8, 96, gq, gqb, qn[:], qnb, sq, sqb, stq, stqb, tmp, tmpb)
            head_norm(P, kf[:], kfb, 8, 96, gk, gkb, kn[:], knb, sq, sqb, stk, stkb, tmp2, tmp2b)
            if C.stage <= 4:
                continue
            P.op("dve", lambda e: e.tensor_copy(out=qb_[:, 0], in_=qn[:]), [qnb], [qbb])
            P.op("dve", lambda e: e.tensor_copy(out=qb_[:, 1, :, 0:64], in_=qn[:, :, 0:64]), [qnb], [qbb])
            rope(P, qn[:, :, 64:96], qnb, c_, c_b, qb_[:, 1, :, 64:96], qbb, tt, ttb, 8)
            P.op("dve", lambda e: e.tensor_copy(out=kb_[:, :, 0:64], in_=kn[:, :, 0:64]), [knb], [kbb])
            rope(P, kn[:, :, 64:96], knb, c_, c_b, kb_[:, :, 64:96], kbb, tt2, tt2b, 8)
            if C.stage <= 5:
                continue
            for ver in range(2):
                for h in range(8):
                    P.op("pe", lambda e, ver=ver, h=h: e.transpose(out=pTq[0:96, ver * 8 + h, :], in_=qb_[:, ver, h, :], identity=ident[:]), [qbb, identb], [pTqb])
            for ver in range(2):
                P.op("act" if ver == 0 else "dve",
                     (lambda e, ver=ver, qs=qs, s=s: e.activation(func=AF.Identity, out=qs[:, ver, :, s * 128:(s + 1) * 128], in_=pTq[0:96, ver * 8:(ver + 1) * 8, :])) if ver == 0 else
                     (lambda e, ver=ver, qs=qs, s=s: e.tensor_copy(out=qs[:, ver, :, s * 128:(s + 1) * 128], in_=pTq[0:96, ver * 8:(ver + 1) * 8, :])),
                     [pTqb], [qsb])
            for h in range(8):
                P.op("pe", lambda e, h=h: e.transpose(out=pTk[0:96, h, :], in_=kb_[:, h, :], identity=ident[:]), [kbb, identb], [pTkb])
            P.op("dve", lambda e, ks=ks, s=s: e.tensor_copy(out=ks[:, :, s * 128:(s + 1) * 128], in_=pTk[0:96, :, :]), [pTkb], [ksb])
        n = nt * 128
        if C.stage <= 6:
            continue
        P.dma("pool", [(C.mla_qT[:, :, qt, :, 0:n].rearrange("v h p n -> p v h n"), qs[:, :, :, 0:n])], qsb, False)
        P.dma("pool", [(C.mla_kT[:, :, t0 * 128:t0 * 128 + n].rearrange("h p n -> p h n"), ks[:, :, 0:n])], ksb, False)
        P.dma("sp", [(C.mla_va[:, :, t0:t0 + nt, :].rearrange("h p k e -> p h k e"), vs[:, :, 0:nt, :])], vsb, False)
    P.end()


def attend_main(P, C, l, name, H, dk, scale, kT_ap, q_ap, va, ybr, chunks_of, bias_src=None, with_ctx_q=True):
    P.begin()
    identf, identfb = P.sb([128, 128], F32, "identf")
    P.dma("sp", [(identf[:], C.identf)], identfb, True)
    kT = [P.sb([dk, T], BF16, f"kT{i}") for i in range(2)]
    vv = [P.sb([128, NT, 66], BF16, f"vv{i}") for i in range(2)]
    nver = 2 if name == "mla" else 1
    qq = [[P.sb([dk, 512], BF16, f"q{v}_{i}") for v in range(nver)] for i in range(2)]
    pt = [P.sb([128, 512], BF16, f"pt{i}") for i in range(3)]
    sbias = [P.sb([128, 512], F32, f"sb{i}") for i in range(2)]
    nb_tiles = 8
    bt = [[P.sb([128, 512], F32, f"bt{j}_{i}") for i in range(nb_tiles)] for j in range(3)] if bias_src else None
    oT, oTb = P.sb([65, 512], F32, "oT")
    rec, recb = P.sb([128, 4], F32, "rec")
    ost = [P.sb([128, 4, 64], F32, f"ost{i}") for i in range(2)]
    pS = [P.ps([128, 512], F32, f"pS{i}") for i in range(3)]
    pO = [P.ps([128, 512], F32, f"pO{i}") for i in range(2)]
    pXf, pXb = P.ps([128, 512], F32, "pX")
    pX = pXf[:, 0:260].rearrange("p (t e) -> p t e", e=65)
    qts = ([0] if with_ctx_q else []) + list(range(1, 17))
    LA = 2
    groups_ = [(h, qt) for h in range(H) for qt in qts]
    items = []
    for gi, (h, qt) in enumerate(groups_):
        ch = chunks_of(qt)
        for n_, (kc, ver, bid) in enumerate(ch):
            items.append((gi, n_, len(ch), kc, ver, bid))
    cur_bias = [None, None, None]
    st = {"bmap": {}, "pend": []}
    loaded_heads = set()
    loaded_groups = set()

    def load_head(h):
        if h >= H or h in loaded_heads:
            return
        loaded_heads.add(h)
        k_, kb = kT[h % 2]
        v_, vb = vv[h % 2]
        P.dma("sp", [(k_[:, 0:T // 2], kT_ap(h)[:, 0:T // 2]), (k_[:, T // 2:T], kT_ap(h)[:, T // 2:T])], kb, True)
        P.dma("pool", [(v_[:], va[h])], vb, True)

    def load_group(gi):
        if gi >= len(groups_) or gi in loaded_groups:
            return
        loaded_groups.add(gi)
        h, qt = groups_[gi]
        nq = 256 if qt == 0 else 512
        for v in range(nver):
            P.dma("sp", [(qq[gi % 2][v][0][:, 0:nq], q_ap(h, qt, v)[:, 0:nq])], qq[gi % 2][v][1], True)
        if bias_src:
            ids = [b for (_, _, b) in chunks_of(qt) if b is not None]
            bm = {}
            if ids:
                key = (h, tuple(ids))
                slot = None
                for j in range(3):
                    if cur_bias[j] == key:
                        slot = j
                if slot is None:
                    slot = ids[0][0]
                    cur_bias[slot] = key
                    for n_, b in enumerate(ids):
                        P.dma("pool" if n_ % 2 else "sp", [(bt[slot][n_][0][:], bias_src(h, b))], bt[slot][n_][1], True)
                for n_, b in enumerate(ids):
                    bm[b] = bt[slot][n_]
            st["bmap"][gi] = bm

    def stage_a(n):
        gi, n_, nch, kc, ver, bid = items[n]
        h, qt = groups_[gi]
        if n_ == 0:
            load_head(h)
            load_group(gi)
            load_group(gi + 1)
            if gi + 1 < len(groups_) and groups_[gi + 1][0] != h:
                load_head(h + 1)
        nq = 256 if qt == 0 else 512
        k_, kb = kT[h % 2]
        s_, sb_ = pS[n % 3]
        p_, pb = pt[n % 3]
        q_, qb = qq[gi % 2][ver]
        P.op("pe", lambda e: e.matmul(s_[:, 0:nq], lhsT=k_[:, kc * 128:(kc + 1) * 128], rhs=q_[:, 0:nq], start=True, stop=True), [kb, qb], [sb_])
        if bid is None:
            P.op("act", lambda e: e.activation(out=p_[:, 0:nq], in_=s_[:, 0:nq], func=AF.Exp, scale=scale), [sb_], [pb])
        else:
            b_, bb = st["bmap"][gi][bid]
            x_, xb = sbias[n % 2]
            P.op("dve", lambda e: e.scalar_tensor_tensor(out=x_[:], in0=s_[:], scalar=scale, in1=b_[:], op0=ALU.mult, op1=ALU.add), [sb_, bb], [xb])
            P.op("act", lambda e: e.activation(out=p_[:], in_=x_[:], func=AF.Exp), [xb], [pb])

    def epilogue(gi):
        h, qt = groups_[gi]
        nq = 256 if qt == 0 else 512
        tok0 = 0 if qt == 0 else 256 + (qt - 1) * 512
        o_, ob = pO[gi % 2]
        nt4 = nq // 128
        P.op("act", lambda e: e.activation(func=AF.Identity, out=oT[:, 0:nq], in_=o_[0:65, 0:nq]), [ob], [oTb])
        for t in range(nt4):
            P.op("pe", lambda e, t=t: e.transpose(out=pX[:, t, :], in_=oT[:, t * 128:(t + 1) * 128], identity=identf[0:65, 0:65]), [oTb, identfb], [pXb])
        P.op("dve", lambda e: e.reciprocal(out=rec[:, 0:nt4], in_=pX[:, 0:nt4, 64]), [pXb], [recb])
        os_, osb = ost[gi % 2]
        P.op("dve", lambda e: e.tensor_tensor(out=os_[:, 0:nt4, :], in0=pX[:, 0:nt4, 0:64], in1=rec[:, 0:nt4].unsqueeze(2).to_broadcast([128, nt4, 64]), op=ALU.mult), [pXb, recb], [osb])
        P.dma("pool", [(ybr[tok0:tok0 + nq, h * 64:(h + 1) * 64].rearrange("(t p) d -> p t d", p=128), os_[:, 0:nt4, :])], osb, False)

    def stage_b(n):
        gi, n_, nch, kc, ver, bid = items[n]
        h, qt = groups_[gi]
        nq = 256 if qt == 0 else 512
        v_, vb = vv[h % 2]
        p_, pb = pt[n % 3]
        o_, ob = pO[gi % 2]
        P.op("pe", lambda e: e.matmul(o_[0:65, 0:nq], lhsT=v_[:, kc, 0:65], rhs=p_[:, 0:nq], start=(n_ == 0), stop=(n_ == nch - 1)), [vb, pb], [ob])
        if n_ == nch - 1:
            st["pend"].append((n + 3, gi))

    N = len(items)
    for n in range(N + LA):
        if n < N:
            stage_a(n)
        if n - LA >= 0:
            stage_b(n - LA)
        while st["pend"] and st["pend"][0][0] <= n:
            epilogue(st["pend"].pop(0)[1])
    while st["pend"]:
        epilogue(st["pend"].pop(0)[1])
    P.end()


def mla_main(P, C, l):
    def chunks_of(qt):
        if qt == 0:
            return [(0, 0, None), (1, 0, None)]
        return [(0, 0, None), (1, 0, None)] + [(kc, 1, None) for kc in range(2, NT)]
    attend_main(P, C, l, "mla", 8, 96, 96 ** -0.5, lambda h: C.mla_kT[h], lambda h, qt, v: C.mla_qT[v, h, qt], C.mla_va, C.ybr[0], chunks_of,
                with_ctx_q=(l == 0))


def nat_prep(P, C, l):
    P.begin()
    ident, identb = P.sb([128, 128], BF16, "ident")
    P.dma("sp", [(ident[:], C.ident)], identb, True)
    gq, gqb = P.sb([128, 64], F32, "gq")
    gk, gkb = P.sb([128, 64], F32, "gk")
    P.dma("sp", [(gq[:], C.nat_gq_rep[l])], gqb, True)
    P.dma("sp", [(gk[:], C.nat_gk_rep[l])], gkb, True)
    seg = [P.sb([128, 1536], F32, f"seg{i}") for i in range(2)]
    sq, sqb = P.sb([128, 512], F32, "sq")
    stq, stqb = P.sb([128, 24], F32, "stq")
    stk, stkb = P.sb([128, 24], F32, "stk")
    tmp, tmpb = P.sb([128, 512], F32, "tmp")
    tmp2, tmp2b = P.sb([128, 512], F32, "tmp2")
    qk, qkb = P.sb([128, 2, 8, 64], BF16, "qk")
    qst = [P.sb([128, 4, 512], BF16, f"qst{i}") for i in range(2)]
    kst = [P.sb([128, 4, 512], BF16, f"kst{i}") for i in range(2)]
    vst = [P.sb([128, 8, 4, 66], BF16, f"vst{i}") for i in range(2)]
    for i in range(2):
        P.op("dve", lambda e, i=i: e.memset(vst[i][0][:], 1.0), [], [vst[i][1]])
    pT, pTb = P.ps([128, 8, 128], BF16, "pT")
    for gi, (t0, nt, qt) in enumerate(groups()[:C.max_groups]):
        qs, qsb = qst[gi % 2]
        ks, ksb = kst[gi % 2]
        vs, vsb = vst[gi % 2]
        for s in range(nt):
            i = t0 + s
            sg, sgb = seg[i % 2]
            P.dma("sp" if i % 2 == 0 else "pool", [(sg[:], C.proj[i * 128:(i + 1) * 128, 416:1952])], sgb, True)
            head_norm(P, sg[:, 0:512], sgb, 8, 64, gq, gqb, qk[:, 0], qkb, sq, sqb, stq, stqb, tmp, tmpb)
            head_norm(P, sg[:, 512:1024], sgb, 8, 64, gk, gkb, qk[:, 1], qkb, sq, sqb, stk, stkb, tmp2, tmp2b)
            P.op("dve", lambda e, sg=sg, vs=vs, s=s: e.tensor_copy(out=vs[:, :, s, 0:64], in_=sg[:, 1024:1536].rearrange("p (h d) -> p h d", h=8)), [sgb], [vsb])
            for w in range(2):
                for pr in range(4):
                    P.op("pe", lambda e, w=w, pr=pr: e.transpose(out=pT[:, w * 4 + pr, :], in_=qk[:, w, 2 * pr:2 * pr + 2, :].rearrange("p h d -> p (h d)"), identity=ident[:]), [qkb, identb], [pTb])
            P.op("act", lambda e, qs=qs, s=s: e.activation(func=AF.Identity, out=qs[:, :, s * 128:(s + 1) * 128], in_=pT[:, 0:4, :]), [pTb], [qsb])
            P.op("act", lambda e, ks=ks, s=s: e.activation(func=AF.Identity, out=ks[:, :, s * 128:(s + 1) * 128], in_=pT[:, 4:8, :]), [pTb], [ksb])
        n = nt * 128
        P.dma("pool", [(C.nat_qT[:, qt, :, 0:n].rearrange("r p n -> p r n"), qs[:, :, 0:n])], qsb, False)
        P.dma("pool", [(C.nat_kT[:, :, t0 * 128:t0 * 128 + n].rearrange("r p n -> p r n"), ks[:, :, 0:n])], ksb, False)
        P.dma("sp", [(C.nat_va[:, :, t0:t0 + nt, :].rearrange("h p k e -> p h k e"), vs[:, :, 0:nt, :])], vsb, False)
    P.end()


def nat_block(j):
    kb = min(max(8 * j - 4, 0), 112)
    pat = 0 if j == 0 else (2 if j == 15 else 1)
    cs_ = range(0, 6) if j == 0 else (range(2, 8) if j == 15 else range(0, 8))
    return kb, pat, list(cs_)


def nat_main(P, C, l):
    def chunks_of(qt):
        if qt == 0:
            return [(0, 0, None), (1, 0, None)]
        kb, pat, cs_ = nat_block(qt - 1)
        return [(0, 0, None), (1, 0, None)] + [(2 + kb // 2 + c, 0, (pat, c)) for c in cs_]
    attend_main(P, C, l, "nat", 8, 64, 64 ** -0.5,
                lambda h: C.nat_kT[h // 2, (h % 2) * 64:(h % 2 + 1) * 64, :],
                lambda h, qt, v: C.nat_qT[h // 2, qt, (h % 2) * 64:(h % 2 + 1) * 64, :],
                C.nat_va, C.ybr[1], chunks_of, bias_src=lambda h, b: C.nat_bias[l, b[0], h, b[1]], with_ctx_q=(l == 0))


def nat_bias_host(rpb):
    L = rpb.shape[0]
    out = np.full((L, 3, 8, 8, 128, 512), -30000.0, np.float32)
    ck = np.arange(64)[:, None]
    cq = np.arange(64)[None, :]
    c0 = np.clip(cq - 8, 0, 48)
    col_ok = (ck >= c0) & (ck < c0 + 16)
    dc = np.clip(ck - cq, -15, 15) + 15
    for pat, j in enumerate((0, 1, 15)):
        kb, _, cs_ = nat_block(j)
        for c in cs_:
            for a in range(2):
                kr = kb + 2 * c + a
                for r in range(8):
                    qr = 8 * j + r
                    r0 = min(max(qr - 4, 0), 120)
                    if not (r0 <= kr < r0 + 8):
                        continue
                    dr = kr - qr + 7
                    blk = np.where(col_ok[None, None], rpb[:, :, dr][:, :, dc], np.float32(-30000.0))
                    out[:, pat, :, c, a * 64:(a + 1) * 64, r * 64:(r + 1) * 64] = blk
    return out


def scan_consts_host():
    j = np.arange(128)[:, None]
    i = np.arange(128)[None, :]
    same = (j // 32) == (i // 32)
    mf = (same & (j <= i)).astype(np.float32)
    mb = (same & (j >= i)).astype(np.float32)
    blk = same.astype(np.float32)
    ind = (j // 32 == np.arange(4)[None, :]).astype(np.float32)
    return np.ascontiguousarray(np.concatenate([mf, mb, blk, ind], 1))


def scan_mixer(P, C, l, kind):
    hg = kind == "hg"
    H, dk, dv = (4, 128, 128) if hg else (4, 64, 128)
    W = H * dk
    c0, cw = SEG["hg"] if hg else SEG["gla"]
    qscale = float(dk) ** -0.5
    br = 2 if hg else 3
    for d in range(2):
        P.begin()
        ident, identb = P.sb([128, 128], BF16, "ident")
        P.dma("sp", [(ident[:], C.ident)], identb, True)
        identf, identfb = P.sb([128, 128], F32, "identf")
        P.dma("sp", [(identf[:], C.identf)], identfb, True)
        sc, scb = P.sb([128, 388], F32, "sconst")
        P.dma("sp", [(sc[:], C.scan_consts)], scb, True)
        Md = sc[:, 0:128] if d == 0 else sc[:, 128:256]
        Blk = sc[:, 256:384]
        Ind = sc[:, 384:388]
        go, gob = P.sb([128, 128], F32, "go")
        P.dma("sp", [(go[:], (C.hg_go_rep if hg else C.gla_go_rep)[l])], gob, True)
        if hg:
            lbt, lbb = P.sb([128, 512], F32, "lbt")
            oml, omlb = P.sb([128, 512], F32, "oml")
            if l == 0:
                P.op("dve", lambda e: e.memset(lbt[:], 0.0), [], [lbb])
                P.op("dve", lambda e: e.memset(oml[:], 1.0), [], [omlb])
            else:
                zz, zzb = P.sb([128, 2, 512], F32, "zz")
                P.dma("sp", [(zz[:], C.hg_lb_rep[:, :, d, :].rearrange("l p n -> p l n"))], zzb, True)
                P.op("dve", lambda e: e.tensor_tensor(out=lbt[:], in0=zz[:, 1, :], in1=zz[:, 0, :], op=ALU.subtract), [zzb], [lbb])
                P.op("act", lambda e: e.activation(out=lbt[:], in_=lbt[:], func=AF.Sigmoid), [lbb], [lbb])
                P.op("dve", lambda e: e.tensor_scalar(out=oml[:], in0=lbt[:], scalar1=-1.0, scalar2=1.0, op0=ALU.mult, op1=ALU.add), [lbb], [omlb])
        else:
            w2, w2b = P.sb([16, 256], F32, "w2")
            b2, b2b = P.sb([128, 256], F32, "b2")
            P.dma("sp", [(w2[:], C.gla_w2[l, d])], w2b, True)
            P.dma("sp", [(b2[:], C.gla_b2_rep[l, :, d, :])], b2b, True)
            rT, rTb = P.sb([16, 128], F32, "rT")
            gx, gxb = P.sb([128, 256], F32, "gx")
        seg = [P.sb([128, cw], F32, f"seg{i}") for i in range(2)]
        qf, qfb = P.sb([128, W], F32, "qf")
        kf, kfb = P.sb([128, W], F32, "kf")
        la, lab = P.sb([128, W], F32, "la")
        vb2 = [P.sb([128, H * dv], BF16, f"vb{i}") for i in range(3)]
        sg1, sg1b = P.sb([128, W], F32, "sg1")
        bs, bsb = P.sb([128, W], F32, "bs")
        eb, ebb = P.sb([128, W], F32, "eb")
        enb, enbb = P.sb([128, W], F32, "enb")
        ebe, ebeb = P.sb([128, W], F32, "ebe")
        dcy2 = [P.sb([128, H * 4], F32, f"dcy{i}") for i in range(2)]
        qt_, qtb = P.sb([128, W], BF16, "qt")
        kt_, ktb = P.sb([128, W], BF16, "kt")
        kh, khb = P.sb([128, W], BF16, "kh")
        kpad2 = [P.sb([128, H, 4, dk], BF16, f"kpad{i}") for i in range(2)]
        qT, qTb = P.sb([128, H, 128], BF16, "qT")
        kT, kTb = P.sb([128, H, 128], BF16, "kT")
        qpad2 = [P.sb([128, H, 608], BF16, f"qpad{i}") for i in range(3)]
        for i_ in range(3):
            P.op("dve", lambda e, i_=i_: e.memset(qpad2[i_][0][:], 0.0), [], [qpad2[i_][1]])
        AT2 = [P.sb([128, H, 128], BF16, f"AT{i}") for i in range(3)]
        S, Sb_ = P.sb([128, H, dv], F32, "S")
        P.op("dve", lambda e: e.memset(S[:], 0.0), [], [Sb_])
        Sbf = [P.sb([128, H, 4, dv], BF16, f"Sbf{i}") for i in range(3)]
        ofw = [P.sb([128, 512], F32, f"ofw{i}") for i in range(3)]
        ost = [P.sb([128, 512], F32, f"ost{i}") for i in range(2)]
        osum, osumb = P.sb([128, 512], F32, "osum")
        sq, sqb = P.sb([128, 512], F32, "sq")
        stn, stnb = P.sb([128, 12], F32, "stn")
        tmpn, tmpnb = P.sb([128, 512], F32, "tmpn")
        pb, pbb = P.ps([128, 512], F32, "pb")
        pbe, pbeb = P.ps([128, 512], F32, "pbe")
        pbT, pbTb = P.ps([128, 512], F32, "pbT")
        pT, pTb = P.ps([128, 8, 128], BF16, "pT")
        pS, pSb = P.ps([128, 512], F32, "pS")
        pO, pOb = P.ps([128, 512], F32, "pO")
        pU = [P.ps([128, 512], F32, f"pU{i}") for i in range(2)]
        tiles = list(range(NT)) if d == 0 else [1, 0] + list(range(NT - 1, 1, -1))
        if C.max_groups:
            tiles = tiles[:C.max_groups] if d == 0 else ([1, 0] + list(range(C.max_groups - 1, 1, -1)))
        cseq = [0, 1, 2, 3] if d == 0 else [3, 2, 1, 0]
        P.op("dve", lambda e: e.memset(Sbf[0][0][:, :, cseq[0], :], 0.0), [], [Sbf[0][1]])
        def front(n, i):
            sg, sgb = seg[n % 2]
            vb, vbb = vb2[n % 3]
            dcy, dcyb = dcy2[n % 2]
            kpad, kpadb = kpad2[n % 2]
            qpad, qpadb = qpad2[n % 3]
            AT, ATb = AT2[n % 3]
            yield
            P.dma("sp" if n % 2 == 0 else "pool", [(sg[:], C.proj[i * 128:(i + 1) * 128, c0:c0 + cw])], sgb, True)
            if d == 1:
                of_, ofb = ofw[n % 3]
                yield
                P.dma("sp", [(of_[:], C.osc[i * 128:(i + 1) * 128, :])], ofb, True)
            if hg:
                fcol = 512 if d == 0 else 1024
                yield
                P.op("act", lambda e, sg=sg: e.activation(out=qf[:], in_=sg[:, 0:512], func=AF.Silu), [sgb], [qfb])
                yield
                P.op("act", lambda e, sg=sg: e.activation(out=sg1[:], in_=sg[:, fcol:fcol + 512], func=AF.Sigmoid), [sgb], [sg1b])
                yield
                P.op("act", lambda e, sg=sg: e.activation(out=kf[:], in_=sg[:, fcol:fcol + 512], func=AF.Sigmoid, scale=-1.0), [sgb], [kfb])
                yield
                P.op("dve", lambda e: e.tensor_tensor(out=kf[:], in0=kf[:], in1=oml[:], op=ALU.mult), [kfb, omlb], [kfb])
                yield
                P.op("dve", lambda e: e.tensor_tensor(out=sg1[:], in0=sg1[:], in1=oml[:], op=ALU.mult), [sg1b, omlb], [sg1b])
                yield
                P.op("dve", lambda e: e.tensor_tensor(out=sg1[:], in0=sg1[:], in1=lbt[:], op=ALU.add), [sg1b, lbb], [sg1b])
                yield
                P.op("act", lambda e: e.activation(out=la[:], in_=sg1[:], func=AF.Ln), [sg1b], [lab])
                yield
                P.op("dve", lambda e, sg=sg: e.tensor_copy(out=vb[:], in_=sg[:, 1536:2048]), [sgb], [vbb])
                qsrc, qsrcb, ksrc, ksrcb = qf[:], qfb, kf[:], kfb
            else:
                rc = 1024 + 16 * d
                yield
                P.op("pe", lambda e, sg=sg: e.transpose(out=pbT[0:16, 0:128], in_=sg[:, rc:rc + 16], identity=identf[:]), [sgb, identfb], [pbTb])
                yield
                P.op("act", lambda e: e.activation(func=AF.Identity, out=rT[:], in_=pbT[0:16, 0:128]), [pbTb], [rTb])
                yield
                P.op("pe", lambda e: e.matmul(pS[:, 0:256], lhsT=rT[:], rhs=w2[:], start=True, stop=True), [rTb, w2b], [pSb])
                yield
                P.op("dve", lambda e: e.tensor_tensor(out=gx[:], in0=pS[:, 0:256], in1=b2[:], op=ALU.add), [pSb, b2b], [gxb])
                yield
                P.op("act", lambda e: e.activation(out=gx[:], in_=gx[:], func=AF.Sigmoid), [gxb], [gxb])
                yield
                P.op("act", lambda e: e.activation(out=gx[:], in_=gx[:], func=AF.Ln), [gxb], [gxb])
                yield
                P.op("dve", lambda e: e.tensor_scalar(out=la[:], in0=gx[:], scalar1=1.0 / 16, scalar2=None, op0=ALU.mult), [gxb], [lab])
                yield
                P.op("dve", lambda e, sg=sg: e.tensor_copy(out=vb[:], in_=sg[:, 512:1024]), [sgb], [vbb])
                qsrc, qsrcb, ksrc, ksrcb = sg[:, 0:256], sgb, sg[:, 256:512], sgb
            yield
            P.op("pe", lambda e: e.matmul(pb[:, 0:W], lhsT=Md, rhs=la[:], start=True, stop=True), [scb, lab], [pbb])
            yield
            P.op("pe", lambda e: e.matmul(pbe[:, 0:W], lhsT=Blk, rhs=la[:], start=True, stop=True), [scb, lab], [pbeb])
            for h in range(H):
                yield
                P.op("pe", lambda e, h=h: e.matmul(pbT[0:dk, h * 4:(h + 1) * 4], lhsT=la[:, h * dk:(h + 1) * dk], rhs=Ind, start=True, stop=True), [lab, scb], [pbTb])
            yield
            P.op("act", lambda e: e.activation(func=AF.Identity, out=bs[:], in_=pb[:, 0:W]), [pbb], [bsb])
            yield
            P.op("act", lambda e: e.activation(out=eb[:], in_=pb[:, 0:W], func=AF.Exp), [pbb], [ebb])
            yield
            P.op("act", lambda e: e.activation(out=enb[:], in_=pb[:, 0:W], func=AF.Exp, scale=-1.0), [pbb], [enbb])
            yield
            P.op("dve", lambda e: e.tensor_tensor(out=ebe[:], in0=pbe[:, 0:W], in1=bs[:], op=ALU.subtract), [pbeb, bsb], [ebeb])
            yield
            P.op("act", lambda e: e.activation(out=ebe[:], in_=ebe[:], func=AF.Exp), [ebeb], [ebeb])
            yield
            P.op("act", lambda e: e.activation(out=dcy[0:dk, :], in_=pbT[0:dk, 0:H * 4], func=AF.Exp), [pbTb], [dcyb])
            yield
            P.op("dve", lambda e, qsrc=qsrc: e.scalar_tensor_tensor(out=qt_[:], in0=qsrc, scalar=qscale, in1=eb[:], op0=ALU.mult, op1=ALU.mult), [qsrcb, ebb], [qtb])
            yield
            P.op("dve", lambda e, ksrc=ksrc: e.tensor_tensor(out=kt_[:], in0=ksrc, in1=enb[:], op=ALU.mult), [ksrcb, enbb], [ktb])
            yield
            P.op("dve", lambda e, ksrc=ksrc: e.tensor_tensor(out=kh[:], in0=ksrc, in1=ebe[:], op=ALU.mult), [ksrcb, ebeb], [khb])
            for h in range(H):
                yield
                P.op("dve", lambda e, h=h: e.tensor_tensor(out=kpad[:, h], in0=kh[:, None, h * dk:(h + 1) * dk].to_broadcast([128, 4, dk]),
                                                           in1=Ind.unsqueeze(2).to_broadcast([128, 4, dk]), op=ALU.mult), [khb, scb], [kpadb])
            for h in range(H):
                yield
                P.op("pe", lambda e, h=h: e.transpose(out=pT[0:dk, h, :], in_=qt_[:, h * dk:(h + 1) * dk], identity=ident[:]), [qtb, identb], [pTb])
                yield
                P.op("pe", lambda e, h=h: e.transpose(out=pT[0:dk, H + h, :], in_=kt_[:, h * dk:(h + 1) * dk], identity=ident[:]), [ktb, identb], [pTb])
            yield
            P.op("act", lambda e: e.activation(func=AF.Identity, out=qT[0:dk], in_=pT[0:dk, 0:H, :]), [pTb], [qTb])
            yield
            P.op("act", lambda e: e.activation(func=AF.Identity, out=kT[0:dk], in_=pT[0:dk, H:2 * H, :]), [pTb], [kTb])
            for h in range(H):
                yield
                P.op("dve", lambda e, h=h: e.tensor_copy(out=qpad[0:dk, h, 96:608].rearrange("p (c n) -> p c n", n=128)[:, :, 0:32],
                                                         in_=pT[0:dk, h, :].rearrange("p (c t) -> p c t", t=32)), [pTb], [qpadb])
            for h in range(H):
                yield
                P.op("pe", lambda e, h=h: e.matmul(pS[:, h * 128:(h + 1) * 128], lhsT=kT[0:dk, h, :], rhs=qT[0:dk, h, :], start=True, stop=True), [kTb, qTb], [pSb])
            yield
            P.op("dve", lambda e: e.tensor_tensor(out=AT[:], in0=pS[:].rearrange("p (h n) -> p h n", h=H), in1=Md[:, None, :].to_broadcast([128, H, 128]), op=ALU.mult), [pSb, scb], [ATb])
        def back(n, i):
            vb, vbb = vb2[n % 3]
            dcy, dcyb = dcy2[n % 2]
            kpad, kpadb = kpad2[n % 2]
            qpad, qpadb = qpad2[n % 3]
            AT, ATb = AT2[n % 3]
            of_, ofb = ofw[n % 3]
            sb_cur, sb_curb = Sbf[n % 3]
            sb_nxt, sb_nxtb = Sbf[(n + 1) % 3]
            for ci, c in enumerate(cseq):
                u_, ub = pU[ci % 2]
                for h in range(H):
                    yield
                    P.op("pe", lambda e, h=h, c=c, u_=u_: e.matmul(u_[0:dk, h * dv:(h + 1) * dv], lhsT=kpad[:, h, c, :], rhs=vb[:, h * dv:(h + 1) * dv], start=True, stop=True), [kpadb, vbb], [ub])
                for h in range(H):
                    yield
                    P.op("dve", lambda e, h=h, c=c, u_=u_: e.scalar_tensor_tensor(out=S[0:dk, h, :], in0=S[0:dk, h, :], scalar=dcy[0:dk, h * 4 + c:h * 4 + c + 1],
                                                                                 in1=u_[0:dk, h * dv:(h + 1) * dv], op0=ALU.mult, op1=ALU.add), [Sb_, dcyb, ub], [Sb_])
                if ci < 3:
                    yield
                    P.op("act", lambda e, ci=ci, sb_cur=sb_cur: e.activation(func=AF.Identity, out=sb_cur[0:dk, :, cseq[ci + 1], :], in_=S[0:dk]), [Sb_], [sb_curb])
                else:
                    yield
                    P.op("act", lambda e, sb_nxt=sb_nxt: e.activation(func=AF.Identity, out=sb_nxt[0:dk, :, cseq[0], :], in_=S[0:dk]), [Sb_], [sb_nxtb])

        def tail(n, i):
            vb, vbb = vb2[n % 3]
            qpad, qpadb = qpad2[n % 3]
            AT, ATb = AT2[n % 3]
            of_, ofb = ofw[n % 3]
            sb_cur, sb_curb = Sbf[n % 3]
            for h in range(H):
                yield
                P.op("pe", lambda e, h=h: e.matmul(pO[:, h * dv:(h + 1) * dv], lhsT=AT[:, h, :], rhs=vb[:, h * dv:(h + 1) * dv], start=True, stop=False), [ATb, vbb], [pOb])
                for ci, c in enumerate(cseq):
                    yield
                    P.op("pe", lambda e, h=h, c=c, ci=ci, sb_cur=sb_cur: e.matmul(pO[:, h * dv:(h + 1) * dv], lhsT=qpad[0:dk, h, 96 + 96 * c:224 + 96 * c], rhs=sb_cur[0:dk, h, c, :],
                                                                  start=False, stop=(ci == 3)), [qpadb, sb_curb], [pOb])
            if d == 0:
                o_, ob = ost[n % 2]
                yield
                P.op("act", lambda e, o_=o_: e.activation(func=AF.Identity, out=o_[:], in_=pO[:]), [pOb], [ob])
                yield
                P.dma("pool", [(C.osc[i * 128:(i + 1) * 128, :], o_[:])], ob, False)
            else:
                o_, ob = ost[n % 2]
                yield
                P.op("dve", lambda e, of_=of_: e.tensor_tensor(out=osum[:], in0=pO[:], in1=of_[:], op=ALU.add), [pOb, ofb], [osumb])
                head_norm(P, osum[:], osumb, 4, 128, go, gob, o_[:].rearrange("p (h d) -> p h d", h=4), ob, sq, sqb, stn, stnb, tmpn, tmpnb)
                yield
                P.dma("pool", [(C.ybr[br][i * 128:(i + 1) * 128, :], o_[:])], ob, False)

        def drive(gens):
            while gens:
                for g_ in list(gens):
                    try:
                        next(g_)
                    except StopIteration:
                        gens.remove(g_)

        drive([front(0, tiles[0])])
        for n, i in enumerate(tiles):
            gs = [back(n, i)]
            if n + 1 < len(tiles):
                gs.insert(0, front(n + 1, tiles[n + 1]))
            if n > 0:
                gs.append(tail(n - 1, tiles[n - 1]))
            drive(gs)
        drive([tail(len(tiles) - 1, tiles[-1])])
        P.end()


def merge_phase(P, C, l):
    tiles = list(range(NT)) if l == 0 else list(range(2, NT))
    if C.max_groups:
        tiles = tiles[:C.max_groups]
    P.begin()
    ident, identb = P.sb([128, 128], BF16, "ident")
    P.dma("sp", [(ident[:], C.ident)], identb, True)
    wbr, wbrb = P.sb([128, 4, 4, 1024], BF16, "wbr")
    wmg, wmgb = P.sb([128, 4, 8, 1024], BF16, "wmg")
    bmg, bmgb = P.sb([1, 4, 1024], BF16, "bmg")
    ones, onesb = P.sb([1, 128], BF16, "ones")
    P.op("dve", lambda e: e.memset(ones[:], 1.0), [], [onesb])
    stg = [P.sb([128, 1024], F32, f"wstg{i}") for i in range(2)]
    n = 0
    for br in range(4):
        for k in range(4):
            s_, sb_ = stg[n % 2]
            P.dma("sp" if n % 2 == 0 else "pool", [(s_[:], C.w_br[l, br, k * 128:(k + 1) * 128, :])], sb_, True)
            if n % 2 == 0:
                P.op("act", lambda e, s_=s_, br=br, k=k: e.activation(func=AF.Identity, out=wbr[:, br, k, :], in_=s_[:]), [sb_], [wbrb])
            else:
                P.op("dve", lambda e, s_=s_, br=br, k=k: e.tensor_copy(out=wbr[:, br, k, :], in_=s_[:]), [sb_], [wbrb])
            n += 1
        for k in range(8):
            s_, sb_ = stg[n % 2]
            P.dma("sp" if n % 2 == 0 else "pool", [(s_[:], C.w_merge[l, br, k * 128:(k + 1) * 128, :])], sb_, True)
            if n % 2 == 0:
                P.op("act", lambda e, s_=s_, br=br, k=k: e.activation(func=AF.Identity, out=wmg[:, br, k, :], in_=s_[:]), [sb_], [wmgb])
            else:
                P.op("dve", lambda e, s_=s_, br=br, k=k: e.tensor_copy(out=wmg[:, br, k, :], in_=s_[:]), [sb_], [wmgb])
            n += 1
    bst, bstb = P.sb([1, 4, 1024], F32, "bst")
    P.dma("sp", [(bst[:], C.b_merge[l:l + 1])], bstb, True)
    P.op("dve", lambda e: e.tensor_copy(out=bmg[:], in_=bst[:]), [bstb], [bmgb])
    yb = [[P.sb([128, 512], F32, f"y{br}_{i}") for br in range(4)] for i in range(2)]
    zt = [P.sb([128, 2048], F32, f"z{i}") for i in range(2)]
    hT = [P.sb([128, 8, 128], BF16, f"hT{i}") for i in range(2)]
    sz, szb = P.sb([128, 2048], F32, "sz")
    u, ub = P.sb([128, 2048], BF16, "u")
    uT, uTb = P.sb([128, 16, 128], BF16, "uT")
    g, gb = P.sb([128, 1024], F32, "g")
    tmp, tmpb = P.sb([128, 1024], F32, "tmp")
    accs = [P.sb([128, 1024], F32, f"acc{i}") for i in range(2)]
    accbf, accbfb = P.sb([128, 1024], BF16, "accbf")
    aT = [P.sb([128, 8, 128], BF16, f"aT{i}") for i in range(2)]
    pT = [P.ps([128, 8, 128], BF16, f"pT{i}") for i in range(2)]
    pP = [P.ps([128, 512], F32, f"pP{i}") for i in range(2)]
    pG = [P.ps([128, 512], F32, f"pG{i}") for i in range(2)]
    uTs = [(uT, uTb), P.sb([128, 16, 128], BF16, "uT2")]
    pA, pAb = P.ps([128, 8, 128], BF16, "pA")

    def front(n, i):
        ys = yb[n % 2]
        z_, zb = zt[n % 2]
        h_, hb = hT[n % 2]
        uT_, uT_b = uTs[n % 2]
        for br in range(4):
            yield
            P.dma("sp" if br % 2 == 0 else "pool", [(ys[br][0][:], C.ybr[br][i * 128:(i + 1) * 128, :])], ys[br][1], True)
        yield
        P.dma("sp", [(z_[:], C.proj[i * 128:(i + 1) * 128, 5056:7104])], zb, True)
        yield
        P.dma("pool", [(h_[:], C.hT[i])], hb, True)
        yield
        P.op("act", lambda e: e.activation(out=sz[:], in_=z_[:], func=AF.Silu), [zb], [szb])
        for br in range(4):
            yield
            P.op("dve", lambda e, br=br: e.tensor_tensor(out=u[:, br * 512:(br + 1) * 512], in0=ys[br][0][:], in1=sz[:, br * 512:(br + 1) * 512], op=ALU.mult),
                 [ys[br][1], szb], [ub])
        for half in range(2):
            p_, pb_ = pT[half]
            for k in range(8):
                yield
                P.op("pe", lambda e, p_=p_, k=k, half=half: e.transpose(out=p_[:, k, :], in_=u[:, (half * 8 + k) * 128:(half * 8 + k + 1) * 128], identity=ident[:]), [ub, identb], [pb_])
            yield
            P.op("act", lambda e, p_=p_, half=half: e.activation(func=AF.Identity, out=uT_[:, half * 8:(half + 1) * 8, :], in_=p_[:]), [pb_], [uT_b])

    def back(n, i):
        h_, hb = hT[n % 2]
        uT_, uT_b = uTs[n % 2]
        acc, accb = accs[n % 2]
        for br in range(4):
            for hf in range(2):
                pp, ppb = pP[hf]
                pg, pgb = pG[hf]
                for k in range(4):
                    yield
                    P.op("pe", lambda e, pp=pp, br=br, k=k, hf=hf: e.matmul(pp[:], lhsT=uT_[:, br * 4 + k, :], rhs=wbr[:, br, k, hf * 512:(hf + 1) * 512], start=(k == 0), stop=(k == 3)), [uT_b, wbrb], [ppb])
                for k in range(8):
                    yield
                    P.op("pe", lambda e, pg=pg, br=br, k=k, hf=hf: e.matmul(pg[:], lhsT=h_[:, k, :], rhs=wmg[:, br, k, hf * 512:(hf + 1) * 512], start=(k == 0), stop=False), [hb, wmgb], [pgb])
                yield
                P.op("pe", lambda e, pg=pg, br=br, hf=hf: e.matmul(pg[:], lhsT=ones[:], rhs=bmg[:, br, hf * 512:(hf + 1) * 512], start=False, stop=True), [onesb, bmgb], [pgb])
                yield
                P.op("act", lambda e, pg=pg, hf=hf: e.activation(out=g[:, hf * 512:(hf + 1) * 512], in_=pg[:], func=AF.Sigmoid), [pgb], [gb])
                if br == 0:
                    yield
                    P.op("dve", lambda e, pp=pp, hf=hf: e.tensor_tensor(out=acc[:, hf * 512:(hf + 1) * 512], in0=pp[:], in1=g[:, hf * 512:(hf + 1) * 512], op=ALU.mult), [ppb, gb], [accb])
                else:
                    yield
                    P.op("dve", lambda e, pp=pp, hf=hf: e.tensor_tensor(out=tmp[:, hf * 512:(hf + 1) * 512], in0=pp[:], in1=g[:, hf * 512:(hf + 1) * 512], op=ALU.mult), [ppb, gb], [tmpb])
                    yield
                    P.op("dve", lambda e, hf=hf: e.tensor_tensor(out=acc[:, hf * 512:(hf + 1) * 512], in0=acc[:, hf * 512:(hf + 1) * 512], in1=tmp[:, hf * 512:(hf + 1) * 512], op=ALU.add), [accb, tmpb], [accb])

    def tail(n, i):
        acc, accb = accs[n % 2]
        yield
        P.op("dve", lambda e: e.tensor_copy(out=accbf[:], in_=acc[:]), [accb], [accbfb])
        a_, ab = aT[n % 2]
        for k in range(8):
            yield
            P.op("pe", lambda e, k=k: e.transpose(out=pA[:, k, :], in_=accbf[:, k * 128:(k + 1) * 128], identity=ident[:]), [accbfb, identb], [pAb])
        yield
        P.op("act", lambda e: e.activation(func=AF.Identity, out=a_[:], in_=pA[:]), [pAb], [ab])
        yield
        P.dma("pool", [(C.accT[i], a_[:])], ab, False)

    drive([front(0, tiles[0])])
    for n, i in enumerate(tiles):
        gs = [back(n, i)]
        if n + 1 < len(tiles):
            gs.append(front(n + 1, tiles[n + 1]))
        if n > 0:
            gs.append(tail(n - 1, tiles[n - 1]))
        drive(gs)
    drive([tail(len(tiles) - 1, tiles[-1])])
    P.end()
    P.begin()
    identf, identfb = P.sb([128, 128], F32, "identf")
    P.dma("sp", [(identf[:], C.identf)], identfb, True)
    onef, onefb = P.sb([128, 128], F32, "onef")
    P.op("dve", lambda e: e.memset(onef[:], 1.0), [], [onefb])
    wo, wob = P.sb([128, 8, 1024], BF16, "wo")
    stg = [P.sb([128, 1024], F32, f"wstg{i}") for i in range(2)]
    for k in range(8):
        s_, sb_ = stg[k % 2]
        P.dma("sp" if k % 2 == 0 else "pool", [(s_[:], C.w_out[l, k * 128:(k + 1) * 128, :])], sb_, True)
        if k % 2 == 0:
            P.op("act", lambda e, s_=s_, k=k: e.activation(func=AF.Identity, out=wo[:, k, :], in_=s_[:]), [sb_], [wob])
        else:
            P.op("dve", lambda e, s_=s_, k=k: e.tensor_copy(out=wo[:, k, :], in_=s_[:]), [sb_], [wob])
    gtr = [P.sb([128, 1024], F32, f"gtr{v}") for v in range(2)]
    dg, dgb = P.sb([128, 128], F32, "dg")
    pR = [P.ps([128, 512], F32, f"pR{i}") for i in range(2)]
    for v in range(2):
        for j in range(8):
            P.op("dve", lambda e, v=v, j=j: e.tensor_scalar(out=dg[:], in0=identf[:], scalar1=C.modt[:, l, 2, j, v:v + 1], scalar2=None, op0=ALU.mult), [identfb, C.modtb], [dgb])
            P.op("pe", lambda e, j=j: e.matmul(pR[j // 4][0][:, (j % 4) * 128:(j % 4 + 1) * 128], lhsT=onef[:], rhs=dg[:], start=True, stop=True), [onefb, dgb], [pR[j // 4][1]])
        for hf in range(2):
            P.op("act", lambda e, v=v, hf=hf: e.activation(func=AF.Identity, out=gtr[v][0][:, hf * 512:(hf + 1) * 512], in_=pR[hf][0][:]), [pR[hf][1]], [gtr[v][1]])
    aT = [P.sb([128, 8, 128], BF16, f"aT{i}") for i in range(2)]
    xt = [P.sb([128, 1024], F32, f"xt{i}") for i in range(2)]
    xo = [P.sb([128, 1024], F32, f"xo{i}") for i in range(2)]
    t2, t2b = P.sb([128, 1024], F32, "t2")
    pO = [P.ps([128, 512], F32, f"pO{i}") for i in range(4)]
    for n, i in enumerate(tiles):
        v = 1 if i < 2 else 0
        a_, ab = aT[n % 2]
        x_, xb = xt[n % 2]
        o_, ob = xo[n % 2]
        P.dma("sp", [(a_[:], C.accT[i])], ab, True)
        P.dma("pool", [(x_[:], C.xsrc[l](i))], xb, True)
        for hf in range(2):
            po, pob = pO[(n % 2) * 2 + hf]
            for k in range(8):
                P.op("pe", lambda e, po=po, a_=a_, k=k, hf=hf: e.matmul(po[:], lhsT=a_[:, k, :], rhs=wo[:, k, hf * 512:(hf + 1) * 512], start=(k == 0), stop=(k == 7)), [ab, wob], [pob])
            P.op("dve", lambda e, po=po, hf=hf, v=v: e.tensor_tensor(out=t2[:, hf * 512:(hf + 1) * 512], in0=po[:], in1=gtr[v][0][:, hf * 512:(hf + 1) * 512], op=ALU.mult), [pob, gtr[v][1]], [t2b])
            P.op("dve", lambda e, hf=hf, x_=x_, o_=o_: e.tensor_tensor(out=o_[:, hf * 512:(hf + 1) * 512], in0=t2[:, hf * 512:(hf + 1) * 512], in1=x_[:, hf * 512:(hf + 1) * 512], op=ALU.add), [t2b, xb], [ob])
        dst = C.xres[i * 128:(i + 1) * 128, :] if l == 0 else C.y[(i - 2) * 128:(i - 1) * 128, :]
        P.dma("sp", [(dst, o_[:])], ob, False)
    P.end()

def build(layers=(0, 1), phases=("p0", "p1", "mla", "nat", "hg", "gla", "merge"), dbg=(), dbg_in=(), max_groups=None, stage=99):
    nc = bass.Bass("TRN2", target_bir_lowering=False)
    P = Prog(nc)
    C = Ctx()
    C.dbg = dbg
    C.max_groups = max_groups
    C.stage = stage

    def scr(name, shape, dt):
        kind = "ExternalOutput" if name in dbg else ("ExternalInput" if name in dbg_in else "Internal")
        return nc.dram_tensor(name, list(shape), dt, kind=kind).ap()
    C.x = dram_in(nc, "x", [NLAT, D])
    C.ctx = dram_in(nc, "ctx", [NCTX, D])
    C.cvec = dram_in(nc, "cvec", [128, 8, 2])
    C.ada_w = dram_in(nc, "ada_w", [2, D, 3 * D])
    C.ada_b_fm = dram_in(nc, "ada_b_fm", [128, 2, 24])
    C.norm_g_fm = dram_in(nc, "norm_g_fm", [128, 2, 8])
    C.w_in = dram_in(nc, "w_in", [2, D, IN_W])
    C.ident = dram_in(nc, "ident", [128, 128], BF16)
    C.identf = dram_in(nc, "identf", [128, 128])
    C.mla_w_uq = dram_in(nc, "mla_w_uq", [2, 256, 768])
    C.mla_w_ukv = dram_in(nc, "mla_w_ukv", [2, 128, 1024])
    C.mla_gc_rep = dram_in(nc, "mla_gc_rep", [2, 128, 384])
    C.mla_gq_rep = dram_in(nc, "mla_gq_rep", [2, 128, 96])
    C.mla_gk_rep = dram_in(nc, "mla_gk_rep", [2, 128, 96])
    C.rope_cs = dram_in(nc, "rope_cs", [T, 32])
    C.mla_qT = scr("mla_qT", [2, 8, 17, 96, 512], BF16)
    C.mla_kT = scr("mla_kT", [8, 96, T], BF16)
    C.mla_va = scr("mla_va", [8, 128, NT, 66], BF16)
    C.nat_gq_rep = dram_in(nc, "nat_gq_rep", [2, 128, 64])
    C.nat_gk_rep = dram_in(nc, "nat_gk_rep", [2, 128, 64])
    C.nat_bias = dram_in(nc, "nat_bias", [2, 3, 8, 8, 128, 512])
    C.nat_qT = scr("nat_qT", [4, 17, 128, 512], BF16)
    C.nat_kT = scr("nat_kT", [4, 128, T], BF16)
    C.nat_va = scr("nat_va", [8, 128, NT, 66], BF16)
    C.scan_consts = dram_in(nc, "scan_consts", [128, 388])
    C.hg_go_rep = dram_in(nc, "hg_go_rep", [2, 128, 128])
    C.gla_go_rep = dram_in(nc, "gla_go_rep", [2, 128, 128])
    C.hg_lb_rep = dram_in(nc, "hg_lb_rep", [2, 128, 2, 512])
    C.gla_w2 = dram_in(nc, "gla_w2", [2, 2, 16, 256])
    C.gla_b2_rep = dram_in(nc, "gla_b2_rep", [2, 128, 2, 256])
    C.w_br = dram_in(nc, "w_br", [2, 4, 512, D])
    C.w_merge = dram_in(nc, "w_merge", [2, 4, D, D])
    C.b_merge = dram_in(nc, "b_merge", [2, 4, D])
    C.w_out = dram_in(nc, "w_out", [2, D, D])
    C.accT = scr("accT", [NT, 128, 8, 128], BF16)
    C.osc = scr("osc", [T, 512], F32)
    C.ybr = [scr(f"ybr{i}", [T, 512], F32) for i in range(4)]
    C.y = nc.dram_tensor("y", [NLAT, D], F32, kind="ExternalOutput").ap()
    C.xres = scr("xres", [T, D], F32)
    C.hT = scr("hT", [NT, 128, 8, 128], BF16)
    C.proj = scr("proj", [T, IN_W], F32)
    C.modt, C.modtb = P.sb([128, 2, 3, 8, 2], F32, "modt", glob=True)
    C.A, C.Ab = P.sb([128, 2, 8, 2], F32, "Amod", glob=True)

    def src0(i):
        return C.ctx[i * 128:(i + 1) * 128, :] if i < 2 else C.x[(i - 2) * 128:(i - 1) * 128, :]

    def src1(i):
        return C.xres[i * 128:(i + 1) * 128, :]
    C.xsrc = [src0, src1]
    if "p0" in phases:
        phase0_adaln(P, C)
    for l in layers:
        if "p1" in phases:
            phase1_inproj(P, C, l)
        if "mla" in phases or "mlaprep" in phases:
            mla_prep(P, C, l)
        if "mla" in phases or "mlamain" in phases:
            mla_main(P, C, l)
        if "nat" in phases:
            nat_prep(P, C, l)
            nat_main(P, C, l)
        if "hg" in phases:
            scan_mixer(P, C, l, "hg")
        if "gla" in phases:
            scan_mixer(P, C, l, "gla")
        if "merge" in phases:
            merge_phase(P, C, l)
    P.finish()
    return nc


def rope_table():
    quarter = 8
    inv_freq = (10000.0 ** (-np.arange(quarter, dtype=np.float32) / quarter)).astype(np.float32)
    t = np.arange(NLAT)
    row = (t // 64).astype(np.float32)
    col = (t % 64).astype(np.float32)
    ang = np.concatenate([row[:, None] * inv_freq, col[:, None] * inv_freq], axis=-1).astype(np.float32)
    cs = np.zeros((T, 32), np.float32)
    cs[:NCTX, 0:16] = 1.0
    cs[NCTX:, 0:16] = np.cos(ang)
    cs[NCTX:, 16:32] = np.sin(ang)
    return cs


def rep(v):
    return np.ascontiguousarray(np.broadcast_to(v[:, None, :], (v.shape[0], 128, v.shape[1]))).astype(np.float32)


def host_inputs(inp, b):
    import ml_dtypes
    d = {}
    d["x"] = np.ascontiguousarray(inp["x"][b])
    d["ctx"] = np.ascontiguousarray(inp["ctx"][b])
    cv = np.stack([inp["c"][b], inp["c_ctx"]], -1)
    d["cvec"] = np.ascontiguousarray(cv.reshape(8, 128, 2).transpose(1, 0, 2))
    d["ada_w"] = inp["ada_w"]
    d["ada_b_fm"] = np.ascontiguousarray(inp["ada_b"].reshape(2, 24, 128).transpose(2, 0, 1))
    d["norm_g_fm"] = np.ascontiguousarray(inp["norm_g"].reshape(2, 8, 128).transpose(2, 0, 1))
    d["w_in"] = inp["w_in"]
    d["ident"] = np.eye(128).astype(ml_dtypes.bfloat16)
    d["identf"] = np.eye(128).astype(np.float32)
    d["mla_w_uq"] = inp["mla_w_uq"]
    d["mla_w_ukv"] = inp["mla_w_ukv"]
    d["mla_gc_rep"] = rep(np.concatenate([inp["mla_g_cq"], inp["mla_g_ckv"]], -1))
    d["mla_gq_rep"] = rep(inp["mla_g_q"])
    d["mla_gk_rep"] = rep(inp["mla_g_k"])
    d["rope_cs"] = rope_table()
    d["nat_gq_rep"] = rep(inp["nat_g_q"])
    d["nat_gk_rep"] = rep(inp["nat_g_k"])
    d["nat_bias"] = nat_bias_host(inp["nat_rpb"])
    d["scan_consts"] = scan_consts_host()
    d["hg_go_rep"] = rep(inp["hg_g_o"])
    d["gla_go_rep"] = rep(inp["gla_g_o"])
    d["hg_lb_rep"] = np.ascontiguousarray(np.broadcast_to(inp["hg_lb_logits"][:, None], (2, 128, 2, 512))).astype(np.float32)
    d["gla_w2"] = inp["gla_w2"]
    d["w_br"] = inp["w_br"]
    d["w_merge"] = inp["w_merge"]
    d["b_merge"] = inp["b_merge"]
    d["w_out"] = inp["w_out"]
    d["gla_b2_rep"] = np.ascontiguousarray(np.broadcast_to(inp["gla_b2"][:, None], (2, 128, 2, 256))).astype(np.float32)
    return d


def kernel(**inputs):
    inp = {k: np.asarray(v) for k, v in inputs.items()}
    nc = build(layers=(0, 1))
    in_maps = [host_inputs(inp, b) for b in range(4)]
    res = run_bass_kernel_spmd(nc, in_maps, core_ids=list(range(4)))
    return np.stack([r["y"] for r in res.results], 0).astype(np.float32)
```

```python
import numpy as np
from contextlib import ExitStack
import concourse.bass as bass
import concourse.mybir as mybir
from concourse.bass_utils import run_bass_kernel_spmd

F32 = mybir.dt.float32
BF16 = mybir.dt.bfloat16
AF = mybir.ActivationFunctionType
ALU = mybir.AluOpType
AX = mybir.AxisListType

ENGS = ("pe", "act", "dve", "pool", "sp")
NDMASEM = 56


class Buf:
    __slots__ = ("name", "w", "r", "sem", "excl")

    def __init__(self, name):
        self.name = name
        self.excl = False
        self.w = None
        self.r = {}
        self.sem = None


class Op:
    __slots__ = ("eng", "fn", "waits", "signal", "token", "ndma")

    def __init__(self, eng, fn):
        self.eng = eng
        self.fn = fn
        self.waits = {}
        self.signal = False
        self.token = None
        self.ndma = 0


class Prog:
    def __init__(self, nc):
        self.nc = nc
        self.ges = ExitStack()
        self.es = None
        self.cnt = {}
        self.sigbase = {e: 0 for e in ENGS}
        self.ntile = 0
        self.sems = {}
        for e in ENGS:
            self.sems[e] = self.ges.enter_context(nc.semaphore("s_" + e))
        for i in range(NDMASEM):
            self.sems[f"d{i}"] = self.ges.enter_context(nc.semaphore(f"s_d{i}"))
        self.gbufs = []
        self.nphase = 0
        self._reset_phase()

    def _reset_phase(self):
        self.ops = {e: [] for e in ENGS}
        self.tok_op = {}
        self.pbufs = []
        self.ndsem = 0
        self.ndsem_sw = 0

    def _stack(self, glob):
        return self.ges if glob else self.es

    def sb(self, shape, dt, name="t", glob=False):
        self.ntile += 1
        t = self._stack(glob).enter_context(self.nc.sbuf_tensor(f"{name}_{self.ntile}", list(shape), dt))
        b = Buf(name)
        (self.gbufs if glob else self.pbufs).append(b)
        return t, b

    def ps(self, shape, dt, name="p"):
        self.ntile += 1
        t = self.es.enter_context(self.nc.psum_tensor(f"{name}_{self.ntile}", list(shape), dt))
        b = Buf(name)
        b.excl = True
        self.pbufs.append(b)
        return t, b

    def begin(self):
        self.es = ExitStack()
        self._reset_phase()

    def _dep(self, op, tok):
        if tok is None:
            return
        key, c = tok
        if key == "pe" and op.eng == "pe":
            return
        if op.waits.get(key, 0) < c:
            op.waits[key] = c
        if key in ENGS:
            self.tok_op[tok].signal = True

    def _track(self, o, reads, writes):
        ex = [b for b in reads if b.excl and o.eng != "pe"]
        if ex:
            reads = [b for b in reads if not (b.excl and o.eng != "pe")]
            writes = list(writes) + ex
        for b in reads:
            self._dep(o, b.w)
        for b in writes:
            self._dep(o, b.w)
            for t in b.r.items():
                self._dep(o, t)
        for b in reads:
            if b.r.get(o.token[0], 0) < o.token[1]:
                b.r[o.token[0]] = o.token[1]
        for b in writes:
            b.w = o.token
            b.r = {}

    def op(self, eng, fn, reads=(), writes=()):
        o = Op(eng, fn)
        c = self.cnt.get(eng, 0) + 1
        self.cnt[eng] = c
        o.token = (eng, c)
        self.tok_op[o.token] = o
        self._track(o, reads, writes)
        self.ops[eng].append(o)
        return o

    def dma(self, q, pairs, sbuf, load):
        sw = (q == "pool")
        if sbuf.sem is None or sbuf.sem[1] != self.nphase:
            if sw:
                assert self.ndsem_sw < NDMASEM // 2, "out of sw dma semaphores"
                sbuf.sem = (f"d{NDMASEM // 2 + self.ndsem_sw}", self.nphase, sw)
                self.ndsem_sw += 1
            else:
                assert self.ndsem < NDMASEM // 2, "out of hw dma semaphores"
                sbuf.sem = (f"d{self.ndsem}", self.nphase, sw)
                self.ndsem += 1
        assert sbuf.sem[2] == sw, "buffer used from both DMA queue kinds: " + sbuf.name
        key = sbuf.sem[0]

        def fn(e, pairs=pairs):
            return [e.dma_start(out=o_, in_=i_) for (o_, i_) in pairs]
        o = Op(q, fn)
        o.ndma = len(pairs)
        c = self.cnt.get(key, 0) + 16 * len(pairs)
        self.cnt[key] = c
        o.token = (key, c)
        o.signal = True
        if load:
            self._track(o, [], [sbuf])
        else:
            self._track(o, [sbuf], [])
        self.ops[q].append(o)
        return o

    def end(self):
        nc = self.nc
        for e in ENGS:
            for o in reversed(self.ops[e]):
                if o.ndma == 0 and o.fn is not None:
                    o.signal = True
                    break
        snap = dict(self.cnt)
        for e in ENGS:
            o = Op(e, None)
            for key, c in snap.items():
                if key == e and e == "pe":
                    continue
                if c > 0:
                    o.waits[key] = c
            self.ops[e].append(o)
        sig_index = {}
        last_sig = dict(self.sigbase)
        for e in ENGS:
            k = self.sigbase[e]
            for o in self.ops[e]:
                if o.fn is None or o.ndma:
                    continue
                if o.signal:
                    k += 1
                sig_index[o.token] = k
            last_sig[e] = k
        prog = self
        sems = self.sems
        base = dict(self.sigbase)

        def section(e):
            def body(eng):
                waited = {}
                for o in prog.ops[e]:
                    for key, c in o.waits.items():
                        if key in ENGS:
                            v = sig_index.get((key, c))
                            if v is None:
                                continue
                            if v <= base[key]:
                                continue
                        else:
                            v = c
                        if waited.get(key, 0) < v:
                            eng.wait_ge(sems[key], v)
                            waited[key] = v
                    if o.fn is None:
                        continue
                    r = o.fn(eng)
                    if o.ndma:
                        for ins in r:
                            ins.then_inc(sems[o.token[0]], 16)
                    elif o.signal:
                        r.then_inc(sems[e], 1)
            return body

        with nc.Block() as block:
            bl = {"pe": block.tensor, "act": block.scalar, "dve": block.vector,
                  "pool": block.gpsimd, "sp": block.sync}
            for e in ENGS:
                bl[e](section(e))
        self.sigbase = last_sig
        for b in self.gbufs + self.pbufs:
            b.w = None
            b.r = {}
        self.es.close()
        self.es = None
        self.nphase += 1

    def finish(self):
        self.ges.close()


D = 1024
NCTX = 256
NLAT = 8192
T = NCTX + NLAT
NT = T // 128
IN_W = 7104
EPS = 1e-6
SEG = {"mla": (0, 416), "nat": (416, 1536), "hg": (1952, 2048), "gla": (4000, 1056), "z": (5056, 2048)}


def dram_in(nc, name, shape, dt=F32):
    return nc.dram_tensor(name, list(shape), dt, kind="ExternalInput").ap()


def dram_scr(nc, name, shape, dt):
    return nc.dram_tensor(name, list(shape), dt, kind="Internal").ap()


class Ctx:
    pass


def drive(gens):
    gens = list(gens)
    while gens:
        for g_ in list(gens):
            try:
                next(g_)
            except StopIteration:
                gens.remove(g_)


def phase0_adaln(P, C):
    P.begin()
    cv, cvb = P.sb([128, 8, 2], F32, "cv")
    sl, slb = P.sb([128, 8, 2], F32, "sl")
    ab, abb = P.sb([128, 2, 24], F32, "ab")
    ng, ngb = P.sb([128, 2, 8], F32, "ng")
    P.dma("sp", [(cv[:], C.cvec)], cvb, True)
    P.dma("sp", [(ab[:], C.ada_b_fm)], abb, True)
    P.dma("sp", [(ng[:], C.norm_g_fm)], ngb, True)
    P.op("act", lambda e: e.activation(out=sl[:], in_=cv[:], func=AF.Silu), [cvb], [slb])
    wbuf = [P.sb([128, 8, 1024], F32, f"adaw{i}") for i in range(2)]
    pp = [P.ps([128, 512], F32, f"pm{i}") for i in range(2)]
    it = 0
    for l in range(2):
        for part in range(3):
            w, wb = wbuf[it % 2]
            src = C.ada_w[l, :, part * 1024:(part + 1) * 1024].rearrange("(k p) n -> p k n", p=128)
            P.dma("sp" if it % 2 == 0 else "pool", [(w[:, 0:4, :], src[:, 0:4, :]), (w[:, 4:8, :], src[:, 4:8, :])], wb, True)
            for j in range(8):
                ps_, psb = pp[j % 2]
                for k in range(8):
                    P.op("pe", lambda e, ps_=ps_, w=w, k=k, j=j: e.matmul(ps_[:, 0:2], lhsT=w[:, k, j * 128:(j + 1) * 128], rhs=sl[:, k, :],
                                                                       start=(k == 0), stop=(k == 7)), [wb, slb], [psb])
                P.op("act", lambda e, ps_=ps_, l=l, part=part, j=j: e.activation(out=C.modt[:, l, part, j, :], in_=ps_[:, 0:2], func=AF.Identity,
                                                                             bias=ab[:, l, part * 8 + j:part * 8 + j + 1], scale=1.0),
                     [psb, abb], [C.modtb])
            it += 1
    for l in range(2):
        for v in range(2):
            P.op("dve", lambda e, l=l, v=v: e.scalar_tensor_tensor(out=C.A[:, l, :, v], in0=C.modt[:, l, 1, :, v], scalar=1.0, in1=ng[:, l, :],
                                                                  op0=ALU.add, op1=ALU.mult), [C.modtb, ngb], [C.Ab])
    P.end()


def phase1_inproj(P, C, l):
    src_rows = C.xsrc[l]
    halves = [(0, 3552), (3552, 3552)]
    for hi, (c0, cw) in enumerate(halves):
        P.begin()
        ident, identb = P.sb([128, 128], BF16, "ident")
        P.dma("sp", [(ident[:], C.ident)], identb, True)
        wsb, wsbb = P.sb([128, 8, cw], BF16, "win")
        stg = [P.sb([128, 1776], F32, f"wstg{i}") for i in range(2)]
        n = 0
        for k in range(8):
            for q in range(cw // 1776):
                s_, sb_ = stg[n % 2]
                P.dma("sp" if n % 2 == 0 else "act", [(s_[:], C.w_in[l, k * 128:(k + 1) * 128, c0 + q * 1776:c0 + (q + 1) * 1776])], sb_, True)
                if n % 2 == 0:
                    P.op("act", lambda e, s_=s_, k=k, q=q: e.activation(func=AF.Identity, out=wsb[:, k, q * 1776:(q + 1) * 1776], in_=s_[:]), [sb_], [wsbb])
                else:
                    P.op("dve", lambda e, s_=s_, k=k, q=q: e.tensor_copy(out=wsb[:, k, q * 1776:(q + 1) * 1776], in_=s_[:]), [sb_], [wsbb])
                n += 1
        xt = [P.sb([128, 1024], F32, f"xt{i}") for i in range(2)]
        sq, sqb = P.sb([128, 1024], F32, "sq")
        st = [P.sb([128, 4], F32, f"st{i}") for i in range(2)]
        xn = [P.sb([128, 1024], BF16, f"xn{i}") for i in range(2)]
        hT = [P.sb([128, 8, 128], BF16, f"hT{i}") for i in range(3)]
        og = [P.sb([128, cw], F32, f"og{i}") for i in range(2)]
        pT = [P.ps([128, 8, 128], BF16, f"pT{i}") for i in range(1)]
        pY = [P.ps([128, 512], F32, f"pY{i}") for i in range(7)]
        ngrp = (cw + 511) // 512
        evc = [0]

        def front(i):
            v = 1 if i < 2 else 0
            h_, hb = hT[i % 3]
            if hi == 0:
                x_, xb = xt[i % 2]
                s_, sb_ = st[i % 2]
                n_, nb = xn[i % 2]
                p_, pb = pT[0]
                yield
                P.dma("sp", [(x_[:], src_rows(i))], xb, True)
                yield
                P.op("act", lambda e: e.activation(out=sq[:], in_=x_[:], func=AF.Square, accum_out=s_[:, 0:1]), [xb], [sqb, sb_])
                yield
                P.op("dve", lambda e: e.tensor_scalar(out=s_[:, 1:2], in0=s_[:, 0:1], scalar1=1.0 / D, scalar2=EPS, op0=ALU.mult, op1=ALU.add), [sb_], [sb_])
                yield
                P.op("act", lambda e: e.activation(out=s_[:, 2:3], in_=s_[:, 1:2], func=AF.Ln), [sb_], [sb_])
                yield
                P.op("act", lambda e: e.activation(out=s_[:, 3:4], in_=s_[:, 2:3], func=AF.Exp, scale=-0.5), [sb_], [sb_])
                yield
                P.op("dve", lambda e: e.tensor_scalar(out=n_[:], in0=x_[:], scalar1=s_[:, 3:4], scalar2=None, op0=ALU.mult), [xb, sb_], [nb])
                for k in range(8):
                    yield
                    P.op("pe", lambda e, k=k: e.transpose(out=p_[:, k, :], in_=n_[:, k * 128:(k + 1) * 128], identity=ident[:]), [nb, identb], [pb])
                for k in range(8):
                    yield
                    P.op("act", lambda e, k=k: e.activation(out=h_[:, k, :], in_=p_[:, k, :], func=AF.Identity,
                                                            scale=C.A[:, l, k, v:v + 1], bias=C.modt[:, l, 0, k, v:v + 1]), [pb, C.Ab, C.modtb], [hb])
                yield
                P.dma("act", [(C.hT[i], h_[:])], hb, False)
            else:
                yield
                P.dma("sp", [(h_[:], C.hT[i])], hb, True)

        def back(i):
            h_, hb = hT[i % 3]
            o_, ob = og[i % 2]
            for (g0, g1) in ((0, 4), (4, ngrp)):
                for k in range(8):
                    for g in range(g0, g1):
                        gw = min(512, cw - g * 512)
                        y_, yb = pY[g]
                        yield
                        P.op("pe", lambda e, k=k, g=g, gw=gw, y_=y_: e.matmul(y_[:, 0:gw], lhsT=h_[:, k, :], rhs=wsb[:, k, g * 512:g * 512 + gw],
                                                                          start=(k == 0), stop=(k == 7)), [hb, wsbb], [yb])
                for g in range(g0, g1):
                    gw = min(512, cw - g * 512)
                    y_, yb = pY[g]
                    if evc[0] % 2 == 0:
                        yield
                        P.op("act", lambda e, g=g, gw=gw, y_=y_: e.activation(func=AF.Identity, out=o_[:, g * 512:g * 512 + gw], in_=y_[:, 0:gw]), [yb], [ob])
                    else:
                        yield
                        P.op("dve", lambda e, g=g, gw=gw, y_=y_: e.tensor_copy(out=o_[:, g * 512:g * 512 + gw], in_=y_[:, 0:gw]), [yb], [ob])
                    evc[0] += 1
            yield
            P.dma("sp" if i % 2 == 0 else "act", [(C.proj[i * 128:(i + 1) * 128, c0:c0 + cw], o_[:])], ob, False)

        drive([front(0)])
        for i in range(NT):
            gs = [back(i)]
            if i + 1 < NT:
                gs.append(front(i + 1))
            drive(gs)
        P.end()


def groups():
    g = [(0, 2, 0)]
    for j in range(16):
        g.append((2 + 4 * j, 4, 1 + j))
    return g


def rstd_from_ms(P, st, stb, n, eps_done=False):
    P.op("dve", lambda e: e.tensor_scalar(out=st[:, n:2 * n], in0=st[:, 0:n], scalar1=1.0, scalar2=EPS, op0=ALU.mult, op1=ALU.add), [stb], [stb])
    P.op("act", lambda e: e.activation(out=st[:, n:2 * n], in_=st[:, n:2 * n], func=AF.Ln), [stb], [stb])
    P.op("act", lambda e: e.activation(out=st[:, 2 * n:3 * n], in_=st[:, n:2 * n], func=AF.Exp, scale=-0.5), [stb], [stb])


def head_norm(P, src, srcb, H, dh, gain, gainb, out, outb, sq, sqb, st, stb, tmp, tmpb, eng2="dve"):
    W = H * dh
    P.op("act", lambda e: e.activation(out=sq[:, 0:W], in_=src, func=AF.Square, scale=float(dh) ** -0.5), [srcb], [sqb])
    P.op("dve", lambda e: e.tensor_reduce(out=st[:, 0:H], in_=sq[:, 0:W].rearrange("p (h d) -> p h d", h=H), op=ALU.add, axis=AX.X), [sqb], [stb])
    rstd_from_ms(P, st, stb, H)
    P.op("dve", lambda e: e.tensor_tensor(out=tmp[:, 0:W].rearrange("p (h d) -> p h d", h=H), in0=src.rearrange("p (h d) -> p h d", h=H),
                                          in1=st[:, 2 * H:3 * H].unsqueeze(2).to_broadcast([128, H, dh]), op=ALU.mult), [srcb, stb], [tmpb])
    P.op(eng2, lambda e: e.tensor_tensor(out=out, in0=tmp[:, 0:W].rearrange("p (h d) -> p h d", h=H),
                                         in1=gain[:, None, :].to_broadcast([128, H, dh]), op=ALU.mult), [tmpb, gainb], [outb])


def rope(P, x, xb, cs, csb, out, outb, tt, ttb, H):
    x1, x2 = x[:, :, 0:16], x[:, :, 16:32]
    cos = cs[:, None, 0:16].to_broadcast([128, H, 16])
    sin = cs[:, None, 16:32].to_broadcast([128, H, 16])
    P.op("dve", lambda e: e.tensor_tensor(out=tt[:, 0], in0=x1, in1=cos, op=ALU.mult), [xb, csb], [ttb])
    P.op("dve", lambda e: e.tensor_tensor(out=tt[:, 1], in0=x2, in1=sin, op=ALU.mult), [xb, csb], [ttb])
    P.op("dve", lambda e: e.tensor_tensor(out=tt[:, 2], in0=x1, in1=sin, op=ALU.mult), [xb, csb], [ttb])
    P.op("dve", lambda e: e.tensor_tensor(out=tt[:, 3], in0=x2, in1=cos, op=ALU.mult), [xb, csb], [ttb])
    P.op("dve", lambda e: e.tensor_tensor(out=out[:, :, 0:16], in0=tt[:, 0], in1=tt[:, 1], op=ALU.subtract), [ttb], [outb])
    P.op("dve", lambda e: e.tensor_tensor(out=out[:, :, 16:32], in0=tt[:, 2], in1=tt[:, 3], op=ALU.add), [ttb], [outb])


def load_cast(P, dst, dstb, src, shape, q="sp", eng="pool", name="wst"):
    s_, sb_ = P.sb(shape, F32, name)
    P.dma(q, [(s_[:], src)], sb_, True)
    P.op(eng, lambda e: e.tensor_copy(out=dst, in_=s_[:]), [sb_], [dstb])


def mla_prep(P, C, l):
    P.begin()
    ident, identb = P.sb([128, 128], BF16, "ident")
    P.dma("sp", [(ident[:], C.ident)], identb, True)
    wuq, wuqb = P.sb([128, 2, 768], BF16, "wuq")
    wukv, wukvb = P.sb([128, 1024], BF16, "wukv")
    load_cast(P, wuq[:], wuqb, C.mla_w_uq[l].rearrange("(k p) n -> p k n", p=128), [128, 2, 768], "sp", "dve", "wst1")
    load_cast(P, wukv[:], wukvb, C.mla_w_ukv[l], [128, 1024], "pool", "dve", "wst2")
    gc, gcb = P.sb([128, 384], F32, "gc")
    gq, gqb = P.sb([128, 96], F32, "gq")
    gk, gkb = P.sb([128, 96], F32, "gk")
    P.dma("sp", [(gc[:], C.mla_gc_rep[l])], gcb, True)
    P.dma("sp", [(gq[:], C.mla_gq_rep[l])], gqb, True)
    P.dma("sp", [(gk[:], C.mla_gk_rep[l])], gkb, True)
    seg = [P.sb([128, 416], F32, f"seg{i}") for i in range(2)]
    cs = [P.sb([128, 32], F32, f"cs{i}") for i in range(2)]
    sq, sqb = P.sb([128, 768], F32, "sq")
    st0, st0b = P.sb([128, 8], F32, "st0")
    stq, stqb = P.sb([128, 24], F32, "stq")
    stk, stkb = P.sb([128, 24], F32, "stk")
    cn, cnb = P.sb([128, 384], BF16, "cn")
    cT, cTb = P.sb([128, 3, 128], BF16, "cT")
    qf, qfb = P.sb([128, 768], F32, "qf")
    qn, qnb = P.sb([128, 8, 96], F32, "qn")
    kf, kfb = P.sb([128, 768], F32, "kf")
    kn, knb = P.sb([128, 8, 96], F32, "kn")
    tmp, tmpb = P.sb([128, 768], F32, "tmp")
    tmp2, tmp2b = P.sb([128, 768], F32, "tmp2")
    qb_, qbb = P.sb([128, 2, 8, 96], BF16, "qb")
    kb_, kbb = P.sb([128, 8, 96], BF16, "kb")
    tt, ttb = P.sb([128, 4, 8, 16], F32, "tt")
    tt2, tt2b = P.sb([128, 4, 8, 16], F32, "tt2")
    qst = [P.sb([96, 2, 8, 512], BF16, f"qst{i}") for i in range(2)]
    kst = [P.sb([96, 8, 512], BF16, f"kst{i}") for i in range(2)]
    vst = [P.sb([128, 8, 4, 66], BF16, f"vst{i}") for i in range(2)]
    for i in range(2):
        P.op("dve", lambda e, i=i: e.memset(vst[i][0][:], 1.0), [], [vst[i][1]])
    pTc, pTcb = P.ps([128, 8, 128], BF16, "pTc")
    pq = [P.ps([128, 512], F32, f"pq{i}") for i in range(2)]
    pk = [P.ps([128, 512], F32, f"pk{i}") for i in range(2)]
    pTq, pTqb = P.ps([128, 16, 128], BF16, "pTq")
    pTk, pTkb = P.ps([128, 8, 128], BF16, "pTk")
    for gi, (t0, nt, qt) in enumerate(groups()[:C.max_groups]):
        qs, qsb = qst[gi % 2]
        ks, ksb = kst[gi % 2]
        vs, vsb = vst[gi % 2]
        for s in range(nt):
            i = t0 + s
            sg, sgb = seg[i % 2]
            c_, c_b = cs[i % 2]
            P.dma("sp", [(sg[:], C.proj[i * 128:(i + 1) * 128, 0:416])], sgb, True)
            P.dma("sp", [(c_[:], C.rope_cs[i * 128:(i + 1) * 128, :])], c_b, True)
            P.op("act", lambda e, sg=sg: e.activation(out=sq[:, 0:256], in_=sg[:, 0:256], func=AF.Square, scale=1.0 / 16, accum_out=st0[:, 0:1]), [sgb], [sqb, st0b])
            P.op("act", lambda e, sg=sg: e.activation(out=sq[:, 256:384], in_=sg[:, 256:384], func=AF.Square, scale=128 ** -0.5, accum_out=st0[:, 1:2]), [sgb], [sqb, st0b])
            rstd_from_ms(P, st0, st0b, 2)
            P.op("dve", lambda e, sg=sg: e.scalar_tensor_tensor(out=cn[:, 0:256], in0=sg[:, 0:256], scalar=st0[:, 4:5], in1=gc[:, 0:256], op0=ALU.mult, op1=ALU.mult), [sgb, st0b, gcb], [cnb])
            P.op("dve", lambda e, sg=sg: e.scalar_tensor_tensor(out=cn[:, 256:384], in0=sg[:, 256:384], scalar=st0[:, 5:6], in1=gc[:, 256:384], op0=ALU.mult, op1=ALU.mult), [sgb, st0b, gcb], [cnb])
            if C.stage <= 1:
                continue
            for k in range(3):
                P.op("pe", lambda e, k=k: e.transpose(out=pTc[:, k, :], in_=cn[:, k * 128:(k + 1) * 128], identity=ident[:]), [cnb, identb], [pTcb])
            P.op("act", lambda e: e.activation(func=AF.Identity, out=cT[:], in_=pTc[:, 0:3, :]), [pTcb], [cTb])
            for k in range(2):
                P.op("pe", lambda e, k=k: e.matmul(pq[0][0][:], lhsT=cT[:, k, :], rhs=wuq[:, k, 0:512], start=(k == 0), stop=(k == 1)), [cTb, wuqb], [pq[0][1]])
            for k in range(2):
                P.op("pe", lambda e, k=k: e.matmul(pq[1][0][:, 0:256], lhsT=cT[:, k, :], rhs=wuq[:, k, 512:768], start=(k == 0), stop=(k == 1)), [cTb, wuqb], [pq[1][1]])
            for j in range(2):
                P.op("pe", lambda e, j=j: e.matmul(pk[j][0][:], lhsT=cT[:, 2, :], rhs=wukv[:, j * 512:(j + 1) * 512], start=True, stop=True), [cTb, wukvb], [pk[j][1]])
            P.op("act", lambda e: e.activation(func=AF.Identity, out=qf[:, 0:512], in_=pq[0][0][:]), [pq[0][1]], [qfb])
            P.op("act", lambda e: e.activation(func=AF.Identity, out=qf[:, 512:768], in_=pq[1][0][:, 0:256]), [pq[1][1]], [qfb])
            if C.stage <= 2.1:
                continue
            for j in range(2):
                if C.stage > 2.2:
                    P.op("dve", lambda e, j=j: e.tensor_copy(out=kf[:].rearrange("p (h d) -> p h d", h=8)[:, j * 4:(j + 1) * 4, 0:64],
                                                         in_=pk[j][0][:].rearrange("p (h d) -> p h d", h=4)[:, :, 0:64]), [pk[j][1]], [kfb])
                if C.stage > 2.4:
                    P.op("act", lambda e, j=j, vs=vs, s=s: e.activation(func=AF.Identity, out=vs[:, j * 4:(j + 1) * 4, s, 0:64],
                                                             in_=pk[j][0][:].rearrange("p (h d) -> p h d", h=4)[:, :, 64:128]), [pk[j][1]], [vsb])
            if C.stage > 2.6:
                P.op("dve", lambda e, sg=sg: e.tensor_copy(out=kf[:].rearrange("p (h d) -> p h d", h=8)[:, :, 64:96],
                                                        in_=sg[:, None, 384:416].to_broadcast([128, 8, 32])), [sgb], [kfb])
            if C.stage <= 3:
                continue
            head_norm(P, qf[:], qfb, 8, 96, gq, gqb, qn[:], qnb, sq, sqb, stq, stqb, tmp, tmpb)
            head_norm(P, kf[:], kfb, 8, 96, gk, gkb, kn[:], knb, sq, sqb, stk, stkb, tmp2, tmp2b)
            if C.stage <= 4:
                continue
            P.op("dve", lambda e: e.tensor_copy(out=qb_[:, 0], in_=qn[:]), [qnb], [qbb])
            P.op("dve", lambda e: e.tensor_copy(out=qb_[:, 1, :, 0:64], in_=qn[:, :, 0:64]), [qnb], [qbb])
            rope(P, qn[:, :, 64:96], qnb, c_, c_b, qb_[:, 1, :, 64:96], qbb, tt, ttb, 8)
            P.op("dve", lambda e: e.tensor_copy(out=kb_[:, :, 0:64], in_=kn[:, :, 0:64]), [knb], [kbb])
            rope(P, kn[:, :, 64:96], knb, c_, c_b, kb_[:, :, 64:96], kbb, tt2, tt2b, 8)
            if C.stage <= 5:
                continue
            for ver in range(2):
                for h in range(8):
                    P.op("pe", lambda e, ver=ver, h=h: e.transpose(out=pTq[0:96, ver * 8 + h, :], in_=qb_[:, ver, h, :], identity=ident[:]), [qbb, identb], [pTqb])
            for ver in range(2):
                P.op("act" if ver == 0 else "dve",
                     (lambda e, ver=ver, qs=qs, s=s: e.activation(func=AF.Identity, out=qs[:, ver, :, s * 128:(s + 1) * 128], in_=pTq[0:96, ver * 8:(ver + 1) * 8, :])) if ver == 0 else
                     (lambda e, ver=ver, qs=qs, s=s: e.tensor_copy(out=qs[:, ver, :, s * 128:(s + 1) * 128], in_=pTq[0:96, ver * 8:(ver + 1) * 8, :])),
                     [pTqb], [qsb])
            for h in range(8):
                P.op("pe", lambda e, h=h: e.transpose(out=pTk[0:96, h, :], in_=kb_[:, h, :], identity=ident[:]), [kbb, identb], [pTkb])
            P.op("dve", lambda e, ks=ks, s=s: e.tensor_copy(out=ks[:, :, s * 128:(s + 1) * 128], in_=pTk[0:96, :, :]), [pTkb], [ksb])
        n = nt * 128
        if C.stage <= 6:
            continue
        P.dma("pool", [(C.mla_qT[:, :, qt, :, 0:n].rearrange("v h p n -> p v h n"), qs[:, :, :, 0:n])], qsb, False)
        P.dma("pool", [(C.mla_kT[:, :, t0 * 128:t0 * 128 + n].rearrange("h p n -> p h n"), ks[:, :, 0:n])], ksb, False)
        P.dma("sp", [(C.mla_va[:, :, t0:t0 + nt, :].rearrange("h p k e -> p h k e"), vs[:, :, 0:nt, :])], vsb, False)
    P.end()


def attend_main(P, C, l, name, H, dk, scale, kT_ap, q_ap, va, ybr, chunks_of, bias_src=None, with_ctx_q=True):
    P.begin()
    identf, identfb = P.sb([128, 128], F32, "identf")
    P.dma("sp", [(identf[:], C.identf)], identfb, True)
    kT = [P.sb([dk, T], BF16, f"kT{i}") for i in range(2)]
    vv = [P.sb([128, NT, 66], BF16, f"vv{i}") for i in range(2)]
    nver = 2 if name == "mla" else 1
    qq = [[P.sb([dk, 512], BF16, f"q{v}_{i}") for v in range(nver)] for i in range(2)]
    pt = [P.sb([128, 512], BF16, f"pt{i}") for i in range(3)]
    sbias = [P.sb([128, 512], F32, f"sb{i}") for i in range(2)]
    nb_tiles = 8
    bt = [[P.sb([128, 512], F32, f"bt{j}_{i}") for i in range(nb_tiles)] for j in range(3)] if bias_src else None
    oT, oTb = P.sb([65, 512], F32, "oT")
    rec, recb = P.sb([128, 4], F32, "rec")
    ost = [P.sb([128, 4, 64], F32, f"ost{i}") for i in range(2)]
    pS = [P.ps([128, 512], F32, f"pS{i}") for i in range(3)]
    pO = [P.ps([128, 512], F32, f"pO{i}") for i in range(2)]
    pXf, pXb = P.ps([128, 512], F32, "pX")
    pX = pXf[:, 0:260].rearrange("p (t e) -> p t e", e=65)
    qts = ([0] if with_ctx_q else []) + list(range(1, 17))
    LA = 2
    groups_ = [(h, qt) for h in range(H) for qt in qts]
    items = []
    for gi, (h, qt) in enumerate(groups_):
        ch = chunks_of(qt)
        for n_, (kc, ver, bid) in enumerate(ch):
            items.append((gi, n_, len(ch), kc, ver, bid))
    cur_bias = [None, None, None]
    st = {"bmap": {}, "pend": []}
    loaded_heads = set()
    loaded_groups = set()

    def load_head(h):
        if h >= H or h in loaded_heads:
            return
        loaded_heads.add(h)
        k_, kb = kT[h % 2]
        v_, vb = vv[h % 2]
        P.dma("sp", [(k_[:, 0:T // 2], kT_ap(h)[:, 0:T // 2]), (k_[:, T // 2:T], kT_ap(h)[:, T // 2:T])], kb, True)
        P.dma("pool", [(v_[:], va[h])], vb, True)

    def load_group(gi):
        if gi >= len(groups_) or gi in loaded_groups:
            return
        loaded_groups.add(gi)
        h, qt = groups_[gi]
        nq = 256 if qt == 0 else 512
        for v in range(nver):
            P.dma("sp", [(qq[gi % 2][v][0][:, 0:nq], q_ap(h, qt, v)[:, 0:nq])], qq[gi % 2][v][1], True)
        if bias_src:
            ids = [b for (_, _, b) in chunks_of(qt) if b is not None]
            bm = {}
            if ids:
                key = (h, tuple(ids))
                slot = None
                for j in range(3):
                    if cur_bias[j] == key:
                        slot = j
                if slot is None:
                    slot = ids[0][0]
                    cur_bias[slot] = key
                    for n_, b in enumerate(ids):
                        P.dma("pool" if n_ % 2 else "sp", [(bt[slot][n_][0][:], bias_src(h, b))], bt[slot][n_][1], True)
                for n_, b in enumerate(ids):
                    bm[b] = bt[slot][n_]
            st["bmap"][gi] = bm

    def stage_a(n):
        gi, n_, nch, kc, ver, bid = items[n]
        h, qt = groups_[gi]
        if n_ == 0:
            load_head(h)
            load_group(gi)
            load_group(gi + 1)
            if gi + 1 < len(groups_) and groups_[gi + 1][0] != h:
                load_head(h + 1)
        nq = 256 if qt == 0 else 512
        k_, kb = kT[h % 2]
        s_, sb_ = pS[n % 3]
        p_, pb = pt[n % 3]
        q_, qb = qq[gi % 2][ver]
        P.op("pe", lambda e: e.matmul(s_[:, 0:nq], lhsT=k_[:, kc * 128:(kc + 1) * 128], rhs=q_[:, 0:nq], start=True, stop=True), [kb, qb], [sb_])
        if bid is None:
            P.op("act", lambda e: e.activation(out=p_[:, 0:nq], in_=s_[:, 0:nq], func=AF.Exp, scale=scale), [sb_], [pb])
        else:
            b_, bb = st["bmap"][gi][bid]
            x_, xb = sbias[n % 2]
            P.op("dve", lambda e: e.scalar_tensor_tensor(out=x_[:], in0=s_[:], scalar=scale, in1=b_[:], op0=ALU.mult, op1=ALU.add), [sb_, bb], [xb])
            P.op("act", lambda e: e.activation(out=p_[:], in_=x_[:], func=AF.Exp), [xb], [pb])

    def epilogue(gi):
        h, qt = groups_[gi]
        nq = 256 if qt == 0 else 512
        tok0 = 0 if qt == 0 else 256 + (qt - 1) * 512
        o_, ob = pO[gi % 2]
        nt4 = nq // 128
        P.op("act", lambda e: e.activation(func=AF.Identity, out=oT[:, 0:nq], in_=o_[0:65, 0:nq]), [ob], [oTb])
        for t in range(nt4):
            P.op("pe", lambda e, t=t: e.transpose(out=pX[:, t, :], in_=oT[:, t * 128:(t + 1) * 128], identity=identf[0:65, 0:65]), [oTb, identfb], [pXb])
        P.op("dve", lambda e: e.reciprocal(out=rec[:, 0:nt4], in_=pX[:, 0:nt4, 64]), [pXb], [recb])
        os_, osb = ost[gi % 2]
        P.op("dve", lambda e: e.tensor_tensor(out=os_[:, 0:nt4, :], in0=pX[:, 0:nt4, 0:64], in1=rec[:, 0:nt4].unsqueeze(2).to_broadcast([128, nt4, 64]), op=ALU.mult), [pXb, recb], [osb])
        P.dma("pool", [(ybr[tok0:tok0 + nq, h * 64:(h + 1) * 64].rearrange("(t p) d -> p t d", p=128), os_[:, 0:nt4, :])], osb, False)

    def stage_b(n):
        gi, n_, nch, kc, ver, bid = items[n]
        h, qt = groups_[gi]
        nq = 256 if qt == 0 else 512
        v_, vb = vv[h % 2]
        p_, pb = pt[n % 3]
        o_, ob = pO[gi % 2]
        P.op("pe", lambda e: e.matmul(o_[0:65, 0:nq], lhsT=v_[:, kc, 0:65], rhs=p_[:, 0:nq], start=(n_ == 0), stop=(n_ == nch - 1)), [vb, pb], [ob])
        if n_ == nch - 1:
            st["pend"].append((n + 3, gi))

    N = len(items)
    for n in range(N + LA):
        if n < N:
            stage_a(n)
        if n - LA >= 0:
            stage_b(n - LA)
        while st["pend"] and st["pend"][0][0] <= n:
            epilogue(st["pend"].pop(0)[1])
    while st["pend"]:
        epilogue(st["pend"].pop(0)[1])
    P.end()


def mla_main(P, C, l):
    def chunks_of(qt):
        if qt == 0:
            return [(0, 0, None), (1, 0, None)]
        return [(0, 0, None), (1, 0, None)] + [(kc, 1, None) for kc in range(2, NT)]
    attend_main(P, C, l, "mla", 8, 96, 96 ** -0.5, lambda h: C.mla_kT[h], lambda h, qt, v: C.mla_qT[v, h, qt], C.mla_va, C.ybr[0], chunks_of,
                with_ctx_q=(l == 0))


def nat_prep(P, C, l):
    P.begin()
    ident, identb = P.sb([128, 128], BF16, "ident")
    P.dma("sp", [(ident[:], C.ident)], identb, True)
    gq, gqb = P.sb([128, 64], F32, "gq")
    gk, gkb = P.sb([128, 64], F32, "gk")
    P.dma("sp", [(gq[:], C.nat_gq_rep[l])], gqb, True)
    P.dma("sp", [(gk[:], C.nat_gk_rep[l])], gkb, True)
    seg = [P.sb([128, 1536], F32, f"seg{i}") for i in range(2)]
    sq, sqb = P.sb([128, 512], F32, "sq")
    stq, stqb = P.sb([128, 24], F32, "stq")
    stk, stkb = P.sb([128, 24], F32, "stk")
    tmp, tmpb = P.sb([128, 512], F32, "tmp")
    tmp2, tmp2b = P.sb([128, 512], F32, "tmp2")
    qk, qkb = P.sb([128, 2, 8, 64], BF16, "qk")
    qst = [P.sb([128, 4, 512], BF16, f"qst{i}") for i in range(2)]
    kst = [P.sb([128, 4, 512], BF16, f"kst{i}") for i in range(2)]
    vst = [P.sb([128, 8, 4, 66], BF16, f"vst{i}") for i in range(2)]
    for i in range(2):
        P.op("dve", lambda e, i=i: e.memset(vst[i][0][:], 1.0), [], [vst[i][1]])
    pT, pTb = P.ps([128, 8, 128], BF16, "pT")
    for gi, (t0, nt, qt) in enumerate(groups()[:C.max_groups]):
        qs, qsb = qst[gi % 2]
        ks, ksb = kst[gi % 2]
        vs, vsb = vst[gi % 2]
        for s in range(nt):
            i = t0 + s
            sg, sgb = seg[i % 2]
            P.dma("sp" if i % 2 == 0 else "pool", [(sg[:], C.proj[i * 128:(i + 1) * 128, 416:1952])], sgb, True)
            head_norm(P, sg[:, 0:512], sgb, 8, 64, gq, gqb, qk[:, 0], qkb, sq, sqb, stq, stqb, tmp, tmpb)
            head_norm(P, sg[:, 512:1024], sgb, 8, 64, gk, gkb, qk[:, 1], qkb, sq, sqb, stk, stkb, tmp2, tmp2b)
            P.op("dve", lambda e, sg=sg, vs=vs, s=s: e.tensor_copy(out=vs[:, :, s, 0:64], in_=sg[:, 1024:1536].rearrange("p (h d) -> p h d", h=8)), [sgb], [vsb])
            for w in range(2):
                for pr in range(4):
                    P.op("pe", lambda e, w=w, pr=pr: e.transpose(out=pT[:, w * 4 + pr, :], in_=qk[:, w, 2 * pr:2 * pr + 2, :].rearrange("p h d -> p (h d)"), identity=ident[:]), [qkb, identb], [pTb])
            P.op("act", lambda e, qs=qs, s=s: e.activation(func=AF.Identity, out=qs[:, :, s * 128:(s + 1) * 128], in_=pT[:, 0:4, :]), [pTb], [qsb])
            P.op("act", lambda e, ks=ks, s=s: e.activation(func=AF.Identity, out=ks[:, :, s * 128:(s + 1) * 128], in_=pT[:, 4:8, :]), [pTb], [ksb])
        n = nt * 128
        P.dma("pool", [(C.nat_qT[:, qt, :, 0:n].rearrange("r p n -> p r n"), qs[:, :, 0:n])], qsb, False)
        P.dma("pool", [(C.nat_kT[:, :, t0 * 128:t0 * 128 + n].rearrange("r p n -> p r n"), ks[:, :, 0:n])], ksb, False)
        P.dma("sp", [(C.nat_va[:, :, t0:t0 + nt, :].rearrange("h p k e -> p h k e"), vs[:, :, 0:nt, :])], vsb, False)
    P.end()


def nat_block(j):
    kb = min(max(8 * j - 4, 0), 112)
    pat = 0 if j == 0 else (2 if j == 15 else 1)
    cs_ = range(0, 6) if j == 0 else (range(2, 8) if j == 15 else range(0, 8))
    return kb, pat, list(cs_)


def nat_main(P, C, l):
    def chunks_of(qt):
        if qt == 0:
            return [(0, 0, None), (1, 0, None)]
        kb, pat, cs_ = nat_block(qt - 1)
        return [(0, 0, None), (1, 0, None)] + [(2 + kb // 2 + c, 0, (pat, c)) for c in cs_]
    attend_main(P, C, l, "nat", 8, 64, 64 ** -0.5,
                lambda h: C.nat_kT[h // 2, (h % 2) * 64:(h % 2 + 1) * 64, :],
                lambda h, qt, v: C.nat_qT[h // 2, qt, (h % 2) * 64:(h % 2 + 1) * 64, :],
                C.nat_va, C.ybr[1], chunks_of, bias_src=lambda h, b: C.nat_bias[l, b[0], h, b[1]], with_ctx_q=(l == 0))


def nat_bias_host(rpb):
    L = rpb.shape[0]
    out = np.full((L, 3, 8, 8, 128, 512), -30000.0, np.float32)
    ck = np.arange(64)[:, None]
    cq = np.arange(64)[None, :]
    c0 = np.clip(cq - 8, 0, 48)
    col_ok = (ck >= c0) & (ck < c0 + 16)
    dc = np.clip(ck - cq, -15, 15) + 15
    for pat, j in enumerate((0, 1, 15)):
        kb, _, cs_ = nat_block(j)
        for c in cs_:
            for a in range(2):
                kr = kb + 2 * c + a
                for r in range(8):
                    qr = 8 * j + r
                    r0 = min(max(qr - 4, 0), 120)
                    if not (r0 <= kr < r0 + 8):
                        continue
                    dr = kr - qr + 7
                    blk = np.where(col_ok[None, None], rpb[:, :, dr][:, :, dc], np.float32(-30000.0))
                    out[:, pat, :, c, a * 64:(a + 1) * 64, r * 64:(r + 1) * 64] = blk
    return out


def scan_consts_host():
    j = np.arange(128)[:, None]
    i = np.arange(128)[None, :]
    same = (j // 32) == (i // 32)
    mf = (same & (j <= i)).astype(np.float32)
    mb = (same & (j >= i)).astype(np.float32)
    blk = same.astype(np.float32)
    ind = (j // 32 == np.arange(4)[None, :]).astype(np.float32)
    return np.ascontiguousarray(np.concatenate([mf, mb, blk, ind], 1))


def scan_mixer(P, C, l, kind):
    hg = kind == "hg"
    H, dk, dv = (4, 128, 128) if hg else (4, 64, 128)
    W = H * dk
    c0, cw = SEG["hg"] if hg else SEG["gla"]
    qscale = float(dk) ** -0.5
    br = 2 if hg else 3
    for d in range(2):
        P.begin()
        ident, identb = P.sb([128, 128], BF16, "ident")
        P.dma("sp", [(ident[:], C.ident)], identb, True)
        identf, identfb = P.sb([128, 128], F32, "identf")
        P.dma("sp", [(identf[:], C.identf)], identfb, True)
        sc, scb = P.sb([128, 388], F32, "sconst")
        P.dma("sp", [(sc[:], C.scan_consts)], scb, True)
        Md = sc[:, 0:128] if d == 0 else sc[:, 128:256]
        Blk = sc[:, 256:384]
        Ind = sc[:, 384:388]
        go, gob = P.sb([128, 128], F32, "go")
        P.dma("sp", [(go[:], (C.hg_go_rep if hg else C.gla_go_rep)[l])], gob, True)
        if hg:
            lbt, lbb = P.sb([128, 512], F32, "lbt")
            oml, omlb = P.sb([128, 512], F32, "oml")
            if l == 0:
                P.op("dve", lambda e: e.memset(lbt[:], 0.0), [], [lbb])
                P.op("dve", lambda e: e.memset(oml[:], 1.0), [], [omlb])
            else:
                zz, zzb = P.sb([128, 2, 512], F32, "zz")
                P.dma("sp", [(zz[:], C.hg_lb_rep[:, :, d, :].rearrange("l p n -> p l n"))], zzb, True)
                P.op("dve", lambda e: e.tensor_tensor(out=lbt[:], in0=zz[:, 1, :], in1=zz[:, 0, :], op=ALU.subtract), [zzb], [lbb])
                P.op("act", lambda e: e.activation(out=lbt[:], in_=lbt[:], func=AF.Sigmoid), [lbb], [lbb])
                P.op("dve", lambda e: e.tensor_scalar(out=oml[:], in0=lbt[:], scalar1=-1.0, scalar2=1.0, op0=ALU.mult, op1=ALU.add), [lbb], [omlb])
        else:
            w2, w2b = P.sb([16, 256], F32, "w2")
            b2, b2b = P.sb([128, 256], F32, "b2")
            P.dma("sp", [(w2[:], C.gla_w2[l, d])], w2b, True)
            P.dma("sp", [(b2[:], C.gla_b2_rep[l, :, d, :])], b2b, True)
            rT, rTb = P.sb([16, 128], F32, "rT")
            gx, gxb = P.sb([128, 256], F32, "gx")
        seg = [P.sb([128, cw], F32, f"seg{i}") for i in range(2)]
        qf, qfb = P.sb([128, W], F32, "qf")
        kf, kfb = P.sb([128, W], F32, "kf")
        la, lab = P.sb([128, W], F32, "la")
        vb2 = [P.sb([128, H * dv], BF16, f"vb{i}") for i in range(3)]
        sg1, sg1b = P.sb([128, W], F32, "sg1")
        bs, bsb = P.sb([128, W], F32, "bs")
        eb, ebb = P.sb([128, W], F32, "eb")
        enb, enbb = P.sb([128, W], F32, "enb")
        ebe, ebeb = P.sb([128, W], F32, "ebe")
        dcy2 = [P.sb([128, H * 4], F32, f"dcy{i}") for i in range(2)]
        qt_, qtb = P.sb([128, W], BF16, "qt")
        kt_, ktb = P.sb([128, W], BF16, "kt")
        kh2 = [P.sb([128, W], BF16, f"kh{i}") for i in range(2)]
        kpad2 = [P.sb([128, H, 4, dk], BF16, f"kpad{i}") for i in range(2)]
        qT, qTb = P.sb([128, H, 128], BF16, "qT")
        kT, kTb = P.sb([128, H, 128], BF16, "kT")
        qpad2 = [P.sb([128, H, 608], BF16, f"qpad{i}") for i in range(3)]
        for i_ in range(3):
            P.op("dve", lambda e, i_=i_: e.memset(qpad2[i_][0][:], 0.0), [], [qpad2[i_][1]])
        AT2 = [P.sb([128, H, 128], BF16, f"AT{i}") for i in range(3)]
        S, Sb_ = P.sb([128, H, dv], F32, "S")
        P.op("dve", lambda e: e.memset(S[:], 0.0), [], [Sb_])
        Sbf = [P.sb([128, H, 4, dv], BF16, f"Sbf{i}") for i in range(3)]
        ofw = [P.sb([128, 512], F32, f"ofw{i}") for i in range(3)]
        ost = [P.sb([128, 512], F32, f"ost{i}") for i in range(2)]
        osum, osumb = P.sb([128, 512], F32, "osum")
        sq, sqb = P.sb([128, 512], F32, "sq")
        stn, stnb = P.sb([128, 12], F32, "stn")
        tmpn, tmpnb = P.sb([128, 512], F32, "tmpn")
        pb, pbb = P.ps([128, 512], F32, "pb")
        pbe, pbeb = P.ps([128, 512], F32, "pbe")
        pbT, pbTb = P.ps([128, 512], F32, "pbT")
        pT, pTb = P.ps([128, 8, 128], BF16, "pT")
        pS, pSb = P.ps([128, 512], F32, "pS")
        pO, pOb = P.ps([128, 512], F32, "pO")
        pU = [P.ps([128, 512], F32, f"pU{i}") for i in range(2)]
        tiles = list(range(NT)) if d == 0 else [1, 0] + list(range(NT - 1, 1, -1))
        if C.max_groups:
            tiles = tiles[:C.max_groups] if d == 0 else ([1, 0] + list(range(C.max_groups - 1, 1, -1)))
        cseq = [0, 1, 2, 3] if d == 0 else [3, 2, 1, 0]
        P.op("dve", lambda e: e.memset(Sbf[0][0][:, :, cseq[0], :], 0.0), [], [Sbf[0][1]])
        def front(n, i):
            sg, sgb = seg[n % 2]
            vb, vbb = vb2[n % 3]
            dcy, dcyb = dcy2[n % 2]
            kh, khb = kh2[n % 2]
            qpad, qpadb = qpad2[n % 3]
            AT, ATb = AT2[n % 3]
            yield
            P.dma("sp" if n % 2 == 0 else "pool", [(sg[:], C.proj[i * 128:(i + 1) * 128, c0:c0 + cw])], sgb, True)
            if d == 1:
                of_, ofb = ofw[n % 3]
                yield
                P.dma("sp", [(of_[:], C.osc[i * 128:(i + 1) * 128, :])], ofb, True)
            if hg:
                fcol = 512 if d == 0 else 1024
                yield
                P.op("act", lambda e, sg=sg: e.activation(out=qf[:], in_=sg[:, 0:512], func=AF.Silu), [sgb], [qfb])
                yield
                P.op("act", lambda e, sg=sg: e.activation(out=sg1[:], in_=sg[:, fcol:fcol + 512], func=AF.Sigmoid), [sgb], [sg1b])
                yield
                P.op("act", lambda e, sg=sg: e.activation(out=kf[:], in_=sg[:, fcol:fcol + 512], func=AF.Sigmoid, scale=-1.0), [sgb], [kfb])
                yield
                P.op("dve", lambda e: e.tensor_tensor(out=kf[:], in0=kf[:], in1=oml[:], op=ALU.mult), [kfb, omlb], [kfb])
                yield
                P.op("dve", lambda e: e.tensor_tensor(out=sg1[:], in0=sg1[:], in1=oml[:], op=ALU.mult), [sg1b, omlb], [sg1b])
                yield
                P.op("dve", lambda e: e.tensor_tensor(out=sg1[:], in0=sg1[:], in1=lbt[:], op=ALU.add), [sg1b, lbb], [sg1b])
                yield
                P.op("act", lambda e: e.activation(out=la[:], in_=sg1[:], func=AF.Ln), [sg1b], [lab])
                yield
                P.op("dve", lambda e, sg=sg: e.tensor_copy(out=vb[:], in_=sg[:, 1536:2048]), [sgb], [vbb])
                qsrc, qsrcb, ksrc, ksrcb = qf[:], qfb, kf[:], kfb
            else:
                rc = 1024 + 16 * d
                yield
                P.op("pe", lambda e, sg=sg: e.transpose(out=pbT[0:16, 0:128], in_=sg[:, rc:rc + 16], identity=identf[:]), [sgb, identfb], [pbTb])
                yield
                P.op("act", lambda e: e.activation(func=AF.Identity, out=rT[:], in_=pbT[0:16, 0:128]), [pbTb], [rTb])
                yield
                P.op("pe", lambda e: e.matmul(pS[:, 0:256], lhsT=rT[:], rhs=w2[:], start=True, stop=True), [rTb, w2b], [pSb])
                yield
                P.op("dve", lambda e: e.tensor_tensor(out=gx[:], in0=pS[:, 0:256], in1=b2[:], op=ALU.add), [pSb, b2b], [gxb])
                yield
                P.op("act", lambda e: e.activation(out=gx[:], in_=gx[:], func=AF.Sigmoid), [gxb], [gxb])
                yield
                P.op("act", lambda e: e.activation(out=gx[:], in_=gx[:], func=AF.Ln), [gxb], [gxb])
                yield
                P.op("dve", lambda e: e.tensor_scalar(out=la[:], in0=gx[:], scalar1=1.0 / 16, scalar2=None, op0=ALU.mult), [gxb], [lab])
                yield
                P.op("dve", lambda e, sg=sg: e.tensor_copy(out=vb[:], in_=sg[:, 512:1024]), [sgb], [vbb])
                qsrc, qsrcb, ksrc, ksrcb = sg[:, 0:256], sgb, sg[:, 256:512], sgb
            yield
            P.op("pe", lambda e: e.matmul(pb[:, 0:W], lhsT=Md, rhs=la[:], start=True, stop=True), [scb, lab], [pbb])
            yield
            P.op("pe", lambda e: e.matmul(pbe[:, 0:W], lhsT=Blk, rhs=la[:], start=True, stop=True), [scb, lab], [pbeb])
            for h in range(H):
                yield
                P.op("pe", lambda e, h=h: e.matmul(pbT[0:dk, h * 4:(h + 1) * 4], lhsT=la[:, h * dk:(h + 1) * dk], rhs=Ind, start=True, stop=True), [lab, scb], [pbTb])
            yield
            P.op("act", lambda e: e.activation(func=AF.Identity, out=bs[:], in_=pb[:, 0:W]), [pbb], [bsb])
            yield
            P.op("act", lambda e: e.activation(out=eb[:], in_=pb[:, 0:W], func=AF.Exp), [pbb], [ebb])
            yield
            P.op("act", lambda e: e.activation(out=enb[:], in_=pb[:, 0:W], func=AF.Exp, scale=-1.0), [pbb], [enbb])
            yield
            P.op("dve", lambda e: e.tensor_tensor(out=ebe[:], in0=pbe[:, 0:W], in1=bs[:], op=ALU.subtract), [pbeb, bsb], [ebeb])
            yield
            P.op("act", lambda e: e.activation(out=ebe[:], in_=ebe[:], func=AF.Exp), [ebeb], [ebeb])
            yield
            P.op("act", lambda e: e.activation(out=dcy[0:dk, :], in_=pbT[0:dk, 0:H * 4], func=AF.Exp), [pbTb], [dcyb])
            yield
            P.op("dve", lambda e, qsrc=qsrc: e.scalar_tensor_tensor(out=qt_[:], in0=qsrc, scalar=qscale, in1=eb[:], op0=ALU.mult, op1=ALU.mult), [qsrcb, ebb], [qtb])
            yield
            P.op("dve", lambda e, ksrc=ksrc: e.tensor_tensor(out=kt_[:], in0=ksrc, in1=enb[:], op=ALU.mult), [ksrcb, enbb], [ktb])
            yield
            P.op("dve", lambda e, ksrc=ksrc: e.tensor_tensor(out=kh[:], in0=ksrc, in1=ebe[:], op=ALU.mult), [ksrcb, ebeb], [khb])
            for h in range(H):
                yield
                P.op("pe", lambda e, h=h: e.transpose(out=pT[0:dk, h, :], in_=qt_[:, h * dk:(h + 1) * dk], identity=ident[:]), [qtb, identb], [pTb])
                yield
                P.op("pe", lambda e, h=h: e.transpose(out=pT[0:dk, H + h, :], in_=kt_[:, h * dk:(h + 1) * dk], identity=ident[:]), [ktb, identb], [pTb])
            yield
            P.op("act", lambda e: e.activation(func=AF.Identity, out=qT[0:dk], in_=pT[0:dk, 0:H, :]), [pTb], [qTb])
            yield
            P.op("act", lambda e: e.activation(func=AF.Identity, out=kT[0:dk], in_=pT[0:dk, H:2 * H, :]), [pTb], [kTb])
            for h in range(H):
                yield
                P.op("dve", lambda e, h=h: e.tensor_copy(out=qpad[0:dk, h, 96:608].rearrange("p (c n) -> p c n", n=128)[:, :, 0:32],
                                                         in_=pT[0:dk, h, :].rearrange("p (c t) -> p c t", t=32)), [pTb], [qpadb])
            for h in range(H):
                yield
                P.op("pe", lambda e, h=h: e.matmul(pS[:, h * 128:(h + 1) * 128], lhsT=kT[0:dk, h, :], rhs=qT[0:dk, h, :], start=True, stop=True), [kTb, qTb], [pSb])
            yield
            P.op("dve", lambda e: e.tensor_tensor(out=AT[:], in0=pS[:].rearrange("p (h n) -> p h n", h=H), in1=Md[:, None, :].to_broadcast([128, H, 128]), op=ALU.mult), [pSb, scb], [ATb])
        def back(n, i):
            vb, vbb = vb2[n % 3]
            dcy, dcyb = dcy2[n % 2]
            kh, khb = kh2[n % 2]
            qpad, qpadb = qpad2[n % 3]
            AT, ATb = AT2[n % 3]
            of_, ofb = ofw[n % 3]
            sb_cur, sb_curb = Sbf[n % 3]
            sb_nxt, sb_nxtb = Sbf[(n + 1) % 3]
            for ci, c in enumerate(cseq):
                u_, ub = pU[ci % 2]
                for h in range(H):
                    yield
                    P.op("pe", lambda e, h=h, c=c, u_=u_: e.matmul(u_[0:dk, h * dv:(h + 1) * dv], lhsT=kh[32 * c:32 * c + 32, h * dk:(h + 1) * dk], rhs=vb[32 * c:32 * c + 32, h * dv:(h + 1) * dv], start=True, stop=True, tile_position=(32 * c, 0)), [khb, vbb], [ub])
                for h in range(H):
                    yield
                    P.op("dve", lambda e, h=h, c=c, u_=u_: e.scalar_tensor_tensor(out=S[0:dk, h, :], in0=S[0:dk, h, :], scalar=dcy[0:dk, h * 4 + c:h * 4 + c + 1],
                                                                                 in1=u_[0:dk, h * dv:(h + 1) * dv], op0=ALU.mult, op1=ALU.add), [Sb_, dcyb, ub], [Sb_])
                if ci < 3:
                    yield
                    P.op("act", lambda e, ci=ci, sb_cur=sb_cur: e.activation(func=AF.Identity, out=sb_cur[0:dk, :, cseq[ci + 1], :], in_=S[0:dk]), [Sb_], [sb_curb])
                else:
                    yield
                    P.op("act", lambda e, sb_nxt=sb_nxt: e.activation(func=AF.Identity, out=sb_nxt[0:dk, :, cseq[0], :], in_=S[0:dk]), [Sb_], [sb_nxtb])

        def tail(n, i):
            vb, vbb = vb2[n % 3]
            qpad, qpadb = qpad2[n % 3]
            AT, ATb = AT2[n % 3]
            of_, ofb = ofw[n % 3]
            sb_cur, sb_curb = Sbf[n % 3]
            for h in range(H):
                yield
                P.op("pe", lambda e, h=h: e.matmul(pO[:, h * dv:(h + 1) * dv], lhsT=AT[:, h, :], rhs=vb[:, h * dv:(h + 1) * dv], start=True, stop=False), [ATb, vbb], [pOb])
                for ci, c in enumerate(cseq):
                    yield
                    P.op("pe", lambda e, h=h, c=c, ci=ci, sb_cur=sb_cur: e.matmul(pO[:, h * dv:(h + 1) * dv], lhsT=qpad[0:dk, h, 96 + 96 * c:224 + 96 * c], rhs=sb_cur[0:dk, h, c, :],
                                                                  start=False, stop=(ci == 3)), [qpadb, sb_curb], [pOb])
            if d == 0:
                o_, ob = ost[n % 2]
                yield
                P.op("act", lambda e, o_=o_: e.activation(func=AF.Identity, out=o_[:], in_=pO[:]), [pOb], [ob])
                yield
                P.dma("pool", [(C.osc[i * 128:(i + 1) * 128, :], o_[:])], ob, False)
            else:
                o_, ob = ost[n % 2]
                yield
                P.op("dve", lambda e, of_=of_: e.tensor_tensor(out=osum[:], in0=pO[:], in1=of_[:], op=ALU.add), [pOb, ofb], [osumb])
                head_norm(P, osum[:], osumb, 4, 128, go, gob, o_[:].rearrange("p (h d) -> p h d", h=4), ob, sq, sqb, stn, stnb, tmpn, tmpnb)
                yield
                P.dma("pool", [(C.ybr[br][i * 128:(i + 1) * 128, :], o_[:])], ob, False)

        def drive(gens):
            while gens:
                for g_ in list(gens):
                    try:
                        next(g_)
                    except StopIteration:
                        gens.remove(g_)

        drive([front(0, tiles[0])])
        for n, i in enumerate(tiles):
            gs = [back(n, i)]
            if n + 1 < len(tiles):
                gs.insert(0, front(n + 1, tiles[n + 1]))
            if n > 0:
                gs.append(tail(n - 1, tiles[n - 1]))
            drive(gs)
        drive([tail(len(tiles) - 1, tiles[-1])])
        P.end()


def merge_phase(P, C, l):
    tiles = list(range(NT)) if l == 0 else list(range(2, NT))
    if C.max_groups:
        tiles = tiles[:C.max_groups]
    P.begin()
    ident, identb = P.sb([128, 128], BF16, "ident")
    P.dma("sp", [(ident[:], C.ident)], identb, True)
    wbr, wbrb = P.sb([128, 4, 4, 1024], BF16, "wbr")
    wmg, wmgb = P.sb([128, 4, 8, 1024], BF16, "wmg")
    bmg, bmgb = P.sb([1, 4, 1024], BF16, "bmg")
    ones, onesb = P.sb([1, 128], BF16, "ones")
    P.op("dve", lambda e: e.memset(ones[:], 1.0), [], [onesb])
    stg = [P.sb([128, 1024], F32, f"wstg{i}") for i in range(2)]
    n = 0
    for br in range(4):
        for k in range(4):
            s_, sb_ = stg[n % 2]
            P.dma("sp" if n % 2 == 0 else "pool", [(s_[:], C.w_br[l, br, k * 128:(k + 1) * 128, :])], sb_, True)
            if n % 2 == 0:
                P.op("act", lambda e, s_=s_, br=br, k=k: e.activation(func=AF.Identity, out=wbr[:, br, k, :], in_=s_[:]), [sb_], [wbrb])
            else:
                P.op("dve", lambda e, s_=s_, br=br, k=k: e.tensor_copy(out=wbr[:, br, k, :], in_=s_[:]), [sb_], [wbrb])
            n += 1
        for k in range(8):
            s_, sb_ = stg[n % 2]
            P.dma("sp" if n % 2 == 0 else "pool", [(s_[:], C.w_merge[l, br, k * 128:(k + 1) * 128, :])], sb_, True)
            if n % 2 == 0:
                P.op("act", lambda e, s_=s_, br=br, k=k: e.activation(func=AF.Identity, out=wmg[:, br, k, :], in_=s_[:]), [sb_], [wmgb])
            else:
                P.op("dve", lambda e, s_=s_, br=br, k=k: e.tensor_copy(out=wmg[:, br, k, :], in_=s_[:]), [sb_], [wmgb])
            n += 1
    bst, bstb = P.sb([1, 4, 1024], F32, "bst")
    P.dma("sp", [(bst[:], C.b_merge[l:l + 1])], bstb, True)
    P.op("dve", lambda e: e.tensor_copy(out=bmg[:], in_=bst[:]), [bstb], [bmgb])
    yb = [[P.sb([128, 512], F32, f"y{br}_{i}") for br in range(4)] for i in range(2)]
    zt = [P.sb([128, 2048], F32, f"z{i}") for i in range(2)]
    hT = [P.sb([128, 8, 128], BF16, f"hT{i}") for i in range(2)]
    sz, szb = P.sb([128, 2048], F32, "sz")
    u, ub = P.sb([128, 2048], BF16, "u")
    uT, uTb = P.sb([128, 16, 128], BF16, "uT")
    g, gb = P.sb([128, 1024], F32, "g")
    tmp, tmpb = P.sb([128, 1024], F32, "tmp")
    accs = [P.sb([128, 1024], F32, f"acc{i}") for i in range(2)]
    accbf, accbfb = P.sb([128, 1024], BF16, "accbf")
    aT = [P.sb([128, 8, 128], BF16, f"aT{i}") for i in range(2)]
    pT = [P.ps([128, 8, 128], BF16, f"pT{i}") for i in range(2)]
    pP = [P.ps([128, 512], F32, f"pP{i}") for i in range(2)]
    pG = [P.ps([128, 512], F32, f"pG{i}") for i in range(2)]
    uTs = [(uT, uTb), P.sb([128, 16, 128], BF16, "uT2")]
    pA, pAb = P.ps([128, 8, 128], BF16, "pA")

    def front(n, i):
        ys = yb[n % 2]
        z_, zb = zt[n % 2]
        h_, hb = hT[n % 2]
        uT_, uT_b = uTs[n % 2]
        for br in range(4):
            yield
            P.dma("sp" if br % 2 == 0 else "pool", [(ys[br][0][:], C.ybr[br][i * 128:(i + 1) * 128, :])], ys[br][1], True)
        yield
        P.dma("sp", [(z_[:], C.proj[i * 128:(i + 1) * 128, 5056:7104])], zb, True)
        yield
        P.dma("pool", [(h_[:], C.hT[i])], hb, True)
        yield
        P.op("act", lambda e: e.activation(out=sz[:], in_=z_[:], func=AF.Silu), [zb], [szb])
        for br in range(4):
            yield
            P.op("dve", lambda e, br=br: e.tensor_tensor(out=u[:, br * 512:(br + 1) * 512], in0=ys[br][0][:], in1=sz[:, br * 512:(br + 1) * 512], op=ALU.mult),
                 [ys[br][1], szb], [ub])
        for half in range(2):
            p_, pb_ = pT[half]
            for k in range(8):
                yield
                P.op("pe", lambda e, p_=p_, k=k, half=half: e.transpose(out=p_[:, k, :], in_=u[:, (half * 8 + k) * 128:(half * 8 + k + 1) * 128], identity=ident[:]), [ub, identb], [pb_])
            yield
            P.op("act", lambda e, p_=p_, half=half: e.activation(func=AF.Identity, out=uT_[:, half * 8:(half + 1) * 8, :], in_=p_[:]), [pb_], [uT_b])

    def back(n, i):
        h_, hb = hT[n % 2]
        uT_, uT_b = uTs[n % 2]
        acc, accb = accs[n % 2]
        for br in range(4):
            for hf in range(2):
                pp, ppb = pP[hf]
                pg, pgb = pG[hf]
                for k in range(4):
                    yield
                    P.op("pe", lambda e, pp=pp, br=br, k=k, hf=hf: e.matmul(pp[:], lhsT=uT_[:, br * 4 + k, :], rhs=wbr[:, br, k, hf * 512:(hf + 1) * 512], start=(k == 0), stop=(k == 3)), [uT_b, wbrb], [ppb])
                for k in range(8):
                    yield
                    P.op("pe", lambda e, pg=pg, br=br, k=k, hf=hf: e.matmul(pg[:], lhsT=h_[:, k, :], rhs=wmg[:, br, k, hf * 512:(hf + 1) * 512], start=(k == 0), stop=False), [hb, wmgb], [pgb])
                yield
                P.op("pe", lambda e, pg=pg, br=br, hf=hf: e.matmul(pg[:], lhsT=ones[:], rhs=bmg[:, br, hf * 512:(hf + 1) * 512], start=False, stop=True), [onesb, bmgb], [pgb])
                yield
                P.op("act", lambda e, pg=pg, hf=hf: e.activation(out=g[:, hf * 512:(hf + 1) * 512], in_=pg[:], func=AF.Sigmoid), [pgb], [gb])
                if br == 0:
                    yield
                    P.op("dve", lambda e, pp=pp, hf=hf: e.tensor_tensor(out=acc[:, hf * 512:(hf + 1) * 512], in0=pp[:], in1=g[:, hf * 512:(hf + 1) * 512], op=ALU.mult), [ppb, gb], [accb])
                else:
                    yield
                    P.op("dve", lambda e, pp=pp, hf=hf: e.tensor_tensor(out=tmp[:, hf * 512:(hf + 1) * 512], in0=pp[:], in1=g[:, hf * 512:(hf + 1) * 512], op=ALU.mult), [ppb, gb], [tmpb])
                    yield
                    P.op("dve", lambda e, hf=hf: e.tensor_tensor(out=acc[:, hf * 512:(hf + 1) * 512], in0=acc[:, hf * 512:(hf + 1) * 512], in1=tmp[:, hf * 512:(hf + 1) * 512], op=ALU.add), [accb, tmpb], [accb])

    def tail(n, i):
        acc, accb = accs[n % 2]
        yield
        P.op("dve", lambda e: e.tensor_copy(out=accbf[:], in_=acc[:]), [accb], [accbfb])
        a_, ab = aT[n % 2]
        for k in range(8):
            yield
            P.op("pe", lambda e, k=k: e.transpose(out=pA[:, k, :], in_=accbf[:, k * 128:(k + 1) * 128], identity=ident[:]), [accbfb, identb], [pAb])
        yield
        P.op("act", lambda e: e.activation(func=AF.Identity, out=a_[:], in_=pA[:]), [pAb], [ab])
        yield
        P.dma("pool", [(C.accT[i], a_[:])], ab, False)

    drive([front(0, tiles[0])])
    for n, i in enumerate(tiles):
        gs = [back(n, i)]
        if n + 1 < len(tiles):
            gs.append(front(n + 1, tiles[n + 1]))
        if n > 0:
            gs.append(tail(n - 1, tiles[n - 1]))
        drive(gs)
    drive([tail(len(tiles) - 1, tiles[-1])])
    P.end()
    P.begin()
    identf, identfb = P.sb([128, 128], F32, "identf")
    P.dma("sp", [(identf[:], C.identf)], identfb, True)
    onef, onefb = P.sb([128, 128], F32, "onef")
    P.op("dve", lambda e: e.memset(onef[:], 1.0), [], [onefb])
    wo, wob = P.sb([128, 8, 1024], BF16, "wo")
    stg = [P.sb([128, 1024], F32, f"wstg{i}") for i in range(2)]
    for k in range(8):
        s_, sb_ = stg[k % 2]
        P.dma("sp" if k % 2 == 0 else "pool", [(s_[:], C.w_out[l, k * 128:(k + 1) * 128, :])], sb_, True)
        if k % 2 == 0:
            P.op("act", lambda e, s_=s_, k=k: e.activation(func=AF.Identity, out=wo[:, k, :], in_=s_[:]), [sb_], [wob])
        else:
            P.op("dve", lambda e, s_=s_, k=k: e.tensor_copy(out=wo[:, k, :], in_=s_[:]), [sb_], [wob])
    gtr = [P.sb([128, 1024], F32, f"gtr{v}") for v in range(2)]
    dg, dgb = P.sb([128, 128], F32, "dg")
    pR = [P.ps([128, 512], F32, f"pR{i}") for i in range(2)]
    for v in range(2):
        for j in range(8):
            P.op("dve", lambda e, v=v, j=j: e.tensor_scalar(out=dg[:], in0=identf[:], scalar1=C.modt[:, l, 2, j, v:v + 1], scalar2=None, op0=ALU.mult), [identfb, C.modtb], [dgb])
            P.op("pe", lambda e, j=j: e.matmul(pR[j // 4][0][:, (j % 4) * 128:(j % 4 + 1) * 128], lhsT=onef[:], rhs=dg[:], start=True, stop=True), [onefb, dgb], [pR[j // 4][1]])
        for hf in range(2):
            P.op("act", lambda e, v=v, hf=hf: e.activation(func=AF.Identity, out=gtr[v][0][:, hf * 512:(hf + 1) * 512], in_=pR[hf][0][:]), [pR[hf][1]], [gtr[v][1]])
    aT = [P.sb([128, 8, 128], BF16, f"aT{i}") for i in range(2)]
    xt = [P.sb([128, 1024], F32, f"xt{i}") for i in range(2)]
    xo = [P.sb([128, 1024], F32, f"xo{i}") for i in range(2)]
    t2, t2b = P.sb([128, 1024], F32, "t2")
    pO = [P.ps([128, 512], F32, f"pO{i}") for i in range(4)]
    for n, i in enumerate(tiles):
        v = 1 if i < 2 else 0
        a_, ab = aT[n % 2]
        x_, xb = xt[n % 2]
        o_, ob = xo[n % 2]
        P.dma("sp", [(a_[:], C.accT[i])], ab, True)
        P.dma("pool", [(x_[:], C.xsrc[l](i))], xb, True)
        for hf in range(2):
            po, pob = pO[(n % 2) * 2 + hf]
            for k in range(8):
                P.op("pe", lambda e, po=po, a_=a_, k=k, hf=hf: e.matmul(po[:], lhsT=a_[:, k, :], rhs=wo[:, k, hf * 512:(hf + 1) * 512], start=(k == 0), stop=(k == 7)), [ab, wob], [pob])
            P.op("dve", lambda e, po=po, hf=hf, v=v: e.tensor_tensor(out=t2[:, hf * 512:(hf + 1) * 512], in0=po[:], in1=gtr[v][0][:, hf * 512:(hf + 1) * 512], op=ALU.mult), [pob, gtr[v][1]], [t2b])
            P.op("dve", lambda e, hf=hf, x_=x_, o_=o_: e.tensor_tensor(out=o_[:, hf * 512:(hf + 1) * 512], in0=t2[:, hf * 512:(hf + 1) * 512], in1=x_[:, hf * 512:(hf + 1) * 512], op=ALU.add), [t2b, xb], [ob])
        dst = C.xres[i * 128:(i + 1) * 128, :] if l == 0 else C.y[(i - 2) * 128:(i - 1) * 128, :]
        P.dma("sp", [(dst, o_[:])], ob, False)
    P.end()

def build(layers=(0, 1), phases=("p0", "p1", "mla", "nat", "hg", "gla", "merge"), dbg=(), dbg_in=(), max_groups=None, stage=99):
    nc = bass.Bass("TRN2", target_bir_lowering=False)
    P = Prog(nc)
    C = Ctx()
    C.dbg = dbg
    C.max_groups = max_groups
    C.stage = stage

    def scr(name, shape, dt):
        kind = "ExternalOutput" if name in dbg else ("ExternalInput" if name in dbg_in else "Internal")
        return nc.dram_tensor(name, list(shape), dt, kind=kind).ap()
    C.x = dram_in(nc, "x", [NLAT, D])
    C.ctx = dram_in(nc, "ctx", [NCTX, D])
    C.cvec = dram_in(nc, "cvec", [128, 8, 2])
    C.ada_w = dram_in(nc, "ada_w", [2, D, 3 * D])
    C.ada_b_fm = dram_in(nc, "ada_b_fm", [128, 2, 24])
    C.norm_g_fm = dram_in(nc, "norm_g_fm", [128, 2, 8])
    C.w_in = dram_in(nc, "w_in", [2, D, IN_W])
    C.ident = dram_in(nc, "ident", [128, 128], BF16)
    C.identf = dram_in(nc, "identf", [128, 128])
    C.mla_w_uq = dram_in(nc, "mla_w_uq", [2, 256, 768])
    C.mla_w_ukv = dram_in(nc, "mla_w_ukv", [2, 128, 1024])
    C.mla_gc_rep = dram_in(nc, "mla_gc_rep", [2, 128, 384])
    C.mla_gq_rep = dram_in(nc, "mla_gq_rep", [2, 128, 96])
    C.mla_gk_rep = dram_in(nc, "mla_gk_rep", [2, 128, 96])
    C.rope_cs = dram_in(nc, "rope_cs", [T, 32])
    C.mla_qT = scr("mla_qT", [2, 8, 17, 96, 512], BF16)
    C.mla_kT = scr("mla_kT", [8, 96, T], BF16)
    C.mla_va = scr("mla_va", [8, 128, NT, 66], BF16)
    C.nat_gq_rep = dram_in(nc, "nat_gq_rep", [2, 128, 64])
    C.nat_gk_rep = dram_in(nc, "nat_gk_rep", [2, 128, 64])
    C.nat_bias = dram_in(nc, "nat_bias", [2, 3, 8, 8, 128, 512])
    C.nat_qT = scr("nat_qT", [4, 17, 128, 512], BF16)
    C.nat_kT = scr("nat_kT", [4, 128, T], BF16)
    C.nat_va = scr("nat_va", [8, 128, NT, 66], BF16)
    C.scan_consts = dram_in(nc, "scan_consts", [128, 388])
    C.hg_go_rep = dram_in(nc, "hg_go_rep", [2, 128, 128])
    C.gla_go_rep = dram_in(nc, "gla_go_rep", [2, 128, 128])
    C.hg_lb_rep = dram_in(nc, "hg_lb_rep", [2, 128, 2, 512])
    C.gla_w2 = dram_in(nc, "gla_w2", [2, 2, 16, 256])
    C.gla_b2_rep = dram_in(nc, "gla_b2_rep", [2, 128, 2, 256])
    C.w_br = dram_in(nc, "w_br", [2, 4, 512, D])
    C.w_merge = dram_in(nc, "w_merge", [2, 4, D, D])
    C.b_merge = dram_in(nc, "b_merge", [2, 4, D])
    C.w_out = dram_in(nc, "w_out", [2, D, D])
    C.accT = scr("accT", [NT, 128, 8, 128], BF16)
    C.osc = scr("osc", [T, 512], F32)
    C.ybr = [scr(f"ybr{i}", [T, 512], F32) for i in range(4)]
    C.y = nc.dram_tensor("y", [NLAT, D], F32, kind="ExternalOutput").ap()
    C.xres = scr("xres", [T, D], F32)
    C.hT = scr("hT", [NT, 128, 8, 128], BF16)
    C.proj = scr("proj", [T, IN_W], F32)
    C.modt, C.modtb = P.sb([128, 2, 3, 8, 2], F32, "modt", glob=True)
    C.A, C.Ab = P.sb([128, 2, 8, 2], F32, "Amod", glob=True)

    def src0(i):
        return C.ctx[i * 128:(i + 1) * 128, :] if i < 2 else C.x[(i - 2) * 128:(i - 1) * 128, :]

    def src1(i):
        return C.xres[i * 128:(i + 1) * 128, :]
    C.xsrc = [src0, src1]
    if "p0" in phases:
        phase0_adaln(P, C)
    for l in layers:
        if "p1" in phases:
            phase1_inproj(P, C, l)
        if "mla" in phases or "mlaprep" in phases:
            mla_prep(P, C, l)
        if "mla" in phases or "mlamain" in phases:
            mla_main(P, C, l)
        if "nat" in phases:
            nat_prep(P, C, l)
            nat_main(P, C, l)
        if "hg" in phases:
            scan_mixer(P, C, l, "hg")
        if "gla" in phases:
            scan_mixer(P, C, l, "gla")
        if "merge" in phases:
            merge_phase(P, C, l)
    P.finish()
    return nc


def rope_table():
    quarter = 8
    inv_freq = (10000.0 ** (-np.arange(quarter, dtype=np.float32) / quarter)).astype(np.float32)
    t = np.arange(NLAT)
    row = (t // 64).astype(np.float32)
    col = (t % 64).astype(np.float32)
    ang = np.concatenate([row[:, None] * inv_freq, col[:, None] * inv_freq], axis=-1).astype(np.float32)
    cs = np.zeros((T, 32), np.float32)
    cs[:NCTX, 0:16] = 1.0
    cs[NCTX:, 0:16] = np.cos(ang)
    cs[NCTX:, 16:32] = np.sin(ang)
    return cs


def rep(v):
    return np.ascontiguousarray(np.broadcast_to(v[:, None, :], (v.shape[0], 128, v.shape[1]))).astype(np.float32)


def host_inputs(inp, b):
    import ml_dtypes
    d = {}
    d["x"] = np.ascontiguousarray(inp["x"][b])
    d["ctx"] = np.ascontiguousarray(inp["ctx"][b])
    cv = np.stack([inp["c"][b], inp["c_ctx"]], -1)
    d["cvec"] = np.ascontiguousarray(cv.reshape(8, 128, 2).transpose(1, 0, 2))
    d["ada_w"] = inp["ada_w"]
    d["ada_b_fm"] = np.ascontiguousarray(inp["ada_b"].reshape(2, 24, 128).transpose(2, 0, 1))
    d["norm_g_fm"] = np.ascontiguousarray(inp["norm_g"].reshape(2, 8, 128).transpose(2, 0, 1))
    d["w_in"] = inp["w_in"]
    d["ident"] = np.eye(128).astype(ml_dtypes.bfloat16)
    d["identf"] = np.eye(128).astype(np.float32)
    d["mla_w_uq"] = inp["mla_w_uq"]
    d["mla_w_ukv"] = inp["mla_w_ukv"]
    d["mla_gc_rep"] = rep(np.concatenate([inp["mla_g_cq"], inp["mla_g_ckv"]], -1))
    d["mla_gq_rep"] = rep(inp["mla_g_q"])
    d["mla_gk_rep"] = rep(inp["mla_g_k"])
    d["rope_cs"] = rope_table()
    d["nat_gq_rep"] = rep(inp["nat_g_q"])
    d["nat_gk_rep"] = rep(inp["nat_g_k"])
    d["nat_bias"] = nat_bias_host(inp["nat_rpb"])
    d["scan_consts"] = scan_consts_host()
    d["hg_go_rep"] = rep(inp["hg_g_o"])
    d["gla_go_rep"] = rep(inp["gla_g_o"])
    d["hg_lb_rep"] = np.ascontiguousarray(np.broadcast_to(inp["hg_lb_logits"][:, None], (2, 128, 2, 512))).astype(np.float32)
    d["gla_w2"] = inp["gla_w2"]
    d["w_br"] = inp["w_br"]
    d["w_merge"] = inp["w_merge"]
    d["b_merge"] = inp["b_merge"]
    d["w_out"] = inp["w_out"]
    d["gla_b2_rep"] = np.ascontiguousarray(np.broadcast_to(inp["gla_b2"][:, None], (2, 128, 2, 256))).astype(np.float32)
    return d


def kernel(**inputs):
    inp = {k: np.asarray(v) for k, v in inputs.items()}
    nc = build(layers=(0, 1))
    in_maps = [host_inputs(inp, b) for b in range(4)]
    res = run_bass_kernel_spmd(nc, in_maps, core_ids=list(range(4)))
    return np.stack([r["y"] for r in res.results], 0).astype(np.float32)
```

```python
import numpy as np
from contextlib import ExitStack
import concourse.bass as bass
import concourse.mybir as mybir
from concourse.bass_utils import run_bass_kernel_spmd

F32 = mybir.dt.float32
BF16 = mybir.dt.bfloat16
AF = mybir.ActivationFunctionType
ALU = mybir.AluOpType
AX = mybir.AxisListType

ENGS = ("pe", "act", "dve", "pool", "sp")
NDMASEM = 56


class Buf:
    __slots__ = ("name", "w", "r", "sem", "excl")

    def __init__(self, name):
        self.name = name
        self.excl = False
        self.w = None
        self.r = {}
        self.sem = None


class Op:
    __slots__ = ("eng", "fn", "waits", "signal", "token", "ndma")

    def __init__(self, eng, fn):
        self.eng = eng
        self.fn = fn
        self.waits = {}
        self.signal = False
        self.token = None
        self.ndma = 0


class Prog:
    def __init__(self, nc):
        self.nc = nc
        self.ges = ExitStack()
        self.es = None
        self.cnt = {}
        self.sigbase = {e: 0 for e in ENGS}
        self.ntile = 0
        self.sems = {}
        for e in ENGS:
            self.sems[e] = self.ges.enter_context(nc.semaphore("s_" + e))
        for i in range(NDMASEM):
            self.sems[f"d{i}"] = self.ges.enter_context(nc.semaphore(f"s_d{i}"))
        self.gbufs = []
        self.nphase = 0
        self._reset_phase()

    def _reset_phase(self):
        self.ops = {e: [] for e in ENGS}
        self.tok_op = {}
        self.pbufs = []
        self.ndsem = 0
        self.ndsem_sw = 0

    def _stack(self, glob):
        return self.ges if glob else self.es

    def sb(self, shape, dt, name="t", glob=False):
        self.ntile += 1
        t = self._stack(glob).enter_context(self.nc.sbuf_tensor(f"{name}_{self.ntile}", list(shape), dt))
        b = Buf(name)
        (self.gbufs if glob else self.pbufs).append(b)
        return t, b

    def ps(self, shape, dt, name="p"):
        self.ntile += 1
        t = self.es.enter_context(self.nc.psum_tensor(f"{name}_{self.ntile}", list(shape), dt))
        b = Buf(name)
        b.excl = True
        self.pbufs.append(b)
        return t, b

    def begin(self):
        self.es = ExitStack()
        self._reset_phase()

    def _dep(self, op, tok):
        if tok is None:
            return
        key, c = tok
        if key == "pe" and op.eng == "pe":
            return
        if op.waits.get(key, 0) < c:
            op.waits[key] = c
        if key in ENGS:
            self.tok_op[tok].signal = True

    def _track(self, o, reads, writes):
        ex = [b for b in reads if b.excl and o.eng != "pe"]
        if ex:
            reads = [b for b in reads if not (b.excl and o.eng != "pe")]
            writes = list(writes) + ex
        for b in reads:
            self._dep(o, b.w)
        for b in writes:
            self._dep(o, b.w)
            for t in b.r.items():
                self._dep(o, t)
        for b in reads:
            if b.r.get(o.token[0], 0) < o.token[1]:
                b.r[o.token[0]] = o.token[1]
        for b in writes:
            b.w = o.token
            b.r = {}

    def op(self, eng, fn, reads=(), writes=()):
        o = Op(eng, fn)
        c = self.cnt.get(eng, 0) + 1
        self.cnt[eng] = c
        o.token = (eng, c)
        self.tok_op[o.token] = o
        self._track(o, reads, writes)
        self.ops[eng].append(o)
        return o

    def dma(self, q, pairs, sbuf, load):
        sw = (q == "pool")
        if sbuf.sem is None or sbuf.sem[1] != self.nphase:
            if sw:
                assert self.ndsem_sw < NDMASEM // 2, "out of sw dma semaphores"
                sbuf.sem = (f"d{NDMASEM // 2 + self.ndsem_sw}", self.nphase, sw)
                self.ndsem_sw += 1
            else:
                assert self.ndsem < NDMASEM // 2, "out of hw dma semaphores"
                sbuf.sem = (f"d{self.ndsem}", self.nphase, sw)
                self.ndsem += 1
        assert sbuf.sem[2] == sw, "buffer used from both DMA queue kinds: " + sbuf.name
        key = sbuf.sem[0]

        def fn(e, pairs=pairs):
            return [e.dma_start(out=o_, in_=i_) for (o_, i_) in pairs]
        o = Op(q, fn)
        o.ndma = len(pairs)
        c = self.cnt.get(key, 0) + 16 * len(pairs)
        self.cnt[key] = c
        o.token = (key, c)
        o.signal = True
        if load:
            self._track(o, [], [sbuf])
        else:
            self._track(o, [sbuf], [])
        self.ops[q].append(o)
        return o

    def end(self):
        nc = self.nc
        for e in ENGS:
            for o in reversed(self.ops[e]):
                if o.ndma == 0 and o.fn is not None:
                    o.signal = True
                    break
        snap = dict(self.cnt)
        for e in ENGS:
            o = Op(e, None)
            for key, c in snap.items():
                if key == e and e == "pe":
                    continue
                if c > 0:
                    o.waits[key] = c
            self.ops[e].append(o)
        sig_index = {}
        last_sig = dict(self.sigbase)
        for e in ENGS:
            k = self.sigbase[e]
            for o in self.ops[e]:
                if o.fn is None or o.ndma:
                    continue
                if o.signal:
                    k += 1
                sig_index[o.token] = k
            last_sig[e] = k
        prog = self
        sems = self.sems
        base = dict(self.sigbase)

        def section(e):
            def body(eng):
                waited = {}
                for o in prog.ops[e]:
                    for key, c in o.waits.items():
                        if key in ENGS:
                            v = sig_index.get((key, c))
                            if v is None:
                                continue
                            if v <= base[key]:
                                continue
                        else:
                            v = c
                        if waited.get(key, 0) < v:
                            eng.wait_ge(sems[key], v)
                            waited[key] = v
                    if o.fn is None:
                        continue
                    r = o.fn(eng)
                    if o.ndma:
                        for ins in r:
                            ins.then_inc(sems[o.token[0]], 16)
                    elif o.signal:
                        r.then_inc(sems[e], 1)
            return body

        with nc.Block() as block:
            bl = {"pe": block.tensor, "act": block.scalar, "dve": block.vector,
                  "pool": block.gpsimd, "sp": block.sync}
            for e in ENGS:
                bl[e](section(e))
        self.sigbase = last_sig
        for b in self.gbufs + self.pbufs:
            b.w = None
            b.r = {}
        self.es.close()
        self.es = None
        self.nphase += 1

    def finish(self):
        self.ges.close()


D = 1024
NCTX = 256
NLAT = 8192
T = NCTX + NLAT
NT = T // 128
IN_W = 7104
EPS = 1e-6
SEG = {"mla": (0, 416), "nat": (416, 1536), "hg": (1952, 2048), "gla": (4000, 1056), "z": (5056, 2048)}


def dram_in(nc, name, shape, dt=F32):
    return nc.dram_tensor(name, list(shape), dt, kind="ExternalInput").ap()


def dram_scr(nc, name, shape, dt):
    return nc.dram_tensor(name, list(shape), dt, kind="Internal").ap()


class Ctx:
    pass


def drive(gens):
    gens = list(gens)
    while gens:
        for g_ in list(gens):
            try:
                next(g_)
            except StopIteration:
                gens.remove(g_)


def phase0_adaln(P, C):
    P.begin()
    cv, cvb = P.sb([128, 8, 2], F32, "cv")
    sl, slb = P.sb([128, 8, 2], F32, "sl")
    ab, abb = P.sb([128, 2, 24], F32, "ab")
    ng, ngb = P.sb([128, 2, 8], F32, "ng")
    P.dma("sp", [(cv[:], C.cvec)], cvb, True)
    P.dma("sp", [(ab[:], C.ada_b_fm)], abb, True)
    P.dma("sp", [(ng[:], C.norm_g_fm)], ngb, True)
    P.op("act", lambda e: e.activation(out=sl[:], in_=cv[:], func=AF.Silu), [cvb], [slb])
    wbuf = [P.sb([128, 8, 1024], F32, f"adaw{i}") for i in range(2)]
    pp = [P.ps([128, 512], F32, f"pm{i}") for i in range(2)]
    it = 0
    for l in range(2):
        for part in range(3):
            w, wb = wbuf[it % 2]
            src = C.ada_w[l, :, part * 1024:(part + 1) * 1024].rearrange("(k p) n -> p k n", p=128)
            P.dma("sp" if it % 2 == 0 else "pool", [(w[:, 0:4, :], src[:, 0:4, :]), (w[:, 4:8, :], src[:, 4:8, :])], wb, True)
            for j in range(8):
                ps_, psb = pp[j % 2]
                for k in range(8):
                    P.op("pe", lambda e, ps_=ps_, w=w, k=k, j=j: e.matmul(ps_[:, 0:2], lhsT=w[:, k, j * 128:(j + 1) * 128], rhs=sl[:, k, :],
                                                                       start=(k == 0), stop=(k == 7)), [wb, slb], [psb])
                P.op("act", lambda e, ps_=ps_, l=l, part=part, j=j: e.activation(out=C.modt[:, l, part, j, :], in_=ps_[:, 0:2], func=AF.Identity,
                                                                             bias=ab[:, l, part * 8 + j:part * 8 + j + 1], scale=1.0),
                     [psb, abb], [C.modtb])
            it += 1
    for l in range(2):
        for v in range(2):
            P.op("dve", lambda e, l=l, v=v: e.scalar_tensor_tensor(out=C.A[:, l, :, v], in0=C.modt[:, l, 1, :, v], scalar=1.0, in1=ng[:, l, :],
                                                                  op0=ALU.add, op1=ALU.mult), [C.modtb, ngb], [C.Ab])
    P.end()


def phase1_inproj(P, C, l):
    src_rows = C.xsrc[l]
    halves = [(0, 3552), (3552, 3552)]
    for hi, (c0, cw) in enumerate(halves):
        P.begin()
        ident, identb = P.sb([128, 128], BF16, "ident")
        P.dma("sp", [(ident[:], C.ident)], identb, True)
        wsb, wsbb = P.sb([128, 8, cw], BF16, "win")
        stg = [P.sb([128, 1776], F32, f"wstg{i}") for i in range(2)]
        n = 0
        for k in range(8):
            for q in range(cw // 1776):
                s_, sb_ = stg[n % 2]
                P.dma("sp" if n % 2 == 0 else "act", [(s_[:], C.w_in[l, k * 128:(k + 1) * 128, c0 + q * 1776:c0 + (q + 1) * 1776])], sb_, True)
                if n % 2 == 0:
                    P.op("act", lambda e, s_=s_, k=k, q=q: e.activation(func=AF.Identity, out=wsb[:, k, q * 1776:(q + 1) * 1776], in_=s_[:]), [sb_], [wsbb])
                else:
                    P.op("dve", lambda e, s_=s_, k=k, q=q: e.tensor_copy(out=wsb[:, k, q * 1776:(q + 1) * 1776], in_=s_[:]), [sb_], [wsbb])
                n += 1
        xt = [P.sb([128, 1024], F32, f"xt{i}") for i in range(2)]
        sq, sqb = P.sb([128, 1024], F32, "sq")
        st = [P.sb([128, 4], F32, f"st{i}") for i in range(2)]
        xn = [P.sb([128, 1024], BF16, f"xn{i}") for i in range(2)]
        hT = [P.sb([128, 8, 128], BF16, f"hT{i}") for i in range(3)]
        og = [P.sb([128, cw], F32, f"og{i}") for i in range(2)]
        pT = [P.ps([128, 8, 128], BF16, f"pT{i}") for i in range(1)]
        pY = [P.ps([128, 512], F32, f"pY{i}") for i in range(7)]
        ngrp = (cw + 511) // 512
        evc = [0]

        def front(i):
            v = 1 if i < 2 else 0
            h_, hb = hT[i % 3]
            if hi == 0:
                x_, xb = xt[i % 2]
                s_, sb_ = st[i % 2]
                n_, nb = xn[i % 2]
                p_, pb = pT[0]
                yield
                P.dma("sp", [(x_[:], src_rows(i))], xb, True)
                yield
                P.op("act", lambda e: e.activation(out=sq[:], in_=x_[:], func=AF.Square, accum_out=s_[:, 0:1]), [xb], [sqb, sb_])
                yield
                P.op("dve", lambda e: e.tensor_scalar(out=s_[:, 1:2], in0=s_[:, 0:1], scalar1=1.0 / D, scalar2=EPS, op0=ALU.mult, op1=ALU.add), [sb_], [sb_])
                yield
                P.op("act", lambda e: e.activation(out=s_[:, 2:3], in_=s_[:, 1:2], func=AF.Ln), [sb_], [sb_])
                yield
                P.op("act", lambda e: e.activation(out=s_[:, 3:4], in_=s_[:, 2:3], func=AF.Exp, scale=-0.5), [sb_], [sb_])
                yield
                P.op("dve", lambda e: e.tensor_scalar(out=n_[:], in0=x_[:], scalar1=s_[:, 3:4], scalar2=None, op0=ALU.mult), [xb, sb_], [nb])
                for k in range(8):
                    yield
                    P.op("pe", lambda e, k=k: e.transpose(out=p_[:, k, :], in_=n_[:, k * 128:(k + 1) * 128], identity=ident[:]), [nb, identb], [pb])
                for k in range(8):
                    yield
                    P.op("act", lambda e, k=k: e.activation(out=h_[:, k, :], in_=p_[:, k, :], func=AF.Identity,
                                                            scale=C.A[:, l, k, v:v + 1], bias=C.modt[:, l, 0, k, v:v + 1]), [pb, C.Ab, C.modtb], [hb])
                yield
                P.dma("act", [(C.hT[i], h_[:])], hb, False)
            else:
                yield
                P.dma("sp", [(h_[:], C.hT[i])], hb, True)

        def back(i):
            h_, hb = hT[i % 3]
            o_, ob = og[i % 2]
            for (g0, g1) in ((0, 4), (4, ngrp)):
                for k in range(8):
                    for g in range(g0, g1):
                        gw = min(512, cw - g * 512)
                        y_, yb = pY[g]
                        yield
                        P.op("pe", lambda e, k=k, g=g, gw=gw, y_=y_: e.matmul(y_[:, 0:gw], lhsT=h_[:, k, :], rhs=wsb[:, k, g * 512:g * 512 + gw],
                                                                          start=(k == 0), stop=(k == 7)), [hb, wsbb], [yb])
                for g in range(g0, g1):
                    gw = min(512, cw - g * 512)
                    y_, yb = pY[g]
                    if evc[0] % 2 == 0:
                        yield
                        P.op("act", lambda e, g=g, gw=gw, y_=y_: e.activation(func=AF.Identity, out=o_[:, g * 512:g * 512 + gw], in_=y_[:, 0:gw]), [yb], [ob])
                    else:
                        yield
                        P.op("dve", lambda e, g=g, gw=gw, y_=y_: e.tensor_copy(out=o_[:, g * 512:g * 512 + gw], in_=y_[:, 0:gw]), [yb], [ob])
                    evc[0] += 1
            yield
            P.dma("sp" if i % 2 == 0 else "act", [(C.proj[i * 128:(i + 1) * 128, c0:c0 + cw], o_[:])], ob, False)

        drive([front(0)])
        for i in range(NT):
            gs = [back(i)]
            if i + 1 < NT:
                gs.append(front(i + 1))
            drive(gs)
        P.end()


def groups():
    g = [(0, 2, 0)]
    for j in range(16):
        g.append((2 + 4 * j, 4, 1 + j))
    return g


def rstd_from_ms(P, st, stb, n, eps_done=False):
    P.op("dve", lambda e: e.tensor_scalar(out=st[:, n:2 * n], in0=st[:, 0:n], scalar1=1.0, scalar2=EPS, op0=ALU.mult, op1=ALU.add), [stb], [stb])
    P.op("act", lambda e: e.activation(out=st[:, n:2 * n], in_=st[:, n:2 * n], func=AF.Ln), [stb], [stb])
    P.op("act", lambda e: e.activation(out=st[:, 2 * n:3 * n], in_=st[:, n:2 * n], func=AF.Exp, scale=-0.5), [stb], [stb])


def head_norm(P, src, srcb, H, dh, gain, gainb, out, outb, sq, sqb, st, stb, tmp, tmpb, eng2="dve"):
    W = H * dh
    P.op("act", lambda e: e.activation(out=sq[:, 0:W], in_=src, func=AF.Square, scale=float(dh) ** -0.5), [srcb], [sqb])
    P.op("dve", lambda e: e.tensor_reduce(out=st[:, 0:H], in_=sq[:, 0:W].rearrange("p (h d) -> p h d", h=H), op=ALU.add, axis=AX.X), [sqb], [stb])
    rstd_from_ms(P, st, stb, H)
    P.op("dve", lambda e: e.tensor_tensor(out=tmp[:, 0:W].rearrange("p (h d) -> p h d", h=H), in0=src.rearrange("p (h d) -> p h d", h=H),
                                          in1=st[:, 2 * H:3 * H].unsqueeze(2).to_broadcast([128, H, dh]), op=ALU.mult), [srcb, stb], [tmpb])
    P.op(eng2, lambda e: e.tensor_tensor(out=out, in0=tmp[:, 0:W].rearrange("p (h d) -> p h d", h=H),
                                         in1=gain[:, None, :].to_broadcast([128, H, dh]), op=ALU.mult), [tmpb, gainb], [outb])


def rope(P, x, xb, cs, csb, out, outb, tt, ttb, H):
    x1, x2 = x[:, :, 0:16], x[:, :, 16:32]
    cos = cs[:, None, 0:16].to_broadcast([128, H, 16])
    sin = cs[:, None, 16:32].to_broadcast([128, H, 16])
    P.op("dve", lambda e: e.tensor_tensor(out=tt[:, 0], in0=x1, in1=cos, op=ALU.mult), [xb, csb], [ttb])
    P.op("dve", lambda e: e.tensor_tensor(out=tt[:, 1], in0=x2, in1=sin, op=ALU.mult), [xb, csb], [ttb])
    P.op("dve", lambda e: e.tensor_tensor(out=tt[:, 2], in0=x1, in1=sin, op=ALU.mult), [xb, csb], [ttb])
    P.op("dve", lambda e: e.tensor_tensor(out=tt[:, 3], in0=x2, in1=cos, op=ALU.mult), [xb, csb], [ttb])
    P.op("dve", lambda e: e.tensor_tensor(out=out[:, :, 0:16], in0=tt[:, 0], in1=tt[:, 1], op=ALU.subtract), [ttb], [outb])
    P.op("dve", lambda e: e.tensor_tensor(out=out[:, :, 16:32], in0=tt[:, 2], in1=tt[:, 3], op=ALU.add), [ttb], [outb])


def load_cast(P, dst, dstb, src, shape, q="sp", eng="pool", name="wst"):
    s_, sb_ = P.sb(shape, F32, name)
    P.dma(q, [(s_[:], src)], sb_, True)
    P.op(eng, lambda e: e.tensor_copy(out=dst, in_=s_[:]), [sb_], [dstb])


def mla_prep(P, C, l):
    P.begin()
    ident, identb = P.sb([128, 128], BF16, "ident")
    P.dma("sp", [(ident[:], C.ident)], identb, True)
    wuq, wuqb = P.sb([128, 2, 768], BF16, "wuq")
    wukv, wukvb = P.sb([128, 1024], BF16, "wukv")
    load_cast(P, wuq[:], wuqb, C.mla_w_uq[l].rearrange("(k p) n -> p k n", p=128), [128, 2, 768], "sp", "dve", "wst1")
    load_cast(P, wukv[:], wukvb, C.mla_w_ukv[l], [128, 1024], "pool", "dve", "wst2")
    gc, gcb = P.sb([128, 384], F32, "gc")
    gq, gqb = P.sb([128, 96], F32, "gq")
    gk, gkb = P.sb([128, 96], F32, "gk")
    P.dma("sp", [(gc[:], C.mla_gc_rep[l])], gcb, True)
    P.dma("sp", [(gq[:], C.mla_gq_rep[l])], gqb, True)
    P.dma("sp", [(gk[:], C.mla_gk_rep[l])], gkb, True)
    seg = [P.sb([128, 416], F32, f"seg{i}") for i in range(2)]
    cs = [P.sb([128, 32], F32, f"cs{i}") for i in range(2)]
    sq, sqb = P.sb([128, 768], F32, "sq")
    st0, st0b = P.sb([128, 8], F32, "st0")
    stq, stqb = P.sb([128, 24], F32, "stq")
    stk, stkb = P.sb([128, 24], F32, "stk")
    cn, cnb = P.sb([128, 384], BF16, "cn")
    cT, cTb = P.sb([128, 3, 128], BF16, "cT")
    qf, qfb = P.sb([128, 768], F32, "qf")
    qn, qnb = P.sb([128, 8, 96], F32, "qn")
    kf, kfb = P.sb([128, 768], F32, "kf")
    kn, knb = P.sb([128, 8, 96], F32, "kn")
    tmp, tmpb = P.sb([128, 768], F32, "tmp")
    tmp2, tmp2b = P.sb([128, 768], F32, "tmp2")
    qb_, qbb = P.sb([128, 2, 8, 96], BF16, "qb")
    kb_, kbb = P.sb([128, 8, 96], BF16, "kb")
    tt, ttb = P.sb([128, 4, 8, 16], F32, "tt")
    tt2, tt2b = P.sb([128, 4, 8, 16], F32, "tt2")
    qst = [P.sb([96, 2, 8, 512], BF16, f"qst{i}") for i in range(2)]
    kst = [P.sb([96, 8, 512], BF16, f"kst{i}") for i in range(2)]
    vst = [P.sb([128, 8, 4, 66], BF16, f"vst{i}") for i in range(2)]
    for i in range(2):
        P.op("dve", lambda e, i=i: e.memset(vst[i][0][:], 1.0), [], [vst[i][1]])
    pTc, pTcb = P.ps([128, 8, 128], BF16, "pTc")
    pq = [P.ps([128, 512], F32, f"pq{i}") for i in range(2)]
    pk = [P.ps([128, 512], F32, f"pk{i}") for i in range(2)]
    pTq, pTqb = P.ps([128, 16, 128], BF16, "pTq")
    pTk, pTkb = P.ps([128, 8, 128], BF16, "pTk")
    for gi, (t0, nt, qt) in enumerate(groups()[:C.max_groups]):
        qs, qsb = qst[gi % 2]
        ks, ksb = kst[gi % 2]
        vs, vsb = vst[gi % 2]
        for s in range(nt):
            i = t0 + s
            sg, sgb = seg[i % 2]
            c_, c_b = cs[i % 2]
            P.dma("sp", [(sg[:], C.proj[i * 128:(i + 1) * 128, 0:416])], sgb, True)
            P.dma("sp", [(c_[:], C.rope_cs[i * 128:(i + 1) * 128, :])], c_b, True)
            P.op("act", lambda e, sg=sg: e.activation(out=sq[:, 0:256], in_=sg[:, 0:256], func=AF.Square, scale=1.0 / 16, accum_out=st0[:, 0:1]), [sgb], [sqb, st0b])
            P.op("act", lambda e, sg=sg: e.activation(out=sq[:, 256:384], in_=sg[:, 256:384], func=AF.Square, scale=128 ** -0.5, accum_out=st0[:, 1:2]), [sgb], [sqb, st0b])
            rstd_from_ms(P, st0, st0b, 2)
            P.op("dve", lambda e, sg=sg: e.scalar_tensor_tensor(out=cn[:, 0:256], in0=sg[:, 0:256], scalar=st0[:, 4:5], in1=gc[:, 0:256], op0=ALU.mult, op1=ALU.mult), [sgb, st0b, gcb], [cnb])
            P.op("dve", lambda e, sg=sg: e.scalar_tensor_tensor(out=cn[:, 256:384], in0=sg[:, 256:384], scalar=st0[:, 5:6], in1=gc[:, 256:384], op0=ALU.mult, op1=ALU.mult), [sgb, st0b, gcb], [cnb])
            if C.stage <= 1:
                continue
            for k in range(3):
                P.op("pe", lambda e, k=k: e.transpose(out=pTc[:, k, :], in_=cn[:, k * 128:(k + 1) * 128], identity=ident[:]), [cnb, identb], [pTcb])
            P.op("act", lambda e: e.activation(func=AF.Identity, out=cT[:], in_=pTc[:, 0:3, :]), [pTcb], [cTb])
            for k in range(2):
                P.op("pe", lambda e, k=k: e.matmul(pq[0][0][:], lhsT=cT[:, k, :], rhs=wuq[:, k, 0:512], start=(k == 0), stop=(k == 1)), [cTb, wuqb], [pq[0][1]])
            for k in range(2):
                P.op("pe", lambda e, k=k: e.matmul(pq[1][0][:, 0:256], lhsT=cT[:, k, :], rhs=wuq[:, k, 512:768], start=(k == 0), stop=(k == 1)), [cTb, wuqb], [pq[1][1]])
            for j in range(2):
                P.op("pe", lambda e, j=j: e.matmul(pk[j][0][:], lhsT=cT[:, 2, :], rhs=wukv[:, j * 512:(j + 1) * 512], start=True, stop=True), [cTb, wukvb], [pk[j][1]])
            P.op("act", lambda e: e.activation(func=AF.Identity, out=qf[:, 0:512], in_=pq[0][0][:]), [pq[0][1]], [qfb])
            P.op("act", lambda e: e.activation(func=AF.Identity, out=qf[:, 512:768], in_=pq[1][0][:, 0:256]), [pq[1][1]], [qfb])
            if C.stage <= 2.1:
                continue
            for j in range(2):
                if C.stage > 2.2:
                    P.op("dve", lambda e, j=j: e.tensor_copy(out=kf[:].rearrange("p (h d) -> p h d", h=8)[:, j * 4:(j + 1) * 4, 0:64],
                                                         in_=pk[j][0][:].rearrange("p (h d) -> p h d", h=4)[:, :, 0:64]), [pk[j][1]], [kfb])
                if C.stage > 2.4:
                    P.op("act", lambda e, j=j, vs=vs, s=s: e.activation(func=AF.Identity, out=vs[:, j * 4:(j + 1) * 4, s, 0:64],
                                                             in_=pk[j][0][:].rearrange("p (h d) -> p h d", h=4)[:, :, 64:128]), [pk[j][1]], [vsb])
            if C.stage > 2.6:
                P.op("dve", lambda e, sg=sg: e.tensor_copy(out=kf[:].rearrange("p (h d) -> p h d", h=8)[:, :, 64:96],
                                                        in_=sg[:, None, 384:416].to_broadcast([128, 8, 32])), [sgb], [kfb])
            if C.stage <= 3:
                continue
            head_norm(P, qf[:], qfb, 8, 96, gq, gqb, qn[:], qnb, sq, sqb, stq, stqb, tmp, tmpb)
            head_norm(P, kf[:], kfb, 8, 96, gk, gkb, kn[:], knb, sq, sqb, stk, stkb, tmp2, tmp2b)
            if C.stage <= 4:
                continue
            P.op("dve", lambda e: e.tensor_copy(out=qb_[:, 0], in_=qn[:]), [qnb], [qbb])
            P.op("dve", lambda e: e.tensor_copy(out=qb_[:, 1, :, 0:64], in_=qn[:, :, 0:64]), [qnb], [qbb])
            rope(P, qn[:, :, 64:96], qnb, c_, c_b, qb_[:, 1, :, 64:96], qbb, tt, ttb, 8)
            P.op("dve", lambda e: e.tensor_copy(out=kb_[:, :, 0:64], in_=kn[:, :, 0:64]), [knb], [kbb])
            rope(P, kn[:, :, 64:96], knb, c_, c_b, kb_[:, :, 64:96], kbb, tt2, tt2b, 8)
            if C.stage <= 5:
                continue
            for ver in range(2):
                for h in range(8):
                    P.op("pe", lambda e, ver=ver, h=h: e.transpose(out=pTq[0:96, ver * 8 + h, :], in_=qb_[:, ver, h, :], identity=ident[:]), [qbb, identb], [pTqb])
            for ver in range(2):
                P.op("act" if ver == 0 else "dve",
                     (lambda e, ver=ver, qs=qs, s=s: e.activation(func=AF.Identity, out=qs[:, ver, :, s * 128:(s + 1) * 128], in_=pTq[0:96, ver * 8:(ver + 1) * 8, :])) if ver == 0 else
                     (lambda e, ver=ver, qs=qs, s=s: e.tensor_copy(out=qs[:, ver, :, s * 128:(s + 1) * 128], in_=pTq[0:96, ver * 8:(ver + 1) * 8, :])),
                     [pTqb], [qsb])
            for h in range(8):
                P.op("pe", lambda e, h=h: e.transpose(out=pTk[0:96, h, :], in_=kb_[:, h, :], identity=ident[:]), [kbb, identb], [pTkb])
            P.op("dve", lambda e, ks=ks, s=s: e.tensor_copy(out=ks[:, :, s * 128:(s + 1) * 128], in_=pTk[0:96, :, :]), [pTkb], [ksb])
        n = nt * 128
        if C.stage <= 6:
            continue
        P.dma("pool", [(C.mla_qT[:, :, qt, :, 0:n].rearrange("v h p n -> p v h n"), qs[:, :, :, 0:n])], qsb, False)
        P.dma("pool", [(C.mla_kT[:, :, t0 * 128:t0 * 128 + n].rearrange("h p n -> p h n"), ks[:, :, 0:n])], ksb, False)
        P.dma("sp", [(C.mla_va[:, :, t0:t0 + nt, :].rearrange("h p k e -> p h k e"), vs[:, :, 0:nt, :])], vsb, False)
    P.end()


def attend_main(P, C, l, name, H, dk, scale, kT_ap, q_ap, va, ybr, chunks_of, bias_src=None, with_ctx_q=True):
    P.begin()
    identf, identfb = P.sb([128, 128], F32, "identf")
    P.dma("sp", [(identf[:], C.identf)], identfb, True)
    kT = [P.sb([dk, T], BF16, f"kT{i}") for i in range(2)]
    vv = [P.sb([128, NT, 66], BF16, f"vv{i}") for i in range(2)]
    nver = 2 if name == "mla" else 1
    qq = [[P.sb([dk, 512], BF16, f"q{v}_{i}") for v in range(nver)] for i in range(2)]
    pt = [P.sb([128, 512], BF16, f"pt{i}") for i in range(3)]
    sbias = [P.sb([128, 512], F32, f"sb{i}") for i in range(2)]
    nb_tiles = 8
    bt = [[P.sb([128, 512], F32, f"bt{j}_{i}") for i in range(nb_tiles)] for j in range(3)] if bias_src else None
    oT, oTb = P.sb([65, 512], F32, "oT")
    rec, recb = P.sb([128, 4], F32, "rec")
    ost = [P.sb([128, 4, 64], F32, f"ost{i}") for i in range(2)]
    pS = [P.ps([128, 512], F32, f"pS{i}") for i in range(3)]
    pO = [P.ps([128, 512], F32, f"pO{i}") for i in range(2)]
    pXf, pXb = P.ps([128, 512], F32, "pX")
    pX = pXf[:, 0:260].rearrange("p (t e) -> p t e", e=65)
    qts = ([0] if with_ctx_q else []) + list(range(1, 17))
    LA = 2
    groups_ = [(h, qt) for h in range(H) for qt in qts]
    items = []
    for gi, (h, qt) in enumerate(groups_):
        ch = chunks_of(qt)
        for n_, (kc, ver, bid) in enumerate(ch):
            items.append((gi, n_, len(ch), kc, ver, bid))
    cur_bias = [None, None, None]
    st = {"bmap": {}, "pend": []}
    loaded_heads = set()
    loaded_groups = set()

    def load_head(h):
        if h >= H or h in loaded_heads:
            return
        loaded_heads.add(h)
        k_, kb = kT[h % 2]
        v_, vb = vv[h % 2]
        P.dma("sp", [(k_[:, 0:T // 2], kT_ap(h)[:, 0:T // 2]), (k_[:, T // 2:T], kT_ap(h)[:, T // 2:T])], kb, True)
        P.dma("pool", [(v_[:], va[h])], vb, True)

    def load_group(gi):
        if gi >= len(groups_) or gi in loaded_groups:
            return
        loaded_groups.add(gi)
        h, qt = groups_[gi]
        nq = 256 if qt == 0 else 512
        for v in range(nver):
            P.dma("sp", [(qq[gi % 2][v][0][:, 0:nq], q_ap(h, qt, v)[:, 0:nq])], qq[gi % 2][v][1], True)
        if bias_src:
            ids = [b for (_, _, b) in chunks_of(qt) if b is not None]
            bm = {}
            if ids:
                key = (h, tuple(ids))
                slot = None
                for j in range(3):
                    if cur_bias[j] == key:
                        slot = j
                if slot is None:
                    slot = ids[0][0]
                    cur_bias[slot] = key
                    for n_, b in enumerate(ids):
                        P.dma("pool" if n_ % 2 else "sp", [(bt[slot][n_][0][:], bias_src(h, b))], bt[slot][n_][1], True)
                for n_, b in enumerate(ids):
                    bm[b] = bt[slot][n_]
            st["bmap"][gi] = bm

    def stage_a(n):
        gi, n_, nch, kc, ver, bid = items[n]
        h, qt = groups_[gi]
        if n_ == 0:
            load_head(h)
            load_group(gi)
            load_group(gi + 1)
            if gi + 1 < len(groups_) and groups_[gi + 1][0] != h:
                load_head(h + 1)
        nq = 256 if qt == 0 else 512
        k_, kb = kT[h % 2]
        s_, sb_ = pS[n % 3]
        p_, pb = pt[n % 3]
        q_, qb = qq[gi % 2][ver]
        P.op("pe", lambda e: e.matmul(s_[:, 0:nq], lhsT=k_[:, kc * 128:(kc + 1) * 128], rhs=q_[:, 0:nq], start=True, stop=True), [kb, qb], [sb_])
        if bid is None:
            P.op("act", lambda e: e.activation(out=p_[:, 0:nq], in_=s_[:, 0:nq], func=AF.Exp, scale=scale), [sb_], [pb])
        else:
            b_, bb = st["bmap"][gi][bid]
            x_, xb = sbias[n % 2]
            P.op("dve", lambda e: e.scalar_tensor_tensor(out=x_[:], in0=s_[:], scalar=scale, in1=b_[:], op0=ALU.mult, op1=ALU.add), [sb_, bb], [xb])
            P.op("act", lambda e: e.activation(out=p_[:], in_=x_[:], func=AF.Exp), [xb], [pb])

    def epilogue(gi):
        h, qt = groups_[gi]
        nq = 256 if qt == 0 else 512
        tok0 = 0 if qt == 0 else 256 + (qt - 1) * 512
        o_, ob = pO[gi % 2]
        nt4 = nq // 128
        P.op("act", lambda e: e.activation(func=AF.Identity, out=oT[:, 0:nq], in_=o_[0:65, 0:nq]), [ob], [oTb])
        for t in range(nt4):
            P.op("pe", lambda e, t=t: e.transpose(out=pX[:, t, :], in_=oT[:, t * 128:(t + 1) * 128], identity=identf[0:65, 0:65]), [oTb, identfb], [pXb])
        P.op("dve", lambda e: e.reciprocal(out=rec[:, 0:nt4], in_=pX[:, 0:nt4, 64]), [pXb], [recb])
        os_, osb = ost[gi % 2]
        P.op("dve", lambda e: e.tensor_tensor(out=os_[:, 0:nt4, :], in0=pX[:, 0:nt4, 0:64], in1=rec[:, 0:nt4].unsqueeze(2).to_broadcast([128, nt4, 64]), op=ALU.mult), [pXb, recb], [osb])
        P.dma("pool", [(ybr[tok0:tok0 + nq, h * 64:(h + 1) * 64].rearrange("(t p) d -> p t d", p=128), os_[:, 0:nt4, :])], osb, False)

    def stage_b(n):
        gi, n_, nch, kc, ver, bid = items[n]
        h, qt = groups_[gi]
        nq = 256 if qt == 0 else 512
        v_, vb = vv[h % 2]
        p_, pb = pt[n % 3]
        o_, ob = pO[gi % 2]
        P.op("pe", lambda e: e.matmul(o_[0:65, 0:nq], lhsT=v_[:, kc, 0:65], rhs=p_[:, 0:nq], start=(n_ == 0), stop=(n_ == nch - 1)), [vb, pb], [ob])
        if n_ == nch - 1:
            st["pend"].append((n + 3, gi))

    N = len(items)
    for n in range(N + LA):
        if n < N:
            stage_a(n)
        if n - LA >= 0:
            stage_b(n - LA)
        while st["pend"] and st["pend"][0][0] <= n:
            epilogue(st["pend"].pop(0)[1])
    while st["pend"]:
        epilogue(st["pend"].pop(0)[1])
    P.end()


def mla_main(P, C, l):
    def chunks_of(qt):
        if qt == 0:
            return [(0, 0, None), (1, 0, None)]
        return [(0, 0, None), (1, 0, None)] + [(kc, 1, None) for kc in range(2, NT)]
    attend_main(P, C, l, "mla", 8, 96, 96 ** -0.5, lambda h: C.mla_kT[h], lambda h, qt, v: C.mla_qT[v, h, qt], C.mla_va, C.ybr[0], chunks_of,
                with_ctx_q=(l == 0))


def nat_prep(P, C, l):
    P.begin()
    ident, identb = P.sb([128, 128], BF16, "ident")
    P.dma("sp", [(ident[:], C.ident)], identb, True)
    gq, gqb = P.sb([128, 64], F32, "gq")
    gk, gkb = P.sb([128, 64], F32, "gk")
    P.dma("sp", [(gq[:], C.nat_gq_rep[l])], gqb, True)
    P.dma("sp", [(gk[:], C.nat_gk_rep[l])], gkb, True)
    seg = [P.sb([128, 1536], F32, f"seg{i}") for i in range(2)]
    sq, sqb = P.sb([128, 512], F32, "sq")
    stq, stqb = P.sb([128, 24], F32, "stq")
    stk, stkb = P.sb([128, 24], F32, "stk")
    tmp, tmpb = P.sb([128, 512], F32, "tmp")
    tmp2, tmp2b = P.sb([128, 512], F32, "tmp2")
    qk, qkb = P.sb([128, 2, 8, 64], BF16, "qk")
    qst = [P.sb([128, 4, 512], BF16, f"qst{i}") for i in range(2)]
    kst = [P.sb([128, 4, 512], BF16, f"kst{i}") for i in range(2)]
    vst = [P.sb([128, 8, 4, 66], BF16, f"vst{i}") for i in range(2)]
    for i in range(2):
        P.op("dve", lambda e, i=i: e.memset(vst[i][0][:], 1.0), [], [vst[i][1]])
    pT, pTb = P.ps([128, 8, 128], BF16, "pT")
    for gi, (t0, nt, qt) in enumerate(groups()[:C.max_groups]):
        qs, qsb = qst[gi % 2]
        ks, ksb = kst[gi % 2]
        vs, vsb = vst[gi % 2]
        for s in range(nt):
            i = t0 + s
            sg, sgb = seg[i % 2]
            P.dma("sp" if i % 2 == 0 else "pool", [(sg[:], C.proj[i * 128:(i + 1) * 128, 416:1952])], sgb, True)
            head_norm(P, sg[:, 0:512], sgb, 8, 64, gq, gqb, qk[:, 0], qkb, sq, sqb, stq, stqb, tmp, tmpb)
            head_norm(P, sg[:, 512:1024], sgb, 8, 64, gk, gkb, qk[:, 1], qkb, sq, sqb, stk, stkb, tmp2, tmp2b)
            P.op("dve", lambda e, sg=sg, vs=vs, s=s: e.tensor_copy(out=vs[:, :, s, 0:64], in_=sg[:, 1024:1536].rearrange("p (h d) -> p h d", h=8)), [sgb], [vsb])
            for w in range(2):
                for pr in range(4):
                    P.op("pe", lambda e, w=w, pr=pr: e.transpose(out=pT[:, w * 4 + pr, :], in_=qk[:, w, 2 * pr:2 * pr + 2, :].rearrange("p h d -> p (h d)"), identity=ident[:]), [qkb, identb], [pTb])
            P.op("act", lambda e, qs=qs, s=s: e.activation(func=AF.Identity, out=qs[:, :, s * 128:(s + 1) * 128], in_=pT[:, 0:4, :]), [pTb], [qsb])
            P.op("act", lambda e, ks=ks, s=s: e.activation(func=AF.Identity, out=ks[:, :, s * 128:(s + 1) * 128], in_=pT[:, 4:8, :]), [pTb], [ksb])
        n = nt * 128
        P.dma("pool", [(C.nat_qT[:, qt, :, 0:n].rearrange("r p n -> p r n"), qs[:, :, 0:n])], qsb, False)
        P.dma("pool", [(C.nat_kT[:, :, t0 * 128:t0 * 128 + n].rearrange("r p n -> p r n"), ks[:, :, 0:n])], ksb, False)
        P.dma("sp", [(C.nat_va[:, :, t0:t0 + nt, :].rearrange("h p k e -> p h k e"), vs[:, :, 0:nt, :])], vsb, False)
    P.end()


def nat_block(j):
    kb = min(max(8 * j - 4, 0), 112)
    pat = 0 if j == 0 else (2 if j == 15 else 1)
    cs_ = range(0, 6) if j == 0 else (range(2, 8) if j == 15 else range(0, 8))
    return kb, pat, list(cs_)


def nat_main(P, C, l):
    def chunks_of(qt):
        if qt == 0:
            return [(0, 0, None), (1, 0, None)]
        kb, pat, cs_ = nat_block(qt - 1)
        return [(0, 0, None), (1, 0, None)] + [(2 + kb // 2 + c, 0, (pat, c)) for c in cs_]
    attend_main(P, C, l, "nat", 8, 64, 64 ** -0.5,
                lambda h: C.nat_kT[h // 2, (h % 2) * 64:(h % 2 + 1) * 64, :],
                lambda h, qt, v: C.nat_qT[h // 2, qt, (h % 2) * 64:(h % 2 + 1) * 64, :],
                C.nat_va, C.ybr[1], chunks_of, bias_src=lambda h, b: C.nat_bias[l, b[0], h, b[1]], with_ctx_q=(l == 0))


def nat_bias_host(rpb):
    L = rpb.shape[0]
    out = np.full((L, 3, 8, 8, 128, 512), -30000.0, np.float32)
    ck = np.arange(64)[:, None]
    cq = np.arange(64)[None, :]
    c0 = np.clip(cq - 8, 0, 48)
    col_ok = (ck >= c0) & (ck < c0 + 16)
    dc = np.clip(ck - cq, -15, 15) + 15
    for pat, j in enumerate((0, 1, 15)):
        kb, _, cs_ = nat_block(j)
        for c in cs_:
            for a in range(2):
                kr = kb + 2 * c + a
                for r in range(8):
                    qr = 8 * j + r
                    r0 = min(max(qr - 4, 0), 120)
                    if not (r0 <= kr < r0 + 8):
                        continue
                    dr = kr - qr + 7
                    blk = np.where(col_ok[None, None], rpb[:, :, dr][:, :, dc], np.float32(-30000.0))
                    out[:, pat, :, c, a * 64:(a + 1) * 64, r * 64:(r + 1) * 64] = blk
    return out


def scan_consts_host():
    j = np.arange(128)[:, None]
    i = np.arange(128)[None, :]
    same = (j // 32) == (i // 32)
    mf = (same & (j <= i)).astype(np.float32)
    mb = (same & (j >= i)).astype(np.float32)
    blk = same.astype(np.float32)
    ind = (j // 32 == np.arange(4)[None, :]).astype(np.float32)
    return np.ascontiguousarray(np.concatenate([mf, mb, blk, ind], 1))


def scan_mixer(P, C, l, kind):
    hg = kind == "hg"
    H, dk, dv = (4, 128, 128) if hg else (4, 64, 128)
    W = H * dk
    c0, cw = SEG["hg"] if hg else SEG["gla"]
    qscale = float(dk) ** -0.5
    br = 2 if hg else 3
    for d in range(2):
        P.begin()
        ident, identb = P.sb([128, 128], BF16, "ident")
        P.dma("sp", [(ident[:], C.ident)], identb, True)
        identf, identfb = P.sb([128, 128], F32, "identf")
        P.dma("sp", [(identf[:], C.identf)], identfb, True)
        sc, scb = P.sb([128, 388], F32, "sconst")
        P.dma("sp", [(sc[:], C.scan_consts)], scb, True)
        Md = sc[:, 0:128] if d == 0 else sc[:, 128:256]
        Blk = sc[:, 256:384]
        Ind = sc[:, 384:388]
        go, gob = P.sb([128, 128], F32, "go")
        P.dma("sp", [(go[:], (C.hg_go_rep if hg else C.gla_go_rep)[l])], gob, True)
        if hg:
            lbt, lbb = P.sb([128, 512], F32, "lbt")
            oml, omlb = P.sb([128, 512], F32, "oml")
            if l == 0:
                P.op("dve", lambda e: e.memset(lbt[:], 0.0), [], [lbb])
                P.op("dve", lambda e: e.memset(oml[:], 1.0), [], [omlb])
            else:
                zz, zzb = P.sb([128, 2, 512], F32, "zz")
                P.dma("sp", [(zz[:], C.hg_lb_rep[:, :, d, :].rearrange("l p n -> p l n"))], zzb, True)
                P.op("dve", lambda e: e.tensor_tensor(out=lbt[:], in0=zz[:, 1, :], in1=zz[:, 0, :], op=ALU.subtract), [zzb], [lbb])
                P.op("act", lambda e: e.activation(out=lbt[:], in_=lbt[:], func=AF.Sigmoid), [lbb], [lbb])
                P.op("dve", lambda e: e.tensor_scalar(out=oml[:], in0=lbt[:], scalar1=-1.0, scalar2=1.0, op0=ALU.mult, op1=ALU.add), [lbb], [omlb])
        else:
            w2, w2b = P.sb([16, 256], F32, "w2")
            b2, b2b = P.sb([128, 256], F32, "b2")
            P.dma("sp", [(w2[:], C.gla_w2[l, d])], w2b, True)
            P.dma("sp", [(b2[:], C.gla_b2_rep[l, :, d, :])], b2b, True)
            rT, rTb = P.sb([16, 128], F32, "rT")
            gx, gxb = P.sb([128, 256], F32, "gx")
        seg = [P.sb([128, cw], F32, f"seg{i}") for i in range(2)]
        onec, onecb = P.sb([128, 1], F32, "onec")
        P.op("dve", lambda e: e.memset(onec[:], 1.0), [], [onecb])
        qf, qfb = P.sb([128, W], F32, "qf")
        kf, kfb = P.sb([128, W], F32, "kf")
        la, lab = P.sb([128, W], F32, "la")
        vb2 = [P.sb([128, H * dv], BF16, f"vb{i}") for i in range(3)]
        sg1, sg1b = P.sb([128, W], F32, "sg1")
        bs, bsb = P.sb([128, W], F32, "bs")
        eb, ebb = P.sb([128, W], F32, "eb")
        enb, enbb = P.sb([128, W], F32, "enb")
        ebe, ebeb = P.sb([128, W], F32, "ebe")
        dcy2 = [P.sb([128, H * 4], F32, f"dcy{i}") for i in range(2)]
        qt_, qtb = P.sb([128, W], BF16, "qt")
        kt_, ktb = P.sb([128, W], BF16, "kt")
        kh2 = [P.sb([128, W], BF16, f"kh{i}") for i in range(2)]
        kpad2 = [P.sb([128, H, 4, dk], BF16, f"kpad{i}") for i in range(2)]
        qT, qTb = P.sb([128, H, 128], BF16, "qT")
        kT, kTb = P.sb([128, H, 128], BF16, "kT")
        qpad2 = [P.sb([128, H, 608], BF16, f"qpad{i}") for i in range(3)]
        for i_ in range(3):
            P.op("dve", lambda e, i_=i_: e.memset(qpad2[i_][0][:], 0.0), [], [qpad2[i_][1]])
        AT2 = [P.sb([128, H, 128], BF16, f"AT{i}") for i in range(3)]
        S, Sb_ = P.sb([128, H, dv], F32, "S")
        P.op("dve", lambda e: e.memset(S[:], 0.0), [], [Sb_])
        Sbf = [P.sb([128, H, 4, dv], BF16, f"Sbf{i}") for i in range(3)]
        ofw = [P.sb([128, 512], F32, f"ofw{i}") for i in range(3)]
        ost = [P.sb([128, 512], F32, f"ost{i}") for i in range(2)]
        osum, osumb = P.sb([128, 512], F32, "osum")
        sq, sqb = P.sb([128, 512], F32, "sq")
        stn, stnb = P.sb([128, 12], F32, "stn")
        tmpn, tmpnb = P.sb([128, 512], F32, "tmpn")
        pb, pbb = P.ps([128, 512], F32, "pb")
        pbe, pbeb = P.ps([128, 512], F32, "pbe")
        pbT, pbTb = P.ps([128, 512], F32, "pbT")
        pT, pTb = P.ps([128, 8, 128], BF16, "pT")
        pS, pSb = P.ps([128, 512], F32, "pS")
        pO, pOb = P.ps([128, 512], F32, "pO")
        pU = [P.ps([128, 512], F32, f"pU{i}") for i in range(2)]
        tiles = list(range(NT)) if d == 0 else [1, 0] + list(range(NT - 1, 1, -1))
        if C.max_groups:
            tiles = tiles[:C.max_groups] if d == 0 else ([1, 0] + list(range(C.max_groups - 1, 1, -1)))
        cseq = [0, 1, 2, 3] if d == 0 else [3, 2, 1, 0]
        P.op("dve", lambda e: e.memset(Sbf[0][0][:, :, cseq[0], :], 0.0), [], [Sbf[0][1]])
        def front(n, i):
            sg, sgb = seg[n % 2]
            vb, vbb = vb2[n % 3]
            dcy, dcyb = dcy2[n % 2]
            kh, khb = kh2[n % 2]
            qpad, qpadb = qpad2[n % 3]
            AT, ATb = AT2[n % 3]
            yield
            P.dma("sp" if n % 2 == 0 else "pool", [(sg[:], C.proj[i * 128:(i + 1) * 128, c0:c0 + cw])], sgb, True)
            if d == 1:
                of_, ofb = ofw[n % 3]
                yield
                P.dma("sp", [(of_[:], C.osc[i * 128:(i + 1) * 128, :])], ofb, True)
            if hg:
                fcol = 512 if d == 0 else 1024
                yield
                P.op("act", lambda e, sg=sg: e.activation(out=qf[:], in_=sg[:, 0:512], func=AF.Silu), [sgb], [qfb])
                yield
                P.op("act", lambda e, sg=sg: e.activation(out=kf[:], in_=sg[:, fcol:fcol + 512], func=AF.Sigmoid, scale=-1.0), [sgb], [kfb])
                if l != 0:
                    yield
                    P.op("dve", lambda e: e.tensor_tensor(out=kf[:], in0=kf[:], in1=oml[:], op=ALU.mult), [kfb, omlb], [kfb])
                yield
                P.op("act", lambda e: e.activation(out=la[:], in_=kf[:], func=AF.Ln, scale=-1.0, bias=onec[:, 0:1]), [kfb, onecb], [lab])
                yield
                P.op("dve", lambda e, sg=sg: e.tensor_copy(out=vb[:], in_=sg[:, 1536:2048]), [sgb], [vbb])
                qsrc, qsrcb, ksrc, ksrcb = qf[:], qfb, kf[:], kfb
            else:
                rc = 1024 + 16 * d
                yield
                P.op("pe", lambda e, sg=sg: e.transpose(out=pbT[0:16, 0:128], in_=sg[:, rc:rc + 16], identity=identf[:]), [sgb, identfb], [pbTb])
                yield
                P.op("act", lambda e: e.activation(func=AF.Identity, out=rT[:], in_=pbT[0:16, 0:128]), [pbTb], [rTb])
                yield
                P.op("pe", lambda e: e.matmul(pS[:, 0:256], lhsT=rT[:], rhs=w2[:], start=True, stop=True), [rTb, w2b], [pSb])
                yield
                P.op("dve", lambda e: e.tensor_tensor(out=gx[:], in0=pS[:, 0:256], in1=b2[:], op=ALU.add), [pSb, b2b], [gxb])
                yield
                P.op("act", lambda e: e.activation(out=gx[:], in_=gx[:], func=AF.Sigmoid), [gxb], [gxb])
                yield
                P.op("act", lambda e: e.activation(out=gx[:], in_=gx[:], func=AF.Ln), [gxb], [gxb])
                yield
                P.op("dve", lambda e: e.tensor_scalar(out=la[:], in0=gx[:], scalar1=1.0 / 16, scalar2=None, op0=ALU.mult), [gxb], [lab])
                yield
                P.op("dve", lambda e, sg=sg: e.tensor_copy(out=vb[:], in_=sg[:, 512:1024]), [sgb], [vbb])
                qsrc, qsrcb, ksrc, ksrcb = sg[:, 0:256], sgb, sg[:, 256:512], sgb
            yield
            P.op("pe", lambda e: e.matmul(pb[:, 0:W], lhsT=Md, rhs=la[:], start=True, stop=True), [scb, lab], [pbb])
            yield
            P.op("pe", lambda e: e.matmul(pbe[:, 0:W], lhsT=Blk, rhs=la[:], start=True, stop=True), [scb, lab], [pbeb])
            for h in range(H):
                yield
                P.op("pe", lambda e, h=h: e.matmul(pbT[0:dk, h * 4:(h + 1) * 4], lhsT=la[:, h * dk:(h + 1) * dk], rhs=Ind, start=True, stop=True), [lab, scb], [pbTb])
            yield
            P.op("act", lambda e: e.activation(out=eb[:], in_=pb[:, 0:W], func=AF.Exp), [pbb], [ebb])
            yield
            P.op("act", lambda e: e.activation(out=enb[:], in_=pb[:, 0:W], func=AF.Exp, scale=-1.0), [pbb], [enbb])
            yield
            P.op("act", lambda e: e.activation(out=ebe[:], in_=pbe[:, 0:W], func=AF.Exp), [pbeb], [ebeb])
            yield
            P.op("act", lambda e: e.activation(out=dcy[0:dk, :], in_=pbT[0:dk, 0:H * 4], func=AF.Exp), [pbTb], [dcyb])
            yield
            P.op("dve", lambda e, qsrc=qsrc: e.scalar_tensor_tensor(out=qt_[:], in0=qsrc, scalar=qscale, in1=eb[:], op0=ALU.mult, op1=ALU.mult), [qsrcb, ebb], [qtb])
            yield
            P.op("dve", lambda e, ksrc=ksrc: e.tensor_tensor(out=kt_[:], in0=ksrc, in1=enb[:], op=ALU.mult), [ksrcb, enbb], [ktb])
            yield
            P.op("dve", lambda e: e.tensor_tensor(out=kh[:], in0=kt_[:], in1=ebe[:], op=ALU.mult), [ktb, ebeb], [khb])
            for h in range(H):
                yield
                P.op("pe", lambda e, h=h: e.transpose(out=pT[0:dk, h, :], in_=qt_[:, h * dk:(h + 1) * dk], identity=ident[:]), [qtb, identb], [pTb])
                yield
                P.op("pe", lambda e, h=h: e.transpose(out=pT[0:dk, H + h, :], in_=kt_[:, h * dk:(h + 1) * dk], identity=ident[:]), [ktb, identb], [pTb])
            yield
            P.op("act", lambda e: e.activation(func=AF.Identity, out=qT[0:dk], in_=pT[0:dk, 0:H, :]), [pTb], [qTb])
            yield
            P.op("act", lambda e: e.activation(func=AF.Identity, out=kT[0:dk], in_=pT[0:dk, H:2 * H, :]), [pTb], [kTb])
            yield
            P.op("dve", lambda e: e.tensor_copy(out=qpad[0:dk, :, 96:608].rearrange("p h (c n) -> p h c n", n=128)[:, :, :, 0:32],
                                                in_=pT[0:dk, 0:H, :].rearrange("p h (c t) -> p h c t", t=32)), [pTb], [qpadb])
            for h in range(H):
                yield
                P.op("pe", lambda e, h=h: e.matmul(pS[:, h * 128:(h + 1) * 128], lhsT=kT[0:dk, h, :], rhs=qT[0:dk, h, :], start=True, stop=True), [kTb, qTb], [pSb])
            yield
            P.op("dve", lambda e: e.tensor_tensor(out=AT[:], in0=pS[:].rearrange("p (h n) -> p h n", h=H), in1=Md[:, None, :].to_broadcast([128, H, 128]), op=ALU.mult), [pSb, scb], [ATb])
        def back(n, i):
            vb, vbb = vb2[n % 3]
            dcy, dcyb = dcy2[n % 2]
            kh, khb = kh2[n % 2]
            qpad, qpadb = qpad2[n % 3]
            AT, ATb = AT2[n % 3]
            of_, ofb = ofw[n % 3]
            sb_cur, sb_curb = Sbf[n % 3]
            sb_nxt, sb_nxtb = Sbf[(n + 1) % 3]
            for ci, c in enumerate(cseq):
                u_, ub = pU[ci % 2]
                for h in range(H):
                    yield
                    P.op("pe", lambda e, h=h, c=c, u_=u_: e.matmul(u_[0:dk, h * dv:(h + 1) * dv], lhsT=kh[32 * c:32 * c + 32, h * dk:(h + 1) * dk], rhs=vb[32 * c:32 * c + 32, h * dv:(h + 1) * dv], start=True, stop=True, tile_position=(32 * c, 0)), [khb, vbb], [ub])
                for h in range(H):
                    yield
                    P.op("dve", lambda e, h=h, c=c, u_=u_: e.scalar_tensor_tensor(out=S[0:dk, h, :], in0=S[0:dk, h, :], scalar=dcy[0:dk, h * 4 + c:h * 4 + c + 1],
                                                                                 in1=u_[0:dk, h * dv:(h + 1) * dv], op0=ALU.mult, op1=ALU.add), [Sb_, dcyb, ub], [Sb_])
                if ci < 3:
                    yield
                    P.op("act", lambda e, ci=ci, sb_cur=sb_cur: e.activation(func=AF.Identity, out=sb_cur[0:dk, :, cseq[ci + 1], :], in_=S[0:dk]), [Sb_], [sb_curb])
                else:
                    yield
                    P.op("act", lambda e, sb_nxt=sb_nxt: e.activation(func=AF.Identity, out=sb_nxt[0:dk, :, cseq[0], :], in_=S[0:dk]), [Sb_], [sb_nxtb])

        def tail(n, i):
            vb, vbb = vb2[n % 3]
            qpad, qpadb = qpad2[n % 3]
            AT, ATb = AT2[n % 3]
            of_, ofb = ofw[n % 3]
            sb_cur, sb_curb = Sbf[n % 3]
            for h in range(H):
                yield
                P.op("pe", lambda e, h=h: e.matmul(pO[:, h * dv:(h + 1) * dv], lhsT=AT[:, h, :], rhs=vb[:, h * dv:(h + 1) * dv], start=True, stop=False), [ATb, vbb], [pOb])
                for ci, c in enumerate(cseq):
                    yield
                    P.op("pe", lambda e, h=h, c=c, ci=ci, sb_cur=sb_cur: e.matmul(pO[:, h * dv:(h + 1) * dv], lhsT=qpad[0:dk, h, 96 + 96 * c:224 + 96 * c], rhs=sb_cur[0:dk, h, c, :],
                                                                  start=False, stop=(ci == 3)), [qpadb, sb_curb], [pOb])
            if d == 0:
                o_, ob = ost[n % 2]
                yield
                P.op("act", lambda e, o_=o_: e.activation(func=AF.Identity, out=o_[:], in_=pO[:]), [pOb], [ob])
                yield
                P.dma("pool", [(C.osc[i * 128:(i + 1) * 128, :], o_[:])], ob, False)
            else:
                o_, ob = ost[n % 2]
                yield
                P.op("dve", lambda e, of_=of_: e.tensor_tensor(out=osum[:], in0=pO[:], in1=of_[:], op=ALU.add), [pOb, ofb], [osumb])
                head_norm(P, osum[:], osumb, 4, 128, go, gob, o_[:].rearrange("p (h d) -> p h d", h=4), ob, sq, sqb, stn, stnb, tmpn, tmpnb)
                yield
                P.dma("pool", [(C.ybr[br][i * 128:(i + 1) * 128, :], o_[:])], ob, False)

        def drive(gens):
            while gens:
                for g_ in list(gens):
                    try:
                        next(g_)
                    except StopIteration:
                        gens.remove(g_)

        drive([front(0, tiles[0])])
        for n, i in enumerate(tiles):
            gs = [back(n, i)]
            if n + 1 < len(tiles):
                gs.insert(0, front(n + 1, tiles[n + 1]))
            if n > 0:
                gs.append(tail(n - 1, tiles[n - 1]))
            drive(gs)
        drive([tail(len(tiles) - 1, tiles[-1])])
        P.end()


def merge_phase(P, C, l):
    tiles = list(range(NT)) if l == 0 else list(range(2, NT))
    if C.max_groups:
        tiles = tiles[:C.max_groups]
    P.begin()
    ident, identb = P.sb([128, 128], BF16, "ident")
    P.dma("sp", [(ident[:], C.ident)], identb, True)
    wbr, wbrb = P.sb([128, 4, 4, 1024], BF16, "wbr")
    wmg, wmgb = P.sb([128, 4, 8, 1024], BF16, "wmg")
    bmg, bmgb = P.sb([1, 4, 1024], BF16, "bmg")
    ones, onesb = P.sb([1, 128], BF16, "ones")
    P.op("dve", lambda e: e.memset(ones[:], 1.0), [], [onesb])
    stg = [P.sb([128, 1024], F32, f"wstg{i}") for i in range(2)]
    n = 0
    for br in range(4):
        for k in range(4):
            s_, sb_ = stg[n % 2]
            P.dma("sp" if n % 2 == 0 else "pool", [(s_[:], C.w_br[l, br, k * 128:(k + 1) * 128, :])], sb_, True)
            if n % 2 == 0:
                P.op("act", lambda e, s_=s_, br=br, k=k: e.activation(func=AF.Identity, out=wbr[:, br, k, :], in_=s_[:]), [sb_], [wbrb])
            else:
                P.op("dve", lambda e, s_=s_, br=br, k=k: e.tensor_copy(out=wbr[:, br, k, :], in_=s_[:]), [sb_], [wbrb])
            n += 1
        for k in range(8):
            s_, sb_ = stg[n % 2]
            P.dma("sp" if n % 2 == 0 else "pool", [(s_[:], C.w_merge[l, br, k * 128:(k + 1) * 128, :])], sb_, True)
            if n % 2 == 0:
                P.op("act", lambda e, s_=s_, br=br, k=k: e.activation(func=AF.Identity, out=wmg[:, br, k, :], in_=s_[:]), [sb_], [wmgb])
            else:
                P.op("dve", lambda e, s_=s_, br=br, k=k: e.tensor_copy(out=wmg[:, br, k, :], in_=s_[:]), [sb_], [wmgb])
            n += 1
    bst, bstb = P.sb([1, 4, 1024], F32, "bst")
    P.dma("sp", [(bst[:], C.b_merge[l:l + 1])], bstb, True)
    P.op("dve", lambda e: e.tensor_copy(out=bmg[:], in_=bst[:]), [bstb], [bmgb])
    yb = [[P.sb([128, 512], F32, f"y{br}_{i}") for br in range(4)] for i in range(2)]
    zt = [P.sb([128, 2048], F32, f"z{i}") for i in range(2)]
    hT = [P.sb([128, 8, 128], BF16, f"hT{i}") for i in range(2)]
    sz, szb = P.sb([128, 2048], F32, "sz")
    u, ub = P.sb([128, 2048], BF16, "u")
    uT, uTb = P.sb([128, 16, 128], BF16, "uT")
    g, gb = P.sb([128, 1024], F32, "g")
    tmp, tmpb = P.sb([128, 1024], F32, "tmp")
    accs = [P.sb([128, 1024], F32, f"acc{i}") for i in range(2)]
    accbf, accbfb = P.sb([128, 1024], BF16, "accbf")
    aT = [P.sb([128, 8, 128], BF16, f"aT{i}") for i in range(2)]
    pT = [P.ps([128, 8, 128], BF16, f"pT{i}") for i in range(2)]
    pP = [P.ps([128, 512], F32, f"pP{i}") for i in range(2)]
    pG = [P.ps([128, 512], F32, f"pG{i}") for i in range(2)]
    uTs = [(uT, uTb), P.sb([128, 16, 128], BF16, "uT2")]
    pA, pAb = P.ps([128, 8, 128], BF16, "pA")

    def front(n, i):
        ys = yb[n % 2]
        z_, zb = zt[n % 2]
        h_, hb = hT[n % 2]
        uT_, uT_b = uTs[n % 2]
        for br in range(4):
            yield
            P.dma("sp" if br % 2 == 0 else "pool", [(ys[br][0][:], C.ybr[br][i * 128:(i + 1) * 128, :])], ys[br][1], True)
        yield
        P.dma("sp", [(z_[:], C.proj[i * 128:(i + 1) * 128, 5056:7104])], zb, True)
        yield
        P.dma("pool", [(h_[:], C.hT[i])], hb, True)
        yield
        P.op("act", lambda e: e.activation(out=sz[:], in_=z_[:], func=AF.Silu), [zb], [szb])
        for br in range(4):
            yield
            P.op("dve", lambda e, br=br: e.tensor_tensor(out=u[:, br * 512:(br + 1) * 512], in0=ys[br][0][:], in1=sz[:, br * 512:(br + 1) * 512], op=ALU.mult),
                 [ys[br][1], szb], [ub])
        for half in range(2):
            p_, pb_ = pT[half]
            for k in range(8):
                yield
                P.op("pe", lambda e, p_=p_, k=k, half=half: e.transpose(out=p_[:, k, :], in_=u[:, (half * 8 + k) * 128:(half * 8 + k + 1) * 128], identity=ident[:]), [ub, identb], [pb_])
            yield
            P.op("act", lambda e, p_=p_, half=half: e.activation(func=AF.Identity, out=uT_[:, half * 8:(half + 1) * 8, :], in_=p_[:]), [pb_], [uT_b])

    def back(n, i):
        h_, hb = hT[n % 2]
        uT_, uT_b = uTs[n % 2]
        acc, accb = accs[n % 2]
        for br in range(4):
            for hf in range(2):
                pp, ppb = pP[hf]
                pg, pgb = pG[hf]
                for k in range(4):
                    yield
                    P.op("pe", lambda e, pp=pp, br=br, k=k, hf=hf: e.matmul(pp[:], lhsT=uT_[:, br * 4 + k, :], rhs=wbr[:, br, k, hf * 512:(hf + 1) * 512], start=(k == 0), stop=(k == 3)), [uT_b, wbrb], [ppb])
                for k in range(8):
                    yield
                    P.op("pe", lambda e, pg=pg, br=br, k=k, hf=hf: e.matmul(pg[:], lhsT=h_[:, k, :], rhs=wmg[:, br, k, hf * 512:(hf + 1) * 512], start=(k == 0), stop=False), [hb, wmgb], [pgb])
                yield
                P.op("pe", lambda e, pg=pg, br=br, hf=hf: e.matmul(pg[:], lhsT=ones[:], rhs=bmg[:, br, hf * 512:(hf + 1) * 512], start=False, stop=True), [onesb, bmgb], [pgb])
                yield
                P.op("act", lambda e, pg=pg, hf=hf: e.activation(out=g[:, hf * 512:(hf + 1) * 512], in_=pg[:], func=AF.Sigmoid), [pgb], [gb])
                if br == 0:
                    yield
                    P.op("dve", lambda e, pp=pp, hf=hf: e.tensor_tensor(out=acc[:, hf * 512:(hf + 1) * 512], in0=pp[:], in1=g[:, hf * 512:(hf + 1) * 512], op=ALU.mult), [ppb, gb], [accb])
                else:
                    yield
                    P.op("dve", lambda e, pp=pp, hf=hf: e.tensor_tensor(out=tmp[:, hf * 512:(hf + 1) * 512], in0=pp[:], in1=g[:, hf * 512:(hf + 1) * 512], op=ALU.mult), [ppb, gb], [tmpb])
                    yield
                    P.op("dve", lambda e, hf=hf: e.tensor_tensor(out=acc[:, hf * 512:(hf + 1) * 512], in0=acc[:, hf * 512:(hf + 1) * 512], in1=tmp[:, hf * 512:(hf + 1) * 512], op=ALU.add), [accb, tmpb], [accb])

    def tail(n, i):
        acc, accb = accs[n % 2]
        yield
        P.op("dve", lambda e: e.tensor_copy(out=accbf[:], in_=acc[:]), [accb], [accbfb])
        a_, ab = aT[n % 2]
        for k in range(8):
            yield
            P.op("pe", lambda e, k=k: e.transpose(out=pA[:, k, :], in_=accbf[:, k * 128:(k + 1) * 128], identity=ident[:]), [accbfb, identb], [pAb])
        yield
        P.op("act", lambda e: e.activation(func=AF.Identity, out=a_[:], in_=pA[:]), [pAb], [ab])
        yield
        P.dma("pool", [(C.accT[i], a_[:])], ab, False)

    drive([front(0, tiles[0])])
    for n, i in enumerate(tiles):
        gs = [back(n, i)]
        if n + 1 < len(tiles):
            gs.append(front(n + 1, tiles[n + 1]))
        if n > 0:
            gs.append(tail(n - 1, tiles[n - 1]))
        drive(gs)
    drive([tail(len(tiles) - 1, tiles[-1])])
    P.end()
    P.begin()
    identf, identfb = P.sb([128, 128], F32, "identf")
    P.dma("sp", [(identf[:], C.identf)], identfb, True)
    onef, onefb = P.sb([128, 128], F32, "onef")
    P.op("dve", lambda e: e.memset(onef[:], 1.0), [], [onefb])
    wo, wob = P.sb([128, 8, 1024], BF16, "wo")
    stg = [P.sb([128, 1024], F32, f"wstg{i}") for i in range(2)]
    for k in range(8):
        s_, sb_ = stg[k % 2]
        P.dma("sp" if k % 2 == 0 else "pool", [(s_[:], C.w_out[l, k * 128:(k + 1) * 128, :])], sb_, True)
        if k % 2 == 0:
            P.op("act", lambda e, s_=s_, k=k: e.activation(func=AF.Identity, out=wo[:, k, :], in_=s_[:]), [sb_], [wob])
        else:
            P.op("dve", lambda e, s_=s_, k=k: e.tensor_copy(out=wo[:, k, :], in_=s_[:]), [sb_], [wob])
    gtr = [P.sb([128, 1024], F32, f"gtr{v}") for v in range(2)]
    dg, dgb = P.sb([128, 128], F32, "dg")
    pR = [P.ps([128, 512], F32, f"pR{i}") for i in range(2)]
    for v in range(2):
        for j in range(8):
            P.op("dve", lambda e, v=v, j=j: e.tensor_scalar(out=dg[:], in0=identf[:], scalar1=C.modt[:, l, 2, j, v:v + 1], scalar2=None, op0=ALU.mult), [identfb, C.modtb], [dgb])
            P.op("pe", lambda e, j=j: e.matmul(pR[j // 4][0][:, (j % 4) * 128:(j % 4 + 1) * 128], lhsT=onef[:], rhs=dg[:], start=True, stop=True), [onefb, dgb], [pR[j // 4][1]])
        for hf in range(2):
            P.op("act", lambda e, v=v, hf=hf: e.activation(func=AF.Identity, out=gtr[v][0][:, hf * 512:(hf + 1) * 512], in_=pR[hf][0][:]), [pR[hf][1]], [gtr[v][1]])
    aT = [P.sb([128, 8, 128], BF16, f"aT{i}") for i in range(2)]
    xt = [P.sb([128, 1024], F32, f"xt{i}") for i in range(2)]
    xo = [P.sb([128, 1024], F32, f"xo{i}") for i in range(2)]
    t2, t2b = P.sb([128, 1024], F32, "t2")
    pO = [P.ps([128, 512], F32, f"pO{i}") for i in range(4)]
    for n, i in enumerate(tiles):
        v = 1 if i < 2 else 0
        a_, ab = aT[n % 2]
        x_, xb = xt[n % 2]
        o_, ob = xo[n % 2]
        P.dma("sp", [(a_[:], C.accT[i])], ab, True)
        P.dma("pool", [(x_[:], C.xsrc[l](i))], xb, True)
        for hf in range(2):
            po, pob = pO[(n % 2) * 2 + hf]
            for k in range(8):
                P.op("pe", lambda e, po=po, a_=a_, k=k, hf=hf: e.matmul(po[:], lhsT=a_[:, k, :], rhs=wo[:, k, hf * 512:(hf + 1) * 512], start=(k == 0), stop=(k == 7)), [ab, wob], [pob])
            P.op("dve", lambda e, po=po, hf=hf, v=v: e.tensor_tensor(out=t2[:, hf * 512:(hf + 1) * 512], in0=po[:], in1=gtr[v][0][:, hf * 512:(hf + 1) * 512], op=ALU.mult), [pob, gtr[v][1]], [t2b])
            P.op("dve", lambda e, hf=hf, x_=x_, o_=o_: e.tensor_tensor(out=o_[:, hf * 512:(hf + 1) * 512], in0=t2[:, hf * 512:(hf + 1) * 512], in1=x_[:, hf * 512:(hf + 1) * 512], op=ALU.add), [t2b, xb], [ob])
        dst = C.xres[i * 128:(i + 1) * 128, :] if l == 0 else C.y[(i - 2) * 128:(i - 1) * 128, :]
        P.dma("sp", [(dst, o_[:])], ob, False)
    P.end()

def build(layers=(0, 1), phases=("p0", "p1", "mla", "nat", "hg", "gla", "merge"), dbg=(), dbg_in=(), max_groups=None, stage=99):
    nc = bass.Bass("TRN2", target_bir_lowering=False)
    P = Prog(nc)
    C = Ctx()
    C.dbg = dbg
    C.max_groups = max_groups
    C.stage = stage

    def scr(name, shape, dt):
        kind = "ExternalOutput" if name in dbg else ("ExternalInput" if name in dbg_in else "Internal")
        return nc.dram_tensor(name, list(shape), dt, kind=kind).ap()
    C.x = dram_in(nc, "x", [NLAT, D])
    C.ctx = dram_in(nc, "ctx", [NCTX, D])
    C.cvec = dram_in(nc, "cvec", [128, 8, 2])
    C.ada_w = dram_in(nc, "ada_w", [2, D, 3 * D])
    C.ada_b_fm = dram_in(nc, "ada_b_fm", [128, 2, 24])
    C.norm_g_fm = dram_in(nc, "norm_g_fm", [128, 2, 8])
    C.w_in = dram_in(nc, "w_in", [2, D, IN_W])
    C.ident = dram_in(nc, "ident", [128, 128], BF16)
    C.identf = dram_in(nc, "identf", [128, 128])
    C.mla_w_uq = dram_in(nc, "mla_w_uq", [2, 256, 768])
    C.mla_w_ukv = dram_in(nc, "mla_w_ukv", [2, 128, 1024])
    C.mla_gc_rep = dram_in(nc, "mla_gc_rep", [2, 128, 384])
    C.mla_gq_rep = dram_in(nc, "mla_gq_rep", [2, 128, 96])
    C.mla_gk_rep = dram_in(nc, "mla_gk_rep", [2, 128, 96])
    C.rope_cs = dram_in(nc, "rope_cs", [T, 32])
    C.mla_qT = scr("mla_qT", [2, 8, 17, 96, 512], BF16)
    C.mla_kT = scr("mla_kT", [8, 96, T], BF16)
    C.mla_va = scr("mla_va", [8, 128, NT, 66], BF16)
    C.nat_gq_rep = dram_in(nc, "nat_gq_rep", [2, 128, 64])
    C.nat_gk_rep = dram_in(nc, "nat_gk_rep", [2, 128, 64])
    C.nat_bias = dram_in(nc, "nat_bias", [2, 3, 8, 8, 128, 512])
    C.nat_qT = scr("nat_qT", [4, 17, 128, 512], BF16)
    C.nat_kT = scr("nat_kT", [4, 128, T], BF16)
    C.nat_va = scr("nat_va", [8, 128, NT, 66], BF16)
    C.scan_consts = dram_in(nc, "scan_consts", [128, 388])
    C.hg_go_rep = dram_in(nc, "hg_go_rep", [2, 128, 128])
    C.gla_go_rep = dram_in(nc, "gla_go_rep", [2, 128, 128])
    C.hg_lb_rep = dram_in(nc, "hg_lb_rep", [2, 128, 2, 512])
    C.gla_w2 = dram_in(nc, "gla_w2", [2, 2, 16, 256])
    C.gla_b2_rep = dram_in(nc, "gla_b2_rep", [2, 128, 2, 256])
    C.w_br = dram_in(nc, "w_br", [2, 4, 512, D])
    C.w_merge = dram_in(nc, "w_merge", [2, 4, D, D])
    C.b_merge = dram_in(nc, "b_merge", [2, 4, D])
    C.w_out = dram_in(nc, "w_out", [2, D, D])
    C.accT = scr("accT", [NT, 128, 8, 128], BF16)
    C.osc = scr("osc", [T, 512], F32)
    C.ybr = [scr(f"ybr{i}", [T, 512], F32) for i in range(4)]
    C.y = nc.dram_tensor("y", [NLAT, D], F32, kind="ExternalOutput").ap()
    C.xres = scr("xres", [T, D], F32)
    C.hT = scr("hT", [NT, 128, 8, 128], BF16)
    C.proj = scr("proj", [T, IN_W], F32)
    C.modt, C.modtb = P.sb([128, 2, 3, 8, 2], F32, "modt", glob=True)
    C.A, C.Ab = P.sb([128, 2, 8, 2], F32, "Amod", glob=True)

    def src0(i):
        return C.ctx[i * 128:(i + 1) * 128, :] if i < 2 else C.x[(i - 2) * 128:(i - 1) * 128, :]

    def src1(i):
        return C.xres[i * 128:(i + 1) * 128, :]
    C.xsrc = [src0, src1]
    if "p0" in phases:
        phase0_adaln(P, C)
    for l in layers:
        if "p1" in phases:
            phase1_inproj(P, C, l)
        if "mla" in phases or "mlaprep" in phases:
            mla_prep(P, C, l)
        if "mla" in phases or "mlamain" in phases:
            mla_main(P, C, l)
        if "nat" in phases:
            nat_prep(P, C, l)
            nat_main(P, C, l)
        if "hg" in phases:
            scan_mixer(P, C, l, "hg")
        if "gla" in phases:
            scan_mixer(P, C, l, "gla")
        if "merge" in phases:
            merge_phase(P, C, l)
    P.finish()
    return nc


def rope_table():
    quarter = 8
    inv_freq = (10000.0 ** (-np.arange(quarter, dtype=np.float32) / quarter)).astype(np.float32)
    t = np.arange(NLAT)
    row = (t // 64).astype(np.float32)
    col = (t % 64).astype(np.float32)
    ang = np.concatenate([row[:, None] * inv_freq, col[:, None] * inv_freq], axis=-1).astype(np.float32)
    cs = np.zeros((T, 32), np.float32)
    cs[:NCTX, 0:16] = 1.0
    cs[NCTX:, 0:16] = np.cos(ang)
    cs[NCTX:, 16:32] = np.sin(ang)
    return cs


def rep(v):
    return np.ascontiguousarray(np.broadcast_to(v[:, None, :], (v.shape[0], 128, v.shape[1]))).astype(np.float32)


def host_inputs(inp, b):
    import ml_dtypes
    d = {}
    d["x"] = np.ascontiguousarray(inp["x"][b])
    d["ctx"] = np.ascontiguousarray(inp["ctx"][b])
    cv = np.stack([inp["c"][b], inp["c_ctx"]], -1)
    d["cvec"] = np.ascontiguousarray(cv.reshape(8, 128, 2).transpose(1, 0, 2))
    d["ada_w"] = inp["ada_w"]
    d["ada_b_fm"] = np.ascontiguousarray(inp["ada_b"].reshape(2, 24, 128).transpose(2, 0, 1))
    d["norm_g_fm"] = np.ascontiguousarray(inp["norm_g"].reshape(2, 8, 128).transpose(2, 0, 1))
    d["w_in"] = inp["w_in"]
    d["ident"] = np.eye(128).astype(ml_dtypes.bfloat16)
    d["identf"] = np.eye(128).astype(np.float32)
    d["mla_w_uq"] = inp["mla_w_uq"]
    d["mla_w_ukv"] = inp["mla_w_ukv"]
    d["mla_gc_rep"] = rep(np.concatenate([inp["mla_g_cq"], inp["mla_g_ckv"]], -1))
    d["mla_gq_rep"] = rep(inp["mla_g_q"])
    d["mla_gk_rep"] = rep(inp["mla_g_k"])
    d["rope_cs"] = rope_table()
    d["nat_gq_rep"] = rep(inp["nat_g_q"])
    d["nat_gk_rep"] = rep(inp["nat_g_k"])
    d["nat_bias"] = nat_bias_host(inp["nat_rpb"])
    d["scan_consts"] = scan_consts_host()
    d["hg_go_rep"] = rep(inp["hg_g_o"])
    d["gla_go_rep"] = rep(inp["gla_g_o"])
    d["hg_lb_rep"] = np.ascontiguousarray(np.broadcast_to(inp["hg_lb_logits"][:, None], (2, 128, 2, 512))).astype(np.float32)
    d["gla_w2"] = inp["gla_w2"]
    d["w_br"] = inp["w_br"]
    d["w_merge"] = inp["w_merge"]
    d["b_merge"] = inp["b_merge"]
    d["w_out"] = inp["w_out"]
    d["gla_b2_rep"] = np.ascontiguousarray(np.broadcast_to(inp["gla_b2"][:, None], (2, 128, 2, 256))).astype(np.float32)
    return d


def kernel(**inputs):
    inp = {k: np.asarray(v) for k, v in inputs.items()}
    nc = build(layers=(0, 1))
    in_maps = [host_inputs(inp, b) for b in range(4)]
    res = run_bass_kernel_spmd(nc, in_maps, core_ids=list(range(4)))
    return np.stack([r["y"] for r in res.results], 0).astype(np.float32)
```
